# Optimizing a Trainium2 kernel written in Bass

```python
import math
import jax, jax.numpy as jnp
from jax import lax
import numpy as np

D_MODEL = 1024
BATCH = 8
SEQ = 2048
DEPTH = 1
DEC_BATCH = 128
DEC_SEQ = 4
PAST_LEN = 2048
PAGE_SIZE = 128

N_HEADS_SB = 8
HEAD_DIM = 64
SB_WIDTH = N_HEADS_SB * HEAD_DIM
SB_SCALE = HEAD_DIM ** -0.5
SB_BIAS_INIT = -8.0
Q_BLOCK = 128
N_HEADS_HG = 8
HG_KEY_DIM = 64
HG_VAL_DIM = 64
HG_QK = N_HEADS_HG * HG_KEY_DIM
HG_V = N_HEADS_HG * HG_VAL_DIM
HG_CHUNK = 64
MIX_WIDTH = SB_WIDTH + HG_V
N_IN = 3 * SB_WIDTH + 2 * HG_QK + 2 * HG_V
SPLITS = (SB_WIDTH, 2 * SB_WIDTH, 3 * SB_WIDTH, 3 * SB_WIDTH + HG_QK,
          3 * SB_WIDTH + 2 * HG_QK, 3 * SB_WIDTH + 2 * HG_QK + HG_V)
D_FF = 4 * D_MODEL
N_ADA = 6
EPS = 1e-6
F32 = jnp.float32

kernel_name = "hymba_stickbreak_hgrn2_step"


def rms_norm(x, g):
    xf = x.astype(F32)
    y = xf * lax.rsqrt(jnp.mean(xf * xf, axis=-1, keepdims=True) + EPS)
    return (y * g.astype(F32)).astype(x.dtype)


def stick_breaking(q, k, v, q_pos, bias):
    z = jnp.einsum('bthd,bshd->bhts', q.astype(F32), k.astype(F32)) * SB_SCALE \
        + bias.astype(F32)[None, :, None, None]
    valid = jnp.arange(k.shape[1])[None, :] < q_pos[:, None]
    log_1m = jnp.where(valid, jax.nn.log_sigmoid(-z), 0.0)
    rev = lax.cumsum(log_1m, axis=3, reverse=True)
    tail = jnp.concatenate([rev[..., 1:], jnp.zeros_like(rev[..., :1])], axis=-1)
    w = jnp.where(valid, jnp.exp(jax.nn.log_sigmoid(z) + tail), 0.0)
    return jnp.einsum('bhts,bshd->bthd', w, v.astype(F32))


def hgrn2_chunked(q, k, v, log_f, s0):
    B, T, H, DK = q.shape
    C = math.gcd(T, HG_CHUNK)
    n = T // C
    rs = lambda a: a.reshape(B, n, C, H, a.shape[-1])
    q, k, v, g = rs(q), rs(k), rs(v), rs(log_f)
    b = jnp.cumsum(g, axis=2)
    b_last = b[:, :, -1:]
    q_dec = q * jnp.exp(b)
    k_dec = k * jnp.exp(-b)
    causal = jnp.tril(jnp.ones((C, C), dtype=bool))
    att = jnp.where(causal, jnp.einsum('bnthk,bnshk->bnhts', q_dec, k_dec), 0.0)
    o_intra = jnp.einsum('bnhts,bnshv->bnthv', att, v)
    d_state = jnp.einsum('bnshk,bnshv->bnhkv', k * jnp.exp(b_last - b), v)
    decay = jnp.exp(b_last[:, :, 0])

    def step(S, inp):
        dS, dec = inp
        return dec[..., None] * S + dS, S

    s_final, s_start = lax.scan(step, s0, (jnp.swapaxes(d_state, 0, 1), jnp.swapaxes(decay, 0, 1)))
    s_start = jnp.swapaxes(s_start, 0, 1)
    o_inter = jnp.einsum('bnthk,bnhkv->bnthv', q_dec, s_start)
    return (o_intra + o_inter).reshape(B, T, H, v.shape[-1]), s_final


def decoder_layer(x, c, w_ada, b_ada, n1, n2, w_in, qg, kg, sb_b, lb, hg_out_g, w_out, w_up, w_down,
                  k_past, v_past, s0):
    B, T, _ = x.shape
    ada = (jax.nn.silu(c) @ w_ada + b_ada)[:, None, :]
    sh1, sc1, g1, sh2, sc2, g2 = jnp.split(ada, N_ADA, axis=-1)
    h = rms_norm(x, n1) * (1 + sc1) + sh1
    proj = h @ w_in
    q_a, k_a, v_a, q_b, f_b, i_b, g_b = jnp.split(proj, SPLITS, axis=-1)
    q_a = rms_norm(q_a.reshape(B, T, N_HEADS_SB, HEAD_DIM), qg)
    k_a = rms_norm(k_a.reshape(B, T, N_HEADS_SB, HEAD_DIM), kg)
    v_a = v_a.reshape(B, T, N_HEADS_SB, HEAD_DIM)
    if k_past is None:
        nb = T // Q_BLOCK
        qb = jnp.swapaxes(q_a.reshape(B, nb, Q_BLOCK, N_HEADS_SB, HEAD_DIM), 0, 1)
        pos = jnp.arange(T).reshape(nb, Q_BLOCK)
        o_a = lax.map(lambda a: stick_breaking(a[0], k_a, v_a, a[1], sb_b), (qb, pos))
        o_a = jnp.swapaxes(o_a, 0, 1).reshape(B, T, SB_WIDTH)
    else:
        P = k_past.shape[1]
        k_all = jnp.concatenate([k_past.astype(k_a.dtype), k_a], axis=1)
        v_all = jnp.concatenate([v_past.astype(v_a.dtype), v_a], axis=1)
        o_a = stick_breaking(q_a, k_all, v_all, P + jnp.arange(T), sb_b).reshape(B, T, SB_WIDTH)
    f = lb + (1.0 - lb) * jax.nn.sigmoid(f_b.astype(F32))
    hq = jax.nn.silu(q_b.astype(F32)).reshape(B, T, N_HEADS_HG, HG_KEY_DIM)
    hk = (1.0 - f).reshape(B, T, N_HEADS_HG, HG_KEY_DIM)
    hlog_f = jnp.log(f).reshape(B, T, N_HEADS_HG, HG_KEY_DIM)
    hv = i_b.astype(F32).reshape(B, T, N_HEADS_HG, HG_VAL_DIM)
    o_b, s_new = hgrn2_chunked(hq, hk, hv, hlog_f, s0.astype(F32))
    o_b = rms_norm(o_b, hg_out_g) * jax.nn.silu(g_b.astype(F32)).reshape(B, T, N_HEADS_HG, HG_VAL_DIM)
    mix = jnp.concatenate([o_a.astype(x.dtype), o_b.reshape(B, T, HG_V).astype(x.dtype)], axis=-1) @ w_out
    x = x + g1 * mix
    h2 = rms_norm(x, n2) * (1 + sc2) + sh2
    x = x + g2 * (jnp.square(jax.nn.relu(h2 @ w_up)) @ w_down)
    return x, k_a, v_a, s_new.astype(x.dtype)


def setup_inputs(seed: int = 0) -> dict:
    key = jax.random.key(seed)
    ks = jax.random.split(key, 24)
    n_pages = PAST_LEN // PAGE_SIZE
    n_used = DEC_BATCH * n_pages
    n_phys = n_used + (n_used + 3) // 4
    nrm = lambda k, s, sc=1.0: jax.random.normal(k, s, F32) * sc
    perm = jax.random.permutation(ks[0], n_phys)
    page_table = perm[:n_used].reshape(DEC_BATCH, n_pages).astype(jnp.int32)
    return {
        "x_prompt": nrm(ks[1], (BATCH, SEQ, D_MODEL)),
        "x_sample": nrm(ks[2], (DEC_BATCH, DEC_SEQ, D_MODEL)),
        "cache_k": nrm(ks[3], (DEPTH, n_phys, PAGE_SIZE, N_HEADS_SB, HEAD_DIM)),
        "cache_v": nrm(ks[4], (DEPTH, n_phys, PAGE_SIZE, N_HEADS_SB, HEAD_DIM)),
        "state_hgrn": nrm(ks[5], (DEPTH, DEC_BATCH, N_HEADS_HG, HG_KEY_DIM, HG_VAL_DIM), 0.5),
        "page_table": page_table,
        "c_prompt": nrm(ks[6], (BATCH, D_MODEL)),
        "c_sample": nrm(ks[7], (DEC_BATCH, D_MODEL)),
        "w_ada": nrm(ks[8], (DEPTH, D_MODEL, N_ADA * D_MODEL), D_MODEL ** -0.5),
        "b_ada": nrm(ks[9], (DEPTH, N_ADA * D_MODEL), 0.02),
        "norm1_g": 1.0 + nrm(ks[10], (DEPTH, D_MODEL), 0.02),
        "norm2_g": 1.0 + nrm(ks[11], (DEPTH, D_MODEL), 0.02),
        "w_in": nrm(ks[12], (DEPTH, D_MODEL, N_IN), D_MODEL ** -0.5),
        "q_norm_g": 1.0 + nrm(ks[13], (DEPTH, HEAD_DIM), 0.02),
        "k_norm_g": 1.0 + nrm(ks[14], (DEPTH, HEAD_DIM), 0.02),
        "sb_bias": SB_BIAS_INIT + nrm(ks[20], (DEPTH, N_HEADS_SB), 0.1),
        "hg_lb_logits": nrm(ks[15], (DEPTH + 1, HG_QK), 0.1),
        "hg_out_g": 1.0 + nrm(ks[16], (DEPTH, HG_VAL_DIM), 0.02),
        "w_out": nrm(ks[17], (DEPTH, MIX_WIDTH, D_MODEL), MIX_WIDTH ** -0.5),
        "w_up": nrm(ks[18], (DEPTH, D_MODEL, D_FF), D_MODEL ** -0.5),
        "w_down": nrm(ks[19], (DEPTH, D_FF, D_MODEL), D_FF ** -0.5),
    }


def reference(x_prompt, x_sample, cache_k, cache_v, state_hgrn, page_table, c_prompt, c_sample,
              w_ada, b_ada, norm1_g, norm2_g, w_in, q_norm_g, k_norm_g, sb_bias, hg_lb_logits, hg_out_g,
              w_out, w_up, w_down):
    lb_sched = jnp.cumsum(jax.nn.softmax(hg_lb_logits.astype(F32), axis=0), axis=0)
    db = x_sample.shape[0]
    s0_prompt = jnp.zeros((x_prompt.shape[0], N_HEADS_HG, HG_KEY_DIM, HG_VAL_DIM), F32)
    y_p, y_s = x_prompt, x_sample
    kp, vp, ks_, vs_, sp, ss = [], [], [], [], [], []
    for l in range(DEPTH):
        lw = (w_ada[l], b_ada[l], norm1_g[l], norm2_g[l], w_in[l], q_norm_g[l], k_norm_g[l],
              sb_bias[l], lb_sched[l], hg_out_g[l], w_out[l], w_up[l], w_down[l])
        y_p, k_new, v_new, s_new = decoder_layer(y_p, c_prompt, *lw, None, None, s0_prompt)
        kp.append(k_new); vp.append(v_new); sp.append(s_new)
        k_past = cache_k[l][page_table].reshape(db, -1, N_HEADS_SB, HEAD_DIM)
        v_past = cache_v[l][page_table].reshape(db, -1, N_HEADS_SB, HEAD_DIM)
        y_s, k_new, v_new, s_new = decoder_layer(y_s, c_sample, *lw, k_past, v_past, state_hgrn[l])
        ks_.append(k_new); vs_.append(v_new); ss.append(s_new)
    return (y_p, y_s, jnp.stack(kp), jnp.stack(vp), jnp.stack(ks_), jnp.stack(vs_), jnp.stack(sp), jnp.stack(ss))
```

```python
import contextlib
import os
import numpy as np
import ml_dtypes
import concourse.bass as bass
import concourse.mybir as mybir
from concourse.bass_utils import run_bass_kernel_spmd

F32 = mybir.dt.float32
BF16 = mybir.dt.bfloat16
I32 = mybir.dt.int32
AF = mybir.ActivationFunctionType
ALU = mybir.AluOpType
AX = mybir.AxisListType

NCORES = 8
D = 1024
TP = 2048
NSEQ = 16
TS = 64
NTOK = TP + TS
NPG = 16
EPS = 1e-6
SCALE = 64 ** -0.5
NT = 17


def _consts():
    f = {}
    idx = np.arange(128)
    f["ident"] = np.eye(128, dtype=np.float32)
    s = idx[:, None]
    t = idx[None, :]
    same64 = (s // 64) == (t // 64)
    f["tri2"] = ((s <= t) & same64).astype(np.float32)
    f["tsu2"] = ((s > t) & same64).astype(np.float32)
    same4 = (s // 4) == (t // 4)
    f["tri4"] = ((s <= t) & same4).astype(np.float32)
    f["tsu4"] = ((s > t) & same4).astype(np.float32)
    ci = np.zeros((128, 2), np.float32)
    ci[:64, 0] = 1
    ci[64:, 1] = 1
    f["chunkind"] = ci
    si = np.zeros((128, 16), np.float32)
    for p in range(64):
        si[p, p // 4] = 1
    f["seqind"] = si
    mh = np.zeros((128, 64), np.float32)
    for p in range(128):
        mh[p, :] = (np.arange(64) >= (p % 64))
    f["maskH"] = mh
    ms = np.zeros((128, 64), np.float32)
    for p in range(64):
        for c in range(64):
            ms[p, c] = (p // 4 == c // 4) and (p % 4 <= c % 4)
    f["maskHS"] = ms
    ma = np.zeros((128, 64), np.float32)
    for p in range(64):
        for c in range(64):
            ma[p, c] = (p // 4 == c // 4) and (p % 4 < c % 4)
    f["maskS"] = ma
    smt = np.zeros((128, 16 * 64), np.float32)
    for b in range(16):
        smt[:, b * 64 + 4 * b: b * 64 + 4 * b + 4] = 1
    f["seqmaskT"] = smt
    md = np.zeros((128, 4 * 512), np.float32)
    for i in range(4):
        md[:, i * 512:(i + 1) * 512] = ((128 * i + idx[:, None]) < np.arange(512)[None, :])
    f["maskD"] = md
    f["uincl"] = (s >= t).astype(np.float32)
    f["lstrict"] = (s < t).astype(np.float32)
    io = np.zeros((128, 1), np.float32)
    off = {}
    cols = 0
    for k, v in f.items():
        off[k] = (cols, v.shape[1])
        cols += v.shape[1]
    bfk = ["maskD", "uincl", "lstrict", "seqmaskT"]
    off = {}
    cols = 0
    for k, v in f.items():
        if k in bfk:
            continue
        off[k] = (cols, v.shape[1])
        cols += v.shape[1]
    arr = np.concatenate([f[k] for k in f if k not in bfk], axis=1).astype(np.float32)
    boff = {}
    cols = 0
    for k in bfk:
        boff[k] = (cols, f[k].shape[1])
        cols += f[k].shape[1]
    barr = np.concatenate([f[k] for k in bfk], axis=1).astype(np.float32)
    return arr, off, barr, boff


_CARR, _COFF, _BARR, _BOFF = _consts()


class _Stop(Exception):
    pass


class Dep:
    __slots__ = ("name", "w", "re", "rd", "sem", "cnt")

    def __init__(self, name):
        self.name = name
        self.w = None
        self.re = {}
        self.rd = None
        self.sem = None
        self.cnt = 0


class Sched:
    def __init__(self, nc, es):
        self.nc = nc
        self.es = es
        self.eng = {}
        for name, h in [("pe", nc.tensor), ("act", nc.scalar), ("dve", nc.vector),
                        ("pool", nc.gpsimd), ("sp", nc.sync)]:
            sem = es.enter_context(nc.semaphore("sem_" + name))
            self.eng[name] = dict(h=h, sem=sem, cnt=0, waited={})
        self.dma_deps = []

    def _wait(self, ename, entry):
        E = self.eng[ename]
        if entry[0] == "e":
            _, src, idx = entry
            if src == ename and ename == "pe":
                return
            sem = self.eng[src]["sem"]
            val = idx
        else:
            _, dep, n = entry
            sem = dep.sem
            val = 16 * n
        key = id(sem)
        if E["waited"].get(key, 0) >= val:
            return
        E["waited"][key] = val
        E["h"].wait_ge(sem, val)

    def _deps(self, ename, reads, writes):
        for d in reads:
            if d.w is not None:
                self._wait(ename, d.w)
        for d in writes:
            if d.w is not None and not (d.w[0] == "e" and d.w[1] == ename):
                self._wait(ename, d.w)
            for src, idx in d.re.items():
                if src != ename:
                    self._wait(ename, ("e", src, idx))
            if d.rd is not None:
                self._wait(ename, d.rd)

    def op(self, ename, fn, reads=(), writes=(), inc=True):
        E = self.eng[ename]
        self._deps(ename, reads, writes)
        ins = fn(E["h"])
        if inc:
            E["cnt"] += 1
            idx = E["cnt"]
            ins.then_inc(E["sem"], 1)
        else:
            idx = E["cnt"] + 1
        for d in reads:
            d.re[ename] = idx
        for d in writes:
            d.w = ("e", ename, idx)
            d.re = {}
            d.rd = None
        return ins

    def dma(self, qname, fn, dep, direction, extra_reads=()):
        E = self.eng[qname]
        if dep.sem is None:
            dep.sem = self.es.enter_context(self.nc.semaphore("dsem_" + dep.name))
            self.dma_deps.append(dep)
        if direction == "in":
            self._deps(qname, list(extra_reads), [dep])
        else:
            self._deps(qname, list(extra_reads) + [dep], [])
        ins = fn(E["h"])
        dep.cnt += 1
        ins.then_inc(dep.sem, 16)
        ent = ("d", dep, dep.cnt)
        if direction == "in":
            dep.w = ent
            dep.re = {}
            dep.rd = None
        else:
            dep.rd = ent
        return ins

    def barrier(self):
        names = ["pe", "act", "dve", "pool", "sp"]
        for a in names:
            for b in names:
                if a != b and self.eng[b]["cnt"] > 0:
                    self._wait(a, ("e", b, self.eng[b]["cnt"]))
            for d in self.dma_deps:
                if d.cnt > 0:
                    self._wait(a, ("d", d, d.cnt))

    def finish(self):
        for d in self.dma_deps:
            if d.cnt > 0:
                self._wait("sp", ("d", d, d.cnt))
        for b in ["pe", "act", "dve", "pool"]:
            if self.eng[b]["cnt"] > 0:
                self._wait("sp", ("e", b, self.eng[b]["cnt"]))


def build(n_phys):
    nc = bass.Bass("TRN2", target_bir_lowering=False)
    es = contextlib.ExitStack()

    def din(name, shape, dt=F32):
        return nc.dram_tensor(name, shape, dt, kind="ExternalInput").ap()

    def dout(name, shape, dt=F32):
        return nc.dram_tensor(name, shape, dt, kind="ExternalOutput").ap()

    xp = din("xp", [TP, D])
    xs_d = din("xs", [TS, D])
    ck = din("ck", [n_phys * 128, 512])
    cv = din("cv", [n_phys * 128, 512])
    st0 = din("st0", [NSEQ, 8, 64, 64])
    pt = din("pt", [1, NSEQ * NPG], I32)
    cc = din("cc", [17, D])
    w_ada = din("w_ada", [D, 6 * D])
    vecs = din("vecs", [64, 128])
    bg = din("bg", [1, 2048])
    w_in = din("w_in", [D, 3584])
    qkg = din("qkg", [1, 3 * 512])
    sbb = din("sbb", [1, 8])
    lbl = din("lbl", [1, 1024])
    w_out = din("w_out", [D, D])
    w_up = din("w_up", [D, 4 * D])
    w_down = din("w_down", [4 * D, D])
    cstf = din("cstf", list(_CARR.shape))
    cstb = din("cstb", list(_BARR.shape))

    yp = dout("yp", [TP, D])
    ys = dout("ys", [TS, D])
    kp = dout("kp", [TP, 512])
    vp = dout("vp", [TP, 512])
    ks = dout("ks", [TS, 512])
    vs = dout("vs", [TS, 512])
    sp_o = dout("sp", [8, 64, 64])
    ss_o = dout("ss", [NSEQ, 8, 64, 64])

    S = Sched(nc, es)
    ucnt = [0]

    def sb(shape, dt=F32, side="left", stack=None, name=None):
        ucnt[0] += 1
        nm = (name or "t") + str(ucnt[0])
        return (stack or es).enter_context(nc.sbuf_tensor(nm, shape, dt, side=side))

    def dep(name="d"):
        ucnt[0] += 1
        return Dep(name + str(ucnt[0]))

    PS = [es.enter_context(nc.psum_tensor(f"ps{i}", [128, 512], F32)) for i in range(8)]
    PD = [dep(f"ps{i}_") for i in range(8)]

    CF = sb([128, _CARR.shape[1]], F32)
    dCF = dep("cf")
    S.dma("sp", lambda e: e.dma_start(out=CF[:], in_=cstf[:, :]), dCF, "in")

    def cf(key, rows=128, c0=0, c1=None):
        o, n = _COFF[key]
        c1 = n if c1 is None else c1
        return CF[0:rows, o + c0:o + c1]

    def load_bf_const(key, dst_ap, ddst, scratch, dscr):
        o, n = _BOFF[key]
        for a in range(0, n, 512):
            w_ = min(512, n - a)
            S.dma("sp", lambda e, a=a, w_=w_: e.dma_start(out=scratch[:, 0:w_], in_=cstb[:, o + a:o + a + w_]), dscr, "in")
            S.op("dve", lambda e, a=a, w_=w_: e.tensor_copy(out=dst_ap[:, a:a + w_], in_=scratch[:, 0:w_]),
                 reads=[dscr], writes=[ddst])

    ident = cf("ident")

    vecT = sb([128, 64], F32)
    dVec = dep("vecT")
    biasT = sb([128, 8], F32)
    dPar = dep("par")
    cT = sb([128, 8, 17], F32)
    dcT = dep("cT")
    mod = sb([128, 4, 8, 17], F32)
    dMod = dep("mod")
    p1s = contextlib.ExitStack()
    qkgB = sb([128, 1536], F32, stack=p1s)
    lbB = sb([128, 512], F32, stack=p1s)
    omlB = sb([128, 512], F32, stack=p1s)
    mixT = sb([128, 8, NTOK], BF16, side="right")
    dMixA = [dep("mixA") for _ in range(NT)]
    dMixB = [dep("mixB") for _ in range(NT)]

    def tile_rows(i):
        return 128 if i < 16 else 64

    def tile_cols(i):
        return (i * 128, i * 128 + tile_rows(i))

    def rsqrt_small(src_ap, dst_ap, n_scale, deps_r, dep_w, tmp_ap):
        S.op("dve", lambda e: e.tensor_scalar(out=tmp_ap, in0=src_ap, scalar1=n_scale, scalar2=EPS,
                                              op0=ALU.mult, op1=ALU.add), reads=deps_r, writes=[dep_w])
        S.op("act", lambda e: e.activation(out=tmp_ap, in_=tmp_ap, func=AF.Ln), reads=[dep_w], writes=[dep_w])
        S.op("act", lambda e: e.activation(out=dst_ap, in_=tmp_ap, func=AF.Exp, scale=-0.5),
             reads=[dep_w], writes=[dep_w])

    stop = int(os.environ.get("KSTOP", "99"))

    def chk(n):
        if stop == n:
            raise _Stop()

    try:
        with contextlib.ExitStack() as p0:
            wst = [sb([128, 8, 512], F32, stack=p0) for _ in range(2)]
            dW = [dep("wst") for _ in range(2)]
            cct = sb([17, D], F32, stack=p0)
            dcc = dep("cc")
            adaT = sb([128, 48, 17], F32, stack=p0)
            dAda = dep("adaT")
            vrow = sb([64, 128], F32, stack=p0)
            lraw = sb([128, 1024], F32, stack=p0)
            dtmp = dep("p0tmp")

            S.dma("sp", lambda e: e.dma_start(out=cct[:], in_=cc[:, :]), dcc, "in")
            S.dma("sp", lambda e: e.dma_start(out=vrow[:], in_=vecs[:, :]), dtmp, "in")
            S.dma("sp", lambda e: e.dma_start(out=qkgB[:], in_=qkg[0:1, :].to_broadcast([128, 1536])), dPar, "in")
            S.dma("sp", lambda e: e.dma_start(out=biasT[:], in_=sbb[0:1, :].to_broadcast([128, 8])), dPar, "in")
            S.dma("sp", lambda e: e.dma_start(out=lraw[:], in_=lbl[0:1, :].to_broadcast([128, 1024])), dtmp, "in")
            S.op("dve", lambda e: e.tensor_tensor(out=lraw[:, 0:512], in0=lraw[:, 0:512], in1=lraw[:, 512:1024],
                                                  op=ALU.subtract), reads=[dtmp], writes=[dtmp])
            S.op("act", lambda e: e.activation(out=lbB[:], in_=lraw[:, 0:512], func=AF.Sigmoid),
                 reads=[dtmp], writes=[dPar])
            S.op("dve", lambda e: e.tensor_scalar(out=omlB[:], in0=lbB[:], scalar1=-1.0, scalar2=1.0,
                                                  op0=ALU.mult, op1=ALU.add), reads=[dPar], writes=[dPar])
            S.op("act", lambda e: e.activation(out=cct[:], in_=cct[:], func=AF.Silu), reads=[dcc], writes=[dcc])
            for c in range(8):
                S.op("pe", lambda e, c=c: e.transpose(out=PS[0][:, c * 17:(c + 1) * 17],
                                                      in_=cct[0:17, c * 128:(c + 1) * 128], identity=cf("ident", 17, 0, 17)),
                     reads=[dcc, dCF], writes=[PD[0]], inc=(c == 7))
            S.op("dve", lambda e: e.tensor_copy(out=cT[:].rearrange("p c s -> p (c s)"), in_=PS[0][:, 0:136]),
                 reads=[PD[0]], writes=[dcT])
            S.op("pe", lambda e: e.transpose(out=PS[1][:, 0:64], in_=vrow[0:64, :], identity=cf("ident", 64, 0, 64)),
                 reads=[dtmp, dCF], writes=[PD[1]])
            S.op("dve", lambda e: e.tensor_copy(out=vecT[:], in_=PS[1][:, 0:64]), reads=[PD[1]], writes=[dVec])

            w_ada_v = w_ada.rearrange("(k p) n -> p k n", p=128)
            pcs = [0, 1, 2, 3, 6, 7, 8, 9]
            for n_, pc in enumerate(pcs):
                buf = n_ % 2
                S.dma("sp", lambda e, pc=pc, buf=buf: e.dma_start(out=wst[buf][:], in_=w_ada_v[:, :, pc * 512:(pc + 1) * 512]),
                      dW[buf], "in")
                for q in range(4):
                    chunk = (pc * 512) // 128 + q
                    bank = 2 + (q % 2)
                    for k in range(8):
                        S.op("pe", lambda e, k=k, q=q, bank=bank, buf=buf: e.matmul(
                            out=PS[bank][:, 0:17], lhsT=wst[buf][:, k, q * 128:(q + 1) * 128], rhs=cT[:, k, :],
                            start=(k == 0), stop=(k == 7)),
                             reads=[dcT, dW[buf]], writes=[PD[bank]], inc=(k == 7))
                    S.op("dve", lambda e, chunk=chunk, bank=bank: e.tensor_scalar(
                        out=adaT[:, chunk, :], in0=PS[bank][:, 0:17], scalar1=vecT[:, chunk:chunk + 1], scalar2=None,
                        op0=ALU.add), reads=[PD[bank], dVec], writes=[dAda])
            for (mi, scc, shc, nof) in ((0, 8, 0, 48), (2, 32, 24, 56)):
                S.op("dve", lambda e, mi=mi, scc=scc, nof=nof: e.scalar_tensor_tensor(
                    out=mod[:, mi, :, :], in0=adaT[:, scc:scc + 8, :], scalar=1.0,
                    in1=vecT[:, nof:nof + 8].unsqueeze(2).to_broadcast([128, 8, 17]), op0=ALU.add, op1=ALU.mult),
                     reads=[dAda, dVec], writes=[dMod])
                S.op("dve", lambda e, mi=mi, shc=shc: e.tensor_copy(out=mod[:, mi + 1, :, :], in_=adaT[:, shc:shc + 8, :]),
                     reads=[dAda], writes=[dMod])
            S.barrier()
            chk(0)

        def load_x_tile(i, xt, dxt, q="sp"):
            T = tile_rows(i)
            src = xp[i * 128:(i + 1) * 128, :] if i < 16 else xs_d[:, :]
            S.dma(q, lambda e: e.dma_start(out=xt[0:T, :], in_=src), dxt, "in")

        def norm_transpose(i, src_ap, dsrc, mi, hT_ap, dhT, work, dwork, banks):
            T = tile_rows(i)
            xn, sqj, st4 = work
            S.op("act", lambda e: e.activation(out=sqj[0:T, :], in_=src_ap, func=AF.Square, accum_out=st4[0:T, 0:1]),
                 reads=[dsrc], writes=[dwork])
            rsqrt_small(st4[0:T, 0:1], st4[0:T, 2:3], 1.0 / D, [dwork], dwork, st4[0:T, 1:2])
            S.op("dve", lambda e: e.tensor_scalar(out=xn[0:T, :], in0=src_ap, scalar1=st4[0:T, 2:3], scalar2=None,
                                                  op0=ALU.mult), reads=[dsrc, dwork], writes=[dwork])
            nb, tpb = (1, 128) if i < 16 else (16, 4)
            c0 = 16 if i < 16 else 0
            for half in range(2):
                bank = banks[half]
                for c4 in range(4):
                    c = half * 4 + c4
                    S.op("pe", lambda e, c=c, c4=c4, bank=bank: e.transpose(
                        out=PS[bank][:, c4 * 128:c4 * 128 + T], in_=xn[0:T, c * 128:(c + 1) * 128],
                        identity=cf("ident", T, 0, T)), reads=[dwork, dCF], writes=[PD[bank]], inc=(c4 == 3))
                pv = PS[bank][:, :].rearrange("p (c t) -> p c t", c=4)[:, :, 0:T].rearrange("p c (b t) -> p c b t", t=tpb)
                sc = mod[:, mi, half * 4:half * 4 + 4, c0:c0 + nb].unsqueeze(3).to_broadcast([128, 4, nb, tpb])
                sh = mod[:, mi + 1, half * 4:half * 4 + 4, c0:c0 + nb].unsqueeze(3).to_broadcast([128, 4, nb, tpb])
                tmpm = xn[:, half * 512:(half + 1) * 512].rearrange("p (c t) -> p c t", c=4)[:, :, 0:T].rearrange(
                    "p c (b t) -> p c b t", t=tpb)
                tm = sqj[:, half * 512:(half + 1) * 512].rearrange("p (c t) -> p c t", c=4)[:, :, 0:T].rearrange(
                    "p c (b t) -> p c b t", t=tpb)
                S.op("dve", lambda e, pv=pv, sc=sc, tm=tm: e.tensor_tensor(out=tm, in0=pv, in1=sc, op=ALU.mult),
                     reads=[PD[bank], dMod], writes=[dwork])
                ho = hT_ap[:, half * 4:half * 4 + 4, :].rearrange("p c (b t) -> p c b t", t=tpb)
                S.op("dve", lambda e, ho=ho, sh=sh, tm=tm: e.tensor_tensor(out=ho, in0=tm, in1=sh, op=ALU.add),
                     reads=[dwork, dMod], writes=[dhT])

        def load_weight_piece(src_ap, dst_bf_ap, wst, dW, buf, ddst, q="sp", cast_eng="pool"):
            S.dma(q, lambda e: e.dma_start(out=wst[buf][:], in_=src_ap), dW[buf], "in")
            S.op(cast_eng, lambda e: e.tensor_copy(out=dst_bf_ap, in_=wst[buf][:]), reads=[dW[buf]], writes=[ddst])

        with contextlib.ExitStack() as p1:
            hT = sb([128, 8, NTOK], BF16, stack=p1)
            dhT = [dep("hT") for _ in range(NT)]
            with contextlib.ExitStack() as p1n:
                xt = [sb([128, D], F32, stack=p1n) for _ in range(2)]
                dxt = [dep("xt") for _ in range(2)]
                wk = [(sb([128, D], F32, stack=p1n), sb([128, D], F32, stack=p1n), sb([128, 4], F32, stack=p1n)) for _ in range(2)]
                dwk = [dep("wk") for _ in range(2)]
                for i in [int(x) for x in os.environ["KTILES"].split(",")] if "KTILES" in os.environ else range(NT):
                    b = i % 2
                    load_x_tile(i, xt[b], dxt[b])
                    c0, c1 = tile_cols(i)
                    norm_transpose(i, xt[b][0:tile_rows(i), :], dxt[b], 0, hT[:, :, c0:c1], dhT[i], wk[b], dwk[b],
                                   (0, 1) if b == 0 else (2, 3))
                S.barrier()
                chk(1)

            w_in_v = w_in.rearrange("(k p) n -> p k n", p=128)

            with contextlib.ExitStack() as pb:
                whg = sb([128, 8, 2048], BF16, stack=pb)
                dwhg = [dep("whg") for _ in range(4)]
                with contextlib.ExitStack() as pw:
                    wst = [sb([128, 8, 512], F32, stack=pw) for _ in range(2)]
                    dW = [dep("wst") for _ in range(2)]
                    for g in range(4):
                        load_weight_piece(w_in_v[:, :, 1536 + g * 512:1536 + (g + 1) * 512], whg[:, :, g * 512:(g + 1) * 512],
                                          wst, dW, g % 2, dwhg[g])
                    S.barrier()

                def wt(shape, dt=F32):
                    return sb(shape, dt, stack=pb)

                AB = (4, 2)
                XB = (6, 3)

                def hsel(ap, h2):
                    return ap.rearrange("p (j a t) -> p j a t", j=4, a=2)[:, :, h2, :]

                hq = wt([128, 512]); ff = wt([128, 512]); logf = wt([128, 512]); omf = wt([128, 512])
                hv = wt([128, 512], BF16); sg = wt([128, 512]); eb = wt([128, 512]); enb = wt([128, 512])
                ec = wt([128, 512]); kk = wt([128, 512], BF16)
                qdT = wt([128, 4, 128], BF16); kdT = wt([128, 4, 128], BF16)
                attm = wt([128, 512], BF16); oo = wt([128, 512]); osq = wt([128, 512])
                st8 = wt([128, 24]); dec = wt([128, 4, 16])
                Sst = wt([128, 4, 64]); Sbf = wt([128, 4, 64], BF16)
                dE = dep("hgE")
                dT_ = dep("hgT")
                dA = dep("attm"); dO = dep("oo"); dS_ = dep("S"); dSb = dep("Sbf"); dDec = dep("dec")
                S.op("dve", lambda e: e.memset(Sst[:], 0.0), writes=[dS_])
                S.op("dve", lambda e: e.memset(Sbf[:], 0.0), writes=[dSb])
                S0t = [wt([128, 16, 64]) for _ in range(2)]
                S0bt = wt([128, 16, 64], BF16)
                qdTm = wt([128, 4, 16, 64], BF16); hvm = wt([64, 16, 128], BF16)
                smT = wt([128, 1024], BF16)
                dS0 = [dep("S0") for _ in range(2)]; dS0b = dep("S0b"); dqm = dep("qdTm"); dhvm = dep("hvm"); dsm = dep("smT")
                st0_v = st0.rearrange("b (j h) k v -> (h k) j b v", h=2)
                ss_v = ss_o.rearrange("b (j h) k v -> (h k) j b v", h=2)
                load_bf_const("seqmaskT", smT, dsm, hq, dE)

                for i in range(NT):
                    T = tile_rows(i)
                    c0, c1 = tile_cols(i)
                    samp = (i == 16)
                    tri = cf("tri4" if samp else "tri2", T, 0, T)
                    tsu = cf("tsu4" if samp else "tsu2", T, 0, T)
                    ncn = 16 if samp else 2
                    ind = cf("seqind" if samp else "chunkind", T)
                    def proj(g, bank):
                        for k in range(8):
                            S.op("pe", lambda e, k=k: e.matmul(out=PS[bank][0:T, :], lhsT=hT[:, k, c0:c1],
                                                               rhs=whg[:, k, g * 512:(g + 1) * 512],
                                                               start=(k == 0), stop=(k == 7)),
                                 reads=[dhT[i], dwhg[g]], writes=[PD[bank]], inc=(k == 7))
                    proj(0, 2)
                    S.op("act", lambda e: e.activation(out=hq[0:T, :], in_=PS[2][0:T, :], func=AF.Silu),
                         reads=[PD[2]], writes=[dE])
                    proj(1, 3)
                    S.op("act", lambda e: e.activation(out=ff[0:T, :], in_=PS[3][0:T, :], func=AF.Sigmoid),
                         reads=[PD[3]], writes=[dE])
                    S.op("dve", lambda e: e.tensor_tensor(out=ff[0:T, :], in0=ff[0:T, :], in1=omlB[0:T, :], op=ALU.mult),
                         reads=[dE, dPar], writes=[dE])
                    S.op("dve", lambda e: e.tensor_tensor(out=ff[0:T, :], in0=ff[0:T, :], in1=lbB[0:T, :], op=ALU.add),
                         reads=[dE, dPar], writes=[dE])
                    S.op("act", lambda e: e.activation(out=logf[0:T, :], in_=ff[0:T, :], func=AF.Ln),
                         reads=[dE], writes=[dE])
                    S.op("dve", lambda e: e.tensor_scalar(out=omf[0:T, :], in0=ff[0:T, :], scalar1=-1.0, scalar2=1.0,
                                                          op0=ALU.mult, op1=ALU.add), reads=[dE], writes=[dE])
                    proj(2, 2)
                    S.op("act", lambda e: e.activation(out=hv[0:T, :], in_=PS[2][0:T, :], func=AF.Copy),
                         reads=[PD[2]], writes=[dE])
                    proj(3, 3)
                    S.op("act", lambda e: e.activation(out=sg[0:T, :], in_=PS[3][0:T, :], func=AF.Silu),
                         reads=[PD[3]], writes=[dE])
                    S.op("pe", lambda e: e.matmul(out=PS[0][0:T, :], lhsT=tri, rhs=logf[0:T, :], start=True, stop=True),
                         reads=[dE, dCF], writes=[PD[0]])
                    S.op("pe", lambda e: e.matmul(out=PS[1][0:T, :], lhsT=tsu, rhs=logf[0:T, :], start=True, stop=True),
                         reads=[dE, dCF], writes=[PD[1]])
                    for j in range(4):
                        S.op("pe", lambda e, j=j: e.matmul(out=PS[7][:, 256 + j * ncn:256 + (j + 1) * ncn],
                                                           lhsT=logf[0:T, j * 128:(j + 1) * 128], rhs=ind,
                                                           start=True, stop=True),
                             reads=[dE, dCF], writes=[PD[7]], inc=(j == 3))
                    S.op("act", lambda e: e.activation(out=eb[0:T, :], in_=PS[0][0:T, :], func=AF.Exp),
                         reads=[PD[0]], writes=[dE])
                    S.op("act", lambda e: e.activation(out=enb[0:T, :], in_=PS[0][0:T, :], func=AF.Exp, scale=-1.0),
                         reads=[PD[0]], writes=[dE])
                    S.op("act", lambda e: e.activation(out=ec[0:T, :], in_=PS[1][0:T, :], func=AF.Exp),
                         reads=[PD[1]], writes=[dE])
                    S.op("act", lambda e: e.activation(out=dec[:, :, 0:ncn],
                                                       in_=PS[7][:, 256:256 + 4 * ncn].rearrange("p (j c) -> p j c", j=4),
                                                       func=AF.Exp), reads=[PD[7]], writes=[dDec])
                    S.op("dve", lambda e: e.tensor_tensor(out=eb[0:T, :], in0=hq[0:T, :], in1=eb[0:T, :], op=ALU.mult),
                         reads=[dE], writes=[dE])
                    S.op("dve", lambda e: e.tensor_tensor(out=enb[0:T, :], in0=omf[0:T, :], in1=enb[0:T, :], op=ALU.mult),
                         reads=[dE], writes=[dE])
                    S.op("dve", lambda e: e.tensor_tensor(out=kk[0:T, :], in0=omf[0:T, :], in1=ec[0:T, :], op=ALU.mult),
                         reads=[dE], writes=[dE])
                    for (src, dst, bank) in ((eb, qdT, 0), (enb, kdT, 1)):
                        for j in range(4):
                            S.op("pe", lambda e, j=j, src=src, bank=bank: e.transpose(
                                out=PS[bank][:, j * 128:j * 128 + T], in_=src[0:T, j * 128:(j + 1) * 128],
                                identity=cf("ident", T, 0, T)), reads=[dE, dCF], writes=[PD[bank]], inc=(j == 3))
                        S.op("act", lambda e, dst=dst, bank=bank: e.activation(
                            out=dst[:, :, 0:T], in_=PS[bank][:, :].rearrange("p (j t) -> p j t", j=4)[:, :, 0:T],
                            func=AF.Copy), reads=[PD[bank]], writes=[dT_])

                    if not samp:
                        for c in range(2):
                            cp = 64 * c
                            for h in range(8):
                                j, h2 = h // 2, h % 2
                                hp = 64 * h2
                                ab = AB[h2]
                                S.op("pe", lambda e, h=h, j=j, hp=hp, cp=cp, ab=ab: e.matmul(
                                    out=PS[ab][cp:cp + 64, h * 64:(h + 1) * 64], lhsT=kdT[hp:hp + 64, j, cp:cp + 64],
                                    rhs=qdT[hp:hp + 64, j, cp:cp + 64], start=True, stop=True),
                                     reads=[dT_], writes=[PD[ab]], inc=(h >= 6))
                            for h2 in range(2):
                                ab = AB[h2]
                                S.op("dve", lambda e, cp=cp, ab=ab, h2=h2: e.tensor_tensor(
                                    out=hsel(attm[cp:cp + 64, :], h2), in0=hsel(PS[ab][cp:cp + 64, :], h2),
                                    in1=cf("maskH")[cp:cp + 64, :].unsqueeze(1).to_broadcast([64, 4, 64]), op=ALU.mult),
                                     reads=[PD[ab], dCF], writes=[dA])
                            for h in range(8):
                                j, h2 = h // 2, h % 2
                                hp = 64 * h2
                                hs = slice(h * 64, (h + 1) * 64)
                                S.op("pe", lambda e, hs=hs, cp=cp: e.matmul(
                                    out=PS[5][cp:cp + 64, hs], lhsT=attm[cp:cp + 64, hs], rhs=hv[cp:cp + 64, hs],
                                    start=True, stop=True), reads=[dA, dE], writes=[PD[5]], inc=False)
                                xb = XB[h2]
                                S.op("pe", lambda e, hs=hs, cp=cp, hp=hp, j=j, xb=xb: e.matmul(
                                    out=PS[xb][cp:cp + 64, hs], lhsT=qdT[hp:hp + 64, j, cp:cp + 64], rhs=Sbf[hp:hp + 64, j, :],
                                    start=True, stop=True), reads=[dT_, dSb], writes=[PD[xb]], inc=False)
                                S.op("pe", lambda e, hs=hs, cp=cp, hp=hp, j=j: e.matmul(
                                    out=PS[7][hp:hp + 64, j * 64:(j + 1) * 64], lhsT=kk[cp:cp + 64, hs], rhs=hv[cp:cp + 64, hs],
                                    start=True, stop=True), reads=[dE], writes=[PD[7]], inc=(h == 7))
                            S.op("dve", lambda e, c=c: e.tensor_tensor(
                                out=Sst[:], in0=Sst[:], in1=dec[:, :, c:c + 1].to_broadcast([128, 4, 64]), op=ALU.mult),
                                 reads=[dS_, dDec], writes=[dS_])
                            S.op("dve", lambda e: e.tensor_tensor(
                                out=Sst[:], in0=Sst[:], in1=PS[7][:, 0:256].rearrange("p (j v) -> p j v", j=4), op=ALU.add),
                                 reads=[dS_, PD[7]], writes=[dS_])
                            S.op("act", lambda e: e.activation(out=Sbf[:], in_=Sst[:], func=AF.Copy),
                                 reads=[dS_], writes=[dSb])
                        if i == 15:
                            S.dma("sp", lambda e: e.dma_start(out=sp_o.rearrange("(j h) k v -> (h k) j v", h=2), in_=Sst[:]),
                                  dS_, "out")
                    else:
                        for h in range(8):
                            j, h2 = h // 2, h % 2
                            hp = 64 * h2
                            ab = AB[h2]
                            S.op("pe", lambda e, h=h, j=j, hp=hp, ab=ab: e.matmul(
                                out=PS[ab][0:64, h * 64:(h + 1) * 64], lhsT=kdT[hp:hp + 64, j, 0:64],
                                rhs=qdT[hp:hp + 64, j, 0:64], start=True, stop=True),
                                 reads=[dT_], writes=[PD[ab]], inc=(h >= 6))
                        for h2 in range(2):
                            ab = AB[h2]
                            S.op("dve", lambda e, ab=ab, h2=h2: e.tensor_tensor(
                                out=hsel(attm[0:64, :], h2), in0=hsel(PS[ab][0:64, :], h2),
                                in1=cf("maskHS")[0:64, :].unsqueeze(1).to_broadcast([64, 4, 64]), op=ALU.mult),
                                 reads=[PD[ab], dCF], writes=[dA])
                        S.op("dve", lambda e: e.tensor_tensor(
                            out=qdTm[:], in0=qdT[:, :, 0:64].unsqueeze(2).to_broadcast([128, 4, 16, 64]),
                            in1=smT[:].rearrange("p (b t) -> p b t", b=16).unsqueeze(1).to_broadcast([128, 4, 16, 64]),
                            op=ALU.mult), reads=[dT_, dsm], writes=[dqm])
                        for h in range(8):
                            hs = slice(h * 64, (h + 1) * 64)
                            S.op("pe", lambda e, hs=hs: e.matmul(out=PS[5][0:64, hs], lhsT=attm[0:64, hs], rhs=hv[0:64, hs],
                                                                 start=True, stop=True),
                                 reads=[dA, dE], writes=[PD[5]], inc=(h == 7))
                        for j in range(4):
                            sb_ = j % 2
                            S0j = S0t[sb_]
                            S.dma("sp", lambda e, j=j, S0j=S0j: e.dma_start(out=S0j[:], in_=st0_v[:, j, :, :]), dS0[sb_], "in")
                            S.op("act", lambda e, S0j=S0j: e.activation(out=S0bt[:].rearrange("p b v -> p (b v)"),
                                                                        in_=S0j[:].rearrange("p b v -> p (b v)"), func=AF.Copy),
                                 reads=[dS0[sb_]], writes=[dS0b])
                            S.op("dve", lambda e, j=j: e.tensor_tensor(
                                out=hvm[:], in0=hv[0:64, j * 128:(j + 1) * 128].unsqueeze(1).to_broadcast([64, 16, 128]),
                                in1=cf("seqind", 64).unsqueeze(2).to_broadcast([64, 16, 128]), op=ALU.mult),
                                 reads=[dE, dCF], writes=[dhvm])
                            for h2 in range(2):
                                h = 2 * j + h2
                                hp = 64 * h2
                                hs = slice(h * 64, (h + 1) * 64)
                                for b in range(16):
                                    S.op("pe", lambda e, hs=hs, hp=hp, j=j, b=b, h2=h2: e.matmul(
                                        out=PS[XB[h2]][0:64, hs], lhsT=qdTm[hp:hp + 64, j, b, :], rhs=S0bt[hp:hp + 64, b, :],
                                        start=(b == 0), stop=(b == 15)),
                                         reads=[dqm, dS0b], writes=[PD[XB[h2]]], inc=(b == 15))
                                for half in range(2):
                                    S.op("pe", lambda e, h=h, hp=hp, half=half, h2=h2: e.matmul(
                                        out=PS[half][hp:hp + 64, :], lhsT=kk[0:64, h * 64:(h + 1) * 64],
                                        rhs=hvm[0:64, half * 8:(half + 1) * 8, h2 * 64:(h2 + 1) * 64],
                                        start=True, stop=True), reads=[dE, dhvm], writes=[PD[half]])
                            for half in range(2):
                                bs = slice(half * 8, (half + 1) * 8)
                                S.op("dve", lambda e, j=j, bs=bs, S0j=S0j: e.tensor_tensor(
                                    out=S0j[:, bs, :], in0=S0j[:, bs, :],
                                    in1=dec[:, j, bs].unsqueeze(2).to_broadcast([128, 8, 64]), op=ALU.mult),
                                     reads=[dS0[sb_], dDec], writes=[dS0[sb_]])
                                S.op("dve", lambda e, bs=bs, half=half, S0j=S0j: e.tensor_tensor(
                                    out=S0j[:, bs, :], in0=S0j[:, bs, :],
                                    in1=PS[half][:, :].rearrange("p (b v) -> p b v", b=8), op=ALU.add),
                                     reads=[dS0[sb_], PD[half]], writes=[dS0[sb_]])
                            S.dma("sp", lambda e, j=j, S0j=S0j: e.dma_start(out=ss_v[:, j, :, :], in_=S0j[:]), dS0[sb_], "out")

                    S.op("act", lambda e: e.activation(out=oo[0:T, :], in_=PS[5][0:T, :], func=AF.Copy),
                         reads=[PD[5]], writes=[dO])
                    for h2 in range(2):
                        S.op("dve", lambda e, h2=h2: e.tensor_tensor(out=hsel(oo[0:T, :], h2), in0=hsel(oo[0:T, :], h2),
                                                              in1=hsel(PS[XB[h2]][0:T, :], h2), op=ALU.add),
                             reads=[dO, PD[XB[h2]]], writes=[dO])
                    S.op("dve", lambda e: e.tensor_tensor(out=osq[0:T, :], in0=oo[0:T, :], in1=oo[0:T, :], op=ALU.mult),
                         reads=[dO], writes=[dO])
                    S.op("dve", lambda e: e.tensor_reduce(out=st8[0:T, 0:8], in_=osq[0:T, :].rearrange("p (h v) -> p h v", h=8),
                                                          axis=AX.X, op=ALU.add), reads=[dO], writes=[dO])
                    rsqrt_small(st8[0:T, 0:8], st8[0:T, 16:24], 1.0 / 64, [dO], dO, st8[0:T, 8:16])
                    S.op("dve", lambda e: e.tensor_tensor(
                        out=oo[0:T, :].rearrange("p (h v) -> p h v", h=8), in0=oo[0:T, :].rearrange("p (h v) -> p h v", h=8),
                        in1=st8[0:T, 16:24].unsqueeze(2).to_broadcast([T, 8, 64]), op=ALU.mult), reads=[dO], writes=[dO])
                    S.op("dve", lambda e: e.tensor_tensor(out=oo[0:T, :], in0=oo[0:T, :], in1=qkgB[0:T, 1024:1536], op=ALU.mult),
                         reads=[dO, dPar], writes=[dO])
                    S.op("dve", lambda e: e.tensor_tensor(out=oo[0:T, :], in0=oo[0:T, :], in1=sg[0:T, :], op=ALU.mult),
                         reads=[dO, dE], writes=[dO])
                    for j in range(4):
                        S.op("pe", lambda e, j=j: e.transpose(out=PS[4][:, j * 128:j * 128 + T], in_=oo[0:T, j * 128:(j + 1) * 128],
                                                              identity=cf("ident", T, 0, T)),
                             reads=[dO, dCF], writes=[PD[4]], inc=(j == 3))
                    S.op("act", lambda e: e.activation(out=mixT[:, 4:8, c0:c1],
                                                       in_=PS[4][:, :].rearrange("p (j t) -> p j t", j=4)[:, :, 0:T],
                                                       func=AF.Copy), reads=[PD[4]], writes=[dMixB[i]])
                S.barrier()
                chk(2)

            pa_r = contextlib.ExitStack()
            qT = sb([128, 4, NTOK], BF16, side="right", stack=pa_r)
            kT = sb([128, 4, NTOK], BF16, side="right", stack=pa_r)
            vres = sb([128, NT, 512], BF16, side="right", stack=pa_r)
            dQ = [dep("qT") for _ in range(NT)]
            dK = [dep("kT") for _ in range(NT)]
            dV = [dep("v") for _ in range(NT)]
            with contextlib.ExitStack() as pa:
                wat = sb([128, 8, 1536], BF16, stack=pa)
                dwat = [dep("wat") for _ in range(3)]
                with contextlib.ExitStack() as pw:
                    wst = [sb([128, 8, 512], F32, stack=pw) for _ in range(2)]
                    dW = [dep("wst") for _ in range(2)]
                    for g in range(3):
                        load_weight_piece(w_in_v[:, :, g * 512:(g + 1) * 512], wat[:, :, g * 512:(g + 1) * 512],
                                          wst, dW, g % 2, dwat[g])
                    S.barrier()
                sq = sb([128, 512], F32, stack=pa)
                qn = [sb([128, 512], F32, stack=pa) for _ in range(2)]
                dqn = [dep("qn") for _ in range(2)]
                kn = [sb([128, 512], F32, stack=pa) for _ in range(2)]
                dkn = [dep("kn") for _ in range(2)]
                vn = [sb([128, 512], F32, stack=pa) for _ in range(2)]
                dvn = [dep("vn") for _ in range(2)]
                s8 = sb([128, 24], F32, stack=pa)
                dsq = dep("sq")
                for i in range(NT):
                    T = tile_rows(i)
                    c0, c1 = tile_cols(i)
                    b = i % 2

                    def proj(g, bank):
                        for k in range(8):
                            S.op("pe", lambda e, k=k: e.matmul(out=PS[bank][0:T, :], lhsT=hT[:, k, c0:c1],
                                                               rhs=wat[:, k, g * 512:(g + 1) * 512],
                                                               start=(k == 0), stop=(k == 7)),
                                 reads=[dhT[i], dwat[g]], writes=[PD[bank]], inc=(k == 7))

                    def qknorm(bank, dst, ddst, goff):
                        S.op("act", lambda e: e.activation(out=sq[0:T, :], in_=PS[bank][0:T, :], func=AF.Square),
                             reads=[PD[bank]], writes=[dsq])
                        S.op("dve", lambda e: e.tensor_reduce(out=s8[0:T, 0:8], in_=sq[0:T, :].rearrange("p (h d) -> p h d", h=8),
                                                              axis=AX.X, op=ALU.add), reads=[dsq], writes=[dsq])
                        rsqrt_small(s8[0:T, 0:8], s8[0:T, 16:24], 1.0 / 64, [dsq], dsq, s8[0:T, 8:16])
                        S.op("dve", lambda e: e.tensor_tensor(
                            out=dst[0:T, :].rearrange("p (h d) -> p h d", h=8),
                            in0=PS[bank][0:T, :].rearrange("p (h d) -> p h d", h=8),
                            in1=s8[0:T, 16:24].unsqueeze(2).to_broadcast([T, 8, 64]), op=ALU.mult),
                             reads=[PD[bank], dsq], writes=[ddst])
                        S.op("dve", lambda e: e.tensor_tensor(out=dst[0:T, :], in0=dst[0:T, :], in1=qkgB[0:T, goff:goff + 512],
                                                              op=ALU.mult), reads=[ddst, dPar], writes=[ddst])

                    def to_featT(src, dsrc, bank, dst, ddst):
                        for j in range(4):
                            S.op("pe", lambda e, j=j: e.transpose(out=PS[bank][:, j * 128:j * 128 + T],
                                                                  in_=src[0:T, j * 128:(j + 1) * 128],
                                                                  identity=cf("ident", T, 0, T)),
                                 reads=[dsrc, dCF], writes=[PD[bank]], inc=(j == 3))
                        S.op("act", lambda e: e.activation(out=dst[:, :, c0:c1],
                                                           in_=PS[bank][:, :].rearrange("p (j t) -> p j t", j=4)[:, :, 0:T],
                                                           func=AF.Copy), reads=[PD[bank]], writes=[ddst])

                    proj(0, 0)
                    qknorm(0, qn[b], dqn[b], 0)
                    to_featT(qn[b], dqn[b], 3, qT, dQ[i])
                    proj(1, 1)
                    qknorm(1, kn[b], dkn[b], 512)
                    to_featT(kn[b], dkn[b], 4, kT, dK[i])
                    kdst = kp[i * 128:(i + 1) * 128, :] if i < 16 else ks[:, :]
                    S.dma("sp", lambda e: e.dma_start(out=kdst, in_=kn[b][0:T, :]), dkn[b], "out")
                    proj(2, 2)
                    S.op("act", lambda e: e.activation(out=vn[b][0:T, :], in_=PS[2][0:T, :], func=AF.Copy),
                         reads=[PD[2]], writes=[dvn[b]])
                    S.op("dve", lambda e: e.tensor_copy(out=vres[0:T, i, :], in_=vn[b][0:T, :]),
                         reads=[dvn[b]], writes=[dV[i]])
                    vdst = vp[i * 128:(i + 1) * 128, :] if i < 16 else vs[:, :]
                    S.dma("sp", lambda e: e.dma_start(out=vdst, in_=vn[b][0:T, :]), dvn[b], "out")
                S.barrier()
                chk(3)
        p1s.close()

        with contextlib.ExitStack() as p2:
            NSET = 2
            eT = [sb([128, 512], F32, stack=p2) for _ in range(NSET)]
            xT_ = [sb([128, 512], F32, stack=p2) for _ in range(NSET)]
            LpT = [sb([128, 512], BF16, stack=p2) for _ in range(NSET)]
            wT = [sb([128, 512], BF16, stack=p2) for _ in range(NSET)]
            de = [dep("e") for _ in range(NSET)]
            dx = [dep("x") for _ in range(NSET)]
            dL = [dep("L") for _ in range(NSET)]
            dw = [dep("w") for _ in range(NSET)]
            ulT = sb([128, 256], BF16, stack=p2)
            mDT = sb([128, 2048], BF16, stack=p2)
            dCB = dep("cb")
            load_bf_const("uincl", ulT[:, 0:128], dCB, eT[0], de[0])
            load_bf_const("lstrict", ulT[:, 128:256], dCB, eT[1], de[1])
            load_bf_const("maskD", mDT, dCB, eT[0], de[0])
            uincl = ulT[:, 0:128]
            lstrict = ulT[:, 128:256]
            stepc = [0, 0]
            for j in range(4):
                for QB in range(4):
                    nkb = 4 * QB + 4
                    qs = slice(QB * 512, (QB + 1) * 512)
                    for st, kb in enumerate(range(nkb - 1, -1, -1)):
                        for h2 in range(2):
                            h = 2 * j + h2
                            hp = 64 * h2
                            Zb, Cb = h2, 2 + h2
                            si = h2
                            stepc[h2] += 1
                            qdeps = [dQ[t] for t in range(QB * 4, QB * 4 + 4)]
                            S.op("pe", lambda e, hp=hp, kb=kb, Zb=Zb: e.matmul(
                                out=PS[Zb][:, :], lhsT=kT[hp:hp + 64, j, kb * 128:(kb + 1) * 128], rhs=qT[hp:hp + 64, j, qs],
                                start=True, stop=True), reads=[dK[kb]] + qdeps, writes=[PD[Zb]])
                            S.op("act", lambda e, si=si, Zb=Zb, h=h: e.activation(
                                out=eT[si][:], in_=PS[Zb][:, :], func=AF.Exp, scale=SCALE, bias=biasT[:, h:h + 1]),
                                 reads=[PD[Zb], dPar], writes=[de[si]])
                            if kb >= 4 * QB:
                                ii = kb - 4 * QB
                                S.op("dve", lambda e, si=si, ii=ii: e.tensor_tensor(
                                    out=eT[si][:], in0=eT[si][:], in1=mDT[:, ii * 512:(ii + 1) * 512], op=ALU.mult),
                                     reads=[de[si], dCB], writes=[de[si]])
                            S.op("act", lambda e, si=si: e.activation(out=LpT[si][:], in_=eT[si][:], func=AF.Ln, bias=1.0),
                                 reads=[de[si]], writes=[dL[si]])
                            S.op("pe", lambda e, si=si, Cb=Cb, st=st: e.matmul(
                                out=PS[Cb][:, :], lhsT=uincl, rhs=LpT[si][:], start=(st == 0), stop=False,
                                skip_group_check=True), reads=[dL[si], dCB], writes=[PD[Cb]])
                            S.op("act", lambda e, si=si, Cb=Cb: e.activation(out=xT_[si][:], in_=PS[Cb][:, :], func=AF.Exp,
                                                                             scale=-1.0),
                                 reads=[PD[Cb]], writes=[dx[si]])
                            if kb > 0:
                                S.op("pe", lambda e, si=si, Cb=Cb, kb=kb: e.matmul(
                                    out=PS[Cb][:, :], lhsT=lstrict, rhs=LpT[si][:], start=False, stop=(kb == 1),
                                    skip_group_check=True), reads=[dL[si], dCB], writes=[PD[Cb]])
                            S.op("pool", lambda e, si=si: e.tensor_tensor(out=wT[si][:], in0=eT[si][:], in1=xT_[si][:],
                                                                          op=ALU.mult),
                                 reads=[de[si], dx[si]], writes=[dw[si]])
                            S.op("pe", lambda e, si=si, hp=hp, h=h, kb=kb, st=st: e.matmul(
                                out=PS[4][hp:hp + 64, :], lhsT=vres[:, kb, h * 64:(h + 1) * 64], rhs=wT[si][:],
                                start=(st == 0), stop=(kb == 0), skip_group_check=True),
                                 reads=[dw[si], dV[kb]], writes=[PD[4]])
                    S.op("act", lambda e: e.activation(out=mixT[:, j, qs], in_=PS[4][:, :], func=AF.Copy),
                         reads=[PD[4]], writes=[dMixA[QB * 4 + tt] for tt in range(4)])
            S.barrier()
            chk(4)

            NQ = 4
            Kst = [sb([128, NQ, 512], F32, stack=p2) for _ in range(2)]
            Vst = [sb([128, NQ, 512], F32, stack=p2) for _ in range(2)]
            KTs = [sb([128, NQ, 4, 128], BF16, stack=p2) for _ in range(2)]
            dKst = [dep("Kst") for _ in range(2)]
            dVst = [dep("Vst") for _ in range(2)]
            dKTs = [dep("KTs") for _ in range(2)]
            ptb = sb([128, NSEQ * NPG], I32, stack=p2)
            idx = sb([128, NSEQ * NPG], I32, stack=p2)
            iop = sb([128, 1], I32, stack=p2)
            dIdx = dep("idx")
            kTp = sb([128, 4, 128], BF16, stack=p2)
            vpad = sb([128, 512], BF16, stack=p2)
            dpad = dep("pad")
            es_ = [sb([128, 64], F32, stack=p2) for _ in range(2)]
            xs2 = [sb([128, 64], F32, stack=p2) for _ in range(2)]
            Ls = [sb([128, 64], BF16, stack=p2) for _ in range(2)]
            ws = [sb([128, 64], F32, stack=p2) for _ in range(2)]
            wsb = [sb([128, 64], BF16, stack=p2) for _ in range(2)]
            des = [dep("es") for _ in range(2)]
            dxs = [dep("xs") for _ in range(2)]
            dLs = [dep("Ls") for _ in range(2)]
            dws = [dep("ws") for _ in range(2)]

            S.dma("pool", lambda e: e.dma_start(out=ptb[:], in_=pt[0:1, :].to_broadcast([128, NSEQ * NPG])), dIdx, "in")
            S.op("pool", lambda e: e.iota(iop[:], pattern=[[0, 1]], base=0, channel_multiplier=1), writes=[dIdx])
            S.op("pool", lambda e: e.tensor_scalar(out=idx[:], in0=ptb[:], scalar1=128, scalar2=None, op0=ALU.mult),
                 reads=[dIdx], writes=[dIdx])
            S.op("pool", lambda e: e.tensor_tensor(out=idx[:], in0=idx[:], in1=iop[:].to_broadcast([128, NSEQ * NPG]),
                                                   op=ALU.add), reads=[dIdx], writes=[dIdx])
            S.op("dve", lambda e: e.memset(kTp[:], 0.0), writes=[dpad])
            S.op("dve", lambda e: e.memset(vpad[:], 0.0), writes=[dpad])
            S.op("dve", lambda e: e.tensor_copy(out=kTp[:, :, 0:64], in_=kT[:, :, TP:NTOK]), reads=[dK[16], dpad], writes=[dpad])
            S.op("dve", lambda e: e.tensor_copy(out=vpad[0:64, :], in_=vres[0:64, 16, :]), reads=[dV[16], dpad], writes=[dpad])

            ZBK = (0, 3)
            CBK = (1, 4)
            dZb = [dep("Zb") for _ in range(2)]
            dCb = [dep("Cb") for _ in range(2)]
            dOs = dep("Os")
            first_c = [True, True]
            first_o = [True, True]
            cnt2 = [0]

            def sample_head_step(h, cols, n, kind, gi=None, buf=None, last=False):
                j, h2 = h // 2, h % 2
                hp = 64 * h2
                b2 = cnt2[0] % 2
                cnt2[0] += 1
                zc = slice(j * 64 + cols.start, j * 64 + cols.stop)
                zb, cbk = ZBK[h2], CBK[h2]
                if kind == "new":
                    S.op("pe", lambda e: e.matmul(out=PS[zb][:, zc], lhsT=kTp[hp:hp + 64, j, :], rhs=qT[hp:hp + 64, j, TP:NTOK],
                                                  start=True, stop=True), reads=[dpad, dQ[16]], writes=[dZb[h2]])
                else:
                    for bi in range(NQ):
                        b = gi * NQ + bi
                        S.op("pe", lambda e, bi=bi, b=b: e.matmul(
                            out=PS[zb][:, j * 64 + b * 4:j * 64 + b * 4 + 4], lhsT=KTs[buf][hp:hp + 64, bi, j, :],
                            rhs=qT[hp:hp + 64, j, TP + b * 4:TP + b * 4 + 4], start=True, stop=True),
                             reads=[dKTs[buf], dQ[16]], writes=[dZb[h2]], inc=(bi == NQ - 1))
                S.op("act", lambda e: e.activation(out=es_[b2][:, 0:n], in_=PS[zb][:, zc], func=AF.Exp, scale=SCALE,
                                                   bias=biasT[:, h:h + 1]), reads=[dZb[h2], dPar], writes=[des[b2]])
                if kind == "new":
                    S.op("dve", lambda e: e.tensor_tensor(out=es_[b2][:, 0:n], in0=es_[b2][:, 0:n], in1=cf("maskS"),
                                                          op=ALU.mult), reads=[des[b2], dCF], writes=[des[b2]])
                S.op("act", lambda e: e.activation(out=Ls[b2][:, 0:n], in_=es_[b2][:, 0:n], func=AF.Ln, bias=1.0),
                     reads=[des[b2]], writes=[dLs[b2]])
                fc = first_c[h2]
                first_c[h2] = False
                S.op("pe", lambda e: e.matmul(out=PS[cbk][:, zc], lhsT=uincl, rhs=Ls[b2][:, 0:n], start=fc, stop=False,
                                              skip_group_check=True), reads=[dLs[b2], dCB], writes=[dCb[h2]])
                S.op("act", lambda e: e.activation(out=xs2[b2][:, 0:n], in_=PS[cbk][:, zc], func=AF.Exp, scale=-1.0),
                     reads=[dCb[h2]], writes=[dxs[b2]])
                if not last:
                    S.op("pe", lambda e: e.matmul(out=PS[cbk][:, zc], lhsT=lstrict, rhs=Ls[b2][:, 0:n], start=False, stop=False,
                                                  skip_group_check=True), reads=[dLs[b2], dCB], writes=[dCb[h2]])
                oc0 = j * 64
                if kind == "new":
                    S.op("dve", lambda e: e.tensor_tensor(out=wsb[b2][:, 0:n], in0=es_[b2][:, 0:n], in1=xs2[b2][:, 0:n],
                                                          op=ALU.mult), reads=[des[b2], dxs[b2]], writes=[dws[b2]])
                    fo = first_o[h2]
                    first_o[h2] = False
                    S.op("pe", lambda e: e.matmul(out=PS[2][hp:hp + 64, oc0:oc0 + 64], lhsT=vpad[:, h * 64:(h + 1) * 64],
                                                  rhs=wsb[b2][:, 0:n], start=fo, stop=False, skip_group_check=True),
                         reads=[dws[b2], dpad], writes=[dOs])
                else:
                    S.op("dve", lambda e: e.tensor_tensor(out=ws[b2][:, 0:n], in0=es_[b2][:, 0:n], in1=xs2[b2][:, 0:n],
                                                          op=ALU.mult), reads=[des[b2], dxs[b2]], writes=[dws[b2]])
                    for bi in range(NQ):
                        b = gi * NQ + bi
                        S.op("pe", lambda e, bi=bi, b=b: e.matmul(
                            out=PS[2][hp:hp + 64, oc0 + b * 4:oc0 + b * 4 + 4], lhsT=Vst[buf][:, bi, h * 64:(h + 1) * 64],
                            rhs=ws[b2][:, bi * 4:bi * 4 + 4], start=False, stop=False, skip_group_check=True),
                             reads=[dws[b2], dVst[buf]], writes=[dOs], inc=(bi == NQ - 1))

            for h in range(8):
                sample_head_step(h, slice(0, 64), 64, "new")
            ck_v = ck
            cv_v = cv
            gcount = 0
            for p in range(NPG - 1, -1, -1):
                for gi in range(NSEQ // NQ):
                    buf = gcount % 2
                    gcount += 1
                    for bi in range(NQ):
                        b = gi * NQ + bi
                        col = b * NPG + p
                        S.dma("pool", lambda e, bi=bi, col=col, buf=buf: e.indirect_dma_start(
                            out=Kst[buf][:, bi, :], out_offset=None, in_=ck_v,
                            in_offset=bass.IndirectOffsetOnAxis(ap=idx[:, col:col + 1], axis=0)),
                              dKst[buf], "in", extra_reads=[dIdx])
                        S.dma("pool", lambda e, bi=bi, col=col, buf=buf: e.indirect_dma_start(
                            out=Vst[buf][:, bi, :], out_offset=None, in_=cv_v,
                            in_offset=bass.IndirectOffsetOnAxis(ap=idx[:, col:col + 1], axis=0)),
                              dVst[buf], "in", extra_reads=[dIdx])
                    for bi in range(NQ):
                        bank = 5 + (bi % 2)
                        for jj in range(4):
                            S.op("pe", lambda e, bi=bi, jj=jj, bank=bank, buf=buf: e.transpose(
                                out=PS[bank][:, jj * 128:(jj + 1) * 128], in_=Kst[buf][:, bi, jj * 128:(jj + 1) * 128],
                                identity=ident), reads=[dKst[buf], dCF], writes=[PD[bank]], inc=(jj == 3))
                        if bi % 2 == 0:
                            S.op("dve", lambda e, bi=bi, bank=bank, buf=buf: e.tensor_copy(
                                out=KTs[buf][:, bi, :, :].rearrange("p j t -> p (j t)"), in_=PS[bank][:, :]),
                                 reads=[PD[bank]], writes=[dKTs[buf]])
                        else:
                            S.op("act", lambda e, bi=bi, bank=bank, buf=buf: e.activation(
                                out=KTs[buf][:, bi, :, :].rearrange("p j t -> p (j t)"), in_=PS[bank][:, :], func=AF.Copy),
                                 reads=[PD[bank]], writes=[dKTs[buf]])
                    for h in range(8):
                        sample_head_step(h, slice(gi * NQ * 4, (gi + 1) * NQ * 4), NQ * 4, "page", gi=gi, buf=buf,
                                         last=(p == 0))
            S.op("act", lambda e: e.activation(out=mixT[:, 0:4, TP:NTOK], in_=PS[2][:, 0:256].rearrange("p (j t) -> p j t", j=4),
                                               func=AF.Copy), reads=[dOs], writes=[dMixA[16]])
            S.barrier()
            chk(5)
        pa_r.close()

        w_out_v = w_out.rearrange("(k p) n -> p k n", p=128)
        w_up_v = w_up.rearrange("(k p) n -> p k n", p=128)
        w_down_v = w_down.rearrange("(c p) n -> p c n", p=128)
        w_ada_v = w_ada.rearrange("(k p) n -> p k n", p=128)
        gBp = sb([128, 2048], F32)
        gBs = sb([64, 2048], F32)
        dG = dep("gB")
        wst = [sb([128, 8, 256], F32) for _ in range(2)]
        dW = [dep("wst") for _ in range(2)]
        with contextlib.ExitStack() as pg:
            cTp = sb([128, 8, 128], F32, stack=pg)
            cTs = sb([128, 8, 64], F32, stack=pg)
            bgB = sb([128, 2048], F32, stack=pg)
            dbg = dep("bgB")
            dcTx = dep("cTx")
            S.dma("sp", lambda e: e.dma_start(out=bgB[:], in_=bg[0:1, :].to_broadcast([128, 2048])), dbg, "in")
            S.op("dve", lambda e: e.tensor_copy(out=cTp[:], in_=cT[:, :, 16:17].to_broadcast([128, 8, 128])),
                 reads=[dcT], writes=[dcTx])
            S.op("dve", lambda e: e.tensor_copy(out=cTs[:].rearrange("p c (b t) -> p c b t", t=4),
                                                in_=cT[:, :, 0:16].unsqueeze(3).to_broadcast([128, 8, 16, 4])),
                 reads=[dcT], writes=[dcTx])
            for n_ in range(8):
                buf = n_ % 2
                acol = (2048 if n_ < 4 else 5120) + (n_ % 4) * 256
                gcol = n_ * 256
                S.dma("sp", lambda e, acol=acol, buf=buf: e.dma_start(out=wst[buf][:], in_=w_ada_v[:, :, acol:acol + 256]),
                      dW[buf], "in")
                for (lhs, rows, dst, bank) in ((cTp, 128, gBp, 2), (cTs, 64, gBs, 3)):
                    for k in range(8):
                        S.op("pe", lambda e, k=k, lhs=lhs, rows=rows, bank=bank, buf=buf: e.matmul(
                            out=PS[bank][0:rows, 0:256], lhsT=lhs[:, k, :], rhs=wst[buf][:, k, :],
                            start=(k == 0), stop=(k == 7)),
                             reads=[dcTx, dW[buf]], writes=[PD[bank]], inc=(k == 7))
                    S.op("dve", lambda e, rows=rows, dst=dst, bank=bank, gcol=gcol: e.tensor_tensor(
                        out=dst[0:rows, gcol:gcol + 256], in0=PS[bank][0:rows, 0:256], in1=bgB[0:rows, gcol:gcol + 256],
                        op=ALU.add), reads=[PD[bank], dbg], writes=[dG])
            S.barrier()
            chk(6)

        def wst_flat(buf, c):
            return wst[buf][:].rearrange("p k n -> p (k n)").rearrange("p (c n) -> p c n", c=c)

        halves = [list(range(0, 8)), list(range(8, 17))]
        for hi, tiles in enumerate(halves):
            with contextlib.ExitStack() as p3:
                nt = len(tiles)
                ncols = sum(tile_rows(i) for i in tiles)
                col0 = tiles[0] * 128
                x1 = sb([128, nt, D], F32, stack=p3)
                dx1 = [dep("x1") for _ in range(nt)]
                h2T = sb([128, 8, ncols], BF16, stack=p3)
                dh2 = [dep("h2T") for _ in range(nt)]
                with contextlib.ExitStack() as p3a:
                    wo = sb([128, 8, D], BF16, stack=p3a)
                    dwo = [dep("wo") for _ in range(4)]
                    for g in range(4):
                        load_weight_piece(w_out_v[:, :, g * 256:(g + 1) * 256], wo[:, :, g * 256:(g + 1) * 256], wst, dW, g % 2,
                                          dwo[g])
                    xt = [sb([128, D], F32, stack=p3a) for _ in range(2)]
                    dxt = [dep("xt") for _ in range(2)]
                    wk = [(sb([128, D], F32, stack=p3a), sb([128, D], F32, stack=p3a), sb([128, 4], F32, stack=p3a))
                          for _ in range(2)]
                    dwk = [dep("wk") for _ in range(2)]
                    tmp = [sb([128, 512], F32, stack=p3a) for _ in range(2)]
                    dtm = [dep("tmp") for _ in range(2)]
                    for li, i in enumerate(tiles):
                        T = tile_rows(i)
                        c0, c1 = tile_cols(i)
                        b = li % 2
                        load_x_tile(i, xt[b], dxt[b])
                        gB = gBp if i < 16 else gBs
                        for nh in range(2):
                            bank = nh
                            ns = slice(nh * 512, (nh + 1) * 512)
                            for k in range(8):
                                md = dMixA[i] if k < 4 else dMixB[i]
                                S.op("pe", lambda e, k=k, bank=bank, ns=ns: e.matmul(
                                    out=PS[bank][0:T, :], lhsT=mixT[:, k, c0:c1], rhs=wo[:, k, ns], start=(k == 0), stop=(k == 7)),
                                     reads=[md, dwo[2 * nh], dwo[2 * nh + 1]], writes=[PD[bank]], inc=(k == 7))
                            S.op("dve", lambda e, bank=bank, ns=ns, nh=nh, gB=gB: e.tensor_tensor(
                                out=tmp[nh][0:T, :], in0=PS[bank][0:T, :], in1=gB[0:T, ns], op=ALU.mult),
                                 reads=[PD[bank], dG], writes=[dtm[nh]])
                            S.op("pool", lambda e, ns=ns, nh=nh, li=li, b=b: e.tensor_tensor(
                                out=x1[0:T, li, ns], in0=tmp[nh][0:T, :], in1=xt[b][0:T, ns], op=ALU.add),
                                 reads=[dtm[nh], dxt[b]], writes=[dx1[li]])
                        lc0 = c0 - col0
                        norm_transpose(i, x1[0:T, li, :], dx1[li], 2, h2T[:, :, lc0:lc0 + T], dh2[li], wk[b], dwk[b],
                                       (2, 3) if b == 0 else (4, 5))
                    S.barrier()
                    chk(7)
                with contextlib.ExitStack() as p4:
                    wu = [sb([128, 8, 512], BF16, stack=p4) for _ in range(2)]
                    wd = [sb([128, 4, D], BF16, stack=p4) for _ in range(2)]
                    dwu = [dep("wu") for _ in range(2)]
                    dwd = [dep("wd") for _ in range(2)]
                    upT = [sb([128, 4, 512], BF16, stack=p4) for _ in range(2)]
                    dup = [dep("up") for _ in range(2)]
                    rl = [sb([128, 512], F32, stack=p4) for _ in range(2)]
                    drl = [dep("rl") for _ in range(2)]
                    tmp = [sb([128, 512], F32, stack=p4) for _ in range(2)]
                    dtm = [dep("tmp") for _ in range(2)]
                    groups = []
                    li = 0
                    while li < nt:
                        g = [l for l in range(li, min(li + 4, nt)) if tile_rows(tiles[l]) == 128]
                        if not g:
                            g = [li]
                        groups.append(g)
                        li = g[-1] + 1
                    gctr = 0
                    rctr = 0
                    for E in range(8):
                        wb = E % 2
                        for hh in range(2):
                            S.dma("sp", lambda e, E=E, hh=hh: e.dma_start(
                                out=wst[0][:], in_=w_up_v[:, :, E * 512 + hh * 256:E * 512 + (hh + 1) * 256]), dW[0], "in")
                            S.op("pool", lambda e, wb=wb, hh=hh: e.tensor_copy(out=wu[wb][:, :, hh * 256:(hh + 1) * 256], in_=wst[0][:]),
                                 reads=[dW[0]], writes=[dwu[wb]])
                            S.dma("sp", lambda e, E=E, hh=hh: e.dma_start(
                                out=wst_flat(1, 2), in_=w_down_v[:, E * 4 + hh * 2:E * 4 + (hh + 1) * 2, :]), dW[1], "in")
                            S.op("pool", lambda e, wb=wb, hh=hh: e.tensor_copy(out=wd[wb][:, hh * 2:(hh + 1) * 2, :], in_=wst_flat(1, 2)),
                                 reads=[dW[1]], writes=[dwd[wb]])
                        for g in groups:
                            gcol0 = tiles[g[0]] * 128 - col0
                            gn = sum(tile_rows(tiles[l]) for l in g)
                            ub = gctr % 2
                            gctr += 1
                            for fc in range(4):
                                bank = fc % 2
                                rb = rctr % 2
                                rctr += 1
                                for k in range(8):
                                    S.op("pe", lambda e, k=k, fc=fc, bank=bank: e.matmul(
                                        out=PS[bank][:, 0:gn], lhsT=wu[wb][:, k, fc * 128:(fc + 1) * 128],
                                        rhs=h2T[:, k, gcol0:gcol0 + gn], start=(k == 0), stop=(k == 7)),
                                         reads=[dwu[wb]] + [dh2[l] for l in g], writes=[PD[bank]], inc=(k == 7))
                                S.op("act", lambda e, bank=bank, rb=rb: e.activation(out=rl[rb][:, 0:gn], in_=PS[bank][:, 0:gn],
                                                                                     func=AF.Relu),
                                     reads=[PD[bank]], writes=[drl[rb]])
                                S.op("pool", lambda e, rb=rb, ub=ub, fc=fc: e.tensor_tensor(
                                    out=upT[ub][:, fc, 0:gn], in0=rl[rb][:, 0:gn], in1=rl[rb][:, 0:gn], op=ALU.mult),
                                     reads=[drl[rb]], writes=[dup[ub]])
                            for l in g:
                                i = tiles[l]
                                T = tile_rows(i)
                                lc = tiles[l] * 128 - col0 - gcol0
                                gB = gBp if i < 16 else gBs
                                for nh in range(2):
                                    bank = 2 + nh
                                    ns = slice(nh * 512, (nh + 1) * 512)
                                    for fc in range(4):
                                        S.op("pe", lambda e, fc=fc, bank=bank, ns=ns, lc=lc: e.matmul(
                                            out=PS[bank][0:T, :], lhsT=upT[ub][:, fc, lc:lc + T], rhs=wd[wb][:, fc, ns],
                                            start=(fc == 0), stop=(fc == 3)),
                                             reads=[dup[ub], dwd[wb]], writes=[PD[bank]], inc=(fc == 3))
                                    S.op("dve", lambda e, bank=bank, nh=nh, gB=gB: e.tensor_tensor(
                                        out=tmp[nh][0:T, :], in0=PS[bank][0:T, :], in1=gB[0:T, 1024 + nh * 512:1024 + (nh + 1) * 512],
                                        op=ALU.mult), reads=[PD[bank], dG], writes=[dtm[nh]])
                                    S.op("dve", lambda e, ns=ns, nh=nh, l=l: e.tensor_tensor(
                                        out=x1[0:T, l, ns], in0=x1[0:T, l, ns], in1=tmp[nh][0:T, :], op=ALU.add),
                                         reads=[dtm[nh], dx1[l]], writes=[dx1[l]])
                                if E == 7:
                                    ydst = yp[i * 128:(i + 1) * 128, :] if i < 16 else ys[:, :]
                                    S.dma("sp", lambda e, l=l, ydst=ydst: e.dma_start(out=ydst, in_=x1[0:T, l, :]), dx1[l], "out")
                    S.barrier()
    except _Stop:
        S.finish()
        return nc
    S.finish()
    es.close()
    return nc


def _core_inputs(c, inp, n_phys=None, ck=None, cv=None, pt=None):
    f = np.float32
    d = {}
    d["xp"] = np.ascontiguousarray(inp["x_prompt"][c], dtype=f)
    d["xs"] = np.ascontiguousarray(inp["x_sample"][16 * c:16 * c + 16].reshape(TS, D), dtype=f)
    d["ck"] = ck if ck is not None else inp["cache_k"][0].reshape(-1, 512)
    d["cv"] = cv if cv is not None else inp["cache_v"][0].reshape(-1, 512)
    d["st0"] = np.ascontiguousarray(inp["state_hgrn"][0, 16 * c:16 * c + 16], dtype=f)
    ptc = pt if pt is not None else inp["page_table"][16 * c:16 * c + 16]
    d["pt"] = np.ascontiguousarray(ptc.reshape(1, -1), dtype=np.int32)
    d["cc"] = np.ascontiguousarray(np.concatenate([inp["c_sample"][16 * c:16 * c + 16], inp["c_prompt"][c:c + 1]], 0), dtype=f)
    d["w_ada"] = np.ascontiguousarray(inp["w_ada"][0], dtype=f)
    b_ada = inp["b_ada"][0]
    d["vecs"] = np.ascontiguousarray(np.concatenate([b_ada.reshape(48, 128), inp["norm1_g"][0].reshape(8, 128),
                                                     inp["norm2_g"][0].reshape(8, 128)], 0), dtype=f)
    d["bg"] = np.ascontiguousarray(np.concatenate([b_ada[2048:3072], b_ada[5120:6144]])[None], dtype=f)
    d["w_in"] = np.ascontiguousarray(inp["w_in"][0], dtype=f)
    d["qkg"] = np.ascontiguousarray(np.concatenate([np.tile(inp["q_norm_g"][0], 8), np.tile(inp["k_norm_g"][0], 8),
                                                    np.tile(inp["hg_out_g"][0], 8)])[None], dtype=f)
    d["sbb"] = np.ascontiguousarray(inp["sb_bias"][0][None], dtype=f)
    d["lbl"] = np.ascontiguousarray(inp["hg_lb_logits"].reshape(1, 1024), dtype=f)
    d["w_out"] = np.ascontiguousarray(inp["w_out"][0], dtype=f)
    d["w_up"] = np.ascontiguousarray(inp["w_up"][0], dtype=f)
    d["w_down"] = np.ascontiguousarray(inp["w_down"][0], dtype=f)
    d["cstf"] = _CARR
    d["cstb"] = _BARR
    return d


def _assemble(results):
    y_p = np.stack([r["yp"] for r in results]).astype(np.float32)
    y_s = np.concatenate([r["ys"].reshape(16, 4, D) for r in results]).astype(np.float32)
    k_p = np.stack([r["kp"].reshape(TP, 8, 64) for r in results])[None].astype(np.float32)
    v_p = np.stack([r["vp"].reshape(TP, 8, 64) for r in results])[None].astype(np.float32)
    k_s = np.concatenate([r["ks"].reshape(16, 4, 8, 64) for r in results])[None].astype(np.float32)
    v_s = np.concatenate([r["vs"].reshape(16, 4, 8, 64) for r in results])[None].astype(np.float32)
    s_p = np.stack([r["sp"] for r in results])[None].astype(np.float32)
    s_s = np.concatenate([r["ss"] for r in results])[None].astype(np.float32)
    return (y_p, y_s, k_p, v_p, k_s, v_s, s_p, s_s)


def kernel(**inputs):
    inp = {k: np.asarray(v) for k, v in inputs.items()}
    n_phys = inp["cache_k"].shape[1]
    nc = build(n_phys)
    ck = np.ascontiguousarray(inp["cache_k"][0].reshape(-1, 512), dtype=np.float32)
    cv = np.ascontiguousarray(inp["cache_v"][0].reshape(-1, 512), dtype=np.float32)
    in_maps = [_core_inputs(c, inp, ck=ck, cv=cv) for c in range(NCORES)]
    res = run_bass_kernel_spmd(nc, in_maps, core_ids=list(range(NCORES)))
    return _assemble(res.results)
```

```python
import contextlib
import os
import numpy as np
import ml_dtypes
import concourse.bass as bass
import concourse.mybir as mybir
from concourse.bass_utils import run_bass_kernel_spmd

F32 = mybir.dt.float32
BF16 = mybir.dt.bfloat16
I32 = mybir.dt.int32
AF = mybir.ActivationFunctionType
ALU = mybir.AluOpType
AX = mybir.AxisListType

NCORES = 8
D = 1024
TP = 2048
NSEQ = 16
TS = 64
NTOK = TP + TS
NPG = 16
EPS = 1e-6
SCALE = 64 ** -0.5
NT = 17


def _consts():
    f = {}
    idx = np.arange(128)
    f["ident"] = np.eye(128, dtype=np.float32)
    s = idx[:, None]
    t = idx[None, :]
    same64 = (s // 64) == (t // 64)
    f["tri2"] = ((s <= t) & same64).astype(np.float32)
    f["tsu2"] = ((s > t) & same64).astype(np.float32)
    same4 = (s // 4) == (t // 4)
    f["tri4"] = ((s <= t) & same4).astype(np.float32)
    f["tsu4"] = ((s > t) & same4).astype(np.float32)
    ci = np.zeros((128, 2), np.float32)
    ci[:64, 0] = 1
    ci[64:, 1] = 1
    f["chunkind"] = ci
    si = np.zeros((128, 16), np.float32)
    for p in range(64):
        si[p, p // 4] = 1
    f["seqind"] = si
    mh = np.zeros((128, 64), np.float32)
    for p in range(128):
        mh[p, :] = (np.arange(64) >= (p % 64))
    f["maskH"] = mh
    ms = np.zeros((128, 64), np.float32)
    for p in range(64):
        for c in range(64):
            ms[p, c] = (p // 4 == c // 4) and (p % 4 <= c % 4)
    f["maskHS"] = ms
    ma = np.zeros((128, 64), np.float32)
    for p in range(64):
        for c in range(64):
            ma[p, c] = (p // 4 == c // 4) and (p % 4 < c % 4)
    f["maskS"] = ma
    smt = np.zeros((128, 16 * 64), np.float32)
    for b in range(16):
        smt[:, b * 64 + 4 * b: b * 64 + 4 * b + 4] = 1
    f["seqmaskT"] = smt
    md = np.zeros((128, 4 * 512), np.float32)
    for i in range(4):
        md[:, i * 512:(i + 1) * 512] = ((128 * i + idx[:, None]) < np.arange(512)[None, :])
    f["maskD"] = md
    f["uincl"] = (s >= t).astype(np.float32)
    f["lstrict"] = (s < t).astype(np.float32)
    io = np.zeros((128, 1), np.float32)
    off = {}
    cols = 0
    for k, v in f.items():
        off[k] = (cols, v.shape[1])
        cols += v.shape[1]
    bfk = ["maskD", "uincl", "lstrict", "seqmaskT"]
    off = {}
    cols = 0
    for k, v in f.items():
        if k in bfk:
            continue
        off[k] = (cols, v.shape[1])
        cols += v.shape[1]
    arr = np.concatenate([f[k] for k in f if k not in bfk], axis=1).astype(np.float32)
    boff = {}
    cols = 0
    for k in bfk:
        boff[k] = (cols, f[k].shape[1])
        cols += f[k].shape[1]
    barr = np.concatenate([f[k] for k in bfk], axis=1).astype(np.float32)
    return arr, off, barr, boff


_CARR, _COFF, _BARR, _BOFF = _consts()


class _Stop(Exception):
    pass


class Dep:
    __slots__ = ("name", "w", "re", "rd", "sem", "cnt")

    def __init__(self, name):
        self.name = name
        self.w = None
        self.re = {}
        self.rd = None
        self.sem = None
        self.cnt = 0


class Sched:
    def __init__(self, nc, es):
        self.nc = nc
        self.es = es
        self.eng = {}
        for name, h in [("pe", nc.tensor), ("act", nc.scalar), ("dve", nc.vector),
                        ("pool", nc.gpsimd), ("sp", nc.sync)]:
            sem = es.enter_context(nc.semaphore("sem_" + name))
            self.eng[name] = dict(h=h, sem=sem, cnt=0, waited={})
        self.dma_deps = []

    def _wait(self, ename, entry):
        E = self.eng[ename]
        if entry[0] == "e":
            _, src, idx = entry
            if src == ename and ename == "pe":
                return
            sem = self.eng[src]["sem"]
            val = idx
        else:
            _, dep, n = entry
            sem = dep.sem
            val = 16 * n
        key = id(sem)
        if E["waited"].get(key, 0) >= val:
            return
        E["waited"][key] = val
        E["h"].wait_ge(sem, val)

    def _deps(self, ename, reads, writes):
        for d in reads:
            if d.w is not None:
                self._wait(ename, d.w)
        for d in writes:
            if d.w is not None and not (d.w[0] == "e" and d.w[1] == ename):
                self._wait(ename, d.w)
            for src, idx in d.re.items():
                if src != ename:
                    self._wait(ename, ("e", src, idx))
            if d.rd is not None:
                self._wait(ename, d.rd)

    def op(self, ename, fn, reads=(), writes=(), inc=True):
        E = self.eng[ename]
        self._deps(ename, reads, writes)
        ins = fn(E["h"])
        if inc:
            E["cnt"] += 1
            idx = E["cnt"]
            ins.then_inc(E["sem"], 1)
        else:
            idx = E["cnt"] + 1
        for d in reads:
            d.re[ename] = idx
        for d in writes:
            d.w = ("e", ename, idx)
            d.re = {}
            d.rd = None
        return ins

    def dma(self, qname, fn, dep, direction, extra_reads=()):
        E = self.eng[qname]
        if dep.sem is None:
            dep.sem = self.es.enter_context(self.nc.semaphore("dsem_" + dep.name))
            self.dma_deps.append(dep)
        if direction == "in":
            if dep.w is not None and dep.w[0] == "d" and dep.w[1] is dep and not dep.re and dep.rd is None:
                for d in extra_reads:
                    if d.w is not None:
                        self._wait(qname, d.w)
            else:
                self._deps(qname, list(extra_reads), [dep])
        else:
            self._deps(qname, list(extra_reads) + [dep], [])
        ins = fn(E["h"])
        dep.cnt += 1
        ins.then_inc(dep.sem, 16)
        ent = ("d", dep, dep.cnt)
        if direction == "in":
            dep.w = ent
            dep.re = {}
            dep.rd = None
        else:
            dep.rd = ent
        return ins

    def barrier(self):
        names = ["pe", "act", "dve", "pool", "sp"]
        for a in names:
            for b in names:
                if a != b and self.eng[b]["cnt"] > 0:
                    self._wait(a, ("e", b, self.eng[b]["cnt"]))
            for d in self.dma_deps:
                if d.cnt > 0:
                    self._wait(a, ("d", d, d.cnt))

    def finish(self):
        for d in self.dma_deps:
            if d.cnt > 0:
                self._wait("sp", ("d", d, d.cnt))
        for b in ["pe", "act", "dve", "pool"]:
            if self.eng[b]["cnt"] > 0:
                self._wait("sp", ("e", b, self.eng[b]["cnt"]))


def build(n_phys):
    nc = bass.Bass("TRN2", target_bir_lowering=False)
    es = contextlib.ExitStack()

    def din(name, shape, dt=F32):
        return nc.dram_tensor(name, shape, dt, kind="ExternalInput").ap()

    def dout(name, shape, dt=F32):
        return nc.dram_tensor(name, shape, dt, kind="ExternalOutput").ap()

    xp = din("xp", [TP, D])
    xs_d = din("xs", [TS, D])
    ck = din("ck", [n_phys * 128, 512])
    cv = din("cv", [n_phys * 128, 512])
    st0 = din("st0", [NSEQ, 8, 64, 64])
    pt = din("pt", [1, NSEQ * NPG], I32)
    cc = din("cc", [17, D])
    w_ada = din("w_ada", [D, 6 * D])
    vecs = din("vecs", [64, 128])
    bg = din("bg", [1, 2048])
    w_in = din("w_in", [D, 3584])
    qkg = din("qkg", [1, 3 * 512])
    sbb = din("sbb", [1, 8])
    lbl = din("lbl", [1, 1024])
    w_out = din("w_out", [D, D])
    w_up = din("w_up", [D, 4 * D])
    w_down = din("w_down", [4 * D, D])
    cstf = din("cstf", list(_CARR.shape))
    cstb = din("cstb", list(_BARR.shape))

    yp = dout("yp", [TP, D])
    ys = dout("ys", [TS, D])
    kp = dout("kp", [TP, 512])
    vp = dout("vp", [TP, 512])
    ks = dout("ks", [TS, 512])
    vs = dout("vs", [TS, 512])
    sp_o = dout("sp", [8, 64, 64])
    ss_o = dout("ss", [NSEQ, 8, 64, 64])

    S = Sched(nc, es)
    ucnt = [0]

    def sb(shape, dt=F32, side="left", stack=None, name=None):
        ucnt[0] += 1
        nm = (name or "t") + str(ucnt[0])
        return (stack or es).enter_context(nc.sbuf_tensor(nm, shape, dt, side=side))

    def dep(name="d"):
        ucnt[0] += 1
        return Dep(name + str(ucnt[0]))

    PS = [es.enter_context(nc.psum_tensor(f"ps{i}", [128, 512], F32)) for i in range(8)]
    PD = [dep(f"ps{i}_") for i in range(8)]

    CF = sb([128, _CARR.shape[1]], F32)
    dCF = dep("cf")
    S.dma("sp", lambda e: e.dma_start(out=CF[:], in_=cstf[:, :]), dCF, "in")

    def cf(key, rows=128, c0=0, c1=None):
        o, n = _COFF[key]
        c1 = n if c1 is None else c1
        return CF[0:rows, o + c0:o + c1]

    def load_bf_const(key, dst_ap, ddst, scratch, dscr):
        o, n = _BOFF[key]
        for a in range(0, n, 512):
            w_ = min(512, n - a)
            S.dma("sp", lambda e, a=a, w_=w_: e.dma_start(out=scratch[:, 0:w_], in_=cstb[:, o + a:o + a + w_]), dscr, "in")
            S.op("dve", lambda e, a=a, w_=w_: e.tensor_copy(out=dst_ap[:, a:a + w_], in_=scratch[:, 0:w_]),
                 reads=[dscr], writes=[ddst])

    ident = cf("ident")

    vecT = sb([128, 64], F32)
    dVec = dep("vecT")
    biasT = sb([128, 8], F32)
    dPar = dep("par")
    cT = sb([128, 8, 17], F32)
    dcT = dep("cT")
    mod = sb([128, 4, 8, 17], F32)
    dMod = dep("mod")
    p1s = contextlib.ExitStack()
    qkgB = sb([128, 1536], F32, stack=p1s)
    lbB = sb([128, 512], F32, stack=p1s)
    omlB = sb([128, 512], F32, stack=p1s)
    mixT = sb([128, 8, NTOK], BF16, side="right")
    dMixA = [dep("mixA") for _ in range(NT)]
    dMixB = [dep("mixB") for _ in range(NT)]

    def tile_rows(i):
        return 128 if i < 16 else 64

    def tile_cols(i):
        return (i * 128, i * 128 + tile_rows(i))

    def rsqrt_small(src_ap, dst_ap, n_scale, deps_r, dep_w, tmp_ap):
        S.op("dve", lambda e: e.tensor_scalar(out=tmp_ap, in0=src_ap, scalar1=n_scale, scalar2=EPS,
                                              op0=ALU.mult, op1=ALU.add), reads=deps_r, writes=[dep_w])
        S.op("act", lambda e: e.activation(out=tmp_ap, in_=tmp_ap, func=AF.Ln), reads=[dep_w], writes=[dep_w])
        S.op("act", lambda e: e.activation(out=dst_ap, in_=tmp_ap, func=AF.Exp, scale=-0.5),
             reads=[dep_w], writes=[dep_w])

    stop = int(os.environ.get("KSTOP", "99"))

    def chk(n):
        if stop == n:
            raise _Stop()

    try:
        with contextlib.ExitStack() as p0:
            wst = [sb([128, 8, 512], F32, stack=p0) for _ in range(2)]
            dW = [dep("wst") for _ in range(2)]
            cct = sb([17, D], F32, stack=p0)
            dcc = dep("cc")
            adaT = sb([128, 48, 17], F32, stack=p0)
            dAda = dep("adaT")
            vrow = sb([64, 128], F32, stack=p0)
            lraw = sb([128, 1024], F32, stack=p0)
            dtmp = dep("p0tmp")

            S.dma("sp", lambda e: e.dma_start(out=cct[:], in_=cc[:, :]), dcc, "in")
            S.dma("sp", lambda e: e.dma_start(out=vrow[:], in_=vecs[:, :]), dtmp, "in")
            S.dma("sp", lambda e: e.dma_start(out=qkgB[:], in_=qkg[0:1, :].to_broadcast([128, 1536])), dPar, "in")
            S.dma("sp", lambda e: e.dma_start(out=biasT[:], in_=sbb[0:1, :].to_broadcast([128, 8])), dPar, "in")
            S.dma("sp", lambda e: e.dma_start(out=lraw[:], in_=lbl[0:1, :].to_broadcast([128, 1024])), dtmp, "in")
            S.op("dve", lambda e: e.tensor_tensor(out=lraw[:, 0:512], in0=lraw[:, 0:512], in1=lraw[:, 512:1024],
                                                  op=ALU.subtract), reads=[dtmp], writes=[dtmp])
            S.op("act", lambda e: e.activation(out=lbB[:], in_=lraw[:, 0:512], func=AF.Sigmoid),
                 reads=[dtmp], writes=[dPar])
            S.op("dve", lambda e: e.tensor_scalar(out=omlB[:], in0=lbB[:], scalar1=-1.0, scalar2=1.0,
                                                  op0=ALU.mult, op1=ALU.add), reads=[dPar], writes=[dPar])
            S.op("act", lambda e: e.activation(out=cct[:], in_=cct[:], func=AF.Silu), reads=[dcc], writes=[dcc])
            for c in range(8):
                S.op("pe", lambda e, c=c: e.transpose(out=PS[0][:, c * 17:(c + 1) * 17],
                                                      in_=cct[0:17, c * 128:(c + 1) * 128], identity=cf("ident", 17, 0, 17)),
                     reads=[dcc, dCF], writes=[PD[0]], inc=(c == 7))
            S.op("dve", lambda e: e.tensor_copy(out=cT[:].rearrange("p c s -> p (c s)"), in_=PS[0][:, 0:136]),
                 reads=[PD[0]], writes=[dcT])
            S.op("pe", lambda e: e.transpose(out=PS[1][:, 0:64], in_=vrow[0:64, :], identity=cf("ident", 64, 0, 64)),
                 reads=[dtmp, dCF], writes=[PD[1]])
            S.op("dve", lambda e: e.tensor_copy(out=vecT[:], in_=PS[1][:, 0:64]), reads=[PD[1]], writes=[dVec])

            w_ada_v = w_ada.rearrange("(k p) n -> p k n", p=128)
            pcs = [0, 1, 2, 3, 6, 7, 8, 9]
            for n_, pc in enumerate(pcs):
                buf = n_ % 2
                S.dma("sp", lambda e, pc=pc, buf=buf: e.dma_start(out=wst[buf][:], in_=w_ada_v[:, :, pc * 512:(pc + 1) * 512]),
                      dW[buf], "in")
                for q in range(4):
                    chunk = (pc * 512) // 128 + q
                    bank = 2 + (q % 2)
                    for k in range(8):
                        S.op("pe", lambda e, k=k, q=q, bank=bank, buf=buf: e.matmul(
                            out=PS[bank][:, 0:17], lhsT=wst[buf][:, k, q * 128:(q + 1) * 128], rhs=cT[:, k, :],
                            start=(k == 0), stop=(k == 7)),
                             reads=[dcT, dW[buf]], writes=[PD[bank]], inc=(k == 7))
                    S.op("dve", lambda e, chunk=chunk, bank=bank: e.tensor_scalar(
                        out=adaT[:, chunk, :], in0=PS[bank][:, 0:17], scalar1=vecT[:, chunk:chunk + 1], scalar2=None,
                        op0=ALU.add), reads=[PD[bank], dVec], writes=[dAda])
            for (mi, scc, shc, nof) in ((0, 8, 0, 48), (2, 32, 24, 56)):
                S.op("dve", lambda e, mi=mi, scc=scc, nof=nof: e.scalar_tensor_tensor(
                    out=mod[:, mi, :, :], in0=adaT[:, scc:scc + 8, :], scalar=1.0,
                    in1=vecT[:, nof:nof + 8].unsqueeze(2).to_broadcast([128, 8, 17]), op0=ALU.add, op1=ALU.mult),
                     reads=[dAda, dVec], writes=[dMod])
                S.op("dve", lambda e, mi=mi, shc=shc: e.tensor_copy(out=mod[:, mi + 1, :, :], in_=adaT[:, shc:shc + 8, :]),
                     reads=[dAda], writes=[dMod])
            S.barrier()
            chk(0)

        def load_x_tile(i, xt, dxt, q="sp"):
            T = tile_rows(i)
            src = xp[i * 128:(i + 1) * 128, :] if i < 16 else xs_d[:, :]
            S.dma(q, lambda e: e.dma_start(out=xt[0:T, :], in_=src), dxt, "in")

        def norm_transpose(i, src_ap, dsrc, mi, hT_ap, dhT, work, dwork, banks):
            T = tile_rows(i)
            xn, sqj, st4 = work
            S.op("act", lambda e: e.activation(out=sqj[0:T, :], in_=src_ap, func=AF.Square, accum_out=st4[0:T, 0:1]),
                 reads=[dsrc], writes=[dwork])
            rsqrt_small(st4[0:T, 0:1], st4[0:T, 2:3], 1.0 / D, [dwork], dwork, st4[0:T, 1:2])
            S.op("dve", lambda e: e.tensor_scalar(out=xn[0:T, :], in0=src_ap, scalar1=st4[0:T, 2:3], scalar2=None,
                                                  op0=ALU.mult), reads=[dsrc, dwork], writes=[dwork])
            nb, tpb = (1, 128) if i < 16 else (16, 4)
            c0 = 16 if i < 16 else 0
            for half in range(2):
                bank = banks[half]
                for c4 in range(4):
                    c = half * 4 + c4
                    S.op("pe", lambda e, c=c, c4=c4, bank=bank: e.transpose(
                        out=PS[bank][:, c4 * 128:c4 * 128 + T], in_=xn[0:T, c * 128:(c + 1) * 128],
                        identity=cf("ident", T, 0, T)), reads=[dwork, dCF], writes=[PD[bank]], inc=(c4 == 3))
                pv = PS[bank][:, :].rearrange("p (c t) -> p c t", c=4)[:, :, 0:T].rearrange("p c (b t) -> p c b t", t=tpb)
                sc = mod[:, mi, half * 4:half * 4 + 4, c0:c0 + nb].unsqueeze(3).to_broadcast([128, 4, nb, tpb])
                sh = mod[:, mi + 1, half * 4:half * 4 + 4, c0:c0 + nb].unsqueeze(3).to_broadcast([128, 4, nb, tpb])
                tmpm = xn[:, half * 512:(half + 1) * 512].rearrange("p (c t) -> p c t", c=4)[:, :, 0:T].rearrange(
                    "p c (b t) -> p c b t", t=tpb)
                tm = sqj[:, half * 512:(half + 1) * 512].rearrange("p (c t) -> p c t", c=4)[:, :, 0:T].rearrange(
                    "p c (b t) -> p c b t", t=tpb)
                S.op("dve", lambda e, pv=pv, sc=sc, tm=tm: e.tensor_tensor(out=tm, in0=pv, in1=sc, op=ALU.mult),
                     reads=[PD[bank], dMod], writes=[dwork])
                ho = hT_ap[:, half * 4:half * 4 + 4, :].rearrange("p c (b t) -> p c b t", t=tpb)
                S.op("dve", lambda e, ho=ho, sh=sh, tm=tm: e.tensor_tensor(out=ho, in0=tm, in1=sh, op=ALU.add),
                     reads=[dwork, dMod], writes=[dhT])

        def load_weight_piece(src_ap, dst_bf_ap, wst, dW, buf, ddst, q="sp", cast_eng="pool"):
            S.dma(q, lambda e: e.dma_start(out=wst[buf][:], in_=src_ap), dW[buf], "in")
            S.op(cast_eng, lambda e: e.tensor_copy(out=dst_bf_ap, in_=wst[buf][:]), reads=[dW[buf]], writes=[ddst])

        with contextlib.ExitStack() as p1:
            hT = sb([128, 8, NTOK], BF16, stack=p1)
            dhT = [dep("hT") for _ in range(NT)]
            with contextlib.ExitStack() as p1n:
                xt = [sb([128, D], F32, stack=p1n) for _ in range(2)]
                dxt = [dep("xt") for _ in range(2)]
                wk = [(sb([128, D], F32, stack=p1n), sb([128, D], F32, stack=p1n), sb([128, 4], F32, stack=p1n)) for _ in range(2)]
                dwk = [dep("wk") for _ in range(2)]
                for i in [int(x) for x in os.environ["KTILES"].split(",")] if "KTILES" in os.environ else range(NT):
                    b = i % 2
                    load_x_tile(i, xt[b], dxt[b])
                    c0, c1 = tile_cols(i)
                    norm_transpose(i, xt[b][0:tile_rows(i), :], dxt[b], 0, hT[:, :, c0:c1], dhT[i], wk[b], dwk[b],
                                   (0, 1) if b == 0 else (2, 3))
                S.barrier()
                chk(1)

            w_in_v = w_in.rearrange("(k p) n -> p k n", p=128)

            with contextlib.ExitStack() as pb:
                whg = sb([128, 8, 2048], BF16, stack=pb)
                dwhg = [dep("whg") for _ in range(4)]
                with contextlib.ExitStack() as pw:
                    wst = [sb([128, 8, 512], F32, stack=pw) for _ in range(2)]
                    dW = [dep("wst") for _ in range(2)]
                    for g in range(4):
                        load_weight_piece(w_in_v[:, :, 1536 + g * 512:1536 + (g + 1) * 512], whg[:, :, g * 512:(g + 1) * 512],
                                          wst, dW, g % 2, dwhg[g])
                    S.barrier()

                def wt(shape, dt=F32):
                    return sb(shape, dt, stack=pb)

                AB = (4, 2)
                XB = (6, 3)

                def hsel(ap, h2):
                    return ap.rearrange("p (j a t) -> p j a t", j=4, a=2)[:, :, h2, :]

                hq = wt([128, 512]); ff = wt([128, 512]); logf = wt([128, 512]); omf = wt([128, 512])
                hv = wt([128, 512], BF16); sg = wt([128, 512]); eb = wt([128, 512]); enb = wt([128, 512])
                ec = wt([128, 512]); kk = wt([128, 512], BF16)
                qdT = wt([128, 4, 128], BF16); kdT = wt([128, 4, 128], BF16)
                attm = wt([128, 512], BF16); oo = wt([128, 512]); osq = wt([128, 512])
                st8 = wt([128, 24]); dec = wt([128, 4, 16])
                Sst = wt([128, 4, 64]); Sbf = wt([128, 4, 64], BF16)
                dE = dep("hgE")
                dT_ = dep("hgT")
                dA = dep("attm"); dO = dep("oo"); dS_ = dep("S"); dSb = dep("Sbf"); dDec = dep("dec")
                S.op("dve", lambda e: e.memset(Sst[:], 0.0), writes=[dS_])
                S.op("dve", lambda e: e.memset(Sbf[:], 0.0), writes=[dSb])
                S0t = [wt([128, 16, 64]) for _ in range(2)]
                S0bt = wt([128, 16, 64], BF16)
                qdTm = wt([128, 4, 16, 64], BF16); hvm = wt([64, 16, 128], BF16)
                smT = wt([128, 1024], BF16)
                dS0 = [dep("S0") for _ in range(2)]; dS0b = dep("S0b"); dqm = dep("qdTm"); dhvm = dep("hvm"); dsm = dep("smT")
                st0_v = st0.rearrange("b (j h) k v -> (h k) j b v", h=2)
                ss_v = ss_o.rearrange("b (j h) k v -> (h k) j b v", h=2)
                load_bf_const("seqmaskT", smT, dsm, hq, dE)

                for i in range(NT):
                    T = tile_rows(i)
                    c0, c1 = tile_cols(i)
                    samp = (i == 16)
                    tri = cf("tri4" if samp else "tri2", T, 0, T)
                    tsu = cf("tsu4" if samp else "tsu2", T, 0, T)
                    ncn = 16 if samp else 2
                    ind = cf("seqind" if samp else "chunkind", T)
                    def proj(g, bank):
                        for k in range(8):
                            S.op("pe", lambda e, k=k: e.matmul(out=PS[bank][0:T, :], lhsT=hT[:, k, c0:c1],
                                                               rhs=whg[:, k, g * 512:(g + 1) * 512],
                                                               start=(k == 0), stop=(k == 7)),
                                 reads=[dhT[i], dwhg[g]], writes=[PD[bank]], inc=(k == 7))
                    proj(0, 2)
                    S.op("act", lambda e: e.activation(out=hq[0:T, :], in_=PS[2][0:T, :], func=AF.Silu),
                         reads=[PD[2]], writes=[dE])
                    proj(1, 3)
                    S.op("act", lambda e: e.activation(out=ff[0:T, :], in_=PS[3][0:T, :], func=AF.Sigmoid),
                         reads=[PD[3]], writes=[dE])
                    S.op("dve", lambda e: e.tensor_tensor(out=ff[0:T, :], in0=ff[0:T, :], in1=omlB[0:T, :], op=ALU.mult),
                         reads=[dE, dPar], writes=[dE])
                    S.op("dve", lambda e: e.tensor_tensor(out=ff[0:T, :], in0=ff[0:T, :], in1=lbB[0:T, :], op=ALU.add),
                         reads=[dE, dPar], writes=[dE])
                    S.op("act", lambda e: e.activation(out=logf[0:T, :], in_=ff[0:T, :], func=AF.Ln),
                         reads=[dE], writes=[dE])
                    S.op("dve", lambda e: e.tensor_scalar(out=omf[0:T, :], in0=ff[0:T, :], scalar1=-1.0, scalar2=1.0,
                                                          op0=ALU.mult, op1=ALU.add), reads=[dE], writes=[dE])
                    proj(2, 2)
                    S.op("act", lambda e: e.activation(out=hv[0:T, :], in_=PS[2][0:T, :], func=AF.Copy),
                         reads=[PD[2]], writes=[dE])
                    proj(3, 3)
                    S.op("act", lambda e: e.activation(out=sg[0:T, :], in_=PS[3][0:T, :], func=AF.Silu),
                         reads=[PD[3]], writes=[dE])
                    S.op("pe", lambda e: e.matmul(out=PS[0][0:T, :], lhsT=tri, rhs=logf[0:T, :], start=True, stop=True),
                         reads=[dE, dCF], writes=[PD[0]])
                    S.op("pe", lambda e: e.matmul(out=PS[1][0:T, :], lhsT=tsu, rhs=logf[0:T, :], start=True, stop=True),
                         reads=[dE, dCF], writes=[PD[1]])
                    for j in range(4):
                        S.op("pe", lambda e, j=j: e.matmul(out=PS[7][:, 256 + j * ncn:256 + (j + 1) * ncn],
                                                           lhsT=logf[0:T, j * 128:(j + 1) * 128], rhs=ind,
                                                           start=True, stop=True),
                             reads=[dE, dCF], writes=[PD[7]], inc=(j == 3))
                    S.op("act", lambda e: e.activation(out=eb[0:T, :], in_=PS[0][0:T, :], func=AF.Exp),
                         reads=[PD[0]], writes=[dE])
                    S.op("act", lambda e: e.activation(out=enb[0:T, :], in_=PS[0][0:T, :], func=AF.Exp, scale=-1.0),
                         reads=[PD[0]], writes=[dE])
                    S.op("act", lambda e: e.activation(out=ec[0:T, :], in_=PS[1][0:T, :], func=AF.Exp),
                         reads=[PD[1]], writes=[dE])
                    S.op("act", lambda e: e.activation(out=dec[:, :, 0:ncn],
                                                       in_=PS[7][:, 256:256 + 4 * ncn].rearrange("p (j c) -> p j c", j=4),
                                                       func=AF.Exp), reads=[PD[7]], writes=[dDec])
                    S.op("dve", lambda e: e.tensor_tensor(out=eb[0:T, :], in0=hq[0:T, :], in1=eb[0:T, :], op=ALU.mult),
                         reads=[dE], writes=[dE])
                    S.op("dve", lambda e: e.tensor_tensor(out=enb[0:T, :], in0=omf[0:T, :], in1=enb[0:T, :], op=ALU.mult),
                         reads=[dE], writes=[dE])
                    S.op("dve", lambda e: e.tensor_tensor(out=kk[0:T, :], in0=omf[0:T, :], in1=ec[0:T, :], op=ALU.mult),
                         reads=[dE], writes=[dE])
                    for (src, dst, bank) in ((eb, qdT, 0), (enb, kdT, 1)):
                        for j in range(4):
                            S.op("pe", lambda e, j=j, src=src, bank=bank: e.transpose(
                                out=PS[bank][:, j * 128:j * 128 + T], in_=src[0:T, j * 128:(j + 1) * 128],
                                identity=cf("ident", T, 0, T)), reads=[dE, dCF], writes=[PD[bank]], inc=(j == 3))
                        S.op("act", lambda e, dst=dst, bank=bank: e.activation(
                            out=dst[:, :, 0:T], in_=PS[bank][:, :].rearrange("p (j t) -> p j t", j=4)[:, :, 0:T],
                            func=AF.Copy), reads=[PD[bank]], writes=[dT_])

                    if not samp:
                        for c in range(2):
                            cp = 64 * c
                            for h in range(8):
                                j, h2 = h // 2, h % 2
                                hp = 64 * h2
                                ab = AB[h2]
                                S.op("pe", lambda e, h=h, j=j, hp=hp, cp=cp, ab=ab: e.matmul(
                                    out=PS[ab][cp:cp + 64, h * 64:(h + 1) * 64], lhsT=kdT[hp:hp + 64, j, cp:cp + 64],
                                    rhs=qdT[hp:hp + 64, j, cp:cp + 64], start=True, stop=True),
                                     reads=[dT_], writes=[PD[ab]], inc=(h >= 6))
                            for h2 in range(2):
                                ab = AB[h2]
                                S.op("dve", lambda e, cp=cp, ab=ab, h2=h2: e.tensor_tensor(
                                    out=hsel(attm[cp:cp + 64, :], h2), in0=hsel(PS[ab][cp:cp + 64, :], h2),
                                    in1=cf("maskH")[cp:cp + 64, :].unsqueeze(1).to_broadcast([64, 4, 64]), op=ALU.mult),
                                     reads=[PD[ab], dCF], writes=[dA])
                            for h in range(8):
                                j, h2 = h // 2, h % 2
                                hp = 64 * h2
                                hs = slice(h * 64, (h + 1) * 64)
                                S.op("pe", lambda e, hs=hs, cp=cp: e.matmul(
                                    out=PS[5][cp:cp + 64, hs], lhsT=attm[cp:cp + 64, hs], rhs=hv[cp:cp + 64, hs],
                                    start=True, stop=True), reads=[dA, dE], writes=[PD[5]], inc=False)
                                xb = XB[h2]
                                S.op("pe", lambda e, hs=hs, cp=cp, hp=hp, j=j, xb=xb: e.matmul(
                                    out=PS[xb][cp:cp + 64, hs], lhsT=qdT[hp:hp + 64, j, cp:cp + 64], rhs=Sbf[hp:hp + 64, j, :],
                                    start=True, stop=True), reads=[dT_, dSb], writes=[PD[xb]], inc=False)
                                S.op("pe", lambda e, hs=hs, cp=cp, hp=hp, j=j: e.matmul(
                                    out=PS[7][hp:hp + 64, j * 64:(j + 1) * 64], lhsT=kk[cp:cp + 64, hs], rhs=hv[cp:cp + 64, hs],
                                    start=True, stop=True), reads=[dE], writes=[PD[7]], inc=(h == 7))
                            S.op("dve", lambda e, c=c: e.tensor_tensor(
                                out=Sst[:], in0=Sst[:], in1=dec[:, :, c:c + 1].to_broadcast([128, 4, 64]), op=ALU.mult),
                                 reads=[dS_, dDec], writes=[dS_])
                            S.op("dve", lambda e: e.tensor_tensor(
                                out=Sst[:], in0=Sst[:], in1=PS[7][:, 0:256].rearrange("p (j v) -> p j v", j=4), op=ALU.add),
                                 reads=[dS_, PD[7]], writes=[dS_])
                            S.op("act", lambda e: e.activation(out=Sbf[:], in_=Sst[:], func=AF.Copy),
                                 reads=[dS_], writes=[dSb])
                        if i == 15:
                            S.dma("sp", lambda e: e.dma_start(out=sp_o.rearrange("(j h) k v -> (h k) j v", h=2), in_=Sst[:]),
                                  dS_, "out")
                    else:
                        for h in range(8):
                            j, h2 = h // 2, h % 2
                            hp = 64 * h2
                            ab = AB[h2]
                            S.op("pe", lambda e, h=h, j=j, hp=hp, ab=ab: e.matmul(
                                out=PS[ab][0:64, h * 64:(h + 1) * 64], lhsT=kdT[hp:hp + 64, j, 0:64],
                                rhs=qdT[hp:hp + 64, j, 0:64], start=True, stop=True),
                                 reads=[dT_], writes=[PD[ab]], inc=(h >= 6))
                        for h2 in range(2):
                            ab = AB[h2]
                            S.op("dve", lambda e, ab=ab, h2=h2: e.tensor_tensor(
                                out=hsel(attm[0:64, :], h2), in0=hsel(PS[ab][0:64, :], h2),
                                in1=cf("maskHS")[0:64, :].unsqueeze(1).to_broadcast([64, 4, 64]), op=ALU.mult),
                                 reads=[PD[ab], dCF], writes=[dA])
                        S.op("dve", lambda e: e.tensor_tensor(
                            out=qdTm[:], in0=qdT[:, :, 0:64].unsqueeze(2).to_broadcast([128, 4, 16, 64]),
                            in1=smT[:].rearrange("p (b t) -> p b t", b=16).unsqueeze(1).to_broadcast([128, 4, 16, 64]),
                            op=ALU.mult), reads=[dT_, dsm], writes=[dqm])
                        for h in range(8):
                            hs = slice(h * 64, (h + 1) * 64)
                            S.op("pe", lambda e, hs=hs: e.matmul(out=PS[5][0:64, hs], lhsT=attm[0:64, hs], rhs=hv[0:64, hs],
                                                                 start=True, stop=True),
                                 reads=[dA, dE], writes=[PD[5]], inc=(h == 7))
                        for j in range(4):
                            sb_ = j % 2
                            S0j = S0t[sb_]
                            S.dma("sp", lambda e, j=j, S0j=S0j: e.dma_start(out=S0j[:], in_=st0_v[:, j, :, :]), dS0[sb_], "in")
                            S.op("act", lambda e, S0j=S0j: e.activation(out=S0bt[:].rearrange("p b v -> p (b v)"),
                                                                        in_=S0j[:].rearrange("p b v -> p (b v)"), func=AF.Copy),
                                 reads=[dS0[sb_]], writes=[dS0b])
                            S.op("dve", lambda e, j=j: e.tensor_tensor(
                                out=hvm[:], in0=hv[0:64, j * 128:(j + 1) * 128].unsqueeze(1).to_broadcast([64, 16, 128]),
                                in1=cf("seqind", 64).unsqueeze(2).to_broadcast([64, 16, 128]), op=ALU.mult),
                                 reads=[dE, dCF], writes=[dhvm])
                            for h2 in range(2):
                                h = 2 * j + h2
                                hp = 64 * h2
                                hs = slice(h * 64, (h + 1) * 64)
                                for b in range(16):
                                    S.op("pe", lambda e, hs=hs, hp=hp, j=j, b=b, h2=h2: e.matmul(
                                        out=PS[XB[h2]][0:64, hs], lhsT=qdTm[hp:hp + 64, j, b, :], rhs=S0bt[hp:hp + 64, b, :],
                                        start=(b == 0), stop=(b == 15)),
                                         reads=[dqm, dS0b], writes=[PD[XB[h2]]], inc=(b == 15))
                                for half in range(2):
                                    S.op("pe", lambda e, h=h, hp=hp, half=half, h2=h2: e.matmul(
                                        out=PS[half][hp:hp + 64, :], lhsT=kk[0:64, h * 64:(h + 1) * 64],
                                        rhs=hvm[0:64, half * 8:(half + 1) * 8, h2 * 64:(h2 + 1) * 64],
                                        start=True, stop=True), reads=[dE, dhvm], writes=[PD[half]])
                            for half in range(2):
                                bs = slice(half * 8, (half + 1) * 8)
                                S.op("dve", lambda e, j=j, bs=bs, S0j=S0j: e.tensor_tensor(
                                    out=S0j[:, bs, :], in0=S0j[:, bs, :],
                                    in1=dec[:, j, bs].unsqueeze(2).to_broadcast([128, 8, 64]), op=ALU.mult),
                                     reads=[dS0[sb_], dDec], writes=[dS0[sb_]])
                                S.op("dve", lambda e, bs=bs, half=half, S0j=S0j: e.tensor_tensor(
                                    out=S0j[:, bs, :], in0=S0j[:, bs, :],
                                    in1=PS[half][:, :].rearrange("p (b v) -> p b v", b=8), op=ALU.add),
                                     reads=[dS0[sb_], PD[half]], writes=[dS0[sb_]])
                            S.dma("sp", lambda e, j=j, S0j=S0j: e.dma_start(out=ss_v[:, j, :, :], in_=S0j[:]), dS0[sb_], "out")

                    S.op("act", lambda e: e.activation(out=oo[0:T, :], in_=PS[5][0:T, :], func=AF.Copy),
                         reads=[PD[5]], writes=[dO])
                    for h2 in range(2):
                        S.op("dve", lambda e, h2=h2: e.tensor_tensor(out=hsel(oo[0:T, :], h2), in0=hsel(oo[0:T, :], h2),
                                                              in1=hsel(PS[XB[h2]][0:T, :], h2), op=ALU.add),
                             reads=[dO, PD[XB[h2]]], writes=[dO])
                    S.op("dve", lambda e: e.tensor_tensor(out=osq[0:T, :], in0=oo[0:T, :], in1=oo[0:T, :], op=ALU.mult),
                         reads=[dO], writes=[dO])
                    S.op("dve", lambda e: e.tensor_reduce(out=st8[0:T, 0:8], in_=osq[0:T, :].rearrange("p (h v) -> p h v", h=8),
                                                          axis=AX.X, op=ALU.add), reads=[dO], writes=[dO])
                    rsqrt_small(st8[0:T, 0:8], st8[0:T, 16:24], 1.0 / 64, [dO], dO, st8[0:T, 8:16])
                    S.op("dve", lambda e: e.tensor_tensor(
                        out=oo[0:T, :].rearrange("p (h v) -> p h v", h=8), in0=oo[0:T, :].rearrange("p (h v) -> p h v", h=8),
                        in1=st8[0:T, 16:24].unsqueeze(2).to_broadcast([T, 8, 64]), op=ALU.mult), reads=[dO], writes=[dO])
                    S.op("dve", lambda e: e.tensor_tensor(out=oo[0:T, :], in0=oo[0:T, :], in1=qkgB[0:T, 1024:1536], op=ALU.mult),
                         reads=[dO, dPar], writes=[dO])
                    S.op("dve", lambda e: e.tensor_tensor(out=oo[0:T, :], in0=oo[0:T, :], in1=sg[0:T, :], op=ALU.mult),
                         reads=[dO, dE], writes=[dO])
                    for j in range(4):
                        S.op("pe", lambda e, j=j: e.transpose(out=PS[4][:, j * 128:j * 128 + T], in_=oo[0:T, j * 128:(j + 1) * 128],
                                                              identity=cf("ident", T, 0, T)),
                             reads=[dO, dCF], writes=[PD[4]], inc=(j == 3))
                    S.op("act", lambda e: e.activation(out=mixT[:, 4:8, c0:c1],
                                                       in_=PS[4][:, :].rearrange("p (j t) -> p j t", j=4)[:, :, 0:T],
                                                       func=AF.Copy), reads=[PD[4]], writes=[dMixB[i]])
                S.barrier()
                chk(2)

            pa_r = contextlib.ExitStack()
            qT = sb([128, 4, NTOK], BF16, side="right", stack=pa_r)
            kT = sb([128, 4, NTOK], BF16, side="right", stack=pa_r)
            vres = sb([128, NT, 512], BF16, side="right", stack=pa_r)
            dQ = [dep("qT") for _ in range(NT)]
            dK = [dep("kT") for _ in range(NT)]
            dV = [dep("v") for _ in range(NT)]
            with contextlib.ExitStack() as pa:
                wat = sb([128, 8, 1536], BF16, stack=pa)
                dwat = [dep("wat") for _ in range(3)]
                with contextlib.ExitStack() as pw:
                    wst = [sb([128, 8, 512], F32, stack=pw) for _ in range(2)]
                    dW = [dep("wst") for _ in range(2)]
                    for g in range(3):
                        load_weight_piece(w_in_v[:, :, g * 512:(g + 1) * 512], wat[:, :, g * 512:(g + 1) * 512],
                                          wst, dW, g % 2, dwat[g])
                    S.barrier()
                sq = sb([128, 512], F32, stack=pa)
                qn = [sb([128, 512], F32, stack=pa) for _ in range(2)]
                dqn = [dep("qn") for _ in range(2)]
                kn = [sb([128, 512], F32, stack=pa) for _ in range(2)]
                dkn = [dep("kn") for _ in range(2)]
                vn = [sb([128, 512], F32, stack=pa) for _ in range(2)]
                dvn = [dep("vn") for _ in range(2)]
                s8 = sb([128, 24], F32, stack=pa)
                dsq = dep("sq")
                for i in range(NT):
                    T = tile_rows(i)
                    c0, c1 = tile_cols(i)
                    b = i % 2

                    def proj(g, bank):
                        for k in range(8):
                            S.op("pe", lambda e, k=k: e.matmul(out=PS[bank][0:T, :], lhsT=hT[:, k, c0:c1],
                                                               rhs=wat[:, k, g * 512:(g + 1) * 512],
                                                               start=(k == 0), stop=(k == 7)),
                                 reads=[dhT[i], dwat[g]], writes=[PD[bank]], inc=(k == 7))

                    def qknorm(bank, dst, ddst, goff):
                        S.op("act", lambda e: e.activation(out=sq[0:T, :], in_=PS[bank][0:T, :], func=AF.Square),
                             reads=[PD[bank]], writes=[dsq])
                        S.op("dve", lambda e: e.tensor_reduce(out=s8[0:T, 0:8], in_=sq[0:T, :].rearrange("p (h d) -> p h d", h=8),
                                                              axis=AX.X, op=ALU.add), reads=[dsq], writes=[dsq])
                        rsqrt_small(s8[0:T, 0:8], s8[0:T, 16:24], 1.0 / 64, [dsq], dsq, s8[0:T, 8:16])
                        S.op("dve", lambda e: e.tensor_tensor(
                            out=dst[0:T, :].rearrange("p (h d) -> p h d", h=8),
                            in0=PS[bank][0:T, :].rearrange("p (h d) -> p h d", h=8),
                            in1=s8[0:T, 16:24].unsqueeze(2).to_broadcast([T, 8, 64]), op=ALU.mult),
                             reads=[PD[bank], dsq], writes=[ddst])
                        S.op("dve", lambda e: e.tensor_tensor(out=dst[0:T, :], in0=dst[0:T, :], in1=qkgB[0:T, goff:goff + 512],
                                                              op=ALU.mult), reads=[ddst, dPar], writes=[ddst])

                    def to_featT(src, dsrc, bank, dst, ddst):
                        for j in range(4):
                            S.op("pe", lambda e, j=j: e.transpose(out=PS[bank][:, j * 128:j * 128 + T],
                                                                  in_=src[0:T, j * 128:(j + 1) * 128],
                                                                  identity=cf("ident", T, 0, T)),
                                 reads=[dsrc, dCF], writes=[PD[bank]], inc=(j == 3))
                        S.op("act", lambda e: e.activation(out=dst[:, :, c0:c1],
                                                           in_=PS[bank][:, :].rearrange("p (j t) -> p j t", j=4)[:, :, 0:T],
                                                           func=AF.Copy), reads=[PD[bank]], writes=[ddst])

                    proj(0, 0)
                    qknorm(0, qn[b], dqn[b], 0)
                    to_featT(qn[b], dqn[b], 3, qT, dQ[i])
                    proj(1, 1)
                    qknorm(1, kn[b], dkn[b], 512)
                    to_featT(kn[b], dkn[b], 4, kT, dK[i])
                    kdst = kp[i * 128:(i + 1) * 128, :] if i < 16 else ks[:, :]
                    S.dma("sp", lambda e: e.dma_start(out=kdst, in_=kn[b][0:T, :]), dkn[b], "out")
                    proj(2, 2)
                    S.op("act", lambda e: e.activation(out=vn[b][0:T, :], in_=PS[2][0:T, :], func=AF.Copy),
                         reads=[PD[2]], writes=[dvn[b]])
                    S.op("dve", lambda e: e.tensor_copy(out=vres[0:T, i, :], in_=vn[b][0:T, :]),
                         reads=[dvn[b]], writes=[dV[i]])
                    vdst = vp[i * 128:(i + 1) * 128, :] if i < 16 else vs[:, :]
                    S.dma("sp", lambda e: e.dma_start(out=vdst, in_=vn[b][0:T, :]), dvn[b], "out")
                S.barrier()
                chk(3)
        p1s.close()

        with contextlib.ExitStack() as p2:
            ulT = sb([128, 256], BF16, stack=p2)
            dCB = dep("cb")
            uincl = ulT[:, 0:128]
            lstrict = ulT[:, 128:256]
            qblk = sb([128, 4, 16, 2, 4], BF16, stack=p2)
            kTp = sb([128, 4, 128], BF16, stack=p2)
            vpad = sb([128, 512], BF16, stack=p2)
            ebF = sb([128, 512], F32, stack=p2)
            mS2 = sb([128, 128], F32, stack=p2)
            dpad = dep("pad")
            with contextlib.ExitStack() as p2a:
                NSET = 4
                eT = [sb([128, 512], F32, stack=p2a) for _ in range(NSET)]
                xT_ = [sb([128, 512], F32, stack=p2a) for _ in range(NSET)]
                LpT = [sb([128, 512], BF16, stack=p2a) for _ in range(NSET)]
                wT = [sb([128, 512], BF16, stack=p2a) for _ in range(NSET)]
                de = [dep("e") for _ in range(NSET)]
                dx = [dep("x") for _ in range(NSET)]
                dL = [dep("L") for _ in range(NSET)]
                dw = [dep("w") for _ in range(NSET)]
                mDT = sb([128, 2048], BF16, stack=p2a)
                load_bf_const("uincl", ulT[:, 0:128], dCB, eT[0], de[0])
                load_bf_const("lstrict", ulT[:, 128:256], dCB, eT[1], de[1])
                load_bf_const("maskD", mDT, dCB, eT[0], de[0])
                units = []
                for j in range(4):
                    for QB in range(4):
                        nkb = 4 * QB + 4
                        for st, kb in enumerate(range(nkb - 1, -1, -1)):
                            for h2 in range(2):
                                units.append((j, QB, st, kb, h2))

                def geom(u):
                    j, QB, st, kb, h2 = u
                    ii = kb - 4 * QB
                    clo = 128 * ii if ii > 0 else 0
                    return ii, clo, slice(clo, 512), slice(QB * 512 + clo, (QB + 1) * 512)

                def stage_a(n):
                    j, QB, st, kb, h2 = units[n]
                    ii, clo, cs, qs = geom(units[n])
                    h = 2 * j + h2
                    hp = 64 * h2
                    Zb = h2
                    si = n % NSET
                    qdeps = [dQ[t] for t in range(QB * 4, QB * 4 + 4)]
                    S.op("pe", lambda e: e.matmul(
                        out=PS[Zb][:, cs], lhsT=kT[hp:hp + 64, j, kb * 128:(kb + 1) * 128], rhs=qT[hp:hp + 64, j, qs],
                        start=True, stop=True), reads=[dK[kb]] + qdeps, writes=[PD[Zb]])
                    S.op("act", lambda e: e.activation(
                        out=eT[si][:, cs], in_=PS[Zb][:, cs], func=AF.Exp, scale=SCALE, bias=biasT[:, h:h + 1]),
                         reads=[PD[Zb], dPar], writes=[de[si]])
                    if ii >= 0:
                        S.op("dve", lambda e: e.tensor_tensor(
                            out=eT[si][:, cs], in0=eT[si][:, cs], in1=mDT[:, ii * 512 + clo:(ii + 1) * 512],
                            op=ALU.mult), reads=[de[si], dCB], writes=[de[si]])
                    S.op("act", lambda e: e.activation(out=LpT[si][:, cs], in_=eT[si][:, cs], func=AF.Ln, bias=1.0),
                         reads=[de[si]], writes=[dL[si]])

                def stage_b(n):
                    j, QB, st, kb, h2 = units[n]
                    ii, clo, cs, qs = geom(units[n])
                    h = 2 * j + h2
                    hp = 64 * h2
                    Cb = 2 + h2
                    si = n % NSET
                    S.op("pe", lambda e: e.matmul(
                        out=PS[Cb][:, cs], lhsT=uincl, rhs=LpT[si][:, cs], start=(st == 0), stop=False,
                        skip_group_check=True), reads=[dL[si], dCB], writes=[PD[Cb]])
                    S.op("act", lambda e: e.activation(out=xT_[si][:, cs], in_=PS[Cb][:, cs], func=AF.Exp, scale=-1.0),
                         reads=[PD[Cb]], writes=[dx[si]])
                    if kb > 0:
                        S.op("pe", lambda e: e.matmul(
                            out=PS[Cb][:, cs], lhsT=lstrict, rhs=LpT[si][:, cs], start=False, stop=(kb == 1),
                            skip_group_check=True), reads=[dL[si], dCB], writes=[PD[Cb]])
                    S.op("dve", lambda e: e.tensor_tensor(out=wT[si][:, cs], in0=eT[si][:, cs], in1=xT_[si][:, cs], op=ALU.mult),
                         reads=[de[si], dx[si]], writes=[dw[si]])
                    ob = 4 + ((j * 4 + QB) % 2)
                    S.op("pe", lambda e: e.matmul(
                        out=PS[ob][hp:hp + 64, cs], lhsT=vres[:, kb, h * 64:(h + 1) * 64], rhs=wT[si][:, cs],
                        start=(st == 0), stop=(kb == 0), skip_group_check=True),
                         reads=[dw[si], dV[kb]], writes=[PD[ob]])
                    if kb == 0 and h2 == 1:
                        S.op("act", lambda e: e.activation(out=mixT[:, j, QB * 512:(QB + 1) * 512], in_=PS[ob][:, :],
                                                           func=AF.Copy),
                             reads=[PD[ob]], writes=[dMixA[QB * 4 + tt] for tt in range(4)])

                LA = int(os.environ.get("KLA", "1"))
                for n in range(min(LA, len(units))):
                    stage_a(n)
                for n in range(len(units)):
                    if n + LA < len(units):
                        stage_a(n + LA)
                    stage_b(n)
                S.op("dve", lambda e: e.memset(kTp[:], 0.0), writes=[dpad])
                S.op("dve", lambda e: e.memset(vpad[:], 0.0), writes=[dpad])
                S.op("dve", lambda e: e.memset(qblk[:].rearrange("p j b a t -> p (j b a t)"), 0.0), writes=[dpad])
                S.op("dve", lambda e: e.tensor_copy(out=kTp[:, :, 0:64], in_=kT[:, :, TP:NTOK]), reads=[dK[16], dpad], writes=[dpad])
                S.op("dve", lambda e: e.tensor_copy(out=vpad[0:64, :], in_=vres[0:64, 16, :]), reads=[dV[16], dpad], writes=[dpad])
                for h2 in range(2):
                    S.op("dve", lambda e, h2=h2: e.tensor_copy(
                        out=qblk[64 * h2:64 * h2 + 64, :, :, h2, :],
                        in_=qT[64 * h2:64 * h2 + 64, :, TP:NTOK].rearrange("p j (b t) -> p j b t", t=4)),
                         reads=[dQ[16], dpad], writes=[dpad])
                S.op("act", lambda e: e.activation(out=eT[0][:, 0:8], in_=biasT[:, :], func=AF.Exp), reads=[dPar, de[0]],
                     writes=[de[0]])
                for j in range(4):
                    S.op("dve", lambda e, j=j: e.tensor_copy(
                        out=ebF[:, j * 128:(j + 1) * 128].rearrange("p (b a t) -> p b a t", b=16, a=2),
                        in_=eT[0][:, 2 * j:2 * j + 2].unsqueeze(1).unsqueeze(3).to_broadcast([128, 16, 2, 4])),
                         reads=[de[0]], writes=[dpad])
                S.op("dve", lambda e: e.tensor_copy(
                    out=mS2[:].rearrange("p (b a t) -> p b a t", b=16, a=2),
                    in_=cf("maskS").rearrange("p (b t) -> p b t", t=4).unsqueeze(2).to_broadcast([128, 16, 2, 4])),
                     reads=[dCF], writes=[dpad])
                S.barrier()
            chk(4)
            pa_r.close()

            NQ = 8
            GW = NQ * 8
            Kst = [sb([128, NQ, 512], F32, stack=p2) for _ in range(2)]
            Vst = [sb([128, NQ, 512], F32, stack=p2) for _ in range(2)]
            KTs = [sb([128, NQ, 4, 128], BF16, stack=p2) for _ in range(2)]
            dKst = [dep("Kst") for _ in range(2)]
            dVst = [dep("Vst") for _ in range(2)]
            dKTs = [dep("KTs") for _ in range(2)]
            ptb = sb([128, NSEQ * NPG], I32, stack=p2)
            idx = sb([128, NSEQ * NPG], I32, stack=p2)
            iop = sb([128, 1], I32, stack=p2)
            dIdx = dep("idx")
            es_ = [sb([128, 4, 128], F32, stack=p2) for _ in range(2)]
            xs2 = [sb([128, 4, 128], F32, stack=p2) for _ in range(2)]
            Ls = [sb([128, 4, 128], BF16, stack=p2) for _ in range(2)]
            ws = [sb([128, 4, 128], F32, stack=p2) for _ in range(2)]
            wsb = sb([128, 4, 128], BF16, stack=p2)
            des = [dep("es") for _ in range(2)]
            dxs = [dep("xs") for _ in range(2)]
            dLs = [dep("Ls") for _ in range(2)]
            dws = [dep("ws") for _ in range(2)]

            S.dma("pool", lambda e: e.dma_start(out=ptb[:], in_=pt[0:1, :].to_broadcast([128, NSEQ * NPG])), dIdx, "in")
            S.op("pool", lambda e: e.iota(iop[:], pattern=[[0, 1]], base=0, channel_multiplier=1), writes=[dIdx])
            S.op("pool", lambda e: e.tensor_scalar(out=idx[:], in0=ptb[:], scalar1=128, scalar2=None, op0=ALU.mult),
                 reads=[dIdx], writes=[dIdx])
            S.op("pool", lambda e: e.tensor_tensor(out=idx[:], in0=idx[:], in1=iop[:].to_broadcast([128, NSEQ * NPG]),
                                                   op=ALU.add), reads=[dIdx], writes=[dIdx])

            dZ, dC, dOs = dep("Z"), dep("C"), dep("Os")

            def bview(bank, c0, w_):
                return PS[bank][:, :].rearrange("p (j c) -> p j c", j=4)[:, :, c0:c0 + w_]

            def sample_step(kind, gi=None, buf=None, last=False, first=False, b2=0):
                c0, w_ = (0, 128) if kind == "new" else (gi * GW, GW)
                if kind == "new":
                    for j in range(4):
                        S.op("pe", lambda e, j=j: e.matmul(out=PS[0][:, j * 128:(j + 1) * 128], lhsT=kTp[:, j, :],
                                                           rhs=qblk[:, j, :, :, :].rearrange("p b a t -> p (b a t)"),
                                                           start=True, stop=True), reads=[dpad], writes=[dZ], inc=(j == 3))
                else:
                    for bi in range(NQ):
                        b = gi * NQ + bi
                        for j in range(4):
                            S.op("pe", lambda e, bi=bi, b=b, j=j: e.matmul(
                                out=PS[0][:, j * 128 + b * 8:j * 128 + b * 8 + 8], lhsT=KTs[buf][:, bi, j, :],
                                rhs=qblk[:, j, b, :, :].rearrange("p a t -> p (a t)"), start=True, stop=True),
                                 reads=[dKTs[buf], dpad], writes=[dZ], inc=(bi == NQ - 1 and j == 3))
                ev = es_[b2][:, :, 0:w_]
                S.op("act", lambda e: e.activation(out=ev, in_=bview(0, c0, w_), func=AF.Exp, scale=SCALE),
                     reads=[dZ], writes=[des[b2]])
                S.op("dve", lambda e: e.tensor_tensor(out=ev, in0=ev,
                                                      in1=ebF[:, :].rearrange("p (j c) -> p j c", j=4)[:, :, c0:c0 + w_],
                                                      op=ALU.mult), reads=[des[b2], dpad], writes=[des[b2]])
                if kind == "new":
                    S.op("dve", lambda e: e.tensor_tensor(out=ev, in0=ev, in1=mS2[:, :].unsqueeze(1).to_broadcast([128, 4, 128]),
                                                          op=ALU.mult), reads=[des[b2], dpad], writes=[des[b2]])
                S.op("act", lambda e: e.activation(out=Ls[b2][:, :, 0:w_], in_=ev, func=AF.Ln, bias=1.0),
                     reads=[des[b2]], writes=[dLs[b2]])
                for j in range(4):
                    S.op("pe", lambda e, j=j: e.matmul(out=PS[1][:, j * 128 + c0:j * 128 + c0 + w_], lhsT=uincl,
                                                       rhs=Ls[b2][:, j, 0:w_], start=(first and j == 0), stop=False,
                                                       skip_group_check=True),
                         reads=[dLs[b2], dCB], writes=[dC], inc=(j == 3))
                S.op("act", lambda e: e.activation(out=xs2[b2][:, :, 0:w_], in_=bview(1, c0, w_), func=AF.Exp, scale=-1.0),
                     reads=[dC], writes=[dxs[b2]])
                if not last:
                    for j in range(4):
                        S.op("pe", lambda e, j=j: e.matmul(out=PS[1][:, j * 128 + c0:j * 128 + c0 + w_], lhsT=lstrict,
                                                           rhs=Ls[b2][:, j, 0:w_], start=False, stop=False,
                                                           skip_group_check=True),
                             reads=[dLs[b2], dCB], writes=[dC], inc=(j == 3))
                if kind == "new":
                    S.op("dve", lambda e: e.tensor_tensor(out=wsb[:, :, :], in0=ev, in1=xs2[b2][:, :, 0:w_], op=ALU.mult),
                         reads=[des[b2], dxs[b2]], writes=[dws[b2]])
                    for j in range(4):
                        S.op("pe", lambda e, j=j: e.matmul(out=PS[2][:, j * 128:(j + 1) * 128],
                                                           lhsT=vpad[:, j * 128:(j + 1) * 128], rhs=wsb[:, j, :],
                                                           start=(j == 0), stop=False, skip_group_check=True),
                             reads=[dws[b2], dpad], writes=[dOs], inc=(j == 3))
                else:
                    S.op("dve", lambda e: e.tensor_tensor(out=ws[b2][:, :, 0:w_], in0=ev, in1=xs2[b2][:, :, 0:w_], op=ALU.mult),
                         reads=[des[b2], dxs[b2]], writes=[dws[b2]])
                    for bi in range(NQ):
                        b = gi * NQ + bi
                        for j in range(4):
                            S.op("pe", lambda e, bi=bi, b=b, j=j: e.matmul(
                                out=PS[2][:, j * 128 + b * 8:j * 128 + b * 8 + 8], lhsT=Vst[buf][:, bi, j * 128:(j + 1) * 128],
                                rhs=ws[b2][:, j, bi * 8:bi * 8 + 8], start=False, stop=False, skip_group_check=True),
                                 reads=[dws[b2], dVst[buf]], writes=[dOs], inc=(bi == NQ - 1 and j == 3))

            sample_step("new", first=True)
            gcount = 0
            for p in range(NPG - 1, -1, -1):
                for gi in range(NSEQ // NQ):
                    buf = gcount % 2
                    gcount += 1
                    for bi in range(NQ):
                        b = gi * NQ + bi
                        col = b * NPG + p
                        S.dma("pool", lambda e, bi=bi, col=col, buf=buf: e.indirect_dma_start(
                            out=Kst[buf][:, bi, :], out_offset=None, in_=ck,
                            in_offset=bass.IndirectOffsetOnAxis(ap=idx[:, col:col + 1], axis=0)),
                              dKst[buf], "in", extra_reads=[dIdx])
                        S.dma("pool", lambda e, bi=bi, col=col, buf=buf: e.indirect_dma_start(
                            out=Vst[buf][:, bi, :], out_offset=None, in_=cv,
                            in_offset=bass.IndirectOffsetOnAxis(ap=idx[:, col:col + 1], axis=0)),
                              dVst[buf], "in", extra_reads=[dIdx])
                    for bi in range(NQ):
                        bank = 5 + (bi % 2)
                        for jj in range(4):
                            S.op("pe", lambda e, bi=bi, jj=jj, bank=bank, buf=buf: e.transpose(
                                out=PS[bank][:, jj * 128:(jj + 1) * 128], in_=Kst[buf][:, bi, jj * 128:(jj + 1) * 128],
                                identity=ident), reads=[dKst[buf], dCF], writes=[PD[bank]], inc=(jj == 3))
                        if bi % 2 == 0:
                            S.op("dve", lambda e, bi=bi, bank=bank, buf=buf: e.tensor_copy(
                                out=KTs[buf][:, bi, :, :].rearrange("p j t -> p (j t)"), in_=PS[bank][:, :]),
                                 reads=[PD[bank]], writes=[dKTs[buf]])
                        else:
                            S.op("act", lambda e, bi=bi, bank=bank, buf=buf: e.activation(
                                out=KTs[buf][:, bi, :, :].rearrange("p j t -> p (j t)"), in_=PS[bank][:, :], func=AF.Copy),
                                 reads=[PD[bank]], writes=[dKTs[buf]])
                    sample_step("page", gi=gi, buf=buf, last=(p == 0), b2=gcount % 2)
            for h2 in range(2):
                S.op("act", lambda e, h2=h2: e.activation(
                    out=mixT[64 * h2:64 * h2 + 64, 0:4, TP:NTOK].rearrange("p j (b t) -> p j b t", t=4),
                    in_=PS[2][64 * h2:64 * h2 + 64, :].rearrange("p (j b a t) -> p j b a t", j=4, b=16, a=2)[:, :, :, h2, :],
                    func=AF.Copy), reads=[dOs], writes=[dMixA[16]])
            S.barrier()

        w_out_v = w_out.rearrange("(k p) n -> p k n", p=128)
        w_up_v = w_up.rearrange("(k p) n -> p k n", p=128)
        w_down_v = w_down.rearrange("(c p) n -> p c n", p=128)
        w_ada_v = w_ada.rearrange("(k p) n -> p k n", p=128)
        gBp = sb([128, 2048], F32)
        gBs = sb([64, 2048], F32)
        dG = dep("gB")
        wst = [sb([128, 8, 256], F32) for _ in range(2)]
        dW = [dep("wst") for _ in range(2)]
        with contextlib.ExitStack() as pg:
            cTp = sb([128, 8, 128], F32, stack=pg)
            cTs = sb([128, 8, 64], F32, stack=pg)
            bgB = sb([128, 2048], F32, stack=pg)
            dbg = dep("bgB")
            dcTx = dep("cTx")
            S.dma("sp", lambda e: e.dma_start(out=bgB[:], in_=bg[0:1, :].to_broadcast([128, 2048])), dbg, "in")
            S.op("dve", lambda e: e.tensor_copy(out=cTp[:], in_=cT[:, :, 16:17].to_broadcast([128, 8, 128])),
                 reads=[dcT], writes=[dcTx])
            S.op("dve", lambda e: e.tensor_copy(out=cTs[:].rearrange("p c (b t) -> p c b t", t=4),
                                                in_=cT[:, :, 0:16].unsqueeze(3).to_broadcast([128, 8, 16, 4])),
                 reads=[dcT], writes=[dcTx])
            for n_ in range(8):
                buf = n_ % 2
                acol = (2048 if n_ < 4 else 5120) + (n_ % 4) * 256
                gcol = n_ * 256
                S.dma("sp", lambda e, acol=acol, buf=buf: e.dma_start(out=wst[buf][:], in_=w_ada_v[:, :, acol:acol + 256]),
                      dW[buf], "in")
                for (lhs, rows, dst, bank) in ((cTp, 128, gBp, 2), (cTs, 64, gBs, 3)):
                    for k in range(8):
                        S.op("pe", lambda e, k=k, lhs=lhs, rows=rows, bank=bank, buf=buf: e.matmul(
                            out=PS[bank][0:rows, 0:256], lhsT=lhs[:, k, :], rhs=wst[buf][:, k, :],
                            start=(k == 0), stop=(k == 7)),
                             reads=[dcTx, dW[buf]], writes=[PD[bank]], inc=(k == 7))
                    S.op("dve", lambda e, rows=rows, dst=dst, bank=bank, gcol=gcol: e.tensor_tensor(
                        out=dst[0:rows, gcol:gcol + 256], in0=PS[bank][0:rows, 0:256], in1=bgB[0:rows, gcol:gcol + 256],
                        op=ALU.add), reads=[PD[bank], dbg], writes=[dG])
            S.barrier()
            chk(6)

        def wst_flat(buf, c):
            return wst[buf][:].rearrange("p k n -> p (k n)").rearrange("p (c n) -> p c n", c=c)

        halves = [list(range(0, 8)), list(range(8, 17))]
        for hi, tiles in enumerate(halves):
            with contextlib.ExitStack() as p3:
                nt = len(tiles)
                ncols = sum(tile_rows(i) for i in tiles)
                col0 = tiles[0] * 128
                x1 = sb([128, nt, D], F32, stack=p3)
                dx1 = [dep("x1") for _ in range(nt)]
                h2T = sb([128, 8, ncols], BF16, stack=p3)
                dh2 = [dep("h2T") for _ in range(nt)]
                with contextlib.ExitStack() as p3a:
                    wo = sb([128, 8, D], BF16, stack=p3a)
                    dwo = [dep("wo") for _ in range(4)]
                    for g in range(4):
                        load_weight_piece(w_out_v[:, :, g * 256:(g + 1) * 256], wo[:, :, g * 256:(g + 1) * 256], wst, dW, g % 2,
                                          dwo[g])
                    xt = [sb([128, D], F32, stack=p3a) for _ in range(2)]
                    dxt = [dep("xt") for _ in range(2)]
                    wk = [(sb([128, D], F32, stack=p3a), sb([128, D], F32, stack=p3a), sb([128, 4], F32, stack=p3a))
                          for _ in range(2)]
                    dwk = [dep("wk") for _ in range(2)]
                    tmp = [sb([128, 512], F32, stack=p3a) for _ in range(2)]
                    dtm = [dep("tmp") for _ in range(2)]
                    for li, i in enumerate(tiles):
                        T = tile_rows(i)
                        c0, c1 = tile_cols(i)
                        b = li % 2
                        load_x_tile(i, xt[b], dxt[b])
                        gB = gBp if i < 16 else gBs
                        for nh in range(2):
                            bank = nh
                            ns = slice(nh * 512, (nh + 1) * 512)
                            for k in range(8):
                                md = dMixA[i] if k < 4 else dMixB[i]
                                S.op("pe", lambda e, k=k, bank=bank, ns=ns: e.matmul(
                                    out=PS[bank][0:T, :], lhsT=mixT[:, k, c0:c1], rhs=wo[:, k, ns], start=(k == 0), stop=(k == 7)),
                                     reads=[md, dwo[2 * nh], dwo[2 * nh + 1]], writes=[PD[bank]], inc=(k == 7))
                            S.op("dve", lambda e, bank=bank, ns=ns, nh=nh, gB=gB: e.tensor_tensor(
                                out=tmp[nh][0:T, :], in0=PS[bank][0:T, :], in1=gB[0:T, ns], op=ALU.mult),
                                 reads=[PD[bank], dG], writes=[dtm[nh]])
                            S.op("pool", lambda e, ns=ns, nh=nh, li=li, b=b: e.tensor_tensor(
                                out=x1[0:T, li, ns], in0=tmp[nh][0:T, :], in1=xt[b][0:T, ns], op=ALU.add),
                                 reads=[dtm[nh], dxt[b]], writes=[dx1[li]])
                        lc0 = c0 - col0
                        norm_transpose(i, x1[0:T, li, :], dx1[li], 2, h2T[:, :, lc0:lc0 + T], dh2[li], wk[b], dwk[b],
                                       (2, 3) if b == 0 else (4, 5))
                    S.barrier()
                    chk(7)
                with contextlib.ExitStack() as p4:
                    wu = [sb([128, 8, 512], BF16, stack=p4) for _ in range(2)]
                    wd = [sb([128, 4, D], BF16, stack=p4) for _ in range(2)]
                    dwu = [dep("wu") for _ in range(2)]
                    dwd = [dep("wd") for _ in range(2)]
                    upT = [sb([128, 4, 512], BF16, stack=p4) for _ in range(2)]
                    dup = [dep("up") for _ in range(2)]
                    rl = [sb([128, 512], F32, stack=p4) for _ in range(2)]
                    drl = [dep("rl") for _ in range(2)]
                    tmp = [sb([128, 512], F32, stack=p4) for _ in range(2)]
                    dtm = [dep("tmp") for _ in range(2)]
                    groups = []
                    li = 0
                    while li < nt:
                        g = [l for l in range(li, min(li + 4, nt)) if tile_rows(tiles[l]) == 128]
                        if not g:
                            g = [li]
                        groups.append(g)
                        li = g[-1] + 1
                    gctr = 0
                    rctr = 0
                    dx1h = [[dep("x1h") for _ in range(2)] for _ in range(nt)]
                    for l_ in range(nt):
                        for nh_ in range(2):
                            dx1h[l_][nh_].w = dx1[l_].w
                    def load_w(E):
                        wb = E % 2
                        for hh in range(2):
                            S.dma("sp", lambda e, hh=hh: e.dma_start(
                                out=wst[0][:], in_=w_up_v[:, :, E * 512 + hh * 256:E * 512 + (hh + 1) * 256]), dW[0], "in")
                            S.op("act", lambda e, hh=hh: e.activation(out=wu[wb][:, :, hh * 256:(hh + 1) * 256], in_=wst[0][:],
                                                                      func=AF.Copy),
                                 reads=[dW[0]], writes=[dwu[wb]])
                            S.dma("sp", lambda e, hh=hh: e.dma_start(
                                out=wst_flat(1, 2), in_=w_down_v[:, E * 4 + hh * 2:E * 4 + (hh + 1) * 2, :]), dW[1], "in")
                            S.op("pool", lambda e, hh=hh: e.tensor_copy(out=wd[wb][:, hh * 2:(hh + 1) * 2, :], in_=wst_flat(1, 2)),
                                 reads=[dW[1]], writes=[dwd[wb]])

                    def up_part(E, g, ub):
                        wb = E % 2
                        gcol0 = tiles[g[0]] * 128 - col0
                        gn = sum(tile_rows(tiles[l]) for l in g)
                        for fc in range(4):
                            bank = fc % 2
                            rb = fc % 2
                            for k in range(8):
                                S.op("pe", lambda e, k=k, fc=fc, bank=bank: e.matmul(
                                    out=PS[bank][:, 0:gn], lhsT=wu[wb][:, k, fc * 128:(fc + 1) * 128],
                                    rhs=h2T[:, k, gcol0:gcol0 + gn], start=(k == 0), stop=(k == 7)),
                                     reads=[dwu[wb]] + [dh2[l] for l in g], writes=[PD[bank]], inc=(k == 7))
                            S.op("act", lambda e, bank=bank, rb=rb: e.activation(out=rl[rb][:, 0:gn], in_=PS[bank][:, 0:gn],
                                                                                 func=AF.Relu),
                                 reads=[PD[bank]], writes=[drl[rb]])
                            S.op("act", lambda e, rb=rb, fc=fc: e.activation(
                                out=upT[ub][:, fc, 0:gn], in_=rl[rb][:, 0:gn], func=AF.Square),
                                 reads=[drl[rb]], writes=[dup[ub]])

                    def down_part(E, g, ub):
                        wb = E % 2
                        gcol0 = tiles[g[0]] * 128 - col0
                        for l in g:
                            i = tiles[l]
                            T = tile_rows(i)
                            lc = tiles[l] * 128 - col0 - gcol0
                            gB = gBp if i < 16 else gBs
                            for nh in range(2):
                                bank = 2 + nh + 2 * (l % 2)
                                ns = slice(nh * 512, (nh + 1) * 512)
                                for fc in range(4):
                                    S.op("pe", lambda e, fc=fc, bank=bank, ns=ns: e.matmul(
                                        out=PS[bank][0:T, :], lhsT=upT[ub][:, fc, lc:lc + T], rhs=wd[wb][:, fc, ns],
                                        start=(fc == 0), stop=(fc == 3)),
                                         reads=[dup[ub], dwd[wb]], writes=[PD[bank]], inc=(fc == 3))
                                S.op("dve", lambda e, bank=bank, nh=nh: e.tensor_tensor(
                                    out=tmp[nh][0:T, :], in0=PS[bank][0:T, :], in1=gB[0:T, 1024 + nh * 512:1024 + (nh + 1) * 512],
                                    op=ALU.mult), reads=[PD[bank], dG], writes=[dtm[nh]])
                                S.op("dve" if nh == 0 else "pool", lambda e, ns=ns, nh=nh: e.tensor_tensor(
                                    out=x1[0:T, l, ns], in0=x1[0:T, l, ns], in1=tmp[nh][0:T, :], op=ALU.add),
                                     reads=[dtm[nh], dx1h[l][nh]], writes=[dx1h[l][nh]])
                            if E == 7:
                                ydst = yp[i * 128:(i + 1) * 128, :] if i < 16 else ys[:, :]
                                for nh in range(2):
                                    S.dma("sp", lambda e, ydst=ydst, nh=nh: e.dma_start(
                                        out=ydst[:, nh * 512:(nh + 1) * 512], in_=x1[0:T, l, nh * 512:(nh + 1) * 512]),
                                          dx1h[l][nh], "out")

                    seq = [(E, g) for E in range(8) for g in groups]
                    load_w(0)
                    up_part(seq[0][0], seq[0][1], 0)
                    for n_, (E, g) in enumerate(seq):
                        if g is groups[0] and E + 1 < 8:
                            load_w(E + 1)
                        if n_ + 1 < len(seq):
                            up_part(seq[n_ + 1][0], seq[n_ + 1][1], (n_ + 1) % 2)
                        down_part(E, g, n_ % 2)
                    S.barrier()
    except _Stop:
        S.finish()
        return nc
    S.finish()
    es.close()
    return nc


def _core_inputs(c, inp, n_phys=None, ck=None, cv=None, pt=None):
    f = np.float32
    d = {}
    d["xp"] = np.ascontiguousarray(inp["x_prompt"][c], dtype=f)
    d["xs"] = np.ascontiguousarray(inp["x_sample"][16 * c:16 * c + 16].reshape(TS, D), dtype=f)
    d["ck"] = ck if ck is not None else inp["cache_k"][0].reshape(-1, 512)
    d["cv"] = cv if cv is not None else inp["cache_v"][0].reshape(-1, 512)
    d["st0"] = np.ascontiguousarray(inp["state_hgrn"][0, 16 * c:16 * c + 16], dtype=f)
    ptc = pt if pt is not None else inp["page_table"][16 * c:16 * c + 16]
    d["pt"] = np.ascontiguousarray(ptc.reshape(1, -1), dtype=np.int32)
    d["cc"] = np.ascontiguousarray(np.concatenate([inp["c_sample"][16 * c:16 * c + 16], inp["c_prompt"][c:c + 1]], 0), dtype=f)
    d["w_ada"] = np.ascontiguousarray(inp["w_ada"][0], dtype=f)
    b_ada = inp["b_ada"][0]
    d["vecs"] = np.ascontiguousarray(np.concatenate([b_ada.reshape(48, 128), inp["norm1_g"][0].reshape(8, 128),
                                                     inp["norm2_g"][0].reshape(8, 128)], 0), dtype=f)
    d["bg"] = np.ascontiguousarray(np.concatenate([b_ada[2048:3072], b_ada[5120:6144]])[None], dtype=f)
    d["w_in"] = np.ascontiguousarray(inp["w_in"][0], dtype=f)
    d["qkg"] = np.ascontiguousarray(np.concatenate([np.tile(inp["q_norm_g"][0], 8), np.tile(inp["k_norm_g"][0], 8),
                                                    np.tile(inp["hg_out_g"][0], 8)])[None], dtype=f)
    d["sbb"] = np.ascontiguousarray(inp["sb_bias"][0][None], dtype=f)
    d["lbl"] = np.ascontiguousarray(inp["hg_lb_logits"].reshape(1, 1024), dtype=f)
    d["w_out"] = np.ascontiguousarray(inp["w_out"][0], dtype=f)
    d["w_up"] = np.ascontiguousarray(inp["w_up"][0], dtype=f)
    d["w_down"] = np.ascontiguousarray(inp["w_down"][0], dtype=f)
    d["cstf"] = _CARR
    d["cstb"] = _BARR
    return d


def _assemble(results):
    y_p = np.stack([r["yp"] for r in results]).astype(np.float32)
    y_s = np.concatenate([r["ys"].reshape(16, 4, D) for r in results]).astype(np.float32)
    k_p = np.stack([r["kp"].reshape(TP, 8, 64) for r in results])[None].astype(np.float32)
    v_p = np.stack([r["vp"].reshape(TP, 8, 64) for r in results])[None].astype(np.float32)
    k_s = np.concatenate([r["ks"].reshape(16, 4, 8, 64) for r in results])[None].astype(np.float32)
    v_s = np.concatenate([r["vs"].reshape(16, 4, 8, 64) for r in results])[None].astype(np.float32)
    s_p = np.stack([r["sp"] for r in results])[None].astype(np.float32)
    s_s = np.concatenate([r["ss"] for r in results])[None].astype(np.float32)
    return (y_p, y_s, k_p, v_p, k_s, v_s, s_p, s_s)


def kernel(**inputs):
    inp = {k: np.asarray(v) for k, v in inputs.items()}
    n_phys = inp["cache_k"].shape[1]
    nc = build(n_phys)
    ck = np.ascontiguousarray(inp["cache_k"][0].reshape(-1, 512), dtype=np.float32)
    cv = np.ascontiguousarray(inp["cache_v"][0].reshape(-1, 512), dtype=np.float32)
    in_maps = [_core_inputs(c, inp, ck=ck, cv=cv) for c in range(NCORES)]
    res = run_bass_kernel_spmd(nc, in_maps, core_ids=list(range(NCORES)))
    return _assemble(res.results)
```

```python
import contextlib
import os
import numpy as np
import ml_dtypes
import concourse.bass as bass
import concourse.mybir as mybir
from concourse.bass_utils import run_bass_kernel_spmd

F32 = mybir.dt.float32
BF16 = mybir.dt.bfloat16
I32 = mybir.dt.int32
AF = mybir.ActivationFunctionType
ALU = mybir.AluOpType
AX = mybir.AxisListType

NCORES = 8
D = 1024
TP = 2048
NSEQ = 16
TS = 64
NTOK = TP + TS
NPG = 16
EPS = 1e-6
SCALE = 64 ** -0.5
NT = 17


def _consts():
    f = {}
    idx = np.arange(128)
    f["ident"] = np.eye(128, dtype=np.float32)
    s = idx[:, None]
    t = idx[None, :]
    same64 = (s // 64) == (t // 64)
    f["tri2"] = ((s <= t) & same64).astype(np.float32)
    f["tsu2"] = ((s > t) & same64).astype(np.float32)
    same4 = (s // 4) == (t // 4)
    f["tri4"] = ((s <= t) & same4).astype(np.float32)
    f["tsu4"] = ((s > t) & same4).astype(np.float32)
    ci = np.zeros((128, 2), np.float32)
    ci[:64, 0] = 1
    ci[64:, 1] = 1
    f["chunkind"] = ci
    si = np.zeros((128, 16), np.float32)
    for p in range(64):
        si[p, p // 4] = 1
    f["seqind"] = si
    mh = np.zeros((128, 64), np.float32)
    for p in range(128):
        mh[p, :] = (np.arange(64) >= (p % 64))
    f["maskH"] = mh
    ms = np.zeros((128, 64), np.float32)
    for p in range(64):
        for c in range(64):
            ms[p, c] = (p // 4 == c // 4) and (p % 4 <= c % 4)
    f["maskHS"] = ms
    ma = np.zeros((128, 64), np.float32)
    for p in range(64):
        for c in range(64):
            ma[p, c] = (p // 4 == c // 4) and (p % 4 < c % 4)
    f["maskS"] = ma
    smt = np.zeros((128, 16 * 64), np.float32)
    for b in range(16):
        smt[:, b * 64 + 4 * b: b * 64 + 4 * b + 4] = 1
    f["seqmaskT"] = smt
    md = np.zeros((128, 4 * 512), np.float32)
    for i in range(4):
        md[:, i * 512:(i + 1) * 512] = ((128 * i + idx[:, None]) < np.arange(512)[None, :])
    f["maskD"] = md
    f["uincl"] = (s >= t).astype(np.float32)
    f["lstrict"] = (s < t).astype(np.float32)
    io = np.zeros((128, 1), np.float32)
    off = {}
    cols = 0
    for k, v in f.items():
        off[k] = (cols, v.shape[1])
        cols += v.shape[1]
    bfk = ["maskD", "uincl", "lstrict", "seqmaskT"]
    off = {}
    cols = 0
    for k, v in f.items():
        if k in bfk:
            continue
        off[k] = (cols, v.shape[1])
        cols += v.shape[1]
    arr = np.concatenate([f[k] for k in f if k not in bfk], axis=1).astype(np.float32)
    boff = {}
    cols = 0
    for k in bfk:
        boff[k] = (cols, f[k].shape[1])
        cols += f[k].shape[1]
    barr = np.concatenate([f[k] for k in bfk], axis=1).astype(np.float32)
    return arr, off, barr, boff


_CARR, _COFF, _BARR, _BOFF = _consts()


class _Stop(Exception):
    pass


class Dep:
    __slots__ = ("name", "w", "re", "rd", "sem", "cnt")

    def __init__(self, name):
        self.name = name
        self.w = None
        self.re = {}
        self.rd = None
        self.sem = None
        self.cnt = 0


class Sched:
    def __init__(self, nc, es):
        self.nc = nc
        self.es = es
        self.eng = {}
        for name, h in [("pe", nc.tensor), ("act", nc.scalar), ("dve", nc.vector),
                        ("pool", nc.gpsimd), ("sp", nc.sync)]:
            sem = es.enter_context(nc.semaphore("sem_" + name))
            self.eng[name] = dict(h=h, sem=sem, cnt=0, waited={})
        self.dma_deps = []

    def _wait(self, ename, entry):
        E = self.eng[ename]
        if entry[0] == "e":
            _, src, idx = entry
            if src == ename and ename == "pe":
                return
            sem = self.eng[src]["sem"]
            val = idx
        else:
            _, dep, n = entry
            sem = dep.sem
            val = 16 * n
        key = id(sem)
        if E["waited"].get(key, 0) >= val:
            return
        E["waited"][key] = val
        E["h"].wait_ge(sem, val)

    def _deps(self, ename, reads, writes):
        for d in reads:
            if d.w is not None:
                self._wait(ename, d.w)
        for d in writes:
            if d.w is not None and not (d.w[0] == "e" and d.w[1] == ename):
                self._wait(ename, d.w)
            for src, idx in d.re.items():
                if src != ename:
                    self._wait(ename, ("e", src, idx))
            if d.rd is not None:
                self._wait(ename, d.rd)

    def op(self, ename, fn, reads=(), writes=(), inc=True):
        E = self.eng[ename]
        self._deps(ename, reads, writes)
        ins = fn(E["h"])
        if inc:
            E["cnt"] += 1
            idx = E["cnt"]
            ins.then_inc(E["sem"], 1)
        else:
            idx = E["cnt"] + 1
        for d in reads:
            d.re[ename] = idx
        for d in writes:
            d.w = ("e", ename, idx)
            d.re = {}
            d.rd = None
        return ins

    def dma(self, qname, fn, dep, direction, extra_reads=()):
        E = self.eng[qname]
        if dep.sem is None:
            dep.sem = self.es.enter_context(self.nc.semaphore("dsem_" + dep.name))
            self.dma_deps.append(dep)
        if direction == "in":
            if dep.w is not None and dep.w[0] == "d" and dep.w[1] is dep and not dep.re and dep.rd is None:
                for d in extra_reads:
                    if d.w is not None:
                        self._wait(qname, d.w)
            else:
                self._deps(qname, list(extra_reads), [dep])
        else:
            self._deps(qname, list(extra_reads) + [dep], [])
        ins = fn(E["h"])
        dep.cnt += 1
        ins.then_inc(dep.sem, 16)
        ent = ("d", dep, dep.cnt)
        if direction == "in":
            dep.w = ent
            dep.re = {}
            dep.rd = None
        else:
            dep.rd = ent
        return ins

    def barrier(self):
        names = ["pe", "act", "dve", "pool", "sp"]
        for a in names:
            for b in names:
                if a != b and self.eng[b]["cnt"] > 0:
                    self._wait(a, ("e", b, self.eng[b]["cnt"]))
            for d in self.dma_deps:
                if d.cnt > 0:
                    self._wait(a, ("d", d, d.cnt))

    def finish(self):
        for d in self.dma_deps:
            if d.cnt > 0:
                self._wait("sp", ("d", d, d.cnt))
        for b in ["pe", "act", "dve", "pool"]:
            if self.eng[b]["cnt"] > 0:
                self._wait("sp", ("e", b, self.eng[b]["cnt"]))


def build(n_phys):
    nc = bass.Bass("TRN2", target_bir_lowering=False)
    es = contextlib.ExitStack()

    def din(name, shape, dt=F32):
        return nc.dram_tensor(name, shape, dt, kind="ExternalInput").ap()

    def dout(name, shape, dt=F32):
        return nc.dram_tensor(name, shape, dt, kind="ExternalOutput").ap()

    xp = din("xp", [TP, D])
    xs_d = din("xs", [TS, D])
    ck = din("ck", [n_phys * 128, 512])
    cv = din("cv", [n_phys * 128, 512])
    st0 = din("st0", [NSEQ, 8, 64, 64])
    pt = din("pt", [1, NSEQ * NPG], I32)
    cc = din("cc", [17, D])
    w_ada = din("w_ada", [D, 6 * D])
    vecs = din("vecs", [64, 128])
    bg = din("bg", [1, 2048])
    w_in = din("w_in", [D, 3584])
    qkg = din("qkg", [1, 3 * 512])
    sbb = din("sbb", [1, 8])
    lbl = din("lbl", [1, 1024])
    w_out = din("w_out", [D, D])
    w_up = din("w_up", [D, 4 * D])
    w_down = din("w_down", [4 * D, D])
    cstf = din("cstf", list(_CARR.shape))
    cstb = din("cstb", list(_BARR.shape))

    yp = dout("yp", [TP, D])
    ys = dout("ys", [TS, D])
    kp = dout("kp", [TP, 512])
    vp = dout("vp", [TP, 512])
    ks = dout("ks", [TS, 512])
    vs = dout("vs", [TS, 512])
    sp_o = dout("sp", [8, 64, 64])
    ss_o = dout("ss", [NSEQ, 8, 64, 64])

    S = Sched(nc, es)
    ucnt = [0]

    def sb(shape, dt=F32, side="left", stack=None, name=None):
        ucnt[0] += 1
        nm = (name or "t") + str(ucnt[0])
        return (stack or es).enter_context(nc.sbuf_tensor(nm, shape, dt, side=side))

    def dep(name="d"):
        ucnt[0] += 1
        return Dep(name + str(ucnt[0]))

    PS = [es.enter_context(nc.psum_tensor(f"ps{i}", [128, 512], F32)) for i in range(8)]
    PD = [dep(f"ps{i}_") for i in range(8)]

    CF = sb([128, _CARR.shape[1]], F32)
    dCF = dep("cf")
    S.dma("sp", lambda e: e.dma_start(out=CF[:], in_=cstf[:, :]), dCF, "in")

    def cf(key, rows=128, c0=0, c1=None):
        o, n = _COFF[key]
        c1 = n if c1 is None else c1
        return CF[0:rows, o + c0:o + c1]

    def load_bf_const(key, dst_ap, ddst, scratch, dscr):
        o, n = _BOFF[key]
        for a in range(0, n, 512):
            w_ = min(512, n - a)
            S.dma("sp", lambda e, a=a, w_=w_: e.dma_start(out=scratch[:, 0:w_], in_=cstb[:, o + a:o + a + w_]), dscr, "in")
            S.op("dve", lambda e, a=a, w_=w_: e.tensor_copy(out=dst_ap[:, a:a + w_], in_=scratch[:, 0:w_]),
                 reads=[dscr], writes=[ddst])

    ident = cf("ident")

    vecT = sb([128, 64], F32)
    dVec = dep("vecT")
    biasT = sb([128, 8], F32)
    dPar = dep("par")
    cT = sb([128, 8, 17], F32)
    dcT = dep("cT")
    mod = sb([128, 4, 8, 17], F32)
    dMod = dep("mod")
    p1s = contextlib.ExitStack()
    qkgB = sb([128, 1536], F32, stack=p1s)
    lbB = sb([128, 512], F32, stack=p1s)
    omlB = sb([128, 512], F32, stack=p1s)
    mixT = sb([128, 8, NTOK], BF16, side="right")
    dMixA = [dep("mixA") for _ in range(NT)]
    dMixB = [dep("mixB") for _ in range(NT)]

    def tile_rows(i):
        return 128 if i < 16 else 64

    def tile_cols(i):
        return (i * 128, i * 128 + tile_rows(i))

    def rsqrt_small(src_ap, dst_ap, n_scale, deps_r, dep_w, tmp_ap):
        S.op("dve", lambda e: e.tensor_scalar(out=tmp_ap, in0=src_ap, scalar1=n_scale, scalar2=EPS,
                                              op0=ALU.mult, op1=ALU.add), reads=deps_r, writes=[dep_w])
        S.op("act", lambda e: e.activation(out=tmp_ap, in_=tmp_ap, func=AF.Ln), reads=[dep_w], writes=[dep_w])
        S.op("act", lambda e: e.activation(out=dst_ap, in_=tmp_ap, func=AF.Exp, scale=-0.5),
             reads=[dep_w], writes=[dep_w])

    stop = int(os.environ.get("KSTOP", "99"))

    def chk(n):
        if stop == n:
            raise _Stop()

    try:
        with contextlib.ExitStack() as p0:
            wst = [sb([128, 8, 512], F32, stack=p0) for _ in range(2)]
            dW = [dep("wst") for _ in range(2)]
            cct = sb([17, D], F32, stack=p0)
            dcc = dep("cc")
            adaT = sb([128, 48, 17], F32, stack=p0)
            dAda = dep("adaT")
            vrow = sb([64, 128], F32, stack=p0)
            lraw = sb([128, 1024], F32, stack=p0)
            dtmp = dep("p0tmp")

            S.dma("sp", lambda e: e.dma_start(out=cct[:], in_=cc[:, :]), dcc, "in")
            S.dma("sp", lambda e: e.dma_start(out=vrow[:], in_=vecs[:, :]), dtmp, "in")
            S.dma("sp", lambda e: e.dma_start(out=qkgB[:], in_=qkg[0:1, :].to_broadcast([128, 1536])), dPar, "in")
            S.dma("sp", lambda e: e.dma_start(out=biasT[:], in_=sbb[0:1, :].to_broadcast([128, 8])), dPar, "in")
            S.dma("sp", lambda e: e.dma_start(out=lraw[:], in_=lbl[0:1, :].to_broadcast([128, 1024])), dtmp, "in")
            S.op("dve", lambda e: e.tensor_tensor(out=lraw[:, 0:512], in0=lraw[:, 0:512], in1=lraw[:, 512:1024],
                                                  op=ALU.subtract), reads=[dtmp], writes=[dtmp])
            S.op("act", lambda e: e.activation(out=lbB[:], in_=lraw[:, 0:512], func=AF.Sigmoid),
                 reads=[dtmp], writes=[dPar])
            S.op("dve", lambda e: e.tensor_scalar(out=omlB[:], in0=lbB[:], scalar1=-1.0, scalar2=1.0,
                                                  op0=ALU.mult, op1=ALU.add), reads=[dPar], writes=[dPar])
            S.op("act", lambda e: e.activation(out=cct[:], in_=cct[:], func=AF.Silu), reads=[dcc], writes=[dcc])
            for c in range(8):
                S.op("pe", lambda e, c=c: e.transpose(out=PS[0][:, c * 17:(c + 1) * 17],
                                                      in_=cct[0:17, c * 128:(c + 1) * 128], identity=cf("ident", 17, 0, 17)),
                     reads=[dcc, dCF], writes=[PD[0]], inc=(c == 7))
            S.op("dve", lambda e: e.tensor_copy(out=cT[:].rearrange("p c s -> p (c s)"), in_=PS[0][:, 0:136]),
                 reads=[PD[0]], writes=[dcT])
            S.op("pe", lambda e: e.transpose(out=PS[1][:, 0:64], in_=vrow[0:64, :], identity=cf("ident", 64, 0, 64)),
                 reads=[dtmp, dCF], writes=[PD[1]])
            S.op("dve", lambda e: e.tensor_copy(out=vecT[:], in_=PS[1][:, 0:64]), reads=[PD[1]], writes=[dVec])

            w_ada_v = w_ada.rearrange("(k p) n -> p k n", p=128)
            pcs = [0, 1, 2, 3, 6, 7, 8, 9]
            for n_, pc in enumerate(pcs):
                buf = n_ % 2
                S.dma("sp", lambda e, pc=pc, buf=buf: e.dma_start(out=wst[buf][:], in_=w_ada_v[:, :, pc * 512:(pc + 1) * 512]),
                      dW[buf], "in")
                for q in range(4):
                    chunk = (pc * 512) // 128 + q
                    bank = 2 + (q % 2)
                    for k in range(8):
                        S.op("pe", lambda e, k=k, q=q, bank=bank, buf=buf: e.matmul(
                            out=PS[bank][:, 0:17], lhsT=wst[buf][:, k, q * 128:(q + 1) * 128], rhs=cT[:, k, :],
                            start=(k == 0), stop=(k == 7)),
                             reads=[dcT, dW[buf]], writes=[PD[bank]], inc=(k == 7))
                    S.op("dve", lambda e, chunk=chunk, bank=bank: e.tensor_scalar(
                        out=adaT[:, chunk, :], in0=PS[bank][:, 0:17], scalar1=vecT[:, chunk:chunk + 1], scalar2=None,
                        op0=ALU.add), reads=[PD[bank], dVec], writes=[dAda])
            for (mi, scc, shc, nof) in ((0, 8, 0, 48), (2, 32, 24, 56)):
                S.op("dve", lambda e, mi=mi, scc=scc, nof=nof: e.scalar_tensor_tensor(
                    out=mod[:, mi, :, :], in0=adaT[:, scc:scc + 8, :], scalar=1.0,
                    in1=vecT[:, nof:nof + 8].unsqueeze(2).to_broadcast([128, 8, 17]), op0=ALU.add, op1=ALU.mult),
                     reads=[dAda, dVec], writes=[dMod])
                S.op("dve", lambda e, mi=mi, shc=shc: e.tensor_copy(out=mod[:, mi + 1, :, :], in_=adaT[:, shc:shc + 8, :]),
                     reads=[dAda], writes=[dMod])
            S.barrier()
            chk(0)

        def load_x_tile(i, xt, dxt, q="sp"):
            T = tile_rows(i)
            src = xp[i * 128:(i + 1) * 128, :] if i < 16 else xs_d[:, :]
            S.dma(q, lambda e: e.dma_start(out=xt[0:T, :], in_=src), dxt, "in")

        def norm_transpose(i, src_ap, dsrc, mi, hT_ap, dhT, work, dwork, banks):
            T = tile_rows(i)
            xn, sqj, st4 = work
            S.op("act", lambda e: e.activation(out=sqj[0:T, :], in_=src_ap, func=AF.Square, accum_out=st4[0:T, 0:1]),
                 reads=[dsrc], writes=[dwork])
            rsqrt_small(st4[0:T, 0:1], st4[0:T, 2:3], 1.0 / D, [dwork], dwork, st4[0:T, 1:2])
            S.op("dve", lambda e: e.tensor_scalar(out=xn[0:T, :], in0=src_ap, scalar1=st4[0:T, 2:3], scalar2=None,
                                                  op0=ALU.mult), reads=[dsrc, dwork], writes=[dwork])
            nb, tpb = (1, 128) if i < 16 else (16, 4)
            c0 = 16 if i < 16 else 0
            for half in range(2):
                bank = banks[half]
                for c4 in range(4):
                    c = half * 4 + c4
                    S.op("pe", lambda e, c=c, c4=c4, bank=bank: e.transpose(
                        out=PS[bank][:, c4 * 128:c4 * 128 + T], in_=xn[0:T, c * 128:(c + 1) * 128],
                        identity=cf("ident", T, 0, T)), reads=[dwork, dCF], writes=[PD[bank]], inc=(c4 == 3))
                pv = PS[bank][:, :].rearrange("p (c t) -> p c t", c=4)[:, :, 0:T].rearrange("p c (b t) -> p c b t", t=tpb)
                sc = mod[:, mi, half * 4:half * 4 + 4, c0:c0 + nb].unsqueeze(3).to_broadcast([128, 4, nb, tpb])
                sh = mod[:, mi + 1, half * 4:half * 4 + 4, c0:c0 + nb].unsqueeze(3).to_broadcast([128, 4, nb, tpb])
                tmpm = xn[:, half * 512:(half + 1) * 512].rearrange("p (c t) -> p c t", c=4)[:, :, 0:T].rearrange(
                    "p c (b t) -> p c b t", t=tpb)
                tm = sqj[:, half * 512:(half + 1) * 512].rearrange("p (c t) -> p c t", c=4)[:, :, 0:T].rearrange(
                    "p c (b t) -> p c b t", t=tpb)
                S.op("dve", lambda e, pv=pv, sc=sc, tm=tm: e.tensor_tensor(out=tm, in0=pv, in1=sc, op=ALU.mult),
                     reads=[PD[bank], dMod], writes=[dwork])
                ho = hT_ap[:, half * 4:half * 4 + 4, :].rearrange("p c (b t) -> p c b t", t=tpb)
                S.op("dve", lambda e, ho=ho, sh=sh, tm=tm: e.tensor_tensor(out=ho, in0=tm, in1=sh, op=ALU.add),
                     reads=[dwork, dMod], writes=[dhT])

        def load_weight_piece(src_ap, dst_bf_ap, wst, dW, buf, ddst, q="sp", cast_eng="pool"):
            S.dma(q, lambda e: e.dma_start(out=wst[buf][:], in_=src_ap), dW[buf], "in")
            S.op(cast_eng, lambda e: e.tensor_copy(out=dst_bf_ap, in_=wst[buf][:]), reads=[dW[buf]], writes=[ddst])

        with contextlib.ExitStack() as p1:
            hT = sb([128, 8, NTOK], BF16, stack=p1)
            dhT = [dep("hT") for _ in range(NT)]
            with contextlib.ExitStack() as p1n:
                xt = [sb([128, D], F32, stack=p1n) for _ in range(2)]
                dxt = [dep("xt") for _ in range(2)]
                wk = [(sb([128, D], F32, stack=p1n), sb([128, D], F32, stack=p1n), sb([128, 4], F32, stack=p1n)) for _ in range(2)]
                dwk = [dep("wk") for _ in range(2)]
                for i in [int(x) for x in os.environ["KTILES"].split(",")] if "KTILES" in os.environ else range(NT):
                    b = i % 2
                    load_x_tile(i, xt[b], dxt[b])
                    c0, c1 = tile_cols(i)
                    norm_transpose(i, xt[b][0:tile_rows(i), :], dxt[b], 0, hT[:, :, c0:c1], dhT[i], wk[b], dwk[b],
                                   (0, 1) if b == 0 else (2, 3))
                S.barrier()
                chk(1)

            w_in_v = w_in.rearrange("(k p) n -> p k n", p=128)

            with contextlib.ExitStack() as pb:
                whg = sb([128, 8, 2048], BF16, stack=pb)
                dwhg = [dep("whg") for _ in range(4)]
                with contextlib.ExitStack() as pw:
                    wst = [sb([128, 8, 512], F32, stack=pw) for _ in range(2)]
                    dW = [dep("wst") for _ in range(2)]
                    for g in range(4):
                        load_weight_piece(w_in_v[:, :, 1536 + g * 512:1536 + (g + 1) * 512], whg[:, :, g * 512:(g + 1) * 512],
                                          wst, dW, g % 2, dwhg[g])
                    S.barrier()

                def wt(shape, dt=F32):
                    return sb(shape, dt, stack=pb)

                AB = (4, 2)
                XB = (6, 3)

                def hsel(ap, h2):
                    return ap.rearrange("p (j a t) -> p j a t", j=4, a=2)[:, :, h2, :]

                hq = wt([128, 512]); ff = wt([128, 512]); logf = wt([128, 512]); omf = wt([128, 512])
                hv = wt([128, 512], BF16); sg = wt([128, 512]); eb = wt([128, 512]); enb = wt([128, 512])
                ec = wt([128, 512]); kk = wt([128, 512], BF16)
                qdT = wt([128, 4, 128], BF16); kdT = wt([128, 4, 128], BF16)
                attm = wt([128, 512], BF16); oo = wt([128, 512]); osq = wt([128, 512])
                st8 = wt([128, 24]); dec = wt([128, 4, 16])
                Sst = wt([128, 4, 64]); Sbf = wt([128, 4, 64], BF16)
                dE = dep("hgE")
                dT_ = dep("hgT")
                dA = dep("attm"); dO = dep("oo"); dS_ = dep("S"); dSb = dep("Sbf"); dDec = dep("dec")
                S.op("dve", lambda e: e.memset(Sst[:], 0.0), writes=[dS_])
                S.op("dve", lambda e: e.memset(Sbf[:], 0.0), writes=[dSb])
                S0t = [wt([128, 16, 64]) for _ in range(2)]
                S0bt = wt([128, 16, 64], BF16)
                qdTm = wt([128, 4, 16, 64], BF16); hvm = wt([64, 16, 128], BF16)
                smT = wt([128, 1024], BF16)
                dS0 = [dep("S0") for _ in range(2)]; dS0b = dep("S0b"); dqm = dep("qdTm"); dhvm = dep("hvm"); dsm = dep("smT")
                st0_v = st0.rearrange("b (j h) k v -> (h k) j b v", h=2)
                ss_v = ss_o.rearrange("b (j h) k v -> (h k) j b v", h=2)
                load_bf_const("seqmaskT", smT, dsm, hq, dE)

                for i in range(NT):
                    T = tile_rows(i)
                    c0, c1 = tile_cols(i)
                    samp = (i == 16)
                    tri = cf("tri4" if samp else "tri2", T, 0, T)
                    tsu = cf("tsu4" if samp else "tsu2", T, 0, T)
                    ncn = 16 if samp else 2
                    ind = cf("seqind" if samp else "chunkind", T)
                    def proj(g, bank):
                        for k in range(8):
                            S.op("pe", lambda e, k=k: e.matmul(out=PS[bank][0:T, :], lhsT=hT[:, k, c0:c1],
                                                               rhs=whg[:, k, g * 512:(g + 1) * 512],
                                                               start=(k == 0), stop=(k == 7)),
                                 reads=[dhT[i], dwhg[g]], writes=[PD[bank]], inc=(k == 7))
                    proj(0, 2)
                    S.op("act", lambda e: e.activation(out=hq[0:T, :], in_=PS[2][0:T, :], func=AF.Silu),
                         reads=[PD[2]], writes=[dE])
                    proj(1, 3)
                    S.op("act", lambda e: e.activation(out=ff[0:T, :], in_=PS[3][0:T, :], func=AF.Sigmoid),
                         reads=[PD[3]], writes=[dE])
                    S.op("dve", lambda e: e.tensor_tensor(out=ff[0:T, :], in0=ff[0:T, :], in1=omlB[0:T, :], op=ALU.mult),
                         reads=[dE, dPar], writes=[dE])
                    S.op("dve", lambda e: e.tensor_tensor(out=ff[0:T, :], in0=ff[0:T, :], in1=lbB[0:T, :], op=ALU.add),
                         reads=[dE, dPar], writes=[dE])
                    proj(3, 2)
                    S.op("act", lambda e: e.activation(out=sg[0:T, :], in_=PS[2][0:T, :], func=AF.Silu),
                         reads=[PD[2]], writes=[dE])
                    proj(2, 3)
                    S.op("act", lambda e: e.activation(out=hv[0:T, :], in_=PS[3][0:T, :], func=AF.Copy),
                         reads=[PD[3]], writes=[dE])
                    S.op("act", lambda e: e.activation(out=logf[0:T, :], in_=ff[0:T, :], func=AF.Ln),
                         reads=[dE], writes=[dE])
                    S.op("dve", lambda e: e.tensor_scalar(out=omf[0:T, :], in0=ff[0:T, :], scalar1=-1.0, scalar2=1.0,
                                                          op0=ALU.mult, op1=ALU.add), reads=[dE], writes=[dE])
                    S.op("pe", lambda e: e.matmul(out=PS[0][0:T, :], lhsT=tri, rhs=logf[0:T, :], start=True, stop=True),
                         reads=[dE, dCF], writes=[PD[0]])
                    S.op("pe", lambda e: e.matmul(out=PS[1][0:T, :], lhsT=tsu, rhs=logf[0:T, :], start=True, stop=True),
                         reads=[dE, dCF], writes=[PD[1]])
                    for j in range(4):
                        S.op("pe", lambda e, j=j: e.matmul(out=PS[7][:, 256 + j * ncn:256 + (j + 1) * ncn],
                                                           lhsT=logf[0:T, j * 128:(j + 1) * 128], rhs=ind,
                                                           start=True, stop=True),
                             reads=[dE, dCF], writes=[PD[7]], inc=(j == 3))
                    S.op("act", lambda e: e.activation(out=eb[0:T, :], in_=PS[0][0:T, :], func=AF.Exp),
                         reads=[PD[0]], writes=[dE])
                    S.op("act", lambda e: e.activation(out=enb[0:T, :], in_=PS[0][0:T, :], func=AF.Exp, scale=-1.0),
                         reads=[PD[0]], writes=[dE])
                    S.op("act", lambda e: e.activation(out=ec[0:T, :], in_=PS[1][0:T, :], func=AF.Exp),
                         reads=[PD[1]], writes=[dE])
                    S.op("act", lambda e: e.activation(out=dec[:, :, 0:ncn],
                                                       in_=PS[7][:, 256:256 + 4 * ncn].rearrange("p (j c) -> p j c", j=4),
                                                       func=AF.Exp), reads=[PD[7]], writes=[dDec])
                    S.op("dve", lambda e: e.tensor_tensor(out=eb[0:T, :], in0=hq[0:T, :], in1=eb[0:T, :], op=ALU.mult),
                         reads=[dE], writes=[dE])
                    S.op("dve", lambda e: e.tensor_tensor(out=enb[0:T, :], in0=omf[0:T, :], in1=enb[0:T, :], op=ALU.mult),
                         reads=[dE], writes=[dE])
                    S.op("dve", lambda e: e.tensor_tensor(out=kk[0:T, :], in0=omf[0:T, :], in1=ec[0:T, :], op=ALU.mult),
                         reads=[dE], writes=[dE])
                    for (src, dst, bank) in ((eb, qdT, 0), (enb, kdT, 1)):
                        for j in range(4):
                            S.op("pe", lambda e, j=j, src=src, bank=bank: e.transpose(
                                out=PS[bank][:, j * 128:j * 128 + T], in_=src[0:T, j * 128:(j + 1) * 128],
                                identity=cf("ident", T, 0, T)), reads=[dE, dCF], writes=[PD[bank]], inc=(j == 3))
                        S.op("act", lambda e, dst=dst, bank=bank: e.activation(
                            out=dst[:, :, 0:T], in_=PS[bank][:, :].rearrange("p (j t) -> p j t", j=4)[:, :, 0:T],
                            func=AF.Copy), reads=[PD[bank]], writes=[dT_])

                    if not samp:
                        for c in range(2):
                            cp = 64 * c
                            for h in range(8):
                                j, h2 = h // 2, h % 2
                                hp = 64 * h2
                                ab = AB[h2]
                                S.op("pe", lambda e, h=h, j=j, hp=hp, cp=cp, ab=ab: e.matmul(
                                    out=PS[ab][cp:cp + 64, h * 64:(h + 1) * 64], lhsT=kdT[hp:hp + 64, j, cp:cp + 64],
                                    rhs=qdT[hp:hp + 64, j, cp:cp + 64], start=True, stop=True),
                                     reads=[dT_], writes=[PD[ab]], inc=(h >= 6))
                            for h2 in range(2):
                                ab = AB[h2]
                                S.op("dve", lambda e, cp=cp, ab=ab, h2=h2: e.tensor_tensor(
                                    out=hsel(attm[cp:cp + 64, :], h2), in0=hsel(PS[ab][cp:cp + 64, :], h2),
                                    in1=cf("maskH")[cp:cp + 64, :].unsqueeze(1).to_broadcast([64, 4, 64]), op=ALU.mult),
                                     reads=[PD[ab], dCF], writes=[dA])
                            for h in range(8):
                                j, h2 = h // 2, h % 2
                                hp = 64 * h2
                                hs = slice(h * 64, (h + 1) * 64)
                                S.op("pe", lambda e, hs=hs, cp=cp: e.matmul(
                                    out=PS[5][cp:cp + 64, hs], lhsT=attm[cp:cp + 64, hs], rhs=hv[cp:cp + 64, hs],
                                    start=True, stop=True), reads=[dA, dE], writes=[PD[5]], inc=False)
                                xb = XB[h2]
                                S.op("pe", lambda e, hs=hs, cp=cp, hp=hp, j=j, xb=xb: e.matmul(
                                    out=PS[xb][cp:cp + 64, hs], lhsT=qdT[hp:hp + 64, j, cp:cp + 64], rhs=Sbf[hp:hp + 64, j, :],
                                    start=True, stop=True), reads=[dT_, dSb], writes=[PD[xb]], inc=False)
                                S.op("pe", lambda e, hs=hs, cp=cp, hp=hp, j=j: e.matmul(
                                    out=PS[7][hp:hp + 64, j * 64:(j + 1) * 64], lhsT=kk[cp:cp + 64, hs], rhs=hv[cp:cp + 64, hs],
                                    start=True, stop=True), reads=[dE], writes=[PD[7]], inc=(h == 7))
                            S.op("dve", lambda e, c=c: e.tensor_tensor(
                                out=Sst[:], in0=Sst[:], in1=dec[:, :, c:c + 1].to_broadcast([128, 4, 64]), op=ALU.mult),
                                 reads=[dS_, dDec], writes=[dS_])
                            S.op("dve", lambda e: e.tensor_tensor(
                                out=Sst[:], in0=Sst[:], in1=PS[7][:, 0:256].rearrange("p (j v) -> p j v", j=4), op=ALU.add),
                                 reads=[dS_, PD[7]], writes=[dS_])
                            S.op("act", lambda e: e.activation(out=Sbf[:], in_=Sst[:], func=AF.Copy),
                                 reads=[dS_], writes=[dSb])
                        if i == 15:
                            S.dma("sp", lambda e: e.dma_start(out=sp_o.rearrange("(j h) k v -> (h k) j v", h=2), in_=Sst[:]),
                                  dS_, "out")
                    else:
                        for h in range(8):
                            j, h2 = h // 2, h % 2
                            hp = 64 * h2
                            ab = AB[h2]
                            S.op("pe", lambda e, h=h, j=j, hp=hp, ab=ab: e.matmul(
                                out=PS[ab][0:64, h * 64:(h + 1) * 64], lhsT=kdT[hp:hp + 64, j, 0:64],
                                rhs=qdT[hp:hp + 64, j, 0:64], start=True, stop=True),
                                 reads=[dT_], writes=[PD[ab]], inc=(h >= 6))
                        for h2 in range(2):
                            ab = AB[h2]
                            S.op("dve", lambda e, ab=ab, h2=h2: e.tensor_tensor(
                                out=hsel(attm[0:64, :], h2), in0=hsel(PS[ab][0:64, :], h2),
                                in1=cf("maskHS")[0:64, :].unsqueeze(1).to_broadcast([64, 4, 64]), op=ALU.mult),
                                 reads=[PD[ab], dCF], writes=[dA])
                        S.op("dve", lambda e: e.tensor_tensor(
                            out=qdTm[:], in0=qdT[:, :, 0:64].unsqueeze(2).to_broadcast([128, 4, 16, 64]),
                            in1=smT[:].rearrange("p (b t) -> p b t", b=16).unsqueeze(1).to_broadcast([128, 4, 16, 64]),
                            op=ALU.mult), reads=[dT_, dsm], writes=[dqm])
                        for h in range(8):
                            hs = slice(h * 64, (h + 1) * 64)
                            S.op("pe", lambda e, hs=hs: e.matmul(out=PS[5][0:64, hs], lhsT=attm[0:64, hs], rhs=hv[0:64, hs],
                                                                 start=True, stop=True),
                                 reads=[dA, dE], writes=[PD[5]], inc=(h == 7))
                        for j in range(4):
                            sb_ = j % 2
                            S0j = S0t[sb_]
                            S.dma("sp", lambda e, j=j, S0j=S0j: e.dma_start(out=S0j[:], in_=st0_v[:, j, :, :]), dS0[sb_], "in")
                            S.op("act", lambda e, S0j=S0j: e.activation(out=S0bt[:].rearrange("p b v -> p (b v)"),
                                                                        in_=S0j[:].rearrange("p b v -> p (b v)"), func=AF.Copy),
                                 reads=[dS0[sb_]], writes=[dS0b])
                            S.op("dve", lambda e, j=j: e.tensor_tensor(
                                out=hvm[:], in0=hv[0:64, j * 128:(j + 1) * 128].unsqueeze(1).to_broadcast([64, 16, 128]),
                                in1=cf("seqind", 64).unsqueeze(2).to_broadcast([64, 16, 128]), op=ALU.mult),
                                 reads=[dE, dCF], writes=[dhvm])
                            for h2 in range(2):
                                h = 2 * j + h2
                                hp = 64 * h2
                                hs = slice(h * 64, (h + 1) * 64)
                                for b in range(16):
                                    S.op("pe", lambda e, hs=hs, hp=hp, j=j, b=b, h2=h2: e.matmul(
                                        out=PS[XB[h2]][0:64, hs], lhsT=qdTm[hp:hp + 64, j, b, :], rhs=S0bt[hp:hp + 64, b, :],
                                        start=(b == 0), stop=(b == 15)),
                                         reads=[dqm, dS0b], writes=[PD[XB[h2]]], inc=(b == 15))
                                for half in range(2):
                                    S.op("pe", lambda e, h=h, hp=hp, half=half, h2=h2: e.matmul(
                                        out=PS[half][hp:hp + 64, :], lhsT=kk[0:64, h * 64:(h + 1) * 64],
                                        rhs=hvm[0:64, half * 8:(half + 1) * 8, h2 * 64:(h2 + 1) * 64],
                                        start=True, stop=True), reads=[dE, dhvm], writes=[PD[half]])
                            for half in range(2):
                                bs = slice(half * 8, (half + 1) * 8)
                                S.op("dve", lambda e, j=j, bs=bs, S0j=S0j: e.tensor_tensor(
                                    out=S0j[:, bs, :], in0=S0j[:, bs, :],
                                    in1=dec[:, j, bs].unsqueeze(2).to_broadcast([128, 8, 64]), op=ALU.mult),
                                     reads=[dS0[sb_], dDec], writes=[dS0[sb_]])
                                S.op("dve", lambda e, bs=bs, half=half, S0j=S0j: e.tensor_tensor(
                                    out=S0j[:, bs, :], in0=S0j[:, bs, :],
                                    in1=PS[half][:, :].rearrange("p (b v) -> p b v", b=8), op=ALU.add),
                                     reads=[dS0[sb_], PD[half]], writes=[dS0[sb_]])
                            S.dma("sp", lambda e, j=j, S0j=S0j: e.dma_start(out=ss_v[:, j, :, :], in_=S0j[:]), dS0[sb_], "out")

                    S.op("act", lambda e: e.activation(out=oo[0:T, :], in_=PS[5][0:T, :], func=AF.Copy),
                         reads=[PD[5]], writes=[dO])
                    for h2 in range(2):
                        S.op("dve", lambda e, h2=h2: e.tensor_tensor(out=hsel(oo[0:T, :], h2), in0=hsel(oo[0:T, :], h2),
                                                              in1=hsel(PS[XB[h2]][0:T, :], h2), op=ALU.add),
                             reads=[dO, PD[XB[h2]]], writes=[dO])
                    S.op("dve", lambda e: e.tensor_tensor(out=osq[0:T, :], in0=oo[0:T, :], in1=oo[0:T, :], op=ALU.mult),
                         reads=[dO], writes=[dO])
                    S.op("dve", lambda e: e.tensor_reduce(out=st8[0:T, 0:8], in_=osq[0:T, :].rearrange("p (h v) -> p h v", h=8),
                                                          axis=AX.X, op=ALU.add), reads=[dO], writes=[dO])
                    rsqrt_small(st8[0:T, 0:8], st8[0:T, 16:24], 1.0 / 64, [dO], dO, st8[0:T, 8:16])
                    S.op("dve", lambda e: e.tensor_tensor(
                        out=oo[0:T, :].rearrange("p (h v) -> p h v", h=8), in0=oo[0:T, :].rearrange("p (h v) -> p h v", h=8),
                        in1=st8[0:T, 16:24].unsqueeze(2).to_broadcast([T, 8, 64]), op=ALU.mult), reads=[dO], writes=[dO])
                    S.op("dve", lambda e: e.tensor_tensor(out=oo[0:T, :], in0=oo[0:T, :], in1=qkgB[0:T, 1024:1536], op=ALU.mult),
                         reads=[dO, dPar], writes=[dO])
                    S.op("dve", lambda e: e.tensor_tensor(out=oo[0:T, :], in0=oo[0:T, :], in1=sg[0:T, :], op=ALU.mult),
                         reads=[dO, dE], writes=[dO])
                    for j in range(4):
                        S.op("pe", lambda e, j=j: e.transpose(out=PS[4][:, j * 128:j * 128 + T], in_=oo[0:T, j * 128:(j + 1) * 128],
                                                              identity=cf("ident", T, 0, T)),
                             reads=[dO, dCF], writes=[PD[4]], inc=(j == 3))
                    S.op("act", lambda e: e.activation(out=mixT[:, 4:8, c0:c1],
                                                       in_=PS[4][:, :].rearrange("p (j t) -> p j t", j=4)[:, :, 0:T],
                                                       func=AF.Copy), reads=[PD[4]], writes=[dMixB[i]])
                S.barrier()
                chk(2)

            pa_r = contextlib.ExitStack()
            qT = sb([128, 4, NTOK], BF16, side="right", stack=pa_r)
            kT = sb([128, 4, NTOK], BF16, side="right", stack=pa_r)
            vres = sb([128, NT, 512], BF16, side="right", stack=pa_r)
            dQ = [dep("qT") for _ in range(NT)]
            dK = [dep("kT") for _ in range(NT)]
            dV = [dep("v") for _ in range(NT)]
            with contextlib.ExitStack() as pa:
                wat = sb([128, 8, 1536], BF16, stack=pa)
                dwat = [dep("wat") for _ in range(3)]
                with contextlib.ExitStack() as pw:
                    wst = [sb([128, 8, 512], F32, stack=pw) for _ in range(2)]
                    dW = [dep("wst") for _ in range(2)]
                    for g in range(3):
                        load_weight_piece(w_in_v[:, :, g * 512:(g + 1) * 512], wat[:, :, g * 512:(g + 1) * 512],
                                          wst, dW, g % 2, dwat[g])
                    S.barrier()
                sq = sb([128, 512], F32, stack=pa)
                qn = [sb([128, 512], F32, stack=pa) for _ in range(2)]
                dqn = [dep("qn") for _ in range(2)]
                kn = [sb([128, 512], F32, stack=pa) for _ in range(2)]
                dkn = [dep("kn") for _ in range(2)]
                vn = [sb([128, 512], F32, stack=pa) for _ in range(2)]
                dvn = [dep("vn") for _ in range(2)]
                s8 = sb([128, 24], F32, stack=pa)
                dsq = dep("sq")
                for i in range(NT):
                    T = tile_rows(i)
                    c0, c1 = tile_cols(i)
                    b = i % 2

                    def proj(g, bank):
                        for k in range(8):
                            S.op("pe", lambda e, k=k: e.matmul(out=PS[bank][0:T, :], lhsT=hT[:, k, c0:c1],
                                                               rhs=wat[:, k, g * 512:(g + 1) * 512],
                                                               start=(k == 0), stop=(k == 7)),
                                 reads=[dhT[i], dwat[g]], writes=[PD[bank]], inc=(k == 7))

                    def qknorm(bank, dst, ddst, goff):
                        S.op("act", lambda e: e.activation(out=sq[0:T, :], in_=PS[bank][0:T, :], func=AF.Square),
                             reads=[PD[bank]], writes=[dsq])
                        S.op("dve", lambda e: e.tensor_reduce(out=s8[0:T, 0:8], in_=sq[0:T, :].rearrange("p (h d) -> p h d", h=8),
                                                              axis=AX.X, op=ALU.add), reads=[dsq], writes=[dsq])
                        rsqrt_small(s8[0:T, 0:8], s8[0:T, 16:24], 1.0 / 64, [dsq], dsq, s8[0:T, 8:16])
                        S.op("dve", lambda e: e.tensor_tensor(
                            out=dst[0:T, :].rearrange("p (h d) -> p h d", h=8),
                            in0=PS[bank][0:T, :].rearrange("p (h d) -> p h d", h=8),
                            in1=s8[0:T, 16:24].unsqueeze(2).to_broadcast([T, 8, 64]), op=ALU.mult),
                             reads=[PD[bank], dsq], writes=[ddst])
                        S.op("dve", lambda e: e.tensor_tensor(out=dst[0:T, :], in0=dst[0:T, :], in1=qkgB[0:T, goff:goff + 512],
                                                              op=ALU.mult), reads=[ddst, dPar], writes=[ddst])

                    def to_featT(src, dsrc, bank, dst, ddst):
                        for j in range(4):
                            S.op("pe", lambda e, j=j: e.transpose(out=PS[bank][:, j * 128:j * 128 + T],
                                                                  in_=src[0:T, j * 128:(j + 1) * 128],
                                                                  identity=cf("ident", T, 0, T)),
                                 reads=[dsrc, dCF], writes=[PD[bank]], inc=(j == 3))
                        S.op("act", lambda e: e.activation(out=dst[:, :, c0:c1],
                                                           in_=PS[bank][:, :].rearrange("p (j t) -> p j t", j=4)[:, :, 0:T],
                                                           func=AF.Copy), reads=[PD[bank]], writes=[ddst])

                    proj(0, 0)
                    qknorm(0, qn[b], dqn[b], 0)
                    to_featT(qn[b], dqn[b], 3, qT, dQ[i])
                    proj(1, 1)
                    qknorm(1, kn[b], dkn[b], 512)
                    to_featT(kn[b], dkn[b], 4, kT, dK[i])
                    kdst = kp[i * 128:(i + 1) * 128, :] if i < 16 else ks[:, :]
                    S.dma("sp", lambda e: e.dma_start(out=kdst, in_=kn[b][0:T, :]), dkn[b], "out")
                    proj(2, 2)
                    S.op("act", lambda e: e.activation(out=vn[b][0:T, :], in_=PS[2][0:T, :], func=AF.Copy),
                         reads=[PD[2]], writes=[dvn[b]])
                    S.op("dve", lambda e: e.tensor_copy(out=vres[0:T, i, :], in_=vn[b][0:T, :]),
                         reads=[dvn[b]], writes=[dV[i]])
                    vdst = vp[i * 128:(i + 1) * 128, :] if i < 16 else vs[:, :]
                    S.dma("sp", lambda e: e.dma_start(out=vdst, in_=vn[b][0:T, :]), dvn[b], "out")
                S.barrier()
                chk(3)
        p1s.close()

        with contextlib.ExitStack() as p2:
            ulT = sb([128, 256], BF16, stack=p2)
            dCB = dep("cb")
            uincl = ulT[:, 0:128]
            lstrict = ulT[:, 128:256]
            qblk = sb([128, 4, 16, 2, 4], BF16, stack=p2)
            kTp = sb([128, 4, 128], BF16, stack=p2)
            vpad = sb([128, 512], BF16, stack=p2)
            ebF = sb([128, 512], F32, stack=p2)
            mS2 = sb([128, 128], F32, stack=p2)
            dpad = dep("pad")
            with contextlib.ExitStack() as p2a:
                NSET = 8
                eT = [sb([128, 512], F32, stack=p2a) for _ in range(NSET)]
                xT_ = [sb([128, 512], F32, stack=p2a) for _ in range(NSET)]
                LpT = [sb([128, 512], BF16, stack=p2a) for _ in range(NSET)]
                wT = [sb([128, 512], BF16, stack=p2a) for _ in range(NSET)]
                de = [dep("e") for _ in range(NSET)]
                dx = [dep("x") for _ in range(NSET)]
                dL = [dep("L") for _ in range(NSET)]
                dw = [dep("w") for _ in range(NSET)]
                mDT = sb([128, 2048], BF16, stack=p2a)
                load_bf_const("uincl", ulT[:, 0:128], dCB, eT[0], de[0])
                load_bf_const("lstrict", ulT[:, 128:256], dCB, eT[1], de[1])
                load_bf_const("maskD", mDT, dCB, eT[0], de[0])
                def stream_units(qbs):
                    out = []
                    for j in range(4):
                        for QB in qbs:
                            nkb = 4 * QB + 4
                            for st, kb in enumerate(range(nkb - 1, -1, -1)):
                                for h2 in range(2):
                                    out.append((j, QB, st, kb, h2))
                    return out
                ua, ub = stream_units([3, 0]), stream_units([2, 1])
                units = []
                for n_ in range(max(len(ua), len(ub))):
                    if n_ < len(ua):
                        units.append(ua[n_] + (0,))
                    if n_ < len(ub):
                        units.append(ub[n_] + (1,))

                def geom(u):
                    j, QB, st, kb, h2, sid = u
                    ii = kb - 4 * QB
                    clo = 128 * ii if ii > 0 else 0
                    return ii, clo, slice(clo, 512), slice(QB * 512 + clo, (QB + 1) * 512)

                def parms(n):
                    j, QB, st, kb, h2, sid = units[n]
                    ii, clo, cs, qs = geom(units[n])
                    return j, QB, st, kb, h2, sid, ii, clo, cs, qs, 2 * j + h2, 64 * h2, n % NSET

                def a1(n):
                    j, QB, st, kb, h2, sid, ii, clo, cs, qs, h, hp, si = parms(n)
                    qdeps = [dQ[t] for t in range(QB * 4, QB * 4 + 4)]
                    S.op("pe", lambda e: e.matmul(
                        out=PS[h2][:, cs], lhsT=kT[hp:hp + 64, j, kb * 128:(kb + 1) * 128], rhs=qT[hp:hp + 64, j, qs],
                        start=True, stop=True), reads=[dK[kb]] + qdeps, writes=[PD[h2]])

                def a2(n):
                    j, QB, st, kb, h2, sid, ii, clo, cs, qs, h, hp, si = parms(n)
                    S.op("act", lambda e: e.activation(
                        out=eT[si][:, cs], in_=PS[h2][:, cs], func=AF.Exp, scale=SCALE, bias=biasT[:, h:h + 1]),
                         reads=[PD[h2], dPar], writes=[de[si]])
                    if ii >= 0:
                        S.op("dve", lambda e: e.tensor_tensor(
                            out=eT[si][:, cs], in0=eT[si][:, cs], in1=mDT[:, ii * 512 + clo:(ii + 1) * 512],
                            op=ALU.mult), reads=[de[si], dCB], writes=[de[si]])
                    S.op("act", lambda e: e.activation(out=LpT[si][:, cs], in_=eT[si][:, cs], func=AF.Ln, bias=1.0),
                         reads=[de[si]], writes=[dL[si]])

                def cbank(sid, h2):
                    return (2 + h2) if sid == 0 else (5 + h2)

                def b1(n):
                    j, QB, st, kb, h2, sid, ii, clo, cs, qs, h, hp, si = parms(n)
                    Cb = cbank(sid, h2)
                    S.op("pe", lambda e: e.matmul(
                        out=PS[Cb][:, cs], lhsT=uincl, rhs=LpT[si][:, cs], start=(st == 0), stop=False,
                        skip_group_check=True), reads=[dL[si], dCB], writes=[PD[Cb]])
                    S.op("act", lambda e: e.activation(out=xT_[si][:, cs], in_=PS[Cb][:, cs], func=AF.Exp, scale=-1.0),
                         reads=[PD[Cb]], writes=[dx[si]])

                def b2(n):
                    j, QB, st, kb, h2, sid, ii, clo, cs, qs, h, hp, si = parms(n)
                    Cb = cbank(sid, h2)
                    if kb > 0:
                        S.op("pe", lambda e: e.matmul(
                            out=PS[Cb][:, cs], lhsT=lstrict, rhs=LpT[si][:, cs], start=False, stop=(kb == 1),
                            skip_group_check=True), reads=[dL[si], dCB], writes=[PD[Cb]])
                    S.op("dve", lambda e: e.tensor_tensor(out=wT[si][:, cs], in0=eT[si][:, cs], in1=xT_[si][:, cs], op=ALU.mult),
                         reads=[de[si], dx[si]], writes=[dw[si]])

                def b3(n):
                    j, QB, st, kb, h2, sid, ii, clo, cs, qs, h, hp, si = parms(n)
                    ob = 4 if sid == 0 else 7
                    S.op("pe", lambda e: e.matmul(
                        out=PS[ob][hp:hp + 64, cs], lhsT=vres[:, kb, h * 64:(h + 1) * 64], rhs=wT[si][:, cs],
                        start=(st == 0), stop=(kb == 0), skip_group_check=True),
                         reads=[dw[si], dV[kb]], writes=[PD[ob]])
                    if kb == 0 and h2 == 1:
                        S.op("act", lambda e: e.activation(out=mixT[:, j, QB * 512:(QB + 1) * 512], in_=PS[ob][:, :],
                                                           func=AF.Copy),
                             reads=[PD[ob]], writes=[dMixA[QB * 4 + tt] for tt in range(4)])

                a1(0)
                a2(0)
                for n in range(len(units)):
                    b1(n)
                    if n + 1 < len(units):
                        a1(n + 1)
                    b2(n)
                    if n + 1 < len(units):
                        a2(n + 1)
                    b3(n)
                S.op("dve", lambda e: e.memset(kTp[:], 0.0), writes=[dpad])
                S.op("dve", lambda e: e.memset(vpad[:], 0.0), writes=[dpad])
                S.op("dve", lambda e: e.memset(qblk[:].rearrange("p j b a t -> p (j b a t)"), 0.0), writes=[dpad])
                S.op("dve", lambda e: e.tensor_copy(out=kTp[:, :, 0:64], in_=kT[:, :, TP:NTOK]), reads=[dK[16], dpad], writes=[dpad])
                S.op("dve", lambda e: e.tensor_copy(out=vpad[0:64, :], in_=vres[0:64, 16, :]), reads=[dV[16], dpad], writes=[dpad])
                for h2 in range(2):
                    S.op("dve", lambda e, h2=h2: e.tensor_copy(
                        out=qblk[64 * h2:64 * h2 + 64, :, :, h2, :],
                        in_=qT[64 * h2:64 * h2 + 64, :, TP:NTOK].rearrange("p j (b t) -> p j b t", t=4)),
                         reads=[dQ[16], dpad], writes=[dpad])
                S.op("act", lambda e: e.activation(out=eT[0][:, 0:8], in_=biasT[:, :], func=AF.Exp), reads=[dPar, de[0]],
                     writes=[de[0]])
                for j in range(4):
                    S.op("dve", lambda e, j=j: e.tensor_copy(
                        out=ebF[:, j * 128:(j + 1) * 128].rearrange("p (b a t) -> p b a t", b=16, a=2),
                        in_=eT[0][:, 2 * j:2 * j + 2].unsqueeze(1).unsqueeze(3).to_broadcast([128, 16, 2, 4])),
                         reads=[de[0]], writes=[dpad])
                S.op("dve", lambda e: e.tensor_copy(
                    out=mS2[:].rearrange("p (b a t) -> p b a t", b=16, a=2),
                    in_=cf("maskS").rearrange("p (b t) -> p b t", t=4).unsqueeze(2).to_broadcast([128, 16, 2, 4])),
                     reads=[dCF], writes=[dpad])
                S.barrier()
            chk(4)
            pa_r.close()

            NQ = 8
            GW = NQ * 8
            Kst = [sb([128, NQ, 512], F32, stack=p2) for _ in range(2)]
            Vst = [sb([128, NQ, 512], F32, stack=p2) for _ in range(2)]
            KTs = [sb([128, NQ, 4, 128], BF16, stack=p2) for _ in range(2)]
            Vb = [sb([128, NQ, 512], BF16, stack=p2) for _ in range(2)]
            dVb = [dep("Vb") for _ in range(2)]
            dKst = [dep("Kst") for _ in range(2)]
            dVst = [dep("Vst") for _ in range(2)]
            dKTs = [dep("KTs") for _ in range(2)]
            ptb = sb([128, NSEQ * NPG], I32, stack=p2)
            idx = sb([128, NSEQ * NPG], I32, stack=p2)
            iop = sb([128, 1], I32, stack=p2)
            dIdx = dep("idx")
            es_ = [sb([128, 4, 128], F32, stack=p2) for _ in range(2)]
            xs2 = [sb([128, 4, 128], F32, stack=p2) for _ in range(2)]
            Ls = [sb([128, 4, 128], BF16, stack=p2) for _ in range(2)]
            ws = [sb([128, 4, 128], BF16, stack=p2) for _ in range(2)]
            wsb = sb([128, 4, 128], BF16, stack=p2)
            des = [dep("es") for _ in range(2)]
            dxs = [dep("xs") for _ in range(2)]
            dLs = [dep("Ls") for _ in range(2)]
            dws = [dep("ws") for _ in range(2)]

            S.dma("pool", lambda e: e.dma_start(out=ptb[:], in_=pt[0:1, :].to_broadcast([128, NSEQ * NPG])), dIdx, "in")
            S.op("pool", lambda e: e.iota(iop[:], pattern=[[0, 1]], base=0, channel_multiplier=1), writes=[dIdx])
            S.op("pool", lambda e: e.tensor_scalar(out=idx[:], in0=ptb[:], scalar1=128, scalar2=None, op0=ALU.mult),
                 reads=[dIdx], writes=[dIdx])
            S.op("pool", lambda e: e.tensor_tensor(out=idx[:], in0=idx[:], in1=iop[:].to_broadcast([128, NSEQ * NPG]),
                                                   op=ALU.add), reads=[dIdx], writes=[dIdx])

            dZ, dC, dOs = dep("Z"), dep("C"), dep("Os")

            def bview(bank, c0, w_):
                return PS[bank][:, :].rearrange("p (j c) -> p j c", j=4)[:, :, c0:c0 + w_]

            def sample_step(kind, gi=None, buf=None, last=False, first=False, b2=0):
                c0, w_ = (0, 128) if kind == "new" else (gi * GW, GW)
                if kind == "new":
                    for j in range(4):
                        S.op("pe", lambda e, j=j: e.matmul(out=PS[0][:, j * 128:(j + 1) * 128], lhsT=kTp[:, j, :],
                                                           rhs=qblk[:, j, :, :, :].rearrange("p b a t -> p (b a t)"),
                                                           start=True, stop=True), reads=[dpad], writes=[dZ], inc=(j == 3))
                else:
                    for bi in range(NQ):
                        b = gi * NQ + bi
                        for j in range(4):
                            S.op("pe", lambda e, bi=bi, b=b, j=j: e.matmul(
                                out=PS[0][:, j * 128 + b * 8:j * 128 + b * 8 + 8], lhsT=KTs[buf][:, bi, j, :],
                                rhs=qblk[:, j, b, :, :].rearrange("p a t -> p (a t)"), start=True, stop=True),
                                 reads=[dKTs[buf], dpad], writes=[dZ], inc=(bi == NQ - 1 and j == 3))
                ev = es_[b2][:, :, 0:w_]
                S.op("act", lambda e: e.activation(out=ev, in_=bview(0, c0, w_), func=AF.Exp, scale=SCALE),
                     reads=[dZ], writes=[des[b2]])
                S.op("dve", lambda e: e.tensor_tensor(out=ev, in0=ev,
                                                      in1=ebF[:, :].rearrange("p (j c) -> p j c", j=4)[:, :, c0:c0 + w_],
                                                      op=ALU.mult), reads=[des[b2], dpad], writes=[des[b2]])
                if kind == "new":
                    S.op("dve", lambda e: e.tensor_tensor(out=ev, in0=ev, in1=mS2[:, :].unsqueeze(1).to_broadcast([128, 4, 128]),
                                                          op=ALU.mult), reads=[des[b2], dpad], writes=[des[b2]])
                S.op("act", lambda e: e.activation(out=Ls[b2][:, :, 0:w_], in_=ev, func=AF.Ln, bias=1.0),
                     reads=[des[b2]], writes=[dLs[b2]])
                for j in range(4):
                    S.op("pe", lambda e, j=j: e.matmul(out=PS[1][:, j * 128 + c0:j * 128 + c0 + w_], lhsT=uincl,
                                                       rhs=Ls[b2][:, j, 0:w_], start=(first and j == 0), stop=False,
                                                       skip_group_check=True),
                         reads=[dLs[b2], dCB], writes=[dC], inc=(j == 3))
                S.op("act", lambda e: e.activation(out=xs2[b2][:, :, 0:w_], in_=bview(1, c0, w_), func=AF.Exp, scale=-1.0),
                     reads=[dC], writes=[dxs[b2]])
                if not last:
                    for j in range(4):
                        S.op("pe", lambda e, j=j: e.matmul(out=PS[1][:, j * 128 + c0:j * 128 + c0 + w_], lhsT=lstrict,
                                                           rhs=Ls[b2][:, j, 0:w_], start=False, stop=False,
                                                           skip_group_check=True),
                             reads=[dLs[b2], dCB], writes=[dC], inc=(j == 3))
                if kind == "new":
                    S.op("dve", lambda e: e.tensor_tensor(out=wsb[:, :, :], in0=ev, in1=xs2[b2][:, :, 0:w_], op=ALU.mult),
                         reads=[des[b2], dxs[b2]], writes=[dws[b2]])
                    for j in range(4):
                        S.op("pe", lambda e, j=j: e.matmul(out=PS[2][:, j * 128:(j + 1) * 128],
                                                           lhsT=vpad[:, j * 128:(j + 1) * 128], rhs=wsb[:, j, :],
                                                           start=(j == 0), stop=False, skip_group_check=True),
                             reads=[dws[b2], dpad], writes=[dOs], inc=(j == 3))
                else:
                    S.op("dve", lambda e: e.tensor_tensor(out=ws[b2][:, :, 0:w_], in0=ev, in1=xs2[b2][:, :, 0:w_], op=ALU.mult),
                         reads=[des[b2], dxs[b2]], writes=[dws[b2]])
                    for bi in range(NQ):
                        b = gi * NQ + bi
                        for j in range(4):
                            S.op("pe", lambda e, bi=bi, b=b, j=j: e.matmul(
                                out=PS[2][:, j * 128 + b * 8:j * 128 + b * 8 + 8], lhsT=Vb[buf][:, bi, j * 128:(j + 1) * 128],
                                rhs=ws[b2][:, j, bi * 8:bi * 8 + 8], start=False, stop=False, skip_group_check=True),
                                 reads=[dws[b2], dVb[buf]], writes=[dOs], inc=(bi == NQ - 1 and j == 3))

            sample_step("new", first=True)
            gcount = 0
            for p in range(NPG - 1, -1, -1):
                for gi in range(NSEQ // NQ):
                    buf = gcount % 2
                    gcount += 1
                    for bi in range(NQ):
                        b = gi * NQ + bi
                        col = b * NPG + p
                        S.dma("pool", lambda e, bi=bi, col=col, buf=buf: e.indirect_dma_start(
                            out=Kst[buf][:, bi, :], out_offset=None, in_=ck,
                            in_offset=bass.IndirectOffsetOnAxis(ap=idx[:, col:col + 1], axis=0)),
                              dKst[buf], "in", extra_reads=[dIdx])
                        S.dma("pool", lambda e, bi=bi, col=col, buf=buf: e.indirect_dma_start(
                            out=Vst[buf][:, bi, :], out_offset=None, in_=cv,
                            in_offset=bass.IndirectOffsetOnAxis(ap=idx[:, col:col + 1], axis=0)),
                              dVst[buf], "in", extra_reads=[dIdx])
                    for bi in range(NQ):
                        bank = 5 + (bi % 2)
                        for jj in range(4):
                            S.op("pe", lambda e, bi=bi, jj=jj, bank=bank, buf=buf: e.transpose(
                                out=PS[bank][:, jj * 128:(jj + 1) * 128], in_=Kst[buf][:, bi, jj * 128:(jj + 1) * 128],
                                identity=ident), reads=[dKst[buf], dCF], writes=[PD[bank]], inc=(jj == 3))
                        if bi % 2 == 0:
                            S.op("dve", lambda e, bi=bi, bank=bank, buf=buf: e.tensor_copy(
                                out=KTs[buf][:, bi, :, :].rearrange("p j t -> p (j t)"), in_=PS[bank][:, :]),
                                 reads=[PD[bank]], writes=[dKTs[buf]])
                        else:
                            S.op("act", lambda e, bi=bi, bank=bank, buf=buf: e.activation(
                                out=KTs[buf][:, bi, :, :].rearrange("p j t -> p (j t)"), in_=PS[bank][:, :], func=AF.Copy),
                                 reads=[PD[bank]], writes=[dKTs[buf]])
                    for hv_ in range(2):
                        S.op("act" if hv_ == 0 else "dve",
                             (lambda e, buf=buf: e.activation(
                                 out=Vb[buf][:, 0:NQ // 2, :].rearrange("p b n -> p (b n)"),
                                 in_=Vst[buf][:, 0:NQ // 2, :].rearrange("p b n -> p (b n)"), func=AF.Copy)) if hv_ == 0 else
                             (lambda e, buf=buf: e.tensor_copy(
                                 out=Vb[buf][:, NQ // 2:NQ, :].rearrange("p b n -> p (b n)"),
                                 in_=Vst[buf][:, NQ // 2:NQ, :].rearrange("p b n -> p (b n)"))),
                             reads=[dVst[buf]], writes=[dVb[buf]])
                    sample_step("page", gi=gi, buf=buf, last=(p == 0), b2=gcount % 2)
            for h2 in range(2):
                S.op("act", lambda e, h2=h2: e.activation(
                    out=mixT[64 * h2:64 * h2 + 64, 0:4, TP:NTOK].rearrange("p j (b t) -> p j b t", t=4),
                    in_=PS[2][64 * h2:64 * h2 + 64, :].rearrange("p (j b a t) -> p j b a t", j=4, b=16, a=2)[:, :, :, h2, :],
                    func=AF.Copy), reads=[dOs], writes=[dMixA[16]])
            S.barrier()

        w_out_v = w_out.rearrange("(k p) n -> p k n", p=128)
        w_up_v = w_up.rearrange("(k p) n -> p k n", p=128)
        w_down_v = w_down.rearrange("(c p) n -> p c n", p=128)
        w_ada_v = w_ada.rearrange("(k p) n -> p k n", p=128)
        gBp = sb([128, 2048], F32)
        gBs = sb([64, 2048], F32)
        dG = dep("gB")
        wst = [sb([128, 8, 256], F32) for _ in range(2)]
        dW = [dep("wst") for _ in range(2)]
        with contextlib.ExitStack() as pg:
            cTp = sb([128, 8, 128], F32, stack=pg)
            cTs = sb([128, 8, 64], F32, stack=pg)
            bgB = sb([128, 2048], F32, stack=pg)
            dbg = dep("bgB")
            dcTx = dep("cTx")
            S.dma("sp", lambda e: e.dma_start(out=bgB[:], in_=bg[0:1, :].to_broadcast([128, 2048])), dbg, "in")
            S.op("dve", lambda e: e.tensor_copy(out=cTp[:], in_=cT[:, :, 16:17].to_broadcast([128, 8, 128])),
                 reads=[dcT], writes=[dcTx])
            S.op("dve", lambda e: e.tensor_copy(out=cTs[:].rearrange("p c (b t) -> p c b t", t=4),
                                                in_=cT[:, :, 0:16].unsqueeze(3).to_broadcast([128, 8, 16, 4])),
                 reads=[dcT], writes=[dcTx])
            for n_ in range(8):
                buf = n_ % 2
                acol = (2048 if n_ < 4 else 5120) + (n_ % 4) * 256
                gcol = n_ * 256
                S.dma("sp", lambda e, acol=acol, buf=buf: e.dma_start(out=wst[buf][:], in_=w_ada_v[:, :, acol:acol + 256]),
                      dW[buf], "in")
                for (lhs, rows, dst, bank) in ((cTp, 128, gBp, 2), (cTs, 64, gBs, 3)):
                    for k in range(8):
                        S.op("pe", lambda e, k=k, lhs=lhs, rows=rows, bank=bank, buf=buf: e.matmul(
                            out=PS[bank][0:rows, 0:256], lhsT=lhs[:, k, :], rhs=wst[buf][:, k, :],
                            start=(k == 0), stop=(k == 7)),
                             reads=[dcTx, dW[buf]], writes=[PD[bank]], inc=(k == 7))
                    S.op("dve", lambda e, rows=rows, dst=dst, bank=bank, gcol=gcol: e.tensor_tensor(
                        out=dst[0:rows, gcol:gcol + 256], in0=PS[bank][0:rows, 0:256], in1=bgB[0:rows, gcol:gcol + 256],
                        op=ALU.add), reads=[PD[bank], dbg], writes=[dG])
            S.barrier()
            chk(6)

        def wst_flat(buf, c):
            return wst[buf][:].rearrange("p k n -> p (k n)").rearrange("p (c n) -> p c n", c=c)

        halves = [list(range(0, 8)), list(range(8, 17))]
        for hi, tiles in enumerate(halves):
            with contextlib.ExitStack() as p3:
                nt = len(tiles)
                ncols = sum(tile_rows(i) for i in tiles)
                col0 = tiles[0] * 128
                x1 = sb([128, nt, D], F32, stack=p3)
                dx1 = [dep("x1") for _ in range(nt)]
                h2T = sb([128, 8, ncols], BF16, stack=p3)
                dh2 = [dep("h2T") for _ in range(nt)]
                with contextlib.ExitStack() as p3a:
                    wo = sb([128, 8, D], BF16, stack=p3a)
                    dwo = [dep("wo") for _ in range(4)]
                    for g in range(4):
                        load_weight_piece(w_out_v[:, :, g * 256:(g + 1) * 256], wo[:, :, g * 256:(g + 1) * 256], wst, dW, g % 2,
                                          dwo[g])
                    xt = [sb([128, D], F32, stack=p3a) for _ in range(2)]
                    dxt = [dep("xt") for _ in range(2)]
                    wk = [(sb([128, D], F32, stack=p3a), sb([128, D], F32, stack=p3a), sb([128, 4], F32, stack=p3a))
                          for _ in range(2)]
                    dwk = [dep("wk") for _ in range(2)]
                    tmp = [sb([128, 512], F32, stack=p3a) for _ in range(2)]
                    dtm = [dep("tmp") for _ in range(2)]
                    for li, i in enumerate(tiles):
                        T = tile_rows(i)
                        c0, c1 = tile_cols(i)
                        b = li % 2
                        load_x_tile(i, xt[b], dxt[b])
                        gB = gBp if i < 16 else gBs
                        for nh in range(2):
                            bank = nh
                            ns = slice(nh * 512, (nh + 1) * 512)
                            for k in range(8):
                                md = dMixA[i] if k < 4 else dMixB[i]
                                S.op("pe", lambda e, k=k, bank=bank, ns=ns: e.matmul(
                                    out=PS[bank][0:T, :], lhsT=mixT[:, k, c0:c1], rhs=wo[:, k, ns], start=(k == 0), stop=(k == 7)),
                                     reads=[md, dwo[2 * nh], dwo[2 * nh + 1]], writes=[PD[bank]], inc=(k == 7))
                            S.op("dve", lambda e, bank=bank, ns=ns, nh=nh, gB=gB: e.tensor_tensor(
                                out=tmp[nh][0:T, :], in0=PS[bank][0:T, :], in1=gB[0:T, ns], op=ALU.mult),
                                 reads=[PD[bank], dG], writes=[dtm[nh]])
                            S.op("pool", lambda e, ns=ns, nh=nh, li=li, b=b: e.tensor_tensor(
                                out=x1[0:T, li, ns], in0=tmp[nh][0:T, :], in1=xt[b][0:T, ns], op=ALU.add),
                                 reads=[dtm[nh], dxt[b]], writes=[dx1[li]])
                        lc0 = c0 - col0
                        norm_transpose(i, x1[0:T, li, :], dx1[li], 2, h2T[:, :, lc0:lc0 + T], dh2[li], wk[b], dwk[b],
                                       (2, 3) if b == 0 else (4, 5))
                    S.barrier()
                    chk(7)
                with contextlib.ExitStack() as p4:
                    wu = [sb([128, 8, 512], BF16, stack=p4) for _ in range(2)]
                    wd = [sb([128, 4, D], BF16, stack=p4) for _ in range(2)]
                    dwu = [dep("wu") for _ in range(2)]
                    dwd = [dep("wd") for _ in range(2)]
                    upT = [sb([128, 4, 512], BF16, stack=p4) for _ in range(2)]
                    dup = [dep("up") for _ in range(2)]
                    rl = [sb([128, 512], F32, stack=p4) for _ in range(2)]
                    drl = [dep("rl") for _ in range(2)]
                    tmp = [sb([128, 512], F32, stack=p4) for _ in range(2)]
                    dtm = [dep("tmp") for _ in range(2)]
                    groups = []
                    li = 0
                    while li < nt:
                        g = [l for l in range(li, min(li + 4, nt)) if tile_rows(tiles[l]) == 128]
                        if not g:
                            g = [li]
                        groups.append(g)
                        li = g[-1] + 1
                    gctr = 0
                    rctr = 0
                    dx1h = [[dep("x1h") for _ in range(2)] for _ in range(nt)]
                    for l_ in range(nt):
                        for nh_ in range(2):
                            dx1h[l_][nh_].w = dx1[l_].w
                    def load_w(E):
                        wb = E % 2
                        for hh in range(2):
                            S.dma("sp", lambda e, hh=hh: e.dma_start(
                                out=wst[0][:], in_=w_up_v[:, :, E * 512 + hh * 256:E * 512 + (hh + 1) * 256]), dW[0], "in")
                            S.op("act", lambda e, hh=hh: e.activation(out=wu[wb][:, :, hh * 256:(hh + 1) * 256], in_=wst[0][:],
                                                                      func=AF.Copy),
                                 reads=[dW[0]], writes=[dwu[wb]])
                            S.dma("sp", lambda e, hh=hh: e.dma_start(
                                out=wst_flat(1, 2), in_=w_down_v[:, E * 4 + hh * 2:E * 4 + (hh + 1) * 2, :]), dW[1], "in")
                            S.op("pool", lambda e, hh=hh: e.tensor_copy(out=wd[wb][:, hh * 2:(hh + 1) * 2, :], in_=wst_flat(1, 2)),
                                 reads=[dW[1]], writes=[dwd[wb]])

                    def up_part(E, g, ub):
                        wb = E % 2
                        gcol0 = tiles[g[0]] * 128 - col0
                        gn = sum(tile_rows(tiles[l]) for l in g)
                        for fc in range(4):
                            bank = fc % 2
                            rb = fc % 2
                            for k in range(8):
                                S.op("pe", lambda e, k=k, fc=fc, bank=bank: e.matmul(
                                    out=PS[bank][:, 0:gn], lhsT=wu[wb][:, k, fc * 128:(fc + 1) * 128],
                                    rhs=h2T[:, k, gcol0:gcol0 + gn], start=(k == 0), stop=(k == 7)),
                                     reads=[dwu[wb]] + [dh2[l] for l in g], writes=[PD[bank]], inc=(k == 7))
                            S.op("act", lambda e, bank=bank, rb=rb: e.activation(out=rl[rb][:, 0:gn], in_=PS[bank][:, 0:gn],
                                                                                 func=AF.Relu),
                                 reads=[PD[bank]], writes=[drl[rb]])
                            S.op("act", lambda e, rb=rb, fc=fc: e.activation(
                                out=upT[ub][:, fc, 0:gn], in_=rl[rb][:, 0:gn], func=AF.Square),
                                 reads=[drl[rb]], writes=[dup[ub]])

                    def down_part(E, g, ub):
                        wb = E % 2
                        gcol0 = tiles[g[0]] * 128 - col0
                        for l in g:
                            i = tiles[l]
                            T = tile_rows(i)
                            lc = tiles[l] * 128 - col0 - gcol0
                            gB = gBp if i < 16 else gBs
                            for nh in range(2):
                                bank = 2 + nh + 2 * (l % 2)
                                ns = slice(nh * 512, (nh + 1) * 512)
                                for fc in range(4):
                                    S.op("pe", lambda e, fc=fc, bank=bank, ns=ns: e.matmul(
                                        out=PS[bank][0:T, :], lhsT=upT[ub][:, fc, lc:lc + T], rhs=wd[wb][:, fc, ns],
                                        start=(fc == 0), stop=(fc == 3)),
                                         reads=[dup[ub], dwd[wb]], writes=[PD[bank]], inc=(fc == 3))
                                S.op("dve", lambda e, bank=bank, nh=nh: e.tensor_tensor(
                                    out=tmp[nh][0:T, :], in0=PS[bank][0:T, :], in1=gB[0:T, 1024 + nh * 512:1024 + (nh + 1) * 512],
                                    op=ALU.mult), reads=[PD[bank], dG], writes=[dtm[nh]])
                                S.op("dve" if nh == 0 else "pool", lambda e, ns=ns, nh=nh: e.tensor_tensor(
                                    out=x1[0:T, l, ns], in0=x1[0:T, l, ns], in1=tmp[nh][0:T, :], op=ALU.add),
                                     reads=[dtm[nh], dx1h[l][nh]], writes=[dx1h[l][nh]])
                            if E == 7:
                                ydst = yp[i * 128:(i + 1) * 128, :] if i < 16 else ys[:, :]
                                for nh in range(2):
                                    S.dma("sp", lambda e, ydst=ydst, nh=nh: e.dma_start(
                                        out=ydst[:, nh * 512:(nh + 1) * 512], in_=x1[0:T, l, nh * 512:(nh + 1) * 512]),
                                          dx1h[l][nh], "out")

                    seq = [(E, g) for E in range(8) for g in groups]
                    load_w(0)
                    up_part(seq[0][0], seq[0][1], 0)
                    for n_, (E, g) in enumerate(seq):
                        if g is groups[0] and E + 1 < 8:
                            load_w(E + 1)
                        if n_ + 1 < len(seq):
                            up_part(seq[n_ + 1][0], seq[n_ + 1][1], (n_ + 1) % 2)
                        down_part(E, g, n_ % 2)
                    S.barrier()
    except _Stop:
        S.finish()
        return nc
    S.finish()
    es.close()
    return nc


def _core_inputs(c, inp, n_phys=None, ck=None, cv=None, pt=None):
    f = np.float32
    d = {}
    d["xp"] = np.ascontiguousarray(inp["x_prompt"][c], dtype=f)
    d["xs"] = np.ascontiguousarray(inp["x_sample"][16 * c:16 * c + 16].reshape(TS, D), dtype=f)
    d["ck"] = ck if ck is not None else inp["cache_k"][0].reshape(-1, 512)
    d["cv"] = cv if cv is not None else inp["cache_v"][0].reshape(-1, 512)
    d["st0"] = np.ascontiguousarray(inp["state_hgrn"][0, 16 * c:16 * c + 16], dtype=f)
    ptc = pt if pt is not None else inp["page_table"][16 * c:16 * c + 16]
    d["pt"] = np.ascontiguousarray(ptc.reshape(1, -1), dtype=np.int32)
    d["cc"] = np.ascontiguousarray(np.concatenate([inp["c_sample"][16 * c:16 * c + 16], inp["c_prompt"][c:c + 1]], 0), dtype=f)
    d["w_ada"] = np.ascontiguousarray(inp["w_ada"][0], dtype=f)
    b_ada = inp["b_ada"][0]
    d["vecs"] = np.ascontiguousarray(np.concatenate([b_ada.reshape(48, 128), inp["norm1_g"][0].reshape(8, 128),
                                                     inp["norm2_g"][0].reshape(8, 128)], 0), dtype=f)
    d["bg"] = np.ascontiguousarray(np.concatenate([b_ada[2048:3072], b_ada[5120:6144]])[None], dtype=f)
    d["w_in"] = np.ascontiguousarray(inp["w_in"][0], dtype=f)
    d["qkg"] = np.ascontiguousarray(np.concatenate([np.tile(inp["q_norm_g"][0], 8), np.tile(inp["k_norm_g"][0], 8),
                                                    np.tile(inp["hg_out_g"][0], 8)])[None], dtype=f)
    d["sbb"] = np.ascontiguousarray(inp["sb_bias"][0][None], dtype=f)
    d["lbl"] = np.ascontiguousarray(inp["hg_lb_logits"].reshape(1, 1024), dtype=f)
    d["w_out"] = np.ascontiguousarray(inp["w_out"][0], dtype=f)
    d["w_up"] = np.ascontiguousarray(inp["w_up"][0], dtype=f)
    d["w_down"] = np.ascontiguousarray(inp["w_down"][0], dtype=f)
    d["cstf"] = _CARR
    d["cstb"] = _BARR
    return d


def _assemble(results):
    y_p = np.stack([r["yp"] for r in results]).astype(np.float32)
    y_s = np.concatenate([r["ys"].reshape(16, 4, D) for r in results]).astype(np.float32)
    k_p = np.stack([r["kp"].reshape(TP, 8, 64) for r in results])[None].astype(np.float32)
    v_p = np.stack([r["vp"].reshape(TP, 8, 64) for r in results])[None].astype(np.float32)
    k_s = np.concatenate([r["ks"].reshape(16, 4, 8, 64) for r in results])[None].astype(np.float32)
    v_s = np.concatenate([r["vs"].reshape(16, 4, 8, 64) for r in results])[None].astype(np.float32)
    s_p = np.stack([r["sp"] for r in results])[None].astype(np.float32)
    s_s = np.concatenate([r["ss"] for r in results])[None].astype(np.float32)
    return (y_p, y_s, k_p, v_p, k_s, v_s, s_p, s_s)


def kernel(**inputs):
    inp = {k: np.asarray(v) for k, v in inputs.items()}
    n_phys = inp["cache_k"].shape[1]
    nc = build(n_phys)
    ck = np.ascontiguousarray(inp["cache_k"][0].reshape(-1, 512), dtype=np.float32)
    cv = np.ascontiguousarray(inp["cache_v"][0].reshape(-1, 512), dtype=np.float32)
    in_maps = [_core_inputs(c, inp, ck=ck, cv=cv) for c in range(NCORES)]
    res = run_bass_kernel_spmd(nc, in_maps, core_ids=list(range(NCORES)))
    return _assemble(res.results)
```

```python
import contextlib
import os
import numpy as np
import ml_dtypes
import concourse.bass as bass
import concourse.mybir as mybir
from concourse.bass_utils import run_bass_kernel_spmd

F32 = mybir.dt.float32
BF16 = mybir.dt.bfloat16
I32 = mybir.dt.int32
AF = mybir.ActivationFunctionType
ALU = mybir.AluOpType
AX = mybir.AxisListType

NCORES = 8
D = 1024
TP = 2048
NSEQ = 16
TS = 64
NTOK = TP + TS
NPG = 16
EPS = 1e-6
SCALE = 64 ** -0.5
NT = 17


def _consts():
    f = {}
    idx = np.arange(128)
    f["ident"] = np.eye(128, dtype=np.float32)
    s = idx[:, None]
    t = idx[None, :]
    same64 = (s // 64) == (t // 64)
    f["tri2"] = ((s <= t) & same64).astype(np.float32)
    f["tsu2"] = ((s > t) & same64).astype(np.float32)
    same4 = (s // 4) == (t // 4)
    f["tri4"] = ((s <= t) & same4).astype(np.float32)
    f["tsu4"] = ((s > t) & same4).astype(np.float32)
    ci = np.zeros((128, 2), np.float32)
    ci[:64, 0] = 1
    ci[64:, 1] = 1
    f["chunkind"] = ci
    si = np.zeros((128, 16), np.float32)
    for p in range(64):
        si[p, p // 4] = 1
    f["seqind"] = si
    mh = np.zeros((128, 64), np.float32)
    for p in range(128):
        mh[p, :] = (np.arange(64) >= (p % 64))
    f["maskH"] = mh
    ms = np.zeros((128, 64), np.float32)
    for p in range(64):
        for c in range(64):
            ms[p, c] = (p // 4 == c // 4) and (p % 4 <= c % 4)
    f["maskHS"] = ms
    ma = np.zeros((128, 64), np.float32)
    for p in range(64):
        for c in range(64):
            ma[p, c] = (p // 4 == c // 4) and (p % 4 < c % 4)
    f["maskS"] = ma
    smt = np.zeros((128, 16 * 64), np.float32)
    for b in range(16):
        smt[:, b * 64 + 4 * b: b * 64 + 4 * b + 4] = 1
    f["seqmaskT"] = smt
    md = np.zeros((128, 4 * 512), np.float32)
    for i in range(4):
        md[:, i * 512:(i + 1) * 512] = ((128 * i + idx[:, None]) < np.arange(512)[None, :])
    f["maskD"] = md
    f["uincl"] = (s >= t).astype(np.float32)
    f["lstrict"] = (s < t).astype(np.float32)
    io = np.zeros((128, 1), np.float32)
    off = {}
    cols = 0
    for k, v in f.items():
        off[k] = (cols, v.shape[1])
        cols += v.shape[1]
    bfk = ["maskD", "uincl", "lstrict", "seqmaskT"]
    off = {}
    cols = 0
    for k, v in f.items():
        if k in bfk:
            continue
        off[k] = (cols, v.shape[1])
        cols += v.shape[1]
    arr = np.concatenate([f[k] for k in f if k not in bfk], axis=1).astype(np.float32)
    boff = {}
    cols = 0
    for k in bfk:
        boff[k] = (cols, f[k].shape[1])
        cols += f[k].shape[1]
    barr = np.concatenate([f[k] for k in bfk], axis=1).astype(np.float32)
    return arr, off, barr, boff


_CARR, _COFF, _BARR, _BOFF = _consts()


class _Stop(Exception):
    pass


class Dep:
    __slots__ = ("name", "w", "re", "rd", "sem", "cnt")

    def __init__(self, name):
        self.name = name
        self.w = None
        self.re = {}
        self.rd = None
        self.sem = None
        self.cnt = 0


class Sched:
    def __init__(self, nc, es):
        self.nc = nc
        self.es = es
        self.eng = {}
        for name, h in [("pe", nc.tensor), ("act", nc.scalar), ("dve", nc.vector),
                        ("pool", nc.gpsimd), ("sp", nc.sync)]:
            sem = es.enter_context(nc.semaphore("sem_" + name))
            self.eng[name] = dict(h=h, sem=sem, cnt=0, waited={})
        self.dma_deps = []

    def _wait(self, ename, entry):
        E = self.eng[ename]
        if entry[0] == "e":
            _, src, idx = entry
            if src == ename and ename == "pe":
                return
            sem = self.eng[src]["sem"]
            val = idx
        else:
            _, dep, n = entry
            sem = dep.sem
            val = 16 * n
        key = id(sem)
        if E["waited"].get(key, 0) >= val:
            return
        E["waited"][key] = val
        E["h"].wait_ge(sem, val)

    def _deps(self, ename, reads, writes):
        for d in reads:
            if d.w is not None:
                self._wait(ename, d.w)
        for d in writes:
            if d.w is not None and not (d.w[0] == "e" and d.w[1] == ename):
                self._wait(ename, d.w)
            for src, idx in d.re.items():
                if src != ename:
                    self._wait(ename, ("e", src, idx))
            if d.rd is not None:
                self._wait(ename, d.rd)

    def op(self, ename, fn, reads=(), writes=(), inc=True):
        E = self.eng[ename]
        self._deps(ename, reads, writes)
        ins = fn(E["h"])
        if inc:
            E["cnt"] += 1
            idx = E["cnt"]
            ins.then_inc(E["sem"], 1)
        else:
            idx = E["cnt"] + 1
        for d in reads:
            d.re[ename] = idx
        for d in writes:
            d.w = ("e", ename, idx)
            d.re = {}
            d.rd = None
        return ins

    def dma(self, qname, fn, dep, direction, extra_reads=()):
        E = self.eng[qname]
        if dep.sem is None:
            dep.sem = self.es.enter_context(self.nc.semaphore("dsem_" + dep.name))
            self.dma_deps.append(dep)
        if direction == "in":
            if dep.w is not None and dep.w[0] == "d" and dep.w[1] is dep and not dep.re and dep.rd is None:
                for d in extra_reads:
                    if d.w is not None:
                        self._wait(qname, d.w)
            else:
                self._deps(qname, list(extra_reads), [dep])
        else:
            self._deps(qname, list(extra_reads) + [dep], [])
        ins = fn(E["h"])
        dep.cnt += 1
        ins.then_inc(dep.sem, 16)
        ent = ("d", dep, dep.cnt)
        if direction == "in":
            dep.w = ent
            dep.re = {}
            dep.rd = None
        else:
            dep.rd = ent
        return ins

    def barrier(self):
        names = ["pe", "act", "dve", "pool", "sp"]
        for a in names:
            for b in names:
                if a != b and self.eng[b]["cnt"] > 0:
                    self._wait(a, ("e", b, self.eng[b]["cnt"]))
            for d in self.dma_deps:
                if d.cnt > 0:
                    self._wait(a, ("d", d, d.cnt))

    def finish(self):
        for d in self.dma_deps:
            if d.cnt > 0:
                self._wait("sp", ("d", d, d.cnt))
        for b in ["pe", "act", "dve", "pool"]:
            if self.eng[b]["cnt"] > 0:
                self._wait("sp", ("e", b, self.eng[b]["cnt"]))


def build(n_phys):
    nc = bass.Bass("TRN2", target_bir_lowering=False)
    es = contextlib.ExitStack()

    def din(name, shape, dt=F32):
        return nc.dram_tensor(name, shape, dt, kind="ExternalInput").ap()

    def dout(name, shape, dt=F32):
        return nc.dram_tensor(name, shape, dt, kind="ExternalOutput").ap()

    xp = din("xp", [TP, D])
    xs_d = din("xs", [TS, D])
    ck = din("ck", [n_phys * 128, 512])
    cv = din("cv", [n_phys * 128, 512])
    st0 = din("st0", [NSEQ, 8, 64, 64])
    pt = din("pt", [1, NSEQ * NPG], I32)
    cc = din("cc", [17, D])
    w_ada = din("w_ada", [D, 6 * D])
    vecs = din("vecs", [64, 128])
    bg = din("bg", [1, 2048])
    w_in = din("w_in", [D, 3584])
    qkg = din("qkg", [1, 3 * 512])
    sbb = din("sbb", [1, 8])
    lbl = din("lbl", [1, 1024])
    w_out = din("w_out", [D, D])
    w_up = din("w_up", [D, 4 * D])
    w_down = din("w_down", [4 * D, D])
    cstf = din("cstf", list(_CARR.shape))
    cstb = din("cstb", list(_BARR.shape))

    yp = dout("yp", [TP, D])
    ys = dout("ys", [TS, D])
    kp = dout("kp", [TP, 512])
    vp = dout("vp", [TP, 512])
    ks = dout("ks", [TS, 512])
    vs = dout("vs", [TS, 512])
    sp_o = dout("sp", [8, 64, 64])
    ss_o = dout("ss", [NSEQ, 8, 64, 64])

    S = Sched(nc, es)
    ucnt = [0]

    def sb(shape, dt=F32, side="left", stack=None, name=None):
        ucnt[0] += 1
        nm = (name or "t") + str(ucnt[0])
        return (stack or es).enter_context(nc.sbuf_tensor(nm, shape, dt, side=side))

    def dep(name="d"):
        ucnt[0] += 1
        return Dep(name + str(ucnt[0]))

    PS = [es.enter_context(nc.psum_tensor(f"ps{i}", [128, 512], F32)) for i in range(8)]
    PD = [dep(f"ps{i}_") for i in range(8)]

    CF = sb([128, _CARR.shape[1]], F32)
    dCF = dep("cf")
    S.dma("sp", lambda e: e.dma_start(out=CF[:], in_=cstf[:, :]), dCF, "in")

    def cf(key, rows=128, c0=0, c1=None):
        o, n = _COFF[key]
        c1 = n if c1 is None else c1
        return CF[0:rows, o + c0:o + c1]

    def load_bf_const(key, dst_ap, ddst, scratch, dscr):
        o, n = _BOFF[key]
        for a in range(0, n, 512):
            w_ = min(512, n - a)
            S.dma("sp", lambda e, a=a, w_=w_: e.dma_start(out=scratch[:, 0:w_], in_=cstb[:, o + a:o + a + w_]), dscr, "in")
            S.op("dve", lambda e, a=a, w_=w_: e.tensor_copy(out=dst_ap[:, a:a + w_], in_=scratch[:, 0:w_]),
                 reads=[dscr], writes=[ddst])

    ident = cf("ident")

    vecT = sb([128, 64], F32)
    dVec = dep("vecT")
    biasT = sb([128, 8], F32)
    dPar = dep("par")
    cT = sb([128, 8, 17], F32)
    dcT = dep("cT")
    mod = sb([128, 4, 8, 17], F32)
    dMod = dep("mod")
    p1s = contextlib.ExitStack()
    qkgB = sb([128, 1536], F32, stack=p1s)
    lbB = sb([128, 512], F32, stack=p1s)
    omlB = sb([128, 512], F32, stack=p1s)
    mixT = sb([128, 8, NTOK], BF16, side="right")
    dMixA = [dep("mixA") for _ in range(NT)]
    dMixB = [dep("mixB") for _ in range(NT)]

    def tile_rows(i):
        return 128 if i < 16 else 64

    def tile_cols(i):
        return (i * 128, i * 128 + tile_rows(i))

    def rsqrt_small(src_ap, dst_ap, n_scale, deps_r, dep_w, tmp_ap):
        S.op("dve", lambda e: e.tensor_scalar(out=tmp_ap, in0=src_ap, scalar1=n_scale, scalar2=EPS,
                                              op0=ALU.mult, op1=ALU.add), reads=deps_r, writes=[dep_w])
        S.op("act", lambda e: e.activation(out=tmp_ap, in_=tmp_ap, func=AF.Ln), reads=[dep_w], writes=[dep_w])
        S.op("act", lambda e: e.activation(out=dst_ap, in_=tmp_ap, func=AF.Exp, scale=-0.5),
             reads=[dep_w], writes=[dep_w])

    stop = int(os.environ.get("KSTOP", "99"))

    def chk(n):
        if stop == n:
            raise _Stop()

    try:
        with contextlib.ExitStack() as p0:
            wst = [sb([128, 8, 512], F32, stack=p0) for _ in range(2)]
            dW = [dep("wst") for _ in range(2)]
            cct = sb([17, D], F32, stack=p0)
            dcc = dep("cc")
            adaT = sb([128, 48, 17], F32, stack=p0)
            dAda = dep("adaT")
            vrow = sb([64, 128], F32, stack=p0)
            lraw = sb([128, 1024], F32, stack=p0)
            dtmp = dep("p0tmp")

            S.dma("sp", lambda e: e.dma_start(out=cct[:], in_=cc[:, :]), dcc, "in")
            S.dma("sp", lambda e: e.dma_start(out=vrow[:], in_=vecs[:, :]), dtmp, "in")
            S.dma("sp", lambda e: e.dma_start(out=qkgB[:], in_=qkg[0:1, :].to_broadcast([128, 1536])), dPar, "in")
            S.dma("sp", lambda e: e.dma_start(out=biasT[:], in_=sbb[0:1, :].to_broadcast([128, 8])), dPar, "in")
            S.dma("sp", lambda e: e.dma_start(out=lraw[:], in_=lbl[0:1, :].to_broadcast([128, 1024])), dtmp, "in")
            S.op("dve", lambda e: e.tensor_tensor(out=lraw[:, 0:512], in0=lraw[:, 0:512], in1=lraw[:, 512:1024],
                                                  op=ALU.subtract), reads=[dtmp], writes=[dtmp])
            S.op("act", lambda e: e.activation(out=lbB[:], in_=lraw[:, 0:512], func=AF.Sigmoid),
                 reads=[dtmp], writes=[dPar])
            S.op("dve", lambda e: e.tensor_scalar(out=omlB[:], in0=lbB[:], scalar1=-1.0, scalar2=1.0,
                                                  op0=ALU.mult, op1=ALU.add), reads=[dPar], writes=[dPar])
            S.op("act", lambda e: e.activation(out=cct[:], in_=cct[:], func=AF.Silu), reads=[dcc], writes=[dcc])
            for c in range(8):
                S.op("pe", lambda e, c=c: e.transpose(out=PS[0][:, c * 17:(c + 1) * 17],
                                                      in_=cct[0:17, c * 128:(c + 1) * 128], identity=cf("ident", 17, 0, 17)),
                     reads=[dcc, dCF], writes=[PD[0]], inc=(c == 7))
            S.op("dve", lambda e: e.tensor_copy(out=cT[:].rearrange("p c s -> p (c s)"), in_=PS[0][:, 0:136]),
                 reads=[PD[0]], writes=[dcT])
            S.op("pe", lambda e: e.transpose(out=PS[1][:, 0:64], in_=vrow[0:64, :], identity=cf("ident", 64, 0, 64)),
                 reads=[dtmp, dCF], writes=[PD[1]])
            S.op("dve", lambda e: e.tensor_copy(out=vecT[:], in_=PS[1][:, 0:64]), reads=[PD[1]], writes=[dVec])

            w_ada_v = w_ada.rearrange("(k p) n -> p k n", p=128)
            pcs = [0, 1, 2, 3, 6, 7, 8, 9]
            for n_, pc in enumerate(pcs):
                buf = n_ % 2
                S.dma("sp", lambda e, pc=pc, buf=buf: e.dma_start(out=wst[buf][:], in_=w_ada_v[:, :, pc * 512:(pc + 1) * 512]),
                      dW[buf], "in")
                for q in range(4):
                    chunk = (pc * 512) // 128 + q
                    bank = 2 + (q % 2)
                    for k in range(8):
                        S.op("pe", lambda e, k=k, q=q, bank=bank, buf=buf: e.matmul(
                            out=PS[bank][:, 0:17], lhsT=wst[buf][:, k, q * 128:(q + 1) * 128], rhs=cT[:, k, :],
                            start=(k == 0), stop=(k == 7)),
                             reads=[dcT, dW[buf]], writes=[PD[bank]], inc=(k == 7))
                    S.op("dve", lambda e, chunk=chunk, bank=bank: e.tensor_scalar(
                        out=adaT[:, chunk, :], in0=PS[bank][:, 0:17], scalar1=vecT[:, chunk:chunk + 1], scalar2=None,
                        op0=ALU.add), reads=[PD[bank], dVec], writes=[dAda])
            for (mi, scc, shc, nof) in ((0, 8, 0, 48), (2, 32, 24, 56)):
                S.op("dve", lambda e, mi=mi, scc=scc, nof=nof: e.scalar_tensor_tensor(
                    out=mod[:, mi, :, :], in0=adaT[:, scc:scc + 8, :], scalar=1.0,
                    in1=vecT[:, nof:nof + 8].unsqueeze(2).to_broadcast([128, 8, 17]), op0=ALU.add, op1=ALU.mult),
                     reads=[dAda, dVec], writes=[dMod])
                S.op("dve", lambda e, mi=mi, shc=shc: e.tensor_copy(out=mod[:, mi + 1, :, :], in_=adaT[:, shc:shc + 8, :]),
                     reads=[dAda], writes=[dMod])
            S.barrier()
            chk(0)

        def load_x_tile(i, xt, dxt, q="sp"):
            T = tile_rows(i)
            src = xp[i * 128:(i + 1) * 128, :] if i < 16 else xs_d[:, :]
            S.dma(q, lambda e: e.dma_start(out=xt[0:T, :], in_=src), dxt, "in")

        def norm_transpose(i, src_ap, dsrc, mi, hT_ap, dhT, work, dwork, banks):
            T = tile_rows(i)
            xn, sqj, st4 = work
            S.op("act", lambda e: e.activation(out=sqj[0:T, :], in_=src_ap, func=AF.Square, accum_out=st4[0:T, 0:1]),
                 reads=[dsrc], writes=[dwork])
            rsqrt_small(st4[0:T, 0:1], st4[0:T, 2:3], 1.0 / D, [dwork], dwork, st4[0:T, 1:2])
            S.op("dve", lambda e: e.tensor_scalar(out=xn[0:T, :], in0=src_ap, scalar1=st4[0:T, 2:3], scalar2=None,
                                                  op0=ALU.mult), reads=[dsrc, dwork], writes=[dwork])
            nb, tpb = (1, 128) if i < 16 else (16, 4)
            c0 = 16 if i < 16 else 0
            for half in range(2):
                bank = banks[half]
                for c4 in range(4):
                    c = half * 4 + c4
                    S.op("pe", lambda e, c=c, c4=c4, bank=bank: e.transpose(
                        out=PS[bank][:, c4 * 128:c4 * 128 + T], in_=xn[0:T, c * 128:(c + 1) * 128],
                        identity=cf("ident", T, 0, T)), reads=[dwork, dCF], writes=[PD[bank]], inc=(c4 == 3))
                pv = PS[bank][:, :].rearrange("p (c t) -> p c t", c=4)[:, :, 0:T].rearrange("p c (b t) -> p c b t", t=tpb)
                sc = mod[:, mi, half * 4:half * 4 + 4, c0:c0 + nb].unsqueeze(3).to_broadcast([128, 4, nb, tpb])
                sh = mod[:, mi + 1, half * 4:half * 4 + 4, c0:c0 + nb].unsqueeze(3).to_broadcast([128, 4, nb, tpb])
                tmpm = xn[:, half * 512:(half + 1) * 512].rearrange("p (c t) -> p c t", c=4)[:, :, 0:T].rearrange(
                    "p c (b t) -> p c b t", t=tpb)
                tm = sqj[:, half * 512:(half + 1) * 512].rearrange("p (c t) -> p c t", c=4)[:, :, 0:T].rearrange(
                    "p c (b t) -> p c b t", t=tpb)
                S.op("dve", lambda e, pv=pv, sc=sc, tm=tm: e.tensor_tensor(out=tm, in0=pv, in1=sc, op=ALU.mult),
                     reads=[PD[bank], dMod], writes=[dwork])
                ho = hT_ap[:, half * 4:half * 4 + 4, :].rearrange("p c (b t) -> p c b t", t=tpb)
                S.op("dve", lambda e, ho=ho, sh=sh, tm=tm: e.tensor_tensor(out=ho, in0=tm, in1=sh, op=ALU.add),
                     reads=[dwork, dMod], writes=[dhT])

        def load_weight_piece(src_ap, dst_bf_ap, wst, dW, buf, ddst, q="sp", cast_eng="pool"):
            S.dma(q, lambda e: e.dma_start(out=wst[buf][:], in_=src_ap), dW[buf], "in")
            S.op(cast_eng, lambda e: e.tensor_copy(out=dst_bf_ap, in_=wst[buf][:]), reads=[dW[buf]], writes=[ddst])

        with contextlib.ExitStack() as p1:
            hT = sb([128, 8, NTOK], BF16, stack=p1)
            dhT = [dep("hT") for _ in range(NT)]
            with contextlib.ExitStack() as p1n:
                xt = [sb([128, D], F32, stack=p1n) for _ in range(2)]
                dxt = [dep("xt") for _ in range(2)]
                wk = [(sb([128, D], F32, stack=p1n), sb([128, D], F32, stack=p1n), sb([128, 4], F32, stack=p1n)) for _ in range(2)]
                dwk = [dep("wk") for _ in range(2)]
                for i in [int(x) for x in os.environ["KTILES"].split(",")] if "KTILES" in os.environ else range(NT):
                    b = i % 2
                    load_x_tile(i, xt[b], dxt[b])
                    c0, c1 = tile_cols(i)
                    norm_transpose(i, xt[b][0:tile_rows(i), :], dxt[b], 0, hT[:, :, c0:c1], dhT[i], wk[b], dwk[b],
                                   (0, 1) if b == 0 else (2, 3))
                S.barrier()
                chk(1)

            w_in_v = w_in.rearrange("(k p) n -> p k n", p=128)

            with contextlib.ExitStack() as pb:
                whg = sb([128, 8, 2048], BF16, stack=pb)
                dwhg = [dep("whg") for _ in range(4)]
                with contextlib.ExitStack() as pw:
                    wst = [sb([128, 8, 512], F32, stack=pw) for _ in range(2)]
                    dW = [dep("wst") for _ in range(2)]
                    for g in range(4):
                        load_weight_piece(w_in_v[:, :, 1536 + g * 512:1536 + (g + 1) * 512], whg[:, :, g * 512:(g + 1) * 512],
                                          wst, dW, g % 2, dwhg[g])
                    S.barrier()

                def wt(shape, dt=F32):
                    return sb(shape, dt, stack=pb)

                AB = (4, 2)
                XB = (6, 3)

                def hsel(ap, h2):
                    return ap.rearrange("p (j a t) -> p j a t", j=4, a=2)[:, :, h2, :]

                hq = wt([128, 512]); ff = wt([128, 512]); logf = wt([128, 512]); omf = wt([128, 512])
                hv = wt([128, 512], BF16); sg = wt([128, 512]); eb = wt([128, 512]); enb = wt([128, 512])
                ec = wt([128, 512]); kk = wt([128, 512], BF16)
                qdT = wt([128, 4, 128], BF16); kdT = wt([128, 4, 128], BF16)
                attm = wt([128, 512], BF16); oo = wt([128, 512]); osq = wt([128, 512])
                st8 = wt([128, 24]); dec = wt([128, 4, 16])
                Sst = wt([128, 4, 64]); Sbf = wt([128, 4, 64], BF16)
                dE = dep("hgE")
                dT_ = dep("hgT")
                dA = dep("attm"); dO = dep("oo"); dS_ = dep("S"); dSb = dep("Sbf"); dDec = dep("dec")
                S.op("dve", lambda e: e.memset(Sst[:], 0.0), writes=[dS_])
                S.op("dve", lambda e: e.memset(Sbf[:], 0.0), writes=[dSb])
                S0t = [wt([128, 16, 64]) for _ in range(2)]
                S0bt = wt([128, 16, 64], BF16)
                qdTm = wt([128, 4, 16, 64], BF16); hvm = wt([64, 16, 128], BF16)
                smT = wt([128, 1024], BF16)
                dS0 = [dep("S0") for _ in range(2)]; dS0b = dep("S0b"); dqm = dep("qdTm"); dhvm = dep("hvm"); dsm = dep("smT")
                st0_v = st0.rearrange("b (j h) k v -> (h k) j b v", h=2)
                ss_v = ss_o.rearrange("b (j h) k v -> (h k) j b v", h=2)
                load_bf_const("seqmaskT", smT, dsm, hq, dE)

                for i in range(NT):
                    T = tile_rows(i)
                    c0, c1 = tile_cols(i)
                    samp = (i == 16)
                    tri = cf("tri4" if samp else "tri2", T, 0, T)
                    tsu = cf("tsu4" if samp else "tsu2", T, 0, T)
                    ncn = 16 if samp else 2
                    ind = cf("seqind" if samp else "chunkind", T)
                    def proj(g, bank):
                        for k in range(8):
                            S.op("pe", lambda e, k=k: e.matmul(out=PS[bank][0:T, :], lhsT=hT[:, k, c0:c1],
                                                               rhs=whg[:, k, g * 512:(g + 1) * 512],
                                                               start=(k == 0), stop=(k == 7)),
                                 reads=[dhT[i], dwhg[g]], writes=[PD[bank]], inc=(k == 7))
                    proj(0, 2)
                    proj(1, 3)
                    proj(3, 0)
                    proj(2, 1)
                    S.op("act", lambda e: e.activation(out=hq[0:T, :], in_=PS[2][0:T, :], func=AF.Silu),
                         reads=[PD[2]], writes=[dE])
                    S.op("act", lambda e: e.activation(out=sg[0:T, :], in_=PS[0][0:T, :], func=AF.Silu),
                         reads=[PD[0]], writes=[dE])
                    S.op("act", lambda e: e.activation(out=ff[0:T, :], in_=PS[3][0:T, :], func=AF.Sigmoid),
                         reads=[PD[3]], writes=[dE])
                    S.op("dve", lambda e: e.tensor_tensor(out=ff[0:T, :], in0=ff[0:T, :], in1=omlB[0:T, :], op=ALU.mult),
                         reads=[dE, dPar], writes=[dE])
                    S.op("dve", lambda e: e.tensor_tensor(out=ff[0:T, :], in0=ff[0:T, :], in1=lbB[0:T, :], op=ALU.add),
                         reads=[dE, dPar], writes=[dE])
                    S.op("act", lambda e: e.activation(out=hv[0:T, :], in_=PS[1][0:T, :], func=AF.Copy),
                         reads=[PD[1]], writes=[dE])
                    S.op("act", lambda e: e.activation(out=logf[0:T, :], in_=ff[0:T, :], func=AF.Ln),
                         reads=[dE], writes=[dE])
                    S.op("dve", lambda e: e.tensor_scalar(out=omf[0:T, :], in0=ff[0:T, :], scalar1=-1.0, scalar2=1.0,
                                                          op0=ALU.mult, op1=ALU.add), reads=[dE], writes=[dE])
                    S.op("pe", lambda e: e.matmul(out=PS[0][0:T, :], lhsT=tri, rhs=logf[0:T, :], start=True, stop=True),
                         reads=[dE, dCF], writes=[PD[0]])
                    S.op("pe", lambda e: e.matmul(out=PS[1][0:T, :], lhsT=tsu, rhs=logf[0:T, :], start=True, stop=True),
                         reads=[dE, dCF], writes=[PD[1]])
                    for j in range(4):
                        S.op("pe", lambda e, j=j: e.matmul(out=PS[7][:, 256 + j * ncn:256 + (j + 1) * ncn],
                                                           lhsT=logf[0:T, j * 128:(j + 1) * 128], rhs=ind,
                                                           start=True, stop=True),
                             reads=[dE, dCF], writes=[PD[7]], inc=(j == 3))
                    S.op("act", lambda e: e.activation(out=eb[0:T, :], in_=PS[0][0:T, :], func=AF.Exp),
                         reads=[PD[0]], writes=[dE])
                    S.op("act", lambda e: e.activation(out=enb[0:T, :], in_=PS[0][0:T, :], func=AF.Exp, scale=-1.0),
                         reads=[PD[0]], writes=[dE])
                    S.op("act", lambda e: e.activation(out=ec[0:T, :], in_=PS[1][0:T, :], func=AF.Exp),
                         reads=[PD[1]], writes=[dE])
                    S.op("act", lambda e: e.activation(out=dec[:, :, 0:ncn],
                                                       in_=PS[7][:, 256:256 + 4 * ncn].rearrange("p (j c) -> p j c", j=4),
                                                       func=AF.Exp), reads=[PD[7]], writes=[dDec])
                    S.op("dve", lambda e: e.tensor_tensor(out=eb[0:T, :], in0=hq[0:T, :], in1=eb[0:T, :], op=ALU.mult),
                         reads=[dE], writes=[dE])
                    S.op("dve", lambda e: e.tensor_tensor(out=enb[0:T, :], in0=omf[0:T, :], in1=enb[0:T, :], op=ALU.mult),
                         reads=[dE], writes=[dE])
                    S.op("dve", lambda e: e.tensor_tensor(out=kk[0:T, :], in0=omf[0:T, :], in1=ec[0:T, :], op=ALU.mult),
                         reads=[dE], writes=[dE])
                    for (src, dst, bank) in ((eb, qdT, 0), (enb, kdT, 1)):
                        for j in range(4):
                            S.op("pe", lambda e, j=j, src=src, bank=bank: e.transpose(
                                out=PS[bank][:, j * 128:j * 128 + T], in_=src[0:T, j * 128:(j + 1) * 128],
                                identity=cf("ident", T, 0, T)), reads=[dE, dCF], writes=[PD[bank]], inc=(j == 3))
                        S.op("act", lambda e, dst=dst, bank=bank: e.activation(
                            out=dst[:, :, 0:T], in_=PS[bank][:, :].rearrange("p (j t) -> p j t", j=4)[:, :, 0:T],
                            func=AF.Copy), reads=[PD[bank]], writes=[dT_])

                    if not samp:
                        for c in range(2):
                            cp = 64 * c
                            for h in range(8):
                                j, h2 = h // 2, h % 2
                                hp = 64 * h2
                                ab = AB[h2]
                                S.op("pe", lambda e, h=h, j=j, hp=hp, cp=cp, ab=ab: e.matmul(
                                    out=PS[ab][cp:cp + 64, h * 64:(h + 1) * 64], lhsT=kdT[hp:hp + 64, j, cp:cp + 64],
                                    rhs=qdT[hp:hp + 64, j, cp:cp + 64], start=True, stop=True),
                                     reads=[dT_], writes=[PD[ab]], inc=(h >= 6))
                            for h2 in range(2):
                                ab = AB[h2]
                                S.op("dve", lambda e, cp=cp, ab=ab, h2=h2: e.tensor_tensor(
                                    out=hsel(attm[cp:cp + 64, :], h2), in0=hsel(PS[ab][cp:cp + 64, :], h2),
                                    in1=cf("maskH")[cp:cp + 64, :].unsqueeze(1).to_broadcast([64, 4, 64]), op=ALU.mult),
                                     reads=[PD[ab], dCF], writes=[dA])
                            for h in range(8):
                                j, h2 = h // 2, h % 2
                                hp = 64 * h2
                                hs = slice(h * 64, (h + 1) * 64)
                                S.op("pe", lambda e, hs=hs, cp=cp: e.matmul(
                                    out=PS[5][cp:cp + 64, hs], lhsT=attm[cp:cp + 64, hs], rhs=hv[cp:cp + 64, hs],
                                    start=True, stop=True), reads=[dA, dE], writes=[PD[5]], inc=False)
                                xb = XB[h2]
                                S.op("pe", lambda e, hs=hs, cp=cp, hp=hp, j=j, xb=xb: e.matmul(
                                    out=PS[xb][cp:cp + 64, hs], lhsT=qdT[hp:hp + 64, j, cp:cp + 64], rhs=Sbf[hp:hp + 64, j, :],
                                    start=True, stop=True), reads=[dT_, dSb], writes=[PD[xb]], inc=False)
                                S.op("pe", lambda e, hs=hs, cp=cp, hp=hp, j=j: e.matmul(
                                    out=PS[7][hp:hp + 64, j * 64:(j + 1) * 64], lhsT=kk[cp:cp + 64, hs], rhs=hv[cp:cp + 64, hs],
                                    start=True, stop=True), reads=[dE], writes=[PD[7]], inc=(h == 7))
                            S.op("dve", lambda e, c=c: e.tensor_tensor(
                                out=Sst[:], in0=Sst[:], in1=dec[:, :, c:c + 1].to_broadcast([128, 4, 64]), op=ALU.mult),
                                 reads=[dS_, dDec], writes=[dS_])
                            S.op("dve", lambda e: e.tensor_tensor(
                                out=Sst[:], in0=Sst[:], in1=PS[7][:, 0:256].rearrange("p (j v) -> p j v", j=4), op=ALU.add),
                                 reads=[dS_, PD[7]], writes=[dS_])
                            S.op("act", lambda e: e.activation(out=Sbf[:], in_=Sst[:], func=AF.Copy),
                                 reads=[dS_], writes=[dSb])
                        if i == 15:
                            S.dma("sp", lambda e: e.dma_start(out=sp_o.rearrange("(j h) k v -> (h k) j v", h=2), in_=Sst[:]),
                                  dS_, "out")
                    else:
                        for h in range(8):
                            j, h2 = h // 2, h % 2
                            hp = 64 * h2
                            ab = AB[h2]
                            S.op("pe", lambda e, h=h, j=j, hp=hp, ab=ab: e.matmul(
                                out=PS[ab][0:64, h * 64:(h + 1) * 64], lhsT=kdT[hp:hp + 64, j, 0:64],
                                rhs=qdT[hp:hp + 64, j, 0:64], start=True, stop=True),
                                 reads=[dT_], writes=[PD[ab]], inc=(h >= 6))
                        for h2 in range(2):
                            ab = AB[h2]
                            S.op("dve", lambda e, ab=ab, h2=h2: e.tensor_tensor(
                                out=hsel(attm[0:64, :], h2), in0=hsel(PS[ab][0:64, :], h2),
                                in1=cf("maskHS")[0:64, :].unsqueeze(1).to_broadcast([64, 4, 64]), op=ALU.mult),
                                 reads=[PD[ab], dCF], writes=[dA])
                        S.op("dve", lambda e: e.tensor_tensor(
                            out=qdTm[:], in0=qdT[:, :, 0:64].unsqueeze(2).to_broadcast([128, 4, 16, 64]),
                            in1=smT[:].rearrange("p (b t) -> p b t", b=16).unsqueeze(1).to_broadcast([128, 4, 16, 64]),
                            op=ALU.mult), reads=[dT_, dsm], writes=[dqm])
                        for h in range(8):
                            hs = slice(h * 64, (h + 1) * 64)
                            S.op("pe", lambda e, hs=hs: e.matmul(out=PS[5][0:64, hs], lhsT=attm[0:64, hs], rhs=hv[0:64, hs],
                                                                 start=True, stop=True),
                                 reads=[dA, dE], writes=[PD[5]], inc=(h == 7))
                        for j in range(4):
                            sb_ = j % 2
                            S0j = S0t[sb_]
                            S.dma("sp", lambda e, j=j, S0j=S0j: e.dma_start(out=S0j[:], in_=st0_v[:, j, :, :]), dS0[sb_], "in")
                            S.op("act", lambda e, S0j=S0j: e.activation(out=S0bt[:].rearrange("p b v -> p (b v)"),
                                                                        in_=S0j[:].rearrange("p b v -> p (b v)"), func=AF.Copy),
                                 reads=[dS0[sb_]], writes=[dS0b])
                            S.op("dve", lambda e, j=j: e.tensor_tensor(
                                out=hvm[:], in0=hv[0:64, j * 128:(j + 1) * 128].unsqueeze(1).to_broadcast([64, 16, 128]),
                                in1=cf("seqind", 64).unsqueeze(2).to_broadcast([64, 16, 128]), op=ALU.mult),
                                 reads=[dE, dCF], writes=[dhvm])
                            for h2 in range(2):
                                h = 2 * j + h2
                                hp = 64 * h2
                                hs = slice(h * 64, (h + 1) * 64)
                                for b in range(16):
                                    S.op("pe", lambda e, hs=hs, hp=hp, j=j, b=b, h2=h2: e.matmul(
                                        out=PS[XB[h2]][0:64, hs], lhsT=qdTm[hp:hp + 64, j, b, :], rhs=S0bt[hp:hp + 64, b, :],
                                        start=(b == 0), stop=(b == 15)),
                                         reads=[dqm, dS0b], writes=[PD[XB[h2]]], inc=(b == 15))
                                for half in range(2):
                                    S.op("pe", lambda e, h=h, hp=hp, half=half, h2=h2: e.matmul(
                                        out=PS[half][hp:hp + 64, :], lhsT=kk[0:64, h * 64:(h + 1) * 64],
                                        rhs=hvm[0:64, half * 8:(half + 1) * 8, h2 * 64:(h2 + 1) * 64],
                                        start=True, stop=True), reads=[dE, dhvm], writes=[PD[half]])
                            for half in range(2):
                                bs = slice(half * 8, (half + 1) * 8)
                                S.op("dve", lambda e, j=j, bs=bs, S0j=S0j: e.tensor_tensor(
                                    out=S0j[:, bs, :], in0=S0j[:, bs, :],
                                    in1=dec[:, j, bs].unsqueeze(2).to_broadcast([128, 8, 64]), op=ALU.mult),
                                     reads=[dS0[sb_], dDec], writes=[dS0[sb_]])
                                S.op("dve", lambda e, bs=bs, half=half, S0j=S0j: e.tensor_tensor(
                                    out=S0j[:, bs, :], in0=S0j[:, bs, :],
                                    in1=PS[half][:, :].rearrange("p (b v) -> p b v", b=8), op=ALU.add),
                                     reads=[dS0[sb_], PD[half]], writes=[dS0[sb_]])
                            S.dma("sp", lambda e, j=j, S0j=S0j: e.dma_start(out=ss_v[:, j, :, :], in_=S0j[:]), dS0[sb_], "out")

                    S.op("act", lambda e: e.activation(out=oo[0:T, :], in_=PS[5][0:T, :], func=AF.Copy),
                         reads=[PD[5]], writes=[dO])
                    for h2 in range(2):
                        S.op("dve", lambda e, h2=h2: e.tensor_tensor(out=hsel(oo[0:T, :], h2), in0=hsel(oo[0:T, :], h2),
                                                              in1=hsel(PS[XB[h2]][0:T, :], h2), op=ALU.add),
                             reads=[dO, PD[XB[h2]]], writes=[dO])
                    S.op("dve", lambda e: e.tensor_tensor(out=osq[0:T, :], in0=oo[0:T, :], in1=oo[0:T, :], op=ALU.mult),
                         reads=[dO], writes=[dO])
                    S.op("dve", lambda e: e.tensor_reduce(out=st8[0:T, 0:8], in_=osq[0:T, :].rearrange("p (h v) -> p h v", h=8),
                                                          axis=AX.X, op=ALU.add), reads=[dO], writes=[dO])
                    rsqrt_small(st8[0:T, 0:8], st8[0:T, 16:24], 1.0 / 64, [dO], dO, st8[0:T, 8:16])
                    S.op("dve", lambda e: e.tensor_tensor(
                        out=oo[0:T, :].rearrange("p (h v) -> p h v", h=8), in0=oo[0:T, :].rearrange("p (h v) -> p h v", h=8),
                        in1=st8[0:T, 16:24].unsqueeze(2).to_broadcast([T, 8, 64]), op=ALU.mult), reads=[dO], writes=[dO])
                    S.op("dve", lambda e: e.tensor_tensor(out=oo[0:T, :], in0=oo[0:T, :], in1=qkgB[0:T, 1024:1536], op=ALU.mult),
                         reads=[dO, dPar], writes=[dO])
                    S.op("dve", lambda e: e.tensor_tensor(out=oo[0:T, :], in0=oo[0:T, :], in1=sg[0:T, :], op=ALU.mult),
                         reads=[dO, dE], writes=[dO])
                    for j in range(4):
                        S.op("pe", lambda e, j=j: e.transpose(out=PS[4][:, j * 128:j * 128 + T], in_=oo[0:T, j * 128:(j + 1) * 128],
                                                              identity=cf("ident", T, 0, T)),
                             reads=[dO, dCF], writes=[PD[4]], inc=(j == 3))
                    S.op("act", lambda e: e.activation(out=mixT[:, 4:8, c0:c1],
                                                       in_=PS[4][:, :].rearrange("p (j t) -> p j t", j=4)[:, :, 0:T],
                                                       func=AF.Copy), reads=[PD[4]], writes=[dMixB[i]])
                S.barrier()
                chk(2)

            pa_r = contextlib.ExitStack()
            qT = sb([128, 4, NTOK], BF16, side="right", stack=pa_r)
            kT = sb([128, 4, NTOK], BF16, side="right", stack=pa_r)
            vres = sb([128, NT, 512], BF16, side="right", stack=pa_r)
            dQ = [dep("qT") for _ in range(NT)]
            dK = [dep("kT") for _ in range(NT)]
            dV = [dep("v") for _ in range(NT)]
            with contextlib.ExitStack() as pa:
                wat = sb([128, 8, 1536], BF16, stack=pa)
                dwat = [dep("wat") for _ in range(3)]
                with contextlib.ExitStack() as pw:
                    wst = [sb([128, 8, 512], F32, stack=pw) for _ in range(2)]
                    dW = [dep("wst") for _ in range(2)]
                    for g in range(3):
                        load_weight_piece(w_in_v[:, :, g * 512:(g + 1) * 512], wat[:, :, g * 512:(g + 1) * 512],
                                          wst, dW, g % 2, dwat[g])
                    S.barrier()
                sq = sb([128, 512], F32, stack=pa)
                qn = [sb([128, 512], F32, stack=pa) for _ in range(2)]
                dqn = [dep("qn") for _ in range(2)]
                kn = [sb([128, 512], F32, stack=pa) for _ in range(2)]
                dkn = [dep("kn") for _ in range(2)]
                vn = [sb([128, 512], F32, stack=pa) for _ in range(2)]
                dvn = [dep("vn") for _ in range(2)]
                s8 = sb([128, 24], F32, stack=pa)
                dsq = dep("sq")
                for i in range(NT):
                    T = tile_rows(i)
                    c0, c1 = tile_cols(i)
                    b = i % 2

                    def proj(g, bank):
                        for k in range(8):
                            S.op("pe", lambda e, k=k: e.matmul(out=PS[bank][0:T, :], lhsT=hT[:, k, c0:c1],
                                                               rhs=wat[:, k, g * 512:(g + 1) * 512],
                                                               start=(k == 0), stop=(k == 7)),
                                 reads=[dhT[i], dwat[g]], writes=[PD[bank]], inc=(k == 7))

                    def qknorm(bank, dst, ddst, goff):
                        S.op("act", lambda e: e.activation(out=sq[0:T, :], in_=PS[bank][0:T, :], func=AF.Square),
                             reads=[PD[bank]], writes=[dsq])
                        S.op("dve", lambda e: e.tensor_reduce(out=s8[0:T, 0:8], in_=sq[0:T, :].rearrange("p (h d) -> p h d", h=8),
                                                              axis=AX.X, op=ALU.add), reads=[dsq], writes=[dsq])
                        rsqrt_small(s8[0:T, 0:8], s8[0:T, 16:24], 1.0 / 64, [dsq], dsq, s8[0:T, 8:16])
                        S.op("dve", lambda e: e.tensor_tensor(
                            out=dst[0:T, :].rearrange("p (h d) -> p h d", h=8),
                            in0=PS[bank][0:T, :].rearrange("p (h d) -> p h d", h=8),
                            in1=s8[0:T, 16:24].unsqueeze(2).to_broadcast([T, 8, 64]), op=ALU.mult),
                             reads=[PD[bank], dsq], writes=[ddst])
                        S.op("dve", lambda e: e.tensor_tensor(out=dst[0:T, :], in0=dst[0:T, :], in1=qkgB[0:T, goff:goff + 512],
                                                              op=ALU.mult), reads=[ddst, dPar], writes=[ddst])

                    def to_featT(src, dsrc, bank, dst, ddst):
                        for j in range(4):
                            S.op("pe", lambda e, j=j: e.transpose(out=PS[bank][:, j * 128:j * 128 + T],
                                                                  in_=src[0:T, j * 128:(j + 1) * 128],
                                                                  identity=cf("ident", T, 0, T)),
                                 reads=[dsrc, dCF], writes=[PD[bank]], inc=(j == 3))
                        S.op("act", lambda e: e.activation(out=dst[:, :, c0:c1],
                                                           in_=PS[bank][:, :].rearrange("p (j t) -> p j t", j=4)[:, :, 0:T],
                                                           func=AF.Copy), reads=[PD[bank]], writes=[ddst])

                    proj(0, 0)
                    proj(1, 1)
                    proj(2, 2)
                    qknorm(0, qn[b], dqn[b], 0)
                    S.op("act", lambda e: e.activation(out=vn[b][0:T, :], in_=PS[2][0:T, :], func=AF.Copy),
                         reads=[PD[2]], writes=[dvn[b]])
                    qknorm(1, kn[b], dkn[b], 512)
                    to_featT(qn[b], dqn[b], 3, qT, dQ[i])
                    to_featT(kn[b], dkn[b], 4, kT, dK[i])
                    kdst = kp[i * 128:(i + 1) * 128, :] if i < 16 else ks[:, :]
                    S.dma("sp", lambda e: e.dma_start(out=kdst, in_=kn[b][0:T, :]), dkn[b], "out")
                    S.op("dve", lambda e: e.tensor_copy(out=vres[0:T, i, :], in_=vn[b][0:T, :]),
                         reads=[dvn[b]], writes=[dV[i]])
                    vdst = vp[i * 128:(i + 1) * 128, :] if i < 16 else vs[:, :]
                    S.dma("sp", lambda e: e.dma_start(out=vdst, in_=vn[b][0:T, :]), dvn[b], "out")
                S.barrier()
                chk(3)
        p1s.close()

        with contextlib.ExitStack() as p2:
            ulT = sb([128, 256], BF16, stack=p2)
            dCB = dep("cb")
            uincl = ulT[:, 0:128]
            lstrict = ulT[:, 128:256]
            qblk = sb([128, 4, 16, 2, 4], BF16, stack=p2)
            kTp = sb([128, 4, 128], BF16, stack=p2)
            vpad = sb([128, 512], BF16, stack=p2)
            ebF = sb([128, 512], F32, stack=p2)
            mS2 = sb([128, 128], F32, stack=p2)
            dpad = dep("pad")
            with contextlib.ExitStack() as p2a:
                NSET = 8
                eT = [sb([128, 512], F32, stack=p2a) for _ in range(NSET)]
                xT_ = [sb([128, 512], F32, stack=p2a) for _ in range(NSET)]
                LpT = [sb([128, 512], BF16, stack=p2a) for _ in range(NSET)]
                wT = [sb([128, 512], BF16, stack=p2a) for _ in range(NSET)]
                de = [dep("e") for _ in range(NSET)]
                dx = [dep("x") for _ in range(NSET)]
                dL = [dep("L") for _ in range(NSET)]
                dw = [dep("w") for _ in range(NSET)]
                mDT = sb([128, 2048], BF16, stack=p2a)
                load_bf_const("uincl", ulT[:, 0:128], dCB, eT[0], de[0])
                load_bf_const("lstrict", ulT[:, 128:256], dCB, eT[1], de[1])
                load_bf_const("maskD", mDT, dCB, eT[0], de[0])
                def stream_units(qbs):
                    out = []
                    for j in range(4):
                        for QB in qbs:
                            nkb = 4 * QB + 4
                            for st, kb in enumerate(range(nkb - 1, -1, -1)):
                                for h2 in range(2):
                                    out.append((j, QB, st, kb, h2))
                    return out
                ua, ub = stream_units([3, 0]), stream_units([2, 1])
                units = []
                for n_ in range(max(len(ua), len(ub))):
                    if n_ < len(ua):
                        units.append(ua[n_] + (0,))
                    if n_ < len(ub):
                        units.append(ub[n_] + (1,))

                def geom(u):
                    j, QB, st, kb, h2, sid = u
                    ii = kb - 4 * QB
                    clo = 128 * ii if ii > 0 else 0
                    return ii, clo, slice(clo, 512), slice(QB * 512 + clo, (QB + 1) * 512)

                def parms(n):
                    j, QB, st, kb, h2, sid = units[n]
                    ii, clo, cs, qs = geom(units[n])
                    return j, QB, st, kb, h2, sid, ii, clo, cs, qs, 2 * j + h2, 64 * h2, n % NSET

                def a1(n):
                    j, QB, st, kb, h2, sid, ii, clo, cs, qs, h, hp, si = parms(n)
                    qdeps = [dQ[t] for t in range(QB * 4, QB * 4 + 4)]
                    S.op("pe", lambda e: e.matmul(
                        out=PS[h2][:, cs], lhsT=kT[hp:hp + 64, j, kb * 128:(kb + 1) * 128], rhs=qT[hp:hp + 64, j, qs],
                        start=True, stop=True), reads=[dK[kb]] + qdeps, writes=[PD[h2]])

                def a2(n):
                    j, QB, st, kb, h2, sid, ii, clo, cs, qs, h, hp, si = parms(n)
                    S.op("act", lambda e: e.activation(
                        out=eT[si][:, cs], in_=PS[h2][:, cs], func=AF.Exp, scale=SCALE, bias=biasT[:, h:h + 1]),
                         reads=[PD[h2], dPar], writes=[de[si]])
                    if ii >= 0:
                        S.op("dve", lambda e: e.tensor_tensor(
                            out=eT[si][:, cs], in0=eT[si][:, cs], in1=mDT[:, ii * 512 + clo:(ii + 1) * 512],
                            op=ALU.mult), reads=[de[si], dCB], writes=[de[si]])
                    S.op("act", lambda e: e.activation(out=LpT[si][:, cs], in_=eT[si][:, cs], func=AF.Ln, bias=1.0),
                         reads=[de[si]], writes=[dL[si]])

                def cbank(sid, h2):
                    return (2 + h2) if sid == 0 else (5 + h2)

                def b1(n):
                    j, QB, st, kb, h2, sid, ii, clo, cs, qs, h, hp, si = parms(n)
                    Cb = cbank(sid, h2)
                    S.op("pe", lambda e: e.matmul(
                        out=PS[Cb][:, cs], lhsT=uincl, rhs=LpT[si][:, cs], start=(st == 0), stop=False,
                        skip_group_check=True), reads=[dL[si], dCB], writes=[PD[Cb]])
                    S.op("act", lambda e: e.activation(out=xT_[si][:, cs], in_=PS[Cb][:, cs], func=AF.Exp, scale=-1.0),
                         reads=[PD[Cb]], writes=[dx[si]])

                def b2(n):
                    j, QB, st, kb, h2, sid, ii, clo, cs, qs, h, hp, si = parms(n)
                    Cb = cbank(sid, h2)
                    if kb > 0:
                        S.op("pe", lambda e: e.matmul(
                            out=PS[Cb][:, cs], lhsT=lstrict, rhs=LpT[si][:, cs], start=False, stop=(kb == 1),
                            skip_group_check=True), reads=[dL[si], dCB], writes=[PD[Cb]])
                    S.op("dve", lambda e: e.tensor_tensor(out=wT[si][:, cs], in0=eT[si][:, cs], in1=xT_[si][:, cs], op=ALU.mult),
                         reads=[de[si], dx[si]], writes=[dw[si]])

                def b3(n):
                    j, QB, st, kb, h2, sid, ii, clo, cs, qs, h, hp, si = parms(n)
                    ob = 4 if sid == 0 else 7
                    S.op("pe", lambda e: e.matmul(
                        out=PS[ob][hp:hp + 64, cs], lhsT=vres[:, kb, h * 64:(h + 1) * 64], rhs=wT[si][:, cs],
                        start=(st == 0), stop=(kb == 0), skip_group_check=True),
                         reads=[dw[si], dV[kb]], writes=[PD[ob]])
                    if kb == 0 and h2 == 1:
                        S.op("act", lambda e: e.activation(out=mixT[:, j, QB * 512:(QB + 1) * 512], in_=PS[ob][:, :],
                                                           func=AF.Copy),
                             reads=[PD[ob]], writes=[dMixA[QB * 4 + tt] for tt in range(4)])

                a1(0)
                a2(0)
                for n in range(len(units)):
                    b1(n)
                    if n + 1 < len(units):
                        a1(n + 1)
                    b2(n)
                    if n + 1 < len(units):
                        a2(n + 1)
                    b3(n)
                S.op("dve", lambda e: e.memset(kTp[:], 0.0), writes=[dpad])
                S.op("dve", lambda e: e.memset(vpad[:], 0.0), writes=[dpad])
                S.op("dve", lambda e: e.memset(qblk[:].rearrange("p j b a t -> p (j b a t)"), 0.0), writes=[dpad])
                S.op("dve", lambda e: e.tensor_copy(out=kTp[:, :, 0:64], in_=kT[:, :, TP:NTOK]), reads=[dK[16], dpad], writes=[dpad])
                S.op("dve", lambda e: e.tensor_copy(out=vpad[0:64, :], in_=vres[0:64, 16, :]), reads=[dV[16], dpad], writes=[dpad])
                for h2 in range(2):
                    S.op("dve", lambda e, h2=h2: e.tensor_copy(
                        out=qblk[64 * h2:64 * h2 + 64, :, :, h2, :],
                        in_=qT[64 * h2:64 * h2 + 64, :, TP:NTOK].rearrange("p j (b t) -> p j b t", t=4)),
                         reads=[dQ[16], dpad], writes=[dpad])
                S.op("act", lambda e: e.activation(out=eT[0][:, 0:8], in_=biasT[:, :], func=AF.Exp), reads=[dPar, de[0]],
                     writes=[de[0]])
                for j in range(4):
                    S.op("dve", lambda e, j=j: e.tensor_copy(
                        out=ebF[:, j * 128:(j + 1) * 128].rearrange("p (b a t) -> p b a t", b=16, a=2),
                        in_=eT[0][:, 2 * j:2 * j + 2].unsqueeze(1).unsqueeze(3).to_broadcast([128, 16, 2, 4])),
                         reads=[de[0]], writes=[dpad])
                S.op("dve", lambda e: e.tensor_copy(
                    out=mS2[:].rearrange("p (b a t) -> p b a t", b=16, a=2),
                    in_=cf("maskS").rearrange("p (b t) -> p b t", t=4).unsqueeze(2).to_broadcast([128, 16, 2, 4])),
                     reads=[dCF], writes=[dpad])
                S.barrier()
            chk(4)
            pa_r.close()

            NQ = 8
            GW = NQ * 8
            Kst = [sb([128, NQ, 512], F32, stack=p2) for _ in range(2)]
            Vst = [sb([128, NQ, 512], F32, stack=p2) for _ in range(2)]
            KTs = [sb([128, NQ, 4, 128], BF16, stack=p2) for _ in range(2)]
            Vb = [sb([128, NQ, 512], BF16, stack=p2) for _ in range(2)]
            dVb = [dep("Vb") for _ in range(2)]
            dKst = [dep("Kst") for _ in range(2)]
            dVst = [dep("Vst") for _ in range(2)]
            dKTs = [dep("KTs") for _ in range(2)]
            ptb = sb([128, NSEQ * NPG], I32, stack=p2)
            idx = sb([128, NSEQ * NPG], I32, stack=p2)
            iop = sb([128, 1], I32, stack=p2)
            dIdx = dep("idx")
            es_ = [sb([128, 4, 128], F32, stack=p2) for _ in range(2)]
            xs2 = [sb([128, 4, 128], F32, stack=p2) for _ in range(2)]
            Ls = [sb([128, 4, 128], BF16, stack=p2) for _ in range(2)]
            ws = [sb([128, 4, 128], BF16, stack=p2) for _ in range(2)]
            wsb = sb([128, 4, 128], BF16, stack=p2)
            des = [dep("es") for _ in range(2)]
            dxs = [dep("xs") for _ in range(2)]
            dLs = [dep("Ls") for _ in range(2)]
            dws = [dep("ws") for _ in range(2)]

            S.dma("pool", lambda e: e.dma_start(out=ptb[:], in_=pt[0:1, :].to_broadcast([128, NSEQ * NPG])), dIdx, "in")
            S.op("pool", lambda e: e.iota(iop[:], pattern=[[0, 1]], base=0, channel_multiplier=1), writes=[dIdx])
            S.op("pool", lambda e: e.tensor_scalar(out=idx[:], in0=ptb[:], scalar1=128, scalar2=None, op0=ALU.mult),
                 reads=[dIdx], writes=[dIdx])
            S.op("pool", lambda e: e.tensor_tensor(out=idx[:], in0=idx[:], in1=iop[:].to_broadcast([128, NSEQ * NPG]),
                                                   op=ALU.add), reads=[dIdx], writes=[dIdx])

            dZ, dC, dOs = dep("Z"), dep("C"), dep("Os")

            def bview(bank, c0, w_):
                return PS[bank][:, :].rearrange("p (j c) -> p j c", j=4)[:, :, c0:c0 + w_]

            def sample_step(kind, gi=None, buf=None, last=False, first=False, b2=0):
                c0, w_ = (0, 128) if kind == "new" else (gi * GW, GW)
                if kind == "new":
                    for j in range(4):
                        S.op("pe", lambda e, j=j: e.matmul(out=PS[0][:, j * 128:(j + 1) * 128], lhsT=kTp[:, j, :],
                                                           rhs=qblk[:, j, :, :, :].rearrange("p b a t -> p (b a t)"),
                                                           start=True, stop=True), reads=[dpad], writes=[dZ], inc=(j == 3))
                else:
                    for bi in range(NQ):
                        b = gi * NQ + bi
                        for j in range(4):
                            S.op("pe", lambda e, bi=bi, b=b, j=j: e.matmul(
                                out=PS[0][:, j * 128 + b * 8:j * 128 + b * 8 + 8], lhsT=KTs[buf][:, bi, j, :],
                                rhs=qblk[:, j, b, :, :].rearrange("p a t -> p (a t)"), start=True, stop=True),
                                 reads=[dKTs[buf], dpad], writes=[dZ], inc=(bi == NQ - 1 and j == 3))
                ev = es_[b2][:, :, 0:w_]
                S.op("act", lambda e: e.activation(out=ev, in_=bview(0, c0, w_), func=AF.Exp, scale=SCALE),
                     reads=[dZ], writes=[des[b2]])
                S.op("dve", lambda e: e.tensor_tensor(out=ev, in0=ev,
                                                      in1=ebF[:, :].rearrange("p (j c) -> p j c", j=4)[:, :, c0:c0 + w_],
                                                      op=ALU.mult), reads=[des[b2], dpad], writes=[des[b2]])
                if kind == "new":
                    S.op("dve", lambda e: e.tensor_tensor(out=ev, in0=ev, in1=mS2[:, :].unsqueeze(1).to_broadcast([128, 4, 128]),
                                                          op=ALU.mult), reads=[des[b2], dpad], writes=[des[b2]])
                S.op("act", lambda e: e.activation(out=Ls[b2][:, :, 0:w_], in_=ev, func=AF.Ln, bias=1.0),
                     reads=[des[b2]], writes=[dLs[b2]])
                for j in range(4):
                    S.op("pe", lambda e, j=j: e.matmul(out=PS[1][:, j * 128 + c0:j * 128 + c0 + w_], lhsT=uincl,
                                                       rhs=Ls[b2][:, j, 0:w_], start=(first and j == 0), stop=False,
                                                       skip_group_check=True),
                         reads=[dLs[b2], dCB], writes=[dC], inc=(j == 3))
                S.op("act", lambda e: e.activation(out=xs2[b2][:, :, 0:w_], in_=bview(1, c0, w_), func=AF.Exp, scale=-1.0),
                     reads=[dC], writes=[dxs[b2]])
                if not last:
                    for j in range(4):
                        S.op("pe", lambda e, j=j: e.matmul(out=PS[1][:, j * 128 + c0:j * 128 + c0 + w_], lhsT=lstrict,
                                                           rhs=Ls[b2][:, j, 0:w_], start=False, stop=False,
                                                           skip_group_check=True),
                             reads=[dLs[b2], dCB], writes=[dC], inc=(j == 3))
                if kind == "new":
                    S.op("dve", lambda e: e.tensor_tensor(out=wsb[:, :, :], in0=ev, in1=xs2[b2][:, :, 0:w_], op=ALU.mult),
                         reads=[des[b2], dxs[b2]], writes=[dws[b2]])
                    for j in range(4):
                        S.op("pe", lambda e, j=j: e.matmul(out=PS[2][:, j * 128:(j + 1) * 128],
                                                           lhsT=vpad[:, j * 128:(j + 1) * 128], rhs=wsb[:, j, :],
                                                           start=(j == 0), stop=False, skip_group_check=True),
                             reads=[dws[b2], dpad], writes=[dOs], inc=(j == 3))
                else:
                    S.op("dve", lambda e: e.tensor_tensor(out=ws[b2][:, :, 0:w_], in0=ev, in1=xs2[b2][:, :, 0:w_], op=ALU.mult),
                         reads=[des[b2], dxs[b2]], writes=[dws[b2]])
                    for bi in range(NQ):
                        b = gi * NQ + bi
                        for j in range(4):
                            S.op("pe", lambda e, bi=bi, b=b, j=j: e.matmul(
                                out=PS[2][:, j * 128 + b * 8:j * 128 + b * 8 + 8], lhsT=Vb[buf][:, bi, j * 128:(j + 1) * 128],
                                rhs=ws[b2][:, j, bi * 8:bi * 8 + 8], start=False, stop=False, skip_group_check=True),
                                 reads=[dws[b2], dVb[buf]], writes=[dOs], inc=(bi == NQ - 1 and j == 3))

            sample_step("new", first=True)
            gcount = 0
            for p in range(NPG - 1, -1, -1):
                for gi in range(NSEQ // NQ):
                    buf = gcount % 2
                    gcount += 1
                    for bi in range(NQ):
                        b = gi * NQ + bi
                        col = b * NPG + p
                        S.dma("pool", lambda e, bi=bi, col=col, buf=buf: e.indirect_dma_start(
                            out=Kst[buf][:, bi, :], out_offset=None, in_=ck,
                            in_offset=bass.IndirectOffsetOnAxis(ap=idx[:, col:col + 1], axis=0)),
                              dKst[buf], "in", extra_reads=[dIdx])
                        S.dma("pool", lambda e, bi=bi, col=col, buf=buf: e.indirect_dma_start(
                            out=Vst[buf][:, bi, :], out_offset=None, in_=cv,
                            in_offset=bass.IndirectOffsetOnAxis(ap=idx[:, col:col + 1], axis=0)),
                              dVst[buf], "in", extra_reads=[dIdx])
                    for bi in range(NQ):
                        bank = 5 + (bi % 2)
                        for jj in range(4):
                            S.op("pe", lambda e, bi=bi, jj=jj, bank=bank, buf=buf: e.transpose(
                                out=PS[bank][:, jj * 128:(jj + 1) * 128], in_=Kst[buf][:, bi, jj * 128:(jj + 1) * 128],
                                identity=ident), reads=[dKst[buf], dCF], writes=[PD[bank]], inc=(jj == 3))
                        if bi % 2 == 0:
                            S.op("dve", lambda e, bi=bi, bank=bank, buf=buf: e.tensor_copy(
                                out=KTs[buf][:, bi, :, :].rearrange("p j t -> p (j t)"), in_=PS[bank][:, :]),
                                 reads=[PD[bank]], writes=[dKTs[buf]])
                        else:
                            S.op("act", lambda e, bi=bi, bank=bank, buf=buf: e.activation(
                                out=KTs[buf][:, bi, :, :].rearrange("p j t -> p (j t)"), in_=PS[bank][:, :], func=AF.Copy),
                                 reads=[PD[bank]], writes=[dKTs[buf]])
                    for hv_ in range(2):
                        S.op("act" if hv_ == 0 else "dve",
                             (lambda e, buf=buf: e.activation(
                                 out=Vb[buf][:, 0:NQ // 2, :].rearrange("p b n -> p (b n)"),
                                 in_=Vst[buf][:, 0:NQ // 2, :].rearrange("p b n -> p (b n)"), func=AF.Copy)) if hv_ == 0 else
                             (lambda e, buf=buf: e.tensor_copy(
                                 out=Vb[buf][:, NQ // 2:NQ, :].rearrange("p b n -> p (b n)"),
                                 in_=Vst[buf][:, NQ // 2:NQ, :].rearrange("p b n -> p (b n)"))),
                             reads=[dVst[buf]], writes=[dVb[buf]])
                    sample_step("page", gi=gi, buf=buf, last=(p == 0), b2=gcount % 2)
            for h2 in range(2):
                S.op("act", lambda e, h2=h2: e.activation(
                    out=mixT[64 * h2:64 * h2 + 64, 0:4, TP:NTOK].rearrange("p j (b t) -> p j b t", t=4),
                    in_=PS[2][64 * h2:64 * h2 + 64, :].rearrange("p (j b a t) -> p j b a t", j=4, b=16, a=2)[:, :, :, h2, :],
                    func=AF.Copy), reads=[dOs], writes=[dMixA[16]])
            S.barrier()

        w_out_v = w_out.rearrange("(k p) n -> p k n", p=128)
        w_up_v = w_up.rearrange("(k p) n -> p k n", p=128)
        w_down_v = w_down.rearrange("(c p) n -> p c n", p=128)
        w_ada_v = w_ada.rearrange("(k p) n -> p k n", p=128)
        gBp = sb([128, 2048], F32)
        gBs = sb([64, 2048], F32)
        dG = dep("gB")
        wst = [sb([128, 8, 256], F32) for _ in range(2)]
        dW = [dep("wst") for _ in range(2)]
        with contextlib.ExitStack() as pg:
            cTp = sb([128, 8, 128], F32, stack=pg)
            cTs = sb([128, 8, 64], F32, stack=pg)
            bgB = sb([128, 2048], F32, stack=pg)
            dbg = dep("bgB")
            dcTx = dep("cTx")
            S.dma("sp", lambda e: e.dma_start(out=bgB[:], in_=bg[0:1, :].to_broadcast([128, 2048])), dbg, "in")
            S.op("dve", lambda e: e.tensor_copy(out=cTp[:], in_=cT[:, :, 16:17].to_broadcast([128, 8, 128])),
                 reads=[dcT], writes=[dcTx])
            S.op("dve", lambda e: e.tensor_copy(out=cTs[:].rearrange("p c (b t) -> p c b t", t=4),
                                                in_=cT[:, :, 0:16].unsqueeze(3).to_broadcast([128, 8, 16, 4])),
                 reads=[dcT], writes=[dcTx])
            for n_ in range(8):
                buf = n_ % 2
                acol = (2048 if n_ < 4 else 5120) + (n_ % 4) * 256
                gcol = n_ * 256
                S.dma("sp", lambda e, acol=acol, buf=buf: e.dma_start(out=wst[buf][:], in_=w_ada_v[:, :, acol:acol + 256]),
                      dW[buf], "in")
                for (lhs, rows, dst, bank) in ((cTp, 128, gBp, 2), (cTs, 64, gBs, 3)):
                    for k in range(8):
                        S.op("pe", lambda e, k=k, lhs=lhs, rows=rows, bank=bank, buf=buf: e.matmul(
                            out=PS[bank][0:rows, 0:256], lhsT=lhs[:, k, :], rhs=wst[buf][:, k, :],
                            start=(k == 0), stop=(k == 7)),
                             reads=[dcTx, dW[buf]], writes=[PD[bank]], inc=(k == 7))
                    S.op("dve", lambda e, rows=rows, dst=dst, bank=bank, gcol=gcol: e.tensor_tensor(
                        out=dst[0:rows, gcol:gcol + 256], in0=PS[bank][0:rows, 0:256], in1=bgB[0:rows, gcol:gcol + 256],
                        op=ALU.add), reads=[PD[bank], dbg], writes=[dG])
            S.barrier()
            chk(6)

        def wst_flat(buf, c):
            return wst[buf][:].rearrange("p k n -> p (k n)").rearrange("p (c n) -> p c n", c=c)

        halves = [list(range(0, 8)), list(range(8, 17))]
        for hi, tiles in enumerate(halves):
            with contextlib.ExitStack() as p3:
                nt = len(tiles)
                ncols = sum(tile_rows(i) for i in tiles)
                col0 = tiles[0] * 128
                x1 = sb([128, nt, D], F32, stack=p3)
                dx1 = [dep("x1") for _ in range(nt)]
                h2T = sb([128, 8, ncols], BF16, stack=p3)
                dh2 = [dep("h2T") for _ in range(nt)]
                with contextlib.ExitStack() as p3a:
                    wo = sb([128, 8, D], BF16, stack=p3a)
                    dwo = [dep("wo") for _ in range(4)]
                    for g in range(4):
                        load_weight_piece(w_out_v[:, :, g * 256:(g + 1) * 256], wo[:, :, g * 256:(g + 1) * 256], wst, dW, g % 2,
                                          dwo[g])
                    xt = [sb([128, D], F32, stack=p3a) for _ in range(2)]
                    dxt = [dep("xt") for _ in range(2)]
                    wk = [(sb([128, D], F32, stack=p3a), sb([128, D], F32, stack=p3a), sb([128, 4], F32, stack=p3a))
                          for _ in range(2)]
                    dwk = [dep("wk") for _ in range(2)]
                    tmp = [sb([128, 512], F32, stack=p3a) for _ in range(2)]
                    dtm = [dep("tmp") for _ in range(2)]
                    def outproj(li):
                        i = tiles[li]
                        T = tile_rows(i)
                        c0, c1 = tile_cols(i)
                        b = li % 2
                        load_x_tile(i, xt[b], dxt[b])
                        gB = gBp if i < 16 else gBs
                        for nh in range(2):
                            bank = nh
                            ns = slice(nh * 512, (nh + 1) * 512)
                            for k in range(8):
                                md = dMixA[i] if k < 4 else dMixB[i]
                                S.op("pe", lambda e, k=k, bank=bank, ns=ns: e.matmul(
                                    out=PS[bank][0:T, :], lhsT=mixT[:, k, c0:c1], rhs=wo[:, k, ns], start=(k == 0), stop=(k == 7)),
                                     reads=[md, dwo[2 * nh], dwo[2 * nh + 1]], writes=[PD[bank]], inc=(k == 7))
                            S.op("dve", lambda e, bank=bank, ns=ns, nh=nh: e.tensor_tensor(
                                out=tmp[nh][0:T, :], in0=PS[bank][0:T, :], in1=gB[0:T, ns], op=ALU.mult),
                                 reads=[PD[bank], dG], writes=[dtm[nh]])
                            S.op("pool", lambda e, ns=ns, nh=nh: e.tensor_tensor(
                                out=x1[0:T, li, ns], in0=tmp[nh][0:T, :], in1=xt[b][0:T, ns], op=ALU.add),
                                 reads=[dtm[nh], dxt[b]], writes=[dx1[li]])

                    def norm2(li):
                        i = tiles[li]
                        T = tile_rows(i)
                        c0, c1 = tile_cols(i)
                        b = li % 2
                        lc0 = c0 - col0
                        norm_transpose(i, x1[0:T, li, :], dx1[li], 2, h2T[:, :, lc0:lc0 + T], dh2[li], wk[b], dwk[b],
                                       (2, 3) if b == 0 else (4, 5))

                    outproj(0)
                    for li in range(nt):
                        if li + 1 < nt:
                            outproj(li + 1)
                        norm2(li)
                    S.barrier()
                    chk(7)
                with contextlib.ExitStack() as p4:
                    wu = [sb([128, 8, 512], BF16, stack=p4) for _ in range(2)]
                    wd = [sb([128, 4, D], BF16, stack=p4) for _ in range(2)]
                    dwu = [dep("wu") for _ in range(2)]
                    dwd = [dep("wd") for _ in range(2)]
                    upT = [sb([128, 4, 512], BF16, stack=p4) for _ in range(2)]
                    dup = [dep("up") for _ in range(2)]
                    rl = [sb([128, 512], F32, stack=p4) for _ in range(2)]
                    drl = [dep("rl") for _ in range(2)]
                    tmp = [sb([128, 512], F32, stack=p4) for _ in range(2)]
                    dtm = [dep("tmp") for _ in range(2)]
                    groups = []
                    li = 0
                    while li < nt:
                        g = [l for l in range(li, min(li + 4, nt)) if tile_rows(tiles[l]) == 128]
                        if not g:
                            g = [li]
                        groups.append(g)
                        li = g[-1] + 1
                    gctr = 0
                    rctr = 0
                    dx1h = [[dep("x1h") for _ in range(2)] for _ in range(nt)]
                    for l_ in range(nt):
                        for nh_ in range(2):
                            dx1h[l_][nh_].w = dx1[l_].w
                    def load_w(E):
                        wb = E % 2
                        for hh in range(2):
                            S.dma("sp", lambda e, hh=hh: e.dma_start(
                                out=wst[0][:], in_=w_up_v[:, :, E * 512 + hh * 256:E * 512 + (hh + 1) * 256]), dW[0], "in")
                            S.op("act", lambda e, hh=hh: e.activation(out=wu[wb][:, :, hh * 256:(hh + 1) * 256], in_=wst[0][:],
                                                                      func=AF.Copy),
                                 reads=[dW[0]], writes=[dwu[wb]])
                            S.dma("sp", lambda e, hh=hh: e.dma_start(
                                out=wst_flat(1, 2), in_=w_down_v[:, E * 4 + hh * 2:E * 4 + (hh + 1) * 2, :]), dW[1], "in")
                            S.op("pool", lambda e, hh=hh: e.tensor_copy(out=wd[wb][:, hh * 2:(hh + 1) * 2, :], in_=wst_flat(1, 2)),
                                 reads=[dW[1]], writes=[dwd[wb]])

                    def up_part(E, g, ub):
                        wb = E % 2
                        gcol0 = tiles[g[0]] * 128 - col0
                        gn = sum(tile_rows(tiles[l]) for l in g)
                        for fc in range(4):
                            bank = fc % 2
                            rb = fc % 2
                            for k in range(8):
                                S.op("pe", lambda e, k=k, fc=fc, bank=bank: e.matmul(
                                    out=PS[bank][:, 0:gn], lhsT=wu[wb][:, k, fc * 128:(fc + 1) * 128],
                                    rhs=h2T[:, k, gcol0:gcol0 + gn], start=(k == 0), stop=(k == 7)),
                                     reads=[dwu[wb]] + [dh2[l] for l in g], writes=[PD[bank]], inc=(k == 7))
                            S.op("act", lambda e, bank=bank, rb=rb: e.activation(out=rl[rb][:, 0:gn], in_=PS[bank][:, 0:gn],
                                                                                 func=AF.Relu),
                                 reads=[PD[bank]], writes=[drl[rb]])
                            S.op("act", lambda e, rb=rb, fc=fc: e.activation(
                                out=upT[ub][:, fc, 0:gn], in_=rl[rb][:, 0:gn], func=AF.Square),
                                 reads=[drl[rb]], writes=[dup[ub]])

                    def down_part(E, g, ub):
                        wb = E % 2
                        gcol0 = tiles[g[0]] * 128 - col0
                        for l in g:
                            i = tiles[l]
                            T = tile_rows(i)
                            lc = tiles[l] * 128 - col0 - gcol0
                            gB = gBp if i < 16 else gBs
                            for nh in range(2):
                                bank = 2 + nh + 2 * (l % 2)
                                ns = slice(nh * 512, (nh + 1) * 512)
                                for fc in range(4):
                                    S.op("pe", lambda e, fc=fc, bank=bank, ns=ns: e.matmul(
                                        out=PS[bank][0:T, :], lhsT=upT[ub][:, fc, lc:lc + T], rhs=wd[wb][:, fc, ns],
                                        start=(fc == 0), stop=(fc == 3)),
                                         reads=[dup[ub], dwd[wb]], writes=[PD[bank]], inc=(fc == 3))
                                S.op("dve", lambda e, bank=bank, nh=nh: e.tensor_tensor(
                                    out=tmp[nh][0:T, :], in0=PS[bank][0:T, :], in1=gB[0:T, 1024 + nh * 512:1024 + (nh + 1) * 512],
                                    op=ALU.mult), reads=[PD[bank], dG], writes=[dtm[nh]])
                                S.op("dve" if nh == 0 else "pool", lambda e, ns=ns, nh=nh: e.tensor_tensor(
                                    out=x1[0:T, l, ns], in0=x1[0:T, l, ns], in1=tmp[nh][0:T, :], op=ALU.add),
                                     reads=[dtm[nh], dx1h[l][nh]], writes=[dx1h[l][nh]])
                            if E == 7:
                                ydst = yp[i * 128:(i + 1) * 128, :] if i < 16 else ys[:, :]
                                for nh in range(2):
                                    S.dma("sp", lambda e, ydst=ydst, nh=nh: e.dma_start(
                                        out=ydst[:, nh * 512:(nh + 1) * 512], in_=x1[0:T, l, nh * 512:(nh + 1) * 512]),
                                          dx1h[l][nh], "out")

                    seq = [(E, g) for E in range(8) for g in groups]
                    load_w(0)
                    up_part(seq[0][0], seq[0][1], 0)
                    for n_, (E, g) in enumerate(seq):
                        if g is groups[0] and E + 1 < 8:
                            load_w(E + 1)
                        if n_ + 1 < len(seq):
                            up_part(seq[n_ + 1][0], seq[n_ + 1][1], (n_ + 1) % 2)
                        down_part(E, g, n_ % 2)
                    S.barrier()
    except _Stop:
        S.finish()
        return nc
    S.finish()
    es.close()
    return nc


def _core_inputs(c, inp, n_phys=None, ck=None, cv=None, pt=None):
    f = np.float32
    d = {}
    d["xp"] = np.ascontiguousarray(inp["x_prompt"][c], dtype=f)
    d["xs"] = np.ascontiguousarray(inp["x_sample"][16 * c:16 * c + 16].reshape(TS, D), dtype=f)
    d["ck"] = ck if ck is not None else inp["cache_k"][0].reshape(-1, 512)
    d["cv"] = cv if cv is not None else inp["cache_v"][0].reshape(-1, 512)
    d["st0"] = np.ascontiguousarray(inp["state_hgrn"][0, 16 * c:16 * c + 16], dtype=f)
    ptc = pt if pt is not None else inp["page_table"][16 * c:16 * c + 16]
    d["pt"] = np.ascontiguousarray(ptc.reshape(1, -1), dtype=np.int32)
    d["cc"] = np.ascontiguousarray(np.concatenate([inp["c_sample"][16 * c:16 * c + 16], inp["c_prompt"][c:c + 1]], 0), dtype=f)
    d["w_ada"] = np.ascontiguousarray(inp["w_ada"][0], dtype=f)
    b_ada = inp["b_ada"][0]
    d["vecs"] = np.ascontiguousarray(np.concatenate([b_ada.reshape(48, 128), inp["norm1_g"][0].reshape(8, 128),
                                                     inp["norm2_g"][0].reshape(8, 128)], 0), dtype=f)
    d["bg"] = np.ascontiguousarray(np.concatenate([b_ada[2048:3072], b_ada[5120:6144]])[None], dtype=f)
    d["w_in"] = np.ascontiguousarray(inp["w_in"][0], dtype=f)
    d["qkg"] = np.ascontiguousarray(np.concatenate([np.tile(inp["q_norm_g"][0], 8), np.tile(inp["k_norm_g"][0], 8),
                                                    np.tile(inp["hg_out_g"][0], 8)])[None], dtype=f)
    d["sbb"] = np.ascontiguousarray(inp["sb_bias"][0][None], dtype=f)
    d["lbl"] = np.ascontiguousarray(inp["hg_lb_logits"].reshape(1, 1024), dtype=f)
    d["w_out"] = np.ascontiguousarray(inp["w_out"][0], dtype=f)
    d["w_up"] = np.ascontiguousarray(inp["w_up"][0], dtype=f)
    d["w_down"] = np.ascontiguousarray(inp["w_down"][0], dtype=f)
    d["cstf"] = _CARR
    d["cstb"] = _BARR
    return d


def _assemble(results):
    y_p = np.stack([r["yp"] for r in results]).astype(np.float32)
    y_s = np.concatenate([r["ys"].reshape(16, 4, D) for r in results]).astype(np.float32)
    k_p = np.stack([r["kp"].reshape(TP, 8, 64) for r in results])[None].astype(np.float32)
    v_p = np.stack([r["vp"].reshape(TP, 8, 64) for r in results])[None].astype(np.float32)
    k_s = np.concatenate([r["ks"].reshape(16, 4, 8, 64) for r in results])[None].astype(np.float32)
    v_s = np.concatenate([r["vs"].reshape(16, 4, 8, 64) for r in results])[None].astype(np.float32)
    s_p = np.stack([r["sp"] for r in results])[None].astype(np.float32)
    s_s = np.concatenate([r["ss"] for r in results])[None].astype(np.float32)
    return (y_p, y_s, k_p, v_p, k_s, v_s, s_p, s_s)


def kernel(**inputs):
    inp = {k: np.asarray(v) for k, v in inputs.items()}
    n_phys = inp["cache_k"].shape[1]
    nc = build(n_phys)
    ck = np.ascontiguousarray(inp["cache_k"][0].reshape(-1, 512), dtype=np.float32)
    cv = np.ascontiguousarray(inp["cache_v"][0].reshape(-1, 512), dtype=np.float32)
    in_maps = [_core_inputs(c, inp, ck=ck, cv=cv) for c in range(NCORES)]
    res = run_bass_kernel_spmd(nc, in_maps, core_ids=list(range(NCORES)))
    return _assemble(res.results)
```

```python
import contextlib
import os
import numpy as np
import ml_dtypes
import concourse.bass as bass
import concourse.mybir as mybir
from concourse.bass_utils import run_bass_kernel_spmd

F32 = mybir.dt.float32
BF16 = mybir.dt.bfloat16
I32 = mybir.dt.int32
AF = mybir.ActivationFunctionType
ALU = mybir.AluOpType
AX = mybir.AxisListType

NCORES = 8
D = 1024
TP = 2048
NSEQ = 16
TS = 64
NTOK = TP + TS
NPG = 16
EPS = 1e-6
SCALE = 64 ** -0.5
NT = 17


def _consts():
    f = {}
    idx = np.arange(128)
    f["ident"] = np.eye(128, dtype=np.float32)
    s = idx[:, None]
    t = idx[None, :]
    same64 = (s // 64) == (t // 64)
    f["tri2"] = ((s <= t) & same64).astype(np.float32)
    f["tsu2"] = ((s > t) & same64).astype(np.float32)
    same4 = (s // 4) == (t // 4)
    f["tri4"] = ((s <= t) & same4).astype(np.float32)
    f["tsu4"] = ((s > t) & same4).astype(np.float32)
    ci = np.zeros((128, 2), np.float32)
    ci[:64, 0] = 1
    ci[64:, 1] = 1
    f["chunkind"] = ci
    si = np.zeros((128, 16), np.float32)
    for p in range(64):
        si[p, p // 4] = 1
    f["seqind"] = si
    mh = np.zeros((128, 64), np.float32)
    for p in range(128):
        mh[p, :] = (np.arange(64) >= (p % 64))
    f["maskH"] = mh
    ms = np.zeros((128, 64), np.float32)
    for p in range(64):
        for c in range(64):
            ms[p, c] = (p // 4 == c // 4) and (p % 4 <= c % 4)
    f["maskHS"] = ms
    ma = np.zeros((128, 64), np.float32)
    for p in range(64):
        for c in range(64):
            ma[p, c] = (p // 4 == c // 4) and (p % 4 < c % 4)
    f["maskS"] = ma
    smt = np.zeros((128, 16 * 64), np.float32)
    for b in range(16):
        smt[:, b * 64 + 4 * b: b * 64 + 4 * b + 4] = 1
    f["seqmaskT"] = smt
    md = np.zeros((128, 4 * 512), np.float32)
    for i in range(4):
        md[:, i * 512:(i + 1) * 512] = ((128 * i + idx[:, None]) < np.arange(512)[None, :])
    f["maskD"] = md
    f["uincl"] = (s >= t).astype(np.float32)
    f["lstrict"] = (s < t).astype(np.float32)
    io = np.zeros((128, 1), np.float32)
    off = {}
    cols = 0
    for k, v in f.items():
        off[k] = (cols, v.shape[1])
        cols += v.shape[1]
    bfk = ["maskD", "uincl", "lstrict", "seqmaskT"]
    off = {}
    cols = 0
    for k, v in f.items():
        if k in bfk:
            continue
        off[k] = (cols, v.shape[1])
        cols += v.shape[1]
    arr = np.concatenate([f[k] for k in f if k not in bfk], axis=1).astype(np.float32)
    boff = {}
    cols = 0
    for k in bfk:
        boff[k] = (cols, f[k].shape[1])
        cols += f[k].shape[1]
    barr = np.concatenate([f[k] for k in bfk], axis=1).astype(np.float32)
    return arr, off, barr, boff


_CARR, _COFF, _BARR, _BOFF = _consts()


class _Stop(Exception):
    pass


class Dep:
    __slots__ = ("name", "w", "re", "rd", "sem", "cnt")

    def __init__(self, name):
        self.name = name
        self.w = None
        self.re = {}
        self.rd = None
        self.sem = None
        self.cnt = 0


class Sched:
    def __init__(self, nc, es):
        self.nc = nc
        self.es = es
        self.eng = {}
        for name, h in [("pe", nc.tensor), ("act", nc.scalar), ("dve", nc.vector),
                        ("pool", nc.gpsimd), ("sp", nc.sync)]:
            sem = es.enter_context(nc.semaphore("sem_" + name))
            self.eng[name] = dict(h=h, sem=sem, cnt=0, waited={})
        self.dma_deps = []

    def _wait(self, ename, entry):
        E = self.eng[ename]
        if entry[0] == "e":
            _, src, idx = entry
            if src == ename and ename == "pe":
                return
            sem = self.eng[src]["sem"]
            val = idx
        else:
            _, dep, n = entry
            sem = dep.sem
            val = 16 * n
        key = id(sem)
        if E["waited"].get(key, 0) >= val:
            return
        E["waited"][key] = val
        E["h"].wait_ge(sem, val)

    def _deps(self, ename, reads, writes):
        for d in reads:
            if d.w is not None:
                self._wait(ename, d.w)
        for d in writes:
            if d.w is not None and not (d.w[0] == "e" and d.w[1] == ename):
                self._wait(ename, d.w)
            for src, idx in d.re.items():
                if src != ename:
                    self._wait(ename, ("e", src, idx))
            if d.rd is not None:
                self._wait(ename, d.rd)

    def op(self, ename, fn, reads=(), writes=(), inc=True):
        E = self.eng[ename]
        self._deps(ename, reads, writes)
        ins = fn(E["h"])
        if inc:
            E["cnt"] += 1
            idx = E["cnt"]
            ins.then_inc(E["sem"], 1)
        else:
            idx = E["cnt"] + 1
        for d in reads:
            d.re[ename] = idx
        for d in writes:
            d.w = ("e", ename, idx)
            d.re = {}
            d.rd = None
        return ins

    def dma(self, qname, fn, dep, direction, extra_reads=()):
        E = self.eng[qname]
        if dep.sem is None:
            dep.sem = self.es.enter_context(self.nc.semaphore("dsem_" + dep.name))
            self.dma_deps.append(dep)
        if direction == "in":
            if dep.w is not None and dep.w[0] == "d" and dep.w[1] is dep and not dep.re and dep.rd is None:
                for d in extra_reads:
                    if d.w is not None:
                        self._wait(qname, d.w)
            else:
                self._deps(qname, list(extra_reads), [dep])
        else:
            self._deps(qname, list(extra_reads) + [dep], [])
        ins = fn(E["h"])
        dep.cnt += 1
        ins.then_inc(dep.sem, 16)
        ent = ("d", dep, dep.cnt)
        if direction == "in":
            dep.w = ent
            dep.re = {}
            dep.rd = None
        else:
            dep.rd = ent
        return ins

    def barrier(self):
        names = ["pe", "act", "dve", "pool", "sp"]
        for a in names:
            for b in names:
                if a != b and self.eng[b]["cnt"] > 0:
                    self._wait(a, ("e", b, self.eng[b]["cnt"]))
            for d in self.dma_deps:
                if d.cnt > 0:
                    self._wait(a, ("d", d, d.cnt))

    def finish(self):
        for d in self.dma_deps:
            if d.cnt > 0:
                self._wait("sp", ("d", d, d.cnt))
        for b in ["pe", "act", "dve", "pool"]:
            if self.eng[b]["cnt"] > 0:
                self._wait("sp", ("e", b, self.eng[b]["cnt"]))


def build(n_phys):
    nc = bass.Bass("TRN2", target_bir_lowering=False)
    es = contextlib.ExitStack()

    def din(name, shape, dt=F32):
        return nc.dram_tensor(name, shape, dt, kind="ExternalInput").ap()

    def dout(name, shape, dt=F32):
        return nc.dram_tensor(name, shape, dt, kind="ExternalOutput").ap()

    xp = din("xp", [TP, D])
    xs_d = din("xs", [TS, D])
    ck = din("ck", [n_phys * 128, 512])
    cv = din("cv", [n_phys * 128, 512])
    st0 = din("st0", [NSEQ, 8, 64, 64])
    pt = din("pt", [1, NSEQ * NPG], I32)
    cc = din("cc", [17, D])
    w_ada = din("w_ada", [D, 6 * D])
    vecs = din("vecs", [64, 128])
    bg = din("bg", [1, 2048])
    w_in = din("w_in", [D, 3584])
    qkg = din("qkg", [1, 3 * 512])
    sbb = din("sbb", [1, 8])
    lbl = din("lbl", [1, 1024])
    w_out = din("w_out", [D, D])
    w_up = din("w_up", [D, 4 * D])
    w_down = din("w_down", [4 * D, D])
    cstf = din("cstf", list(_CARR.shape))
    cstb = din("cstb", list(_BARR.shape))

    yp = dout("yp", [TP, D])
    ys = dout("ys", [TS, D])
    kp = dout("kp", [TP, 512])
    vp = dout("vp", [TP, 512])
    ks = dout("ks", [TS, 512])
    vs = dout("vs", [TS, 512])
    sp_o = dout("sp", [8, 64, 64])
    ss_o = dout("ss", [NSEQ, 8, 64, 64])

    S = Sched(nc, es)
    ucnt = [0]

    def sb(shape, dt=F32, side="left", stack=None, name=None):
        ucnt[0] += 1
        nm = (name or "t") + str(ucnt[0])
        return (stack or es).enter_context(nc.sbuf_tensor(nm, shape, dt, side=side))

    def dep(name="d"):
        ucnt[0] += 1
        return Dep(name + str(ucnt[0]))

    PS = [es.enter_context(nc.psum_tensor(f"ps{i}", [128, 512], F32)) for i in range(8)]
    PD = [dep(f"ps{i}_") for i in range(8)]

    CF = sb([128, _CARR.shape[1]], F32)
    dCF = dep("cf")
    S.dma("sp", lambda e: e.dma_start(out=CF[:], in_=cstf[:, :]), dCF, "in")

    def cf(key, rows=128, c0=0, c1=None):
        o, n = _COFF[key]
        c1 = n if c1 is None else c1
        return CF[0:rows, o + c0:o + c1]

    def load_bf_const(key, dst_ap, ddst, scratch, dscr):
        o, n = _BOFF[key]
        for a in range(0, n, 512):
            w_ = min(512, n - a)
            S.dma("sp", lambda e, a=a, w_=w_: e.dma_start(out=scratch[:, 0:w_], in_=cstb[:, o + a:o + a + w_]), dscr, "in")
            S.op("dve", lambda e, a=a, w_=w_: e.tensor_copy(out=dst_ap[:, a:a + w_], in_=scratch[:, 0:w_]),
                 reads=[dscr], writes=[ddst])

    ident = cf("ident")

    vecT = sb([128, 64], F32)
    dVec = dep("vecT")
    biasT = sb([128, 8], F32)
    dPar = dep("par")
    cT = sb([128, 8, 17], F32)
    dcT = dep("cT")
    mod = sb([128, 4, 8, 17], F32)
    dMod = dep("mod")
    p1s = contextlib.ExitStack()
    qkgB = sb([128, 1536], F32, stack=p1s)
    lbB = sb([128, 512], F32, stack=p1s)
    omlB = sb([128, 512], F32, stack=p1s)
    mixT = sb([128, 8, NTOK], BF16, side="right")
    dMixA = [dep("mixA") for _ in range(NT)]
    dMixB = [dep("mixB") for _ in range(NT)]

    def tile_rows(i):
        return 128 if i < 16 else 64

    def tile_cols(i):
        return (i * 128, i * 128 + tile_rows(i))

    def rsqrt_small(src_ap, dst_ap, n_scale, deps_r, dep_w, tmp_ap):
        S.op("dve", lambda e: e.tensor_scalar(out=tmp_ap, in0=src_ap, scalar1=n_scale, scalar2=EPS,
                                              op0=ALU.mult, op1=ALU.add), reads=deps_r, writes=[dep_w])
        S.op("act", lambda e: e.activation(out=tmp_ap, in_=tmp_ap, func=AF.Ln), reads=[dep_w], writes=[dep_w])
        S.op("act", lambda e: e.activation(out=dst_ap, in_=tmp_ap, func=AF.Exp, scale=-0.5),
             reads=[dep_w], writes=[dep_w])

    stop = int(os.environ.get("KSTOP", "99"))

    def chk(n):
        if stop == n:
            raise _Stop()

    try:
        with contextlib.ExitStack() as p0:
            wst = [sb([128, 8, 512], F32, stack=p0) for _ in range(2)]
            dW = [dep("wst") for _ in range(2)]
            cct = sb([17, D], F32, stack=p0)
            dcc = dep("cc")
            adaT = sb([128, 48, 17], F32, stack=p0)
            dAda = dep("adaT")
            vrow = sb([64, 128], F32, stack=p0)
            lraw = sb([128, 1024], F32, stack=p0)
            dtmp = dep("p0tmp")

            S.dma("sp", lambda e: e.dma_start(out=cct[:], in_=cc[:, :]), dcc, "in")
            S.dma("sp", lambda e: e.dma_start(out=vrow[:], in_=vecs[:, :]), dtmp, "in")
            S.dma("sp", lambda e: e.dma_start(out=qkgB[:], in_=qkg[0:1, :].to_broadcast([128, 1536])), dPar, "in")
            S.dma("sp", lambda e: e.dma_start(out=biasT[:], in_=sbb[0:1, :].to_broadcast([128, 8])), dPar, "in")
            S.dma("sp", lambda e: e.dma_start(out=lraw[:], in_=lbl[0:1, :].to_broadcast([128, 1024])), dtmp, "in")
            S.op("dve", lambda e: e.tensor_tensor(out=lraw[:, 0:512], in0=lraw[:, 0:512], in1=lraw[:, 512:1024],
                                                  op=ALU.subtract), reads=[dtmp], writes=[dtmp])
            S.op("act", lambda e: e.activation(out=lbB[:], in_=lraw[:, 0:512], func=AF.Sigmoid),
                 reads=[dtmp], writes=[dPar])
            S.op("dve", lambda e: e.tensor_scalar(out=omlB[:], in0=lbB[:], scalar1=-1.0, scalar2=1.0,
                                                  op0=ALU.mult, op1=ALU.add), reads=[dPar], writes=[dPar])
            S.op("act", lambda e: e.activation(out=cct[:], in_=cct[:], func=AF.Silu), reads=[dcc], writes=[dcc])
            for c in range(8):
                S.op("pe", lambda e, c=c: e.transpose(out=PS[0][:, c * 17:(c + 1) * 17],
                                                      in_=cct[0:17, c * 128:(c + 1) * 128], identity=cf("ident", 17, 0, 17)),
                     reads=[dcc, dCF], writes=[PD[0]], inc=(c == 7))
            S.op("dve", lambda e: e.tensor_copy(out=cT[:].rearrange("p c s -> p (c s)"), in_=PS[0][:, 0:136]),
                 reads=[PD[0]], writes=[dcT])
            S.op("pe", lambda e: e.transpose(out=PS[1][:, 0:64], in_=vrow[0:64, :], identity=cf("ident", 64, 0, 64)),
                 reads=[dtmp, dCF], writes=[PD[1]])
            S.op("dve", lambda e: e.tensor_copy(out=vecT[:], in_=PS[1][:, 0:64]), reads=[PD[1]], writes=[dVec])

            w_ada_v = w_ada.rearrange("(k p) n -> p k n", p=128)
            pcs = [0, 1, 2, 3, 6, 7, 8, 9]
            for n_, pc in enumerate(pcs):
                buf = n_ % 2
                S.dma("sp", lambda e, pc=pc, buf=buf: e.dma_start(out=wst[buf][:], in_=w_ada_v[:, :, pc * 512:(pc + 1) * 512]),
                      dW[buf], "in")
                for q in range(4):
                    chunk = (pc * 512) // 128 + q
                    bank = 2 + (q % 2)
                    for k in range(8):
                        S.op("pe", lambda e, k=k, q=q, bank=bank, buf=buf: e.matmul(
                            out=PS[bank][:, 0:17], lhsT=wst[buf][:, k, q * 128:(q + 1) * 128], rhs=cT[:, k, :],
                            start=(k == 0), stop=(k == 7)),
                             reads=[dcT, dW[buf]], writes=[PD[bank]], inc=(k == 7))
                    S.op("dve", lambda e, chunk=chunk, bank=bank: e.tensor_scalar(
                        out=adaT[:, chunk, :], in0=PS[bank][:, 0:17], scalar1=vecT[:, chunk:chunk + 1], scalar2=None,
                        op0=ALU.add), reads=[PD[bank], dVec], writes=[dAda])
            for (mi, scc, shc, nof) in ((0, 8, 0, 48), (2, 32, 24, 56)):
                S.op("dve", lambda e, mi=mi, scc=scc, nof=nof: e.scalar_tensor_tensor(
                    out=mod[:, mi, :, :], in0=adaT[:, scc:scc + 8, :], scalar=1.0,
                    in1=vecT[:, nof:nof + 8].unsqueeze(2).to_broadcast([128, 8, 17]), op0=ALU.add, op1=ALU.mult),
                     reads=[dAda, dVec], writes=[dMod])
                S.op("dve", lambda e, mi=mi, shc=shc: e.tensor_copy(out=mod[:, mi + 1, :, :], in_=adaT[:, shc:shc + 8, :]),
                     reads=[dAda], writes=[dMod])
            S.barrier()
            chk(0)

        def load_x_tile(i, xt, dxt, q="sp"):
            T = tile_rows(i)
            src = xp[i * 128:(i + 1) * 128, :] if i < 16 else xs_d[:, :]
            S.dma(q, lambda e: e.dma_start(out=xt[0:T, :], in_=src), dxt, "in")

        def norm_transpose(i, src_ap, dsrc, mi, hT_ap, dhT, work, dwork, banks):
            T = tile_rows(i)
            xn, sqj, st4 = work
            S.op("act", lambda e: e.activation(out=sqj[0:T, :], in_=src_ap, func=AF.Square, accum_out=st4[0:T, 0:1]),
                 reads=[dsrc], writes=[dwork])
            rsqrt_small(st4[0:T, 0:1], st4[0:T, 2:3], 1.0 / D, [dwork], dwork, st4[0:T, 1:2])
            S.op("dve", lambda e: e.tensor_scalar(out=xn[0:T, :], in0=src_ap, scalar1=st4[0:T, 2:3], scalar2=None,
                                                  op0=ALU.mult), reads=[dsrc, dwork], writes=[dwork])
            nb, tpb = (1, 128) if i < 16 else (16, 4)
            c0 = 16 if i < 16 else 0
            for half in range(2):
                bank = banks[half]
                for c4 in range(4):
                    c = half * 4 + c4
                    S.op("pe", lambda e, c=c, c4=c4, bank=bank: e.transpose(
                        out=PS[bank][:, c4 * 128:c4 * 128 + T], in_=xn[0:T, c * 128:(c + 1) * 128],
                        identity=cf("ident", T, 0, T)), reads=[dwork, dCF], writes=[PD[bank]], inc=(c4 == 3))
                pv = PS[bank][:, :].rearrange("p (c t) -> p c t", c=4)[:, :, 0:T].rearrange("p c (b t) -> p c b t", t=tpb)
                sc = mod[:, mi, half * 4:half * 4 + 4, c0:c0 + nb].unsqueeze(3).to_broadcast([128, 4, nb, tpb])
                sh = mod[:, mi + 1, half * 4:half * 4 + 4, c0:c0 + nb].unsqueeze(3).to_broadcast([128, 4, nb, tpb])
                tmpm = xn[:, half * 512:(half + 1) * 512].rearrange("p (c t) -> p c t", c=4)[:, :, 0:T].rearrange(
                    "p c (b t) -> p c b t", t=tpb)
                tm = sqj[:, half * 512:(half + 1) * 512].rearrange("p (c t) -> p c t", c=4)[:, :, 0:T].rearrange(
                    "p c (b t) -> p c b t", t=tpb)
                S.op("dve", lambda e, pv=pv, sc=sc, tm=tm: e.tensor_tensor(out=tm, in0=pv, in1=sc, op=ALU.mult),
                     reads=[PD[bank], dMod], writes=[dwork])
                ho = hT_ap[:, half * 4:half * 4 + 4, :].rearrange("p c (b t) -> p c b t", t=tpb)
                S.op("dve", lambda e, ho=ho, sh=sh, tm=tm: e.tensor_tensor(out=ho, in0=tm, in1=sh, op=ALU.add),
                     reads=[dwork, dMod], writes=[dhT])

        def load_weight_piece(src_ap, dst_bf_ap, wst, dW, buf, ddst, q="sp", cast_eng="pool"):
            S.dma(q, lambda e: e.dma_start(out=wst[buf][:], in_=src_ap), dW[buf], "in")
            S.op(cast_eng, lambda e: e.tensor_copy(out=dst_bf_ap, in_=wst[buf][:]), reads=[dW[buf]], writes=[ddst])

        with contextlib.ExitStack() as p1:
            hT = sb([128, 8, NTOK], BF16, stack=p1)
            dhT = [dep("hT") for _ in range(NT)]
            with contextlib.ExitStack() as p1n:
                xt = [sb([128, D], F32, stack=p1n) for _ in range(2)]
                dxt = [dep("xt") for _ in range(2)]
                wk = [(sb([128, D], F32, stack=p1n), sb([128, D], F32, stack=p1n), sb([128, 4], F32, stack=p1n)) for _ in range(2)]
                dwk = [dep("wk") for _ in range(2)]
                for i in [int(x) for x in os.environ["KTILES"].split(",")] if "KTILES" in os.environ else range(NT):
                    b = i % 2
                    load_x_tile(i, xt[b], dxt[b])
                    c0, c1 = tile_cols(i)
                    norm_transpose(i, xt[b][0:tile_rows(i), :], dxt[b], 0, hT[:, :, c0:c1], dhT[i], wk[b], dwk[b],
                                   (0, 1) if b == 0 else (2, 3))
                S.barrier()
                chk(1)

            w_in_v = w_in.rearrange("(k p) n -> p k n", p=128)

            with contextlib.ExitStack() as pb:
                whg = sb([128, 8, 2048], BF16, stack=pb)
                dwhg = [dep("whg") for _ in range(4)]
                with contextlib.ExitStack() as pw:
                    wst = [sb([128, 8, 512], F32, stack=pw) for _ in range(2)]
                    dW = [dep("wst") for _ in range(2)]
                    for g in range(4):
                        load_weight_piece(w_in_v[:, :, 1536 + g * 512:1536 + (g + 1) * 512], whg[:, :, g * 512:(g + 1) * 512],
                                          wst, dW, g % 2, dwhg[g])
                    S.barrier()

                def wt(shape, dt=F32):
                    return sb(shape, dt, stack=pb)

                AB = (4, 2)
                XB = (6, 3)

                def hsel(ap, h2):
                    return ap.rearrange("p (j a t) -> p j a t", j=4, a=2)[:, :, h2, :]

                hq = wt([128, 512]); ff = wt([128, 512]); logf = wt([128, 512]); omf = wt([128, 512])
                hvL = [wt([128, 512], BF16) for _ in range(2)]; sgL = [wt([128, 512]) for _ in range(2)]
                eb = wt([128, 512]); enb = wt([128, 512])
                ec = wt([128, 512]); kkL = [wt([128, 512], BF16) for _ in range(2)]
                qdTL = [wt([128, 4, 128], BF16) for _ in range(2)]; kdTL = [wt([128, 4, 128], BF16) for _ in range(2)]
                attm = wt([128, 512], BF16); oo = wt([128, 512]); osq = wt([128, 512])
                st8 = wt([128, 24]); decL = [wt([128, 4, 16]) for _ in range(2)]
                Sst = wt([128, 4, 64]); Sbf = wt([128, 4, 64], BF16)
                dE = dep("hgE")
                dTL = [dep("hgT") for _ in range(2)]
                dEbL = [dep("hgEb") for _ in range(2)]
                dDecL = [dep("dec") for _ in range(2)]
                dA = dep("attm"); dO = dep("oo"); dS_ = dep("S"); dSb = dep("Sbf")
                S.op("dve", lambda e: e.memset(Sst[:], 0.0), writes=[dS_])
                S.op("dve", lambda e: e.memset(Sbf[:], 0.0), writes=[dSb])
                S0t = [wt([128, 16, 64]) for _ in range(2)]
                S0bt = wt([128, 16, 64], BF16)
                qdTm = wt([128, 4, 16, 64], BF16); hvm = wt([64, 16, 128], BF16)
                smT = wt([128, 1024], BF16)
                dS0 = [dep("S0") for _ in range(2)]; dS0b = dep("S0b"); dqm = dep("qdTm"); dhvm = dep("hvm"); dsm = dep("smT")
                st0_v = st0.rearrange("b (j h) k v -> (h k) j b v", h=2)
                ss_v = ss_o.rearrange("b (j h) k v -> (h k) j b v", h=2)
                load_bf_const("seqmaskT", smT, dsm, hq, dE)

                def tile_gen(i):
                    s_ = i % 2
                    hv, sg, kk, qdT, kdT, dec = hvL[s_], sgL[s_], kkL[s_], qdTL[s_], kdTL[s_], decL[s_]
                    dT_, dEb, dDec = dTL[s_], dEbL[s_], dDecL[s_]
                    T = tile_rows(i)
                    c0, c1 = tile_cols(i)
                    samp = (i == 16)
                    tri = cf("tri4" if samp else "tri2", T, 0, T)
                    tsu = cf("tsu4" if samp else "tsu2", T, 0, T)
                    ncn = 16 if samp else 2
                    ind = cf("seqind" if samp else "chunkind", T)
                    def proj(g, bank):
                        for k in range(8):
                            S.op("pe", lambda e, k=k: e.matmul(out=PS[bank][0:T, :], lhsT=hT[:, k, c0:c1],
                                                               rhs=whg[:, k, g * 512:(g + 1) * 512],
                                                               start=(k == 0), stop=(k == 7)),
                                 reads=[dhT[i], dwhg[g]], writes=[PD[bank]], inc=(k == 7))
                    proj(0, 2)
                    proj(1, 3)
                    proj(3, 0)
                    proj(2, 1)
                    S.op("act", lambda e: e.activation(out=hq[0:T, :], in_=PS[2][0:T, :], func=AF.Silu),
                         reads=[PD[2]], writes=[dE])
                    S.op("act", lambda e: e.activation(out=sg[0:T, :], in_=PS[0][0:T, :], func=AF.Silu),
                         reads=[PD[0]], writes=[dEb])
                    S.op("act", lambda e: e.activation(out=ff[0:T, :], in_=PS[3][0:T, :], func=AF.Sigmoid),
                         reads=[PD[3]], writes=[dE])
                    S.op("dve", lambda e: e.tensor_tensor(out=ff[0:T, :], in0=ff[0:T, :], in1=omlB[0:T, :], op=ALU.mult),
                         reads=[dE, dPar], writes=[dE])
                    S.op("dve", lambda e: e.tensor_tensor(out=ff[0:T, :], in0=ff[0:T, :], in1=lbB[0:T, :], op=ALU.add),
                         reads=[dE, dPar], writes=[dE])
                    S.op("act", lambda e: e.activation(out=hv[0:T, :], in_=PS[1][0:T, :], func=AF.Copy),
                         reads=[PD[1]], writes=[dEb])
                    S.op("act", lambda e: e.activation(out=logf[0:T, :], in_=ff[0:T, :], func=AF.Ln),
                         reads=[dE], writes=[dE])
                    S.op("dve", lambda e: e.tensor_scalar(out=omf[0:T, :], in0=ff[0:T, :], scalar1=-1.0, scalar2=1.0,
                                                          op0=ALU.mult, op1=ALU.add), reads=[dE], writes=[dE])
                    yield
                    S.op("pe", lambda e: e.matmul(out=PS[0][0:T, :], lhsT=tri, rhs=logf[0:T, :], start=True, stop=True),
                         reads=[dE, dCF], writes=[PD[0]])
                    S.op("pe", lambda e: e.matmul(out=PS[1][0:T, :], lhsT=tsu, rhs=logf[0:T, :], start=True, stop=True),
                         reads=[dE, dCF], writes=[PD[1]])
                    for j in range(4):
                        S.op("pe", lambda e, j=j: e.matmul(out=PS[7][:, 256 + j * ncn:256 + (j + 1) * ncn],
                                                           lhsT=logf[0:T, j * 128:(j + 1) * 128], rhs=ind,
                                                           start=True, stop=True),
                             reads=[dE, dCF], writes=[PD[7]], inc=(j == 3))
                    S.op("act", lambda e: e.activation(out=eb[0:T, :], in_=PS[0][0:T, :], func=AF.Exp),
                         reads=[PD[0]], writes=[dE])
                    S.op("act", lambda e: e.activation(out=enb[0:T, :], in_=PS[0][0:T, :], func=AF.Exp, scale=-1.0),
                         reads=[PD[0]], writes=[dE])
                    S.op("act", lambda e: e.activation(out=ec[0:T, :], in_=PS[1][0:T, :], func=AF.Exp),
                         reads=[PD[1]], writes=[dE])
                    S.op("act", lambda e: e.activation(out=dec[:, :, 0:ncn],
                                                       in_=PS[7][:, 256:256 + 4 * ncn].rearrange("p (j c) -> p j c", j=4),
                                                       func=AF.Exp), reads=[PD[7]], writes=[dDec])
                    S.op("dve", lambda e: e.tensor_tensor(out=eb[0:T, :], in0=hq[0:T, :], in1=eb[0:T, :], op=ALU.mult),
                         reads=[dE], writes=[dE])
                    S.op("dve", lambda e: e.tensor_tensor(out=enb[0:T, :], in0=omf[0:T, :], in1=enb[0:T, :], op=ALU.mult),
                         reads=[dE], writes=[dE])
                    S.op("dve", lambda e: e.tensor_tensor(out=kk[0:T, :], in0=omf[0:T, :], in1=ec[0:T, :], op=ALU.mult),
                         reads=[dE], writes=[dEb])
                    yield
                    for (src, dst, bank) in ((eb, qdT, 0), (enb, kdT, 1)):
                        for j in range(4):
                            S.op("pe", lambda e, j=j, src=src, bank=bank: e.transpose(
                                out=PS[bank][:, j * 128:j * 128 + T], in_=src[0:T, j * 128:(j + 1) * 128],
                                identity=cf("ident", T, 0, T)), reads=[dE, dCF], writes=[PD[bank]], inc=(j == 3))
                        S.op("act", lambda e, dst=dst, bank=bank: e.activation(
                            out=dst[:, :, 0:T], in_=PS[bank][:, :].rearrange("p (j t) -> p j t", j=4)[:, :, 0:T],
                            func=AF.Copy), reads=[PD[bank]], writes=[dT_])

                    yield
                    if not samp:
                        for c in range(2):
                            cp = 64 * c
                            for h in range(8):
                                j, h2 = h // 2, h % 2
                                hp = 64 * h2
                                ab = AB[h2]
                                S.op("pe", lambda e, h=h, j=j, hp=hp, cp=cp, ab=ab: e.matmul(
                                    out=PS[ab][cp:cp + 64, h * 64:(h + 1) * 64], lhsT=kdT[hp:hp + 64, j, cp:cp + 64],
                                    rhs=qdT[hp:hp + 64, j, cp:cp + 64], start=True, stop=True),
                                     reads=[dT_], writes=[PD[ab]], inc=(h >= 6))
                            for h2 in range(2):
                                ab = AB[h2]
                                S.op("dve", lambda e, cp=cp, ab=ab, h2=h2: e.tensor_tensor(
                                    out=hsel(attm[cp:cp + 64, :], h2), in0=hsel(PS[ab][cp:cp + 64, :], h2),
                                    in1=cf("maskH")[cp:cp + 64, :].unsqueeze(1).to_broadcast([64, 4, 64]), op=ALU.mult),
                                     reads=[PD[ab], dCF], writes=[dA])
                            for h in range(8):
                                j, h2 = h // 2, h % 2
                                hp = 64 * h2
                                hs = slice(h * 64, (h + 1) * 64)
                                S.op("pe", lambda e, hs=hs, cp=cp: e.matmul(
                                    out=PS[5][cp:cp + 64, hs], lhsT=attm[cp:cp + 64, hs], rhs=hv[cp:cp + 64, hs],
                                    start=True, stop=True), reads=[dA, dEb], writes=[PD[5]], inc=False)
                                xb = XB[h2]
                                S.op("pe", lambda e, hs=hs, cp=cp, hp=hp, j=j, xb=xb: e.matmul(
                                    out=PS[xb][cp:cp + 64, hs], lhsT=qdT[hp:hp + 64, j, cp:cp + 64], rhs=Sbf[hp:hp + 64, j, :],
                                    start=True, stop=True), reads=[dT_, dSb], writes=[PD[xb]], inc=False)
                                S.op("pe", lambda e, hs=hs, cp=cp, hp=hp, j=j: e.matmul(
                                    out=PS[7][hp:hp + 64, j * 64:(j + 1) * 64], lhsT=kk[cp:cp + 64, hs], rhs=hv[cp:cp + 64, hs],
                                    start=True, stop=True), reads=[dEb], writes=[PD[7]], inc=(h == 7))
                            S.op("dve", lambda e, c=c: e.tensor_tensor(
                                out=Sst[:], in0=Sst[:], in1=dec[:, :, c:c + 1].to_broadcast([128, 4, 64]), op=ALU.mult),
                                 reads=[dS_, dDec], writes=[dS_])
                            S.op("dve", lambda e: e.tensor_tensor(
                                out=Sst[:], in0=Sst[:], in1=PS[7][:, 0:256].rearrange("p (j v) -> p j v", j=4), op=ALU.add),
                                 reads=[dS_, PD[7]], writes=[dS_])
                            S.op("act", lambda e: e.activation(out=Sbf[:], in_=Sst[:], func=AF.Copy),
                                 reads=[dS_], writes=[dSb])
                            yield
                        if i == 15:
                            S.dma("sp", lambda e: e.dma_start(out=sp_o.rearrange("(j h) k v -> (h k) j v", h=2), in_=Sst[:]),
                                  dS_, "out")
                    else:
                        for h in range(8):
                            j, h2 = h // 2, h % 2
                            hp = 64 * h2
                            ab = AB[h2]
                            S.op("pe", lambda e, h=h, j=j, hp=hp, ab=ab: e.matmul(
                                out=PS[ab][0:64, h * 64:(h + 1) * 64], lhsT=kdT[hp:hp + 64, j, 0:64],
                                rhs=qdT[hp:hp + 64, j, 0:64], start=True, stop=True),
                                 reads=[dT_], writes=[PD[ab]], inc=(h >= 6))
                        for h2 in range(2):
                            ab = AB[h2]
                            S.op("dve", lambda e, ab=ab, h2=h2: e.tensor_tensor(
                                out=hsel(attm[0:64, :], h2), in0=hsel(PS[ab][0:64, :], h2),
                                in1=cf("maskHS")[0:64, :].unsqueeze(1).to_broadcast([64, 4, 64]), op=ALU.mult),
                                 reads=[PD[ab], dCF], writes=[dA])
                        S.op("dve", lambda e: e.tensor_tensor(
                            out=qdTm[:], in0=qdT[:, :, 0:64].unsqueeze(2).to_broadcast([128, 4, 16, 64]),
                            in1=smT[:].rearrange("p (b t) -> p b t", b=16).unsqueeze(1).to_broadcast([128, 4, 16, 64]),
                            op=ALU.mult), reads=[dT_, dsm], writes=[dqm])
                        for h in range(8):
                            hs = slice(h * 64, (h + 1) * 64)
                            S.op("pe", lambda e, hs=hs: e.matmul(out=PS[5][0:64, hs], lhsT=attm[0:64, hs], rhs=hv[0:64, hs],
                                                                 start=True, stop=True),
                                 reads=[dA, dEb], writes=[PD[5]], inc=(h == 7))
                        for j in range(4):
                            sb_ = j % 2
                            S0j = S0t[sb_]
                            S.dma("sp", lambda e, j=j, S0j=S0j: e.dma_start(out=S0j[:], in_=st0_v[:, j, :, :]), dS0[sb_], "in")
                            S.op("act", lambda e, S0j=S0j: e.activation(out=S0bt[:].rearrange("p b v -> p (b v)"),
                                                                        in_=S0j[:].rearrange("p b v -> p (b v)"), func=AF.Copy),
                                 reads=[dS0[sb_]], writes=[dS0b])
                            S.op("dve", lambda e, j=j: e.tensor_tensor(
                                out=hvm[:], in0=hv[0:64, j * 128:(j + 1) * 128].unsqueeze(1).to_broadcast([64, 16, 128]),
                                in1=cf("seqind", 64).unsqueeze(2).to_broadcast([64, 16, 128]), op=ALU.mult),
                                 reads=[dEb, dCF], writes=[dhvm])
                            for h2 in range(2):
                                h = 2 * j + h2
                                hp = 64 * h2
                                hs = slice(h * 64, (h + 1) * 64)
                                for b in range(16):
                                    S.op("pe", lambda e, hs=hs, hp=hp, j=j, b=b, h2=h2: e.matmul(
                                        out=PS[XB[h2]][0:64, hs], lhsT=qdTm[hp:hp + 64, j, b, :], rhs=S0bt[hp:hp + 64, b, :],
                                        start=(b == 0), stop=(b == 15)),
                                         reads=[dqm, dS0b], writes=[PD[XB[h2]]], inc=(b == 15))
                                for half in range(2):
                                    S.op("pe", lambda e, h=h, hp=hp, half=half, h2=h2: e.matmul(
                                        out=PS[half][hp:hp + 64, :], lhsT=kk[0:64, h * 64:(h + 1) * 64],
                                        rhs=hvm[0:64, half * 8:(half + 1) * 8, h2 * 64:(h2 + 1) * 64],
                                        start=True, stop=True), reads=[dEb, dhvm], writes=[PD[half]])
                            for half in range(2):
                                bs = slice(half * 8, (half + 1) * 8)
                                S.op("dve", lambda e, j=j, bs=bs, S0j=S0j: e.tensor_tensor(
                                    out=S0j[:, bs, :], in0=S0j[:, bs, :],
                                    in1=dec[:, j, bs].unsqueeze(2).to_broadcast([128, 8, 64]), op=ALU.mult),
                                     reads=[dS0[sb_], dDec], writes=[dS0[sb_]])
                                S.op("dve", lambda e, bs=bs, half=half, S0j=S0j: e.tensor_tensor(
                                    out=S0j[:, bs, :], in0=S0j[:, bs, :],
                                    in1=PS[half][:, :].rearrange("p (b v) -> p b v", b=8), op=ALU.add),
                                     reads=[dS0[sb_], PD[half]], writes=[dS0[sb_]])
                            S.dma("sp", lambda e, j=j, S0j=S0j: e.dma_start(out=ss_v[:, j, :, :], in_=S0j[:]), dS0[sb_], "out")

                    S.op("act", lambda e: e.activation(out=oo[0:T, :], in_=PS[5][0:T, :], func=AF.Copy),
                         reads=[PD[5]], writes=[dO])
                    for h2 in range(2):
                        S.op("dve", lambda e, h2=h2: e.tensor_tensor(out=hsel(oo[0:T, :], h2), in0=hsel(oo[0:T, :], h2),
                                                              in1=hsel(PS[XB[h2]][0:T, :], h2), op=ALU.add),
                             reads=[dO, PD[XB[h2]]], writes=[dO])
                    S.op("dve", lambda e: e.tensor_tensor(out=osq[0:T, :], in0=oo[0:T, :], in1=oo[0:T, :], op=ALU.mult),
                         reads=[dO], writes=[dO])
                    S.op("dve", lambda e: e.tensor_reduce(out=st8[0:T, 0:8], in_=osq[0:T, :].rearrange("p (h v) -> p h v", h=8),
                                                          axis=AX.X, op=ALU.add), reads=[dO], writes=[dO])
                    rsqrt_small(st8[0:T, 0:8], st8[0:T, 16:24], 1.0 / 64, [dO], dO, st8[0:T, 8:16])
                    S.op("dve", lambda e: e.tensor_tensor(
                        out=oo[0:T, :].rearrange("p (h v) -> p h v", h=8), in0=oo[0:T, :].rearrange("p (h v) -> p h v", h=8),
                        in1=st8[0:T, 16:24].unsqueeze(2).to_broadcast([T, 8, 64]), op=ALU.mult), reads=[dO], writes=[dO])
                    S.op("dve", lambda e: e.tensor_tensor(out=oo[0:T, :], in0=oo[0:T, :], in1=qkgB[0:T, 1024:1536], op=ALU.mult),
                         reads=[dO, dPar], writes=[dO])
                    S.op("dve", lambda e: e.tensor_tensor(out=oo[0:T, :], in0=oo[0:T, :], in1=sg[0:T, :], op=ALU.mult),
                         reads=[dO, dEb], writes=[dO])
                    for j in range(4):
                        S.op("pe", lambda e, j=j: e.transpose(out=PS[4][:, j * 128:j * 128 + T], in_=oo[0:T, j * 128:(j + 1) * 128],
                                                              identity=cf("ident", T, 0, T)),
                             reads=[dO, dCF], writes=[PD[4]], inc=(j == 3))
                    S.op("act", lambda e: e.activation(out=mixT[:, 4:8, c0:c1],
                                                       in_=PS[4][:, :].rearrange("p (j t) -> p j t", j=4)[:, :, 0:T],
                                                       func=AF.Copy), reads=[PD[4]], writes=[dMixB[i]])
                gens = [tile_gen(i) for i in range(NT)]
                for _ in range(3):
                    next(gens[0])
                for i in range(NT):
                    nx = gens[i + 1] if i + 1 < NT else None
                    if nx is not None:
                        next(nx)
                    next(gens[i], None)
                    if nx is not None:
                        next(nx)
                    next(gens[i], None)
                    if nx is not None:
                        next(nx)
                    for _ in gens[i]:
                        pass
                S.barrier()
                chk(2)

            pa_r = contextlib.ExitStack()
            qT = sb([128, 4, NTOK], BF16, side="right", stack=pa_r)
            kT = sb([128, 4, NTOK], BF16, side="right", stack=pa_r)
            vres = sb([128, NT, 512], BF16, side="right", stack=pa_r)
            dQ = [dep("qT") for _ in range(NT)]
            dK = [dep("kT") for _ in range(NT)]
            dV = [dep("v") for _ in range(NT)]
            with contextlib.ExitStack() as pa:
                wat = sb([128, 8, 1536], BF16, stack=pa)
                dwat = [dep("wat") for _ in range(3)]
                with contextlib.ExitStack() as pw:
                    wst = [sb([128, 8, 512], F32, stack=pw) for _ in range(2)]
                    dW = [dep("wst") for _ in range(2)]
                    for g in range(3):
                        load_weight_piece(w_in_v[:, :, g * 512:(g + 1) * 512], wat[:, :, g * 512:(g + 1) * 512],
                                          wst, dW, g % 2, dwat[g])
                    S.barrier()
                sq = sb([128, 512], F32, stack=pa)
                qn = [sb([128, 512], F32, stack=pa) for _ in range(2)]
                dqn = [dep("qn") for _ in range(2)]
                kn = [sb([128, 512], F32, stack=pa) for _ in range(2)]
                dkn = [dep("kn") for _ in range(2)]
                vn = [sb([128, 512], F32, stack=pa) for _ in range(2)]
                dvn = [dep("vn") for _ in range(2)]
                s8 = sb([128, 24], F32, stack=pa)
                dsq = dep("sq")
                for i in range(NT):
                    T = tile_rows(i)
                    c0, c1 = tile_cols(i)
                    b = i % 2

                    def proj(g, bank):
                        for k in range(8):
                            S.op("pe", lambda e, k=k: e.matmul(out=PS[bank][0:T, :], lhsT=hT[:, k, c0:c1],
                                                               rhs=wat[:, k, g * 512:(g + 1) * 512],
                                                               start=(k == 0), stop=(k == 7)),
                                 reads=[dhT[i], dwat[g]], writes=[PD[bank]], inc=(k == 7))

                    def qknorm(bank, dst, ddst, goff):
                        S.op("act", lambda e: e.activation(out=sq[0:T, :], in_=PS[bank][0:T, :], func=AF.Square),
                             reads=[PD[bank]], writes=[dsq])
                        S.op("dve", lambda e: e.tensor_reduce(out=s8[0:T, 0:8], in_=sq[0:T, :].rearrange("p (h d) -> p h d", h=8),
                                                              axis=AX.X, op=ALU.add), reads=[dsq], writes=[dsq])
                        rsqrt_small(s8[0:T, 0:8], s8[0:T, 16:24], 1.0 / 64, [dsq], dsq, s8[0:T, 8:16])
                        S.op("dve", lambda e: e.tensor_tensor(
                            out=dst[0:T, :].rearrange("p (h d) -> p h d", h=8),
                            in0=PS[bank][0:T, :].rearrange("p (h d) -> p h d", h=8),
                            in1=s8[0:T, 16:24].unsqueeze(2).to_broadcast([T, 8, 64]), op=ALU.mult),
                             reads=[PD[bank], dsq], writes=[ddst])
                        S.op("dve", lambda e: e.tensor_tensor(out=dst[0:T, :], in0=dst[0:T, :], in1=qkgB[0:T, goff:goff + 512],
                                                              op=ALU.mult), reads=[ddst, dPar], writes=[ddst])

                    def to_featT(src, dsrc, bank, dst, ddst):
                        for j in range(4):
                            S.op("pe", lambda e, j=j: e.transpose(out=PS[bank][:, j * 128:j * 128 + T],
                                                                  in_=src[0:T, j * 128:(j + 1) * 128],
                                                                  identity=cf("ident", T, 0, T)),
                                 reads=[dsrc, dCF], writes=[PD[bank]], inc=(j == 3))
                        S.op("act", lambda e: e.activation(out=dst[:, :, c0:c1],
                                                           in_=PS[bank][:, :].rearrange("p (j t) -> p j t", j=4)[:, :, 0:T],
                                                           func=AF.Copy), reads=[PD[bank]], writes=[ddst])

                    proj(0, 0)
                    proj(1, 1)
                    proj(2, 2)
                    qknorm(0, qn[b], dqn[b], 0)
                    S.op("act", lambda e: e.activation(out=vn[b][0:T, :], in_=PS[2][0:T, :], func=AF.Copy),
                         reads=[PD[2]], writes=[dvn[b]])
                    qknorm(1, kn[b], dkn[b], 512)
                    to_featT(qn[b], dqn[b], 3, qT, dQ[i])
                    to_featT(kn[b], dkn[b], 4, kT, dK[i])
                    kdst = kp[i * 128:(i + 1) * 128, :] if i < 16 else ks[:, :]
                    S.dma("sp", lambda e: e.dma_start(out=kdst, in_=kn[b][0:T, :]), dkn[b], "out")
                    S.op("dve", lambda e: e.tensor_copy(out=vres[0:T, i, :], in_=vn[b][0:T, :]),
                         reads=[dvn[b]], writes=[dV[i]])
                    vdst = vp[i * 128:(i + 1) * 128, :] if i < 16 else vs[:, :]
                    S.dma("sp", lambda e: e.dma_start(out=vdst, in_=vn[b][0:T, :]), dvn[b], "out")
                S.barrier()
                chk(3)
        p1s.close()

        with contextlib.ExitStack() as p2:
            ulT = sb([128, 256], BF16, stack=p2)
            dCB = dep("cb")
            uincl = ulT[:, 0:128]
            lstrict = ulT[:, 128:256]
            qblk = sb([128, 4, 16, 2, 4], BF16, stack=p2)
            kTp = sb([128, 4, 128], BF16, stack=p2)
            vpad = sb([128, 512], BF16, stack=p2)
            ebF = sb([128, 512], F32, stack=p2)
            mS2 = sb([128, 128], F32, stack=p2)
            dpad = dep("pad")
            with contextlib.ExitStack() as p2a:
                NSET = 8
                eT = [sb([128, 512], F32, stack=p2a) for _ in range(NSET)]
                xT_ = [sb([128, 512], F32, stack=p2a) for _ in range(NSET)]
                LpT = [sb([128, 512], BF16, stack=p2a) for _ in range(NSET)]
                wT = [sb([128, 512], BF16, stack=p2a) for _ in range(NSET)]
                de = [dep("e") for _ in range(NSET)]
                dx = [dep("x") for _ in range(NSET)]
                dL = [dep("L") for _ in range(NSET)]
                dw = [dep("w") for _ in range(NSET)]
                mDT = sb([128, 2048], BF16, stack=p2a)
                load_bf_const("uincl", ulT[:, 0:128], dCB, eT[0], de[0])
                load_bf_const("lstrict", ulT[:, 128:256], dCB, eT[1], de[1])
                load_bf_const("maskD", mDT, dCB, eT[0], de[0])
                def stream_units(qbs):
                    out = []
                    for j in range(4):
                        for QB in qbs:
                            nkb = 4 * QB + 4
                            for st, kb in enumerate(range(nkb - 1, -1, -1)):
                                for h2 in range(2):
                                    out.append((j, QB, st, kb, h2))
                    return out
                ua, ub = stream_units([3, 0]), stream_units([2, 1])
                units = []
                for n_ in range(max(len(ua), len(ub))):
                    if n_ < len(ua):
                        units.append(ua[n_] + (0,))
                    if n_ < len(ub):
                        units.append(ub[n_] + (1,))

                def geom(u):
                    j, QB, st, kb, h2, sid = u
                    ii = kb - 4 * QB
                    clo = 128 * ii if ii > 0 else 0
                    return ii, clo, slice(clo, 512), slice(QB * 512 + clo, (QB + 1) * 512)

                def parms(n):
                    j, QB, st, kb, h2, sid = units[n]
                    ii, clo, cs, qs = geom(units[n])
                    return j, QB, st, kb, h2, sid, ii, clo, cs, qs, 2 * j + h2, 64 * h2, n % NSET

                def a1(n):
                    j, QB, st, kb, h2, sid, ii, clo, cs, qs, h, hp, si = parms(n)
                    qdeps = [dQ[t] for t in range(QB * 4, QB * 4 + 4)]
                    S.op("pe", lambda e: e.matmul(
                        out=PS[h2][:, cs], lhsT=kT[hp:hp + 64, j, kb * 128:(kb + 1) * 128], rhs=qT[hp:hp + 64, j, qs],
                        start=True, stop=True), reads=[dK[kb]] + qdeps, writes=[PD[h2]])

                def a2(n):
                    j, QB, st, kb, h2, sid, ii, clo, cs, qs, h, hp, si = parms(n)
                    S.op("act", lambda e: e.activation(
                        out=eT[si][:, cs], in_=PS[h2][:, cs], func=AF.Exp, scale=SCALE, bias=biasT[:, h:h + 1]),
                         reads=[PD[h2], dPar], writes=[de[si]])
                    if ii >= 0:
                        S.op("dve", lambda e: e.tensor_tensor(
                            out=eT[si][:, cs], in0=eT[si][:, cs], in1=mDT[:, ii * 512 + clo:(ii + 1) * 512],
                            op=ALU.mult), reads=[de[si], dCB], writes=[de[si]])
                    S.op("act", lambda e: e.activation(out=LpT[si][:, cs], in_=eT[si][:, cs], func=AF.Ln, bias=1.0),
                         reads=[de[si]], writes=[dL[si]])

                def cbank(sid, h2):
                    return (2 + h2) if sid == 0 else (5 + h2)

                def b1(n):
                    j, QB, st, kb, h2, sid, ii, clo, cs, qs, h, hp, si = parms(n)
                    Cb = cbank(sid, h2)
                    S.op("pe", lambda e: e.matmul(
                        out=PS[Cb][:, cs], lhsT=uincl, rhs=LpT[si][:, cs], start=(st == 0), stop=False,
                        skip_group_check=True), reads=[dL[si], dCB], writes=[PD[Cb]])
                    S.op("act", lambda e: e.activation(out=xT_[si][:, cs], in_=PS[Cb][:, cs], func=AF.Exp, scale=-1.0),
                         reads=[PD[Cb]], writes=[dx[si]])

                def b2(n):
                    j, QB, st, kb, h2, sid, ii, clo, cs, qs, h, hp, si = parms(n)
                    Cb = cbank(sid, h2)
                    if kb > 0:
                        S.op("pe", lambda e: e.matmul(
                            out=PS[Cb][:, cs], lhsT=lstrict, rhs=LpT[si][:, cs], start=False, stop=(kb == 1),
                            skip_group_check=True), reads=[dL[si], dCB], writes=[PD[Cb]])
                    S.op("dve", lambda e: e.tensor_tensor(out=wT[si][:, cs], in0=eT[si][:, cs], in1=xT_[si][:, cs], op=ALU.mult),
                         reads=[de[si], dx[si]], writes=[dw[si]])

                def b3(n):
                    j, QB, st, kb, h2, sid, ii, clo, cs, qs, h, hp, si = parms(n)
                    ob = 4 if sid == 0 else 7
                    S.op("pe", lambda e: e.matmul(
                        out=PS[ob][hp:hp + 64, cs], lhsT=vres[:, kb, h * 64:(h + 1) * 64], rhs=wT[si][:, cs],
                        start=(st == 0), stop=(kb == 0), skip_group_check=True),
                         reads=[dw[si], dV[kb]], writes=[PD[ob]])
                    if kb == 0 and h2 == 1:
                        S.op("act", lambda e: e.activation(out=mixT[:, j, QB * 512:(QB + 1) * 512], in_=PS[ob][:, :],
                                                           func=AF.Copy),
                             reads=[PD[ob]], writes=[dMixA[QB * 4 + tt] for tt in range(4)])

                a1(0)
                a2(0)
                for n in range(len(units)):
                    b1(n)
                    if n + 1 < len(units):
                        a1(n + 1)
                    b2(n)
                    if n + 1 < len(units):
                        a2(n + 1)
                    b3(n)
                S.op("dve", lambda e: e.memset(kTp[:], 0.0), writes=[dpad])
                S.op("dve", lambda e: e.memset(vpad[:], 0.0), writes=[dpad])
                S.op("dve", lambda e: e.memset(qblk[:].rearrange("p j b a t -> p (j b a t)"), 0.0), writes=[dpad])
                S.op("dve", lambda e: e.tensor_copy(out=kTp[:, :, 0:64], in_=kT[:, :, TP:NTOK]), reads=[dK[16], dpad], writes=[dpad])
                S.op("dve", lambda e: e.tensor_copy(out=vpad[0:64, :], in_=vres[0:64, 16, :]), reads=[dV[16], dpad], writes=[dpad])
                for h2 in range(2):
                    S.op("dve", lambda e, h2=h2: e.tensor_copy(
                        out=qblk[64 * h2:64 * h2 + 64, :, :, h2, :],
                        in_=qT[64 * h2:64 * h2 + 64, :, TP:NTOK].rearrange("p j (b t) -> p j b t", t=4)),
                         reads=[dQ[16], dpad], writes=[dpad])
                S.op("act", lambda e: e.activation(out=eT[0][:, 0:8], in_=biasT[:, :], func=AF.Exp), reads=[dPar, de[0]],
                     writes=[de[0]])
                for j in range(4):
                    S.op("dve", lambda e, j=j: e.tensor_copy(
                        out=ebF[:, j * 128:(j + 1) * 128].rearrange("p (b a t) -> p b a t", b=16, a=2),
                        in_=eT[0][:, 2 * j:2 * j + 2].unsqueeze(1).unsqueeze(3).to_broadcast([128, 16, 2, 4])),
                         reads=[de[0]], writes=[dpad])
                S.op("dve", lambda e: e.tensor_copy(
                    out=mS2[:].rearrange("p (b a t) -> p b a t", b=16, a=2),
                    in_=cf("maskS").rearrange("p (b t) -> p b t", t=4).unsqueeze(2).to_broadcast([128, 16, 2, 4])),
                     reads=[dCF], writes=[dpad])
                S.barrier()
            chk(4)
            pa_r.close()

            NQ = 8
            GW = NQ * 8
            Kst = [sb([128, NQ, 512], F32, stack=p2) for _ in range(2)]
            Vst = [sb([128, NQ, 512], F32, stack=p2) for _ in range(2)]
            KTs = [sb([128, NQ, 4, 128], BF16, stack=p2) for _ in range(2)]
            Vb = [sb([128, NQ, 512], BF16, stack=p2) for _ in range(2)]
            dVb = [dep("Vb") for _ in range(2)]
            dKst = [dep("Kst") for _ in range(2)]
            dVst = [dep("Vst") for _ in range(2)]
            dKTs = [dep("KTs") for _ in range(2)]
            ptb = sb([128, NSEQ * NPG], I32, stack=p2)
            idx = sb([128, NSEQ * NPG], I32, stack=p2)
            iop = sb([128, 1], I32, stack=p2)
            dIdx = dep("idx")
            es_ = [sb([128, 4, 128], F32, stack=p2) for _ in range(2)]
            xs2 = [sb([128, 4, 128], F32, stack=p2) for _ in range(2)]
            Ls = [sb([128, 4, 128], BF16, stack=p2) for _ in range(2)]
            ws = [sb([128, 4, 128], BF16, stack=p2) for _ in range(2)]
            wsb = sb([128, 4, 128], BF16, stack=p2)
            des = [dep("es") for _ in range(2)]
            dxs = [dep("xs") for _ in range(2)]
            dLs = [dep("Ls") for _ in range(2)]
            dws = [dep("ws") for _ in range(2)]

            S.dma("pool", lambda e: e.dma_start(out=ptb[:], in_=pt[0:1, :].to_broadcast([128, NSEQ * NPG])), dIdx, "in")
            S.op("pool", lambda e: e.iota(iop[:], pattern=[[0, 1]], base=0, channel_multiplier=1), writes=[dIdx])
            S.op("pool", lambda e: e.tensor_scalar(out=idx[:], in0=ptb[:], scalar1=128, scalar2=None, op0=ALU.mult),
                 reads=[dIdx], writes=[dIdx])
            S.op("pool", lambda e: e.tensor_tensor(out=idx[:], in0=idx[:], in1=iop[:].to_broadcast([128, NSEQ * NPG]),
                                                   op=ALU.add), reads=[dIdx], writes=[dIdx])

            dZ, dC, dOs = dep("Z"), dep("C"), dep("Os")

            def bview(bank, c0, w_):
                return PS[bank][:, :].rearrange("p (j c) -> p j c", j=4)[:, :, c0:c0 + w_]

            def sample_step(kind, gi=None, buf=None, last=False, first=False, b2=0):
                c0, w_ = (0, 128) if kind == "new" else (gi * GW, GW)
                if kind == "new":
                    for j in range(4):
                        S.op("pe", lambda e, j=j: e.matmul(out=PS[0][:, j * 128:(j + 1) * 128], lhsT=kTp[:, j, :],
                                                           rhs=qblk[:, j, :, :, :].rearrange("p b a t -> p (b a t)"),
                                                           start=True, stop=True), reads=[dpad], writes=[dZ], inc=(j == 3))
                else:
                    for bi in range(NQ):
                        b = gi * NQ + bi
                        for j in range(4):
                            S.op("pe", lambda e, bi=bi, b=b, j=j: e.matmul(
                                out=PS[0][:, j * 128 + b * 8:j * 128 + b * 8 + 8], lhsT=KTs[buf][:, bi, j, :],
                                rhs=qblk[:, j, b, :, :].rearrange("p a t -> p (a t)"), start=True, stop=True),
                                 reads=[dKTs[buf], dpad], writes=[dZ], inc=(bi == NQ - 1 and j == 3))
                ev = es_[b2][:, :, 0:w_]
                S.op("act", lambda e: e.activation(out=ev, in_=bview(0, c0, w_), func=AF.Exp, scale=SCALE),
                     reads=[dZ], writes=[des[b2]])
                S.op("dve", lambda e: e.tensor_tensor(out=ev, in0=ev,
                                                      in1=ebF[:, :].rearrange("p (j c) -> p j c", j=4)[:, :, c0:c0 + w_],
                                                      op=ALU.mult), reads=[des[b2], dpad], writes=[des[b2]])
                if kind == "new":
                    S.op("dve", lambda e: e.tensor_tensor(out=ev, in0=ev, in1=mS2[:, :].unsqueeze(1).to_broadcast([128, 4, 128]),
                                                          op=ALU.mult), reads=[des[b2], dpad], writes=[des[b2]])
                S.op("act", lambda e: e.activation(out=Ls[b2][:, :, 0:w_], in_=ev, func=AF.Ln, bias=1.0),
                     reads=[des[b2]], writes=[dLs[b2]])
                for j in range(4):
                    S.op("pe", lambda e, j=j: e.matmul(out=PS[1][:, j * 128 + c0:j * 128 + c0 + w_], lhsT=uincl,
                                                       rhs=Ls[b2][:, j, 0:w_], start=(first and j == 0), stop=False,
                                                       skip_group_check=True),
                         reads=[dLs[b2], dCB], writes=[dC], inc=(j == 3))
                S.op("act", lambda e: e.activation(out=xs2[b2][:, :, 0:w_], in_=bview(1, c0, w_), func=AF.Exp, scale=-1.0),
                     reads=[dC], writes=[dxs[b2]])
                if not last:
                    for j in range(4):
                        S.op("pe", lambda e, j=j: e.matmul(out=PS[1][:, j * 128 + c0:j * 128 + c0 + w_], lhsT=lstrict,
                                                           rhs=Ls[b2][:, j, 0:w_], start=False, stop=False,
                                                           skip_group_check=True),
                             reads=[dLs[b2], dCB], writes=[dC], inc=(j == 3))
                if kind == "new":
                    S.op("dve", lambda e: e.tensor_tensor(out=wsb[:, :, :], in0=ev, in1=xs2[b2][:, :, 0:w_], op=ALU.mult),
                         reads=[des[b2], dxs[b2]], writes=[dws[b2]])
                    for j in range(4):
                        S.op("pe", lambda e, j=j: e.matmul(out=PS[2][:, j * 128:(j + 1) * 128],
                                                           lhsT=vpad[:, j * 128:(j + 1) * 128], rhs=wsb[:, j, :],
                                                           start=(j == 0), stop=False, skip_group_check=True),
                             reads=[dws[b2], dpad], writes=[dOs], inc=(j == 3))
                else:
                    S.op("dve", lambda e: e.tensor_tensor(out=ws[b2][:, :, 0:w_], in0=ev, in1=xs2[b2][:, :, 0:w_], op=ALU.mult),
                         reads=[des[b2], dxs[b2]], writes=[dws[b2]])
                    for bi in range(NQ):
                        b = gi * NQ + bi
                        for j in range(4):
                            S.op("pe", lambda e, bi=bi, b=b, j=j: e.matmul(
                                out=PS[2][:, j * 128 + b * 8:j * 128 + b * 8 + 8], lhsT=Vb[buf][:, bi, j * 128:(j + 1) * 128],
                                rhs=ws[b2][:, j, bi * 8:bi * 8 + 8], start=False, stop=False, skip_group_check=True),
                                 reads=[dws[b2], dVb[buf]], writes=[dOs], inc=(bi == NQ - 1 and j == 3))

            sample_step("new", first=True)
            gcount = 0
            for p in range(NPG - 1, -1, -1):
                for gi in range(NSEQ // NQ):
                    buf = gcount % 2
                    gcount += 1
                    for bi in range(NQ):
                        b = gi * NQ + bi
                        col = b * NPG + p
                        S.dma("pool", lambda e, bi=bi, col=col, buf=buf: e.indirect_dma_start(
                            out=Kst[buf][:, bi, :], out_offset=None, in_=ck,
                            in_offset=bass.IndirectOffsetOnAxis(ap=idx[:, col:col + 1], axis=0)),
                              dKst[buf], "in", extra_reads=[dIdx])
                        S.dma("pool", lambda e, bi=bi, col=col, buf=buf: e.indirect_dma_start(
                            out=Vst[buf][:, bi, :], out_offset=None, in_=cv,
                            in_offset=bass.IndirectOffsetOnAxis(ap=idx[:, col:col + 1], axis=0)),
                              dVst[buf], "in", extra_reads=[dIdx])
                    for bi in range(NQ):
                        bank = 5 + (bi % 2)
                        for jj in range(4):
                            S.op("pe", lambda e, bi=bi, jj=jj, bank=bank, buf=buf: e.transpose(
                                out=PS[bank][:, jj * 128:(jj + 1) * 128], in_=Kst[buf][:, bi, jj * 128:(jj + 1) * 128],
                                identity=ident), reads=[dKst[buf], dCF], writes=[PD[bank]], inc=(jj == 3))
                        if bi % 2 == 0:
                            S.op("dve", lambda e, bi=bi, bank=bank, buf=buf: e.tensor_copy(
                                out=KTs[buf][:, bi, :, :].rearrange("p j t -> p (j t)"), in_=PS[bank][:, :]),
                                 reads=[PD[bank]], writes=[dKTs[buf]])
                        else:
                            S.op("act", lambda e, bi=bi, bank=bank, buf=buf: e.activation(
                                out=KTs[buf][:, bi, :, :].rearrange("p j t -> p (j t)"), in_=PS[bank][:, :], func=AF.Copy),
                                 reads=[PD[bank]], writes=[dKTs[buf]])
                    for hv_ in range(2):
                        S.op("act" if hv_ == 0 else "dve",
                             (lambda e, buf=buf: e.activation(
                                 out=Vb[buf][:, 0:NQ // 2, :].rearrange("p b n -> p (b n)"),
                                 in_=Vst[buf][:, 0:NQ // 2, :].rearrange("p b n -> p (b n)"), func=AF.Copy)) if hv_ == 0 else
                             (lambda e, buf=buf: e.tensor_copy(
                                 out=Vb[buf][:, NQ // 2:NQ, :].rearrange("p b n -> p (b n)"),
                                 in_=Vst[buf][:, NQ // 2:NQ, :].rearrange("p b n -> p (b n)"))),
                             reads=[dVst[buf]], writes=[dVb[buf]])
                    sample_step("page", gi=gi, buf=buf, last=(p == 0), b2=gcount % 2)
            for h2 in range(2):
                S.op("act", lambda e, h2=h2: e.activation(
                    out=mixT[64 * h2:64 * h2 + 64, 0:4, TP:NTOK].rearrange("p j (b t) -> p j b t", t=4),
                    in_=PS[2][64 * h2:64 * h2 + 64, :].rearrange("p (j b a t) -> p j b a t", j=4, b=16, a=2)[:, :, :, h2, :],
                    func=AF.Copy), reads=[dOs], writes=[dMixA[16]])
            S.barrier()

        w_out_v = w_out.rearrange("(k p) n -> p k n", p=128)
        w_up_v = w_up.rearrange("(k p) n -> p k n", p=128)
        w_down_v = w_down.rearrange("(c p) n -> p c n", p=128)
        w_ada_v = w_ada.rearrange("(k p) n -> p k n", p=128)
        gBp = sb([128, 2048], F32)
        gBs = sb([64, 2048], F32)
        dG = dep("gB")
        wst = [sb([128, 8, 256], F32) for _ in range(2)]
        dW = [dep("wst") for _ in range(2)]
        with contextlib.ExitStack() as pg:
            cTp = sb([128, 8, 128], F32, stack=pg)
            cTs = sb([128, 8, 64], F32, stack=pg)
            bgB = sb([128, 2048], F32, stack=pg)
            dbg = dep("bgB")
            dcTx = dep("cTx")
            S.dma("sp", lambda e: e.dma_start(out=bgB[:], in_=bg[0:1, :].to_broadcast([128, 2048])), dbg, "in")
            S.op("dve", lambda e: e.tensor_copy(out=cTp[:], in_=cT[:, :, 16:17].to_broadcast([128, 8, 128])),
                 reads=[dcT], writes=[dcTx])
            S.op("dve", lambda e: e.tensor_copy(out=cTs[:].rearrange("p c (b t) -> p c b t", t=4),
                                                in_=cT[:, :, 0:16].unsqueeze(3).to_broadcast([128, 8, 16, 4])),
                 reads=[dcT], writes=[dcTx])
            for n_ in range(8):
                buf = n_ % 2
                acol = (2048 if n_ < 4 else 5120) + (n_ % 4) * 256
                gcol = n_ * 256
                S.dma("sp", lambda e, acol=acol, buf=buf: e.dma_start(out=wst[buf][:], in_=w_ada_v[:, :, acol:acol + 256]),
                      dW[buf], "in")
                for (lhs, rows, dst, bank) in ((cTp, 128, gBp, 2), (cTs, 64, gBs, 3)):
                    for k in range(8):
                        S.op("pe", lambda e, k=k, lhs=lhs, rows=rows, bank=bank, buf=buf: e.matmul(
                            out=PS[bank][0:rows, 0:256], lhsT=lhs[:, k, :], rhs=wst[buf][:, k, :],
                            start=(k == 0), stop=(k == 7)),
                             reads=[dcTx, dW[buf]], writes=[PD[bank]], inc=(k == 7))
                    S.op("dve", lambda e, rows=rows, dst=dst, bank=bank, gcol=gcol: e.tensor_tensor(
                        out=dst[0:rows, gcol:gcol + 256], in0=PS[bank][0:rows, 0:256], in1=bgB[0:rows, gcol:gcol + 256],
                        op=ALU.add), reads=[PD[bank], dbg], writes=[dG])
            S.barrier()
            chk(6)

        def wst_flat(buf, c):
            return wst[buf][:].rearrange("p k n -> p (k n)").rearrange("p (c n) -> p c n", c=c)

        halves = [list(range(0, 8)), list(range(8, 17))]
        for hi, tiles in enumerate(halves):
            with contextlib.ExitStack() as p3:
                nt = len(tiles)
                ncols = sum(tile_rows(i) for i in tiles)
                col0 = tiles[0] * 128
                x1 = sb([128, nt, D], F32, stack=p3)
                dx1 = [dep("x1") for _ in range(nt)]
                h2T = sb([128, 8, ncols], BF16, stack=p3)
                dh2 = [dep("h2T") for _ in range(nt)]
                with contextlib.ExitStack() as p3a:
                    wo = sb([128, 8, D], BF16, stack=p3a)
                    dwo = [dep("wo") for _ in range(4)]
                    for g in range(4):
                        load_weight_piece(w_out_v[:, :, g * 256:(g + 1) * 256], wo[:, :, g * 256:(g + 1) * 256], wst, dW, g % 2,
                                          dwo[g])
                    xt = [sb([128, D], F32, stack=p3a) for _ in range(2)]
                    dxt = [dep("xt") for _ in range(2)]
                    wk = [(sb([128, D], F32, stack=p3a), sb([128, D], F32, stack=p3a), sb([128, 4], F32, stack=p3a))
                          for _ in range(2)]
                    dwk = [dep("wk") for _ in range(2)]
                    tmp = [sb([128, 512], F32, stack=p3a) for _ in range(2)]
                    dtm = [dep("tmp") for _ in range(2)]
                    def outproj(li):
                        i = tiles[li]
                        T = tile_rows(i)
                        c0, c1 = tile_cols(i)
                        b = li % 2
                        load_x_tile(i, xt[b], dxt[b])
                        gB = gBp if i < 16 else gBs
                        for nh in range(2):
                            bank = nh
                            ns = slice(nh * 512, (nh + 1) * 512)
                            for k in range(8):
                                md = dMixA[i] if k < 4 else dMixB[i]
                                S.op("pe", lambda e, k=k, bank=bank, ns=ns: e.matmul(
                                    out=PS[bank][0:T, :], lhsT=mixT[:, k, c0:c1], rhs=wo[:, k, ns], start=(k == 0), stop=(k == 7)),
                                     reads=[md, dwo[2 * nh], dwo[2 * nh + 1]], writes=[PD[bank]], inc=(k == 7))
                            S.op("dve", lambda e, bank=bank, ns=ns, nh=nh: e.tensor_tensor(
                                out=tmp[nh][0:T, :], in0=PS[bank][0:T, :], in1=gB[0:T, ns], op=ALU.mult),
                                 reads=[PD[bank], dG], writes=[dtm[nh]])
                            S.op("pool", lambda e, ns=ns, nh=nh: e.tensor_tensor(
                                out=x1[0:T, li, ns], in0=tmp[nh][0:T, :], in1=xt[b][0:T, ns], op=ALU.add),
                                 reads=[dtm[nh], dxt[b]], writes=[dx1[li]])

                    def norm2(li):
                        i = tiles[li]
                        T = tile_rows(i)
                        c0, c1 = tile_cols(i)
                        b = li % 2
                        lc0 = c0 - col0
                        norm_transpose(i, x1[0:T, li, :], dx1[li], 2, h2T[:, :, lc0:lc0 + T], dh2[li], wk[b], dwk[b],
                                       (2, 3) if b == 0 else (4, 5))

                    outproj(0)
                    for li in range(nt):
                        if li + 1 < nt:
                            outproj(li + 1)
                        norm2(li)
                    S.barrier()
                    chk(7)
                with contextlib.ExitStack() as p4:
                    wu = [sb([128, 8, 512], BF16, stack=p4) for _ in range(2)]
                    wd = [sb([128, 4, D], BF16, stack=p4) for _ in range(2)]
                    dwu = [dep("wu") for _ in range(2)]
                    dwd = [dep("wd") for _ in range(2)]
                    upT = [sb([128, 4, 512], BF16, stack=p4) for _ in range(2)]
                    dup = [dep("up") for _ in range(2)]
                    rl = [sb([128, 512], F32, stack=p4) for _ in range(2)]
                    drl = [dep("rl") for _ in range(2)]
                    tmp = [sb([128, 512], F32, stack=p4) for _ in range(2)]
                    dtm = [dep("tmp") for _ in range(2)]
                    groups = []
                    li = 0
                    while li < nt:
                        g = [l for l in range(li, min(li + 4, nt)) if tile_rows(tiles[l]) == 128]
                        if not g:
                            g = [li]
                        groups.append(g)
                        li = g[-1] + 1
                    gctr = 0
                    rctr = 0
                    dx1h = [[dep("x1h") for _ in range(2)] for _ in range(nt)]
                    for l_ in range(nt):
                        for nh_ in range(2):
                            dx1h[l_][nh_].w = dx1[l_].w
                    def load_w(E):
                        wb = E % 2
                        for hh in range(2):
                            S.dma("sp", lambda e, hh=hh: e.dma_start(
                                out=wst[0][:], in_=w_up_v[:, :, E * 512 + hh * 256:E * 512 + (hh + 1) * 256]), dW[0], "in")
                            S.op("act", lambda e, hh=hh: e.activation(out=wu[wb][:, :, hh * 256:(hh + 1) * 256], in_=wst[0][:],
                                                                      func=AF.Copy),
                                 reads=[dW[0]], writes=[dwu[wb]])
                            S.dma("sp", lambda e, hh=hh: e.dma_start(
                                out=wst_flat(1, 2), in_=w_down_v[:, E * 4 + hh * 2:E * 4 + (hh + 1) * 2, :]), dW[1], "in")
                            S.op("pool", lambda e, hh=hh: e.tensor_copy(out=wd[wb][:, hh * 2:(hh + 1) * 2, :], in_=wst_flat(1, 2)),
                                 reads=[dW[1]], writes=[dwd[wb]])

                    def up_part(E, g, ub):
                        wb = E % 2
                        gcol0 = tiles[g[0]] * 128 - col0
                        gn = sum(tile_rows(tiles[l]) for l in g)
                        for fc in range(4):
                            bank = fc % 2
                            rb = fc % 2
                            for k in range(8):
                                S.op("pe", lambda e, k=k, fc=fc, bank=bank: e.matmul(
                                    out=PS[bank][:, 0:gn], lhsT=wu[wb][:, k, fc * 128:(fc + 1) * 128],
                                    rhs=h2T[:, k, gcol0:gcol0 + gn], start=(k == 0), stop=(k == 7)),
                                     reads=[dwu[wb]] + [dh2[l] for l in g], writes=[PD[bank]], inc=(k == 7))
                            S.op("act", lambda e, bank=bank, rb=rb: e.activation(out=rl[rb][:, 0:gn], in_=PS[bank][:, 0:gn],
                                                                                 func=AF.Relu),
                                 reads=[PD[bank]], writes=[drl[rb]])
                            S.op("act", lambda e, rb=rb, fc=fc: e.activation(
                                out=upT[ub][:, fc, 0:gn], in_=rl[rb][:, 0:gn], func=AF.Square),
                                 reads=[drl[rb]], writes=[dup[ub]])

                    def down_part(E, g, ub):
                        wb = E % 2
                        gcol0 = tiles[g[0]] * 128 - col0
                        for l in g:
                            i = tiles[l]
                            T = tile_rows(i)
                            lc = tiles[l] * 128 - col0 - gcol0
                            gB = gBp if i < 16 else gBs
                            for nh in range(2):
                                bank = 2 + nh + 2 * (l % 2)
                                ns = slice(nh * 512, (nh + 1) * 512)
                                for fc in range(4):
                                    S.op("pe", lambda e, fc=fc, bank=bank, ns=ns: e.matmul(
                                        out=PS[bank][0:T, :], lhsT=upT[ub][:, fc, lc:lc + T], rhs=wd[wb][:, fc, ns],
                                        start=(fc == 0), stop=(fc == 3)),
                                         reads=[dup[ub], dwd[wb]], writes=[PD[bank]], inc=(fc == 3))
                                S.op("dve", lambda e, bank=bank, nh=nh: e.tensor_tensor(
                                    out=tmp[nh][0:T, :], in0=PS[bank][0:T, :], in1=gB[0:T, 1024 + nh * 512:1024 + (nh + 1) * 512],
                                    op=ALU.mult), reads=[PD[bank], dG], writes=[dtm[nh]])
                                S.op("dve" if nh == 0 else "pool", lambda e, ns=ns, nh=nh: e.tensor_tensor(
                                    out=x1[0:T, l, ns], in0=x1[0:T, l, ns], in1=tmp[nh][0:T, :], op=ALU.add),
                                     reads=[dtm[nh], dx1h[l][nh]], writes=[dx1h[l][nh]])
                            if E == 7:
                                ydst = yp[i * 128:(i + 1) * 128, :] if i < 16 else ys[:, :]
                                for nh in range(2):
                                    S.dma("sp", lambda e, ydst=ydst, nh=nh: e.dma_start(
                                        out=ydst[:, nh * 512:(nh + 1) * 512], in_=x1[0:T, l, nh * 512:(nh + 1) * 512]),
                                          dx1h[l][nh], "out")

                    seq = [(E, g) for E in range(8) for g in groups]
                    load_w(0)
                    up_part(seq[0][0], seq[0][1], 0)
                    for n_, (E, g) in enumerate(seq):
                        if g is groups[0] and E + 1 < 8:
                            load_w(E + 1)
                        if n_ + 1 < len(seq):
                            up_part(seq[n_ + 1][0], seq[n_ + 1][1], (n_ + 1) % 2)
                        down_part(E, g, n_ % 2)
                    S.barrier()
    except _Stop:
        S.finish()
        return nc
    S.finish()
    es.close()
    return nc


def _core_inputs(c, inp, n_phys=None, ck=None, cv=None, pt=None):
    f = np.float32
    d = {}
    d["xp"] = np.ascontiguousarray(inp["x_prompt"][c], dtype=f)
    d["xs"] = np.ascontiguousarray(inp["x_sample"][16 * c:16 * c + 16].reshape(TS, D), dtype=f)
    d["ck"] = ck if ck is not None else inp["cache_k"][0].reshape(-1, 512)
    d["cv"] = cv if cv is not None else inp["cache_v"][0].reshape(-1, 512)
    d["st0"] = np.ascontiguousarray(inp["state_hgrn"][0, 16 * c:16 * c + 16], dtype=f)
    ptc = pt if pt is not None else inp["page_table"][16 * c:16 * c + 16]
    d["pt"] = np.ascontiguousarray(ptc.reshape(1, -1), dtype=np.int32)
    d["cc"] = np.ascontiguousarray(np.concatenate([inp["c_sample"][16 * c:16 * c + 16], inp["c_prompt"][c:c + 1]], 0), dtype=f)
    d["w_ada"] = np.ascontiguousarray(inp["w_ada"][0], dtype=f)
    b_ada = inp["b_ada"][0]
    d["vecs"] = np.ascontiguousarray(np.concatenate([b_ada.reshape(48, 128), inp["norm1_g"][0].reshape(8, 128),
                                                     inp["norm2_g"][0].reshape(8, 128)], 0), dtype=f)
    d["bg"] = np.ascontiguousarray(np.concatenate([b_ada[2048:3072], b_ada[5120:6144]])[None], dtype=f)
    d["w_in"] = np.ascontiguousarray(inp["w_in"][0], dtype=f)
    d["qkg"] = np.ascontiguousarray(np.concatenate([np.tile(inp["q_norm_g"][0], 8), np.tile(inp["k_norm_g"][0], 8),
                                                    np.tile(inp["hg_out_g"][0], 8)])[None], dtype=f)
    d["sbb"] = np.ascontiguousarray(inp["sb_bias"][0][None], dtype=f)
    d["lbl"] = np.ascontiguousarray(inp["hg_lb_logits"].reshape(1, 1024), dtype=f)
    d["w_out"] = np.ascontiguousarray(inp["w_out"][0], dtype=f)
    d["w_up"] = np.ascontiguousarray(inp["w_up"][0], dtype=f)
    d["w_down"] = np.ascontiguousarray(inp["w_down"][0], dtype=f)
    d["cstf"] = _CARR
    d["cstb"] = _BARR
    return d


def _assemble(results):
    y_p = np.stack([r["yp"] for r in results]).astype(np.float32)
    y_s = np.concatenate([r["ys"].reshape(16, 4, D) for r in results]).astype(np.float32)
    k_p = np.stack([r["kp"].reshape(TP, 8, 64) for r in results])[None].astype(np.float32)
    v_p = np.stack([r["vp"].reshape(TP, 8, 64) for r in results])[None].astype(np.float32)
    k_s = np.concatenate([r["ks"].reshape(16, 4, 8, 64) for r in results])[None].astype(np.float32)
    v_s = np.concatenate([r["vs"].reshape(16, 4, 8, 64) for r in results])[None].astype(np.float32)
    s_p = np.stack([r["sp"] for r in results])[None].astype(np.float32)
    s_s = np.concatenate([r["ss"] for r in results])[None].astype(np.float32)
    return (y_p, y_s, k_p, v_p, k_s, v_s, s_p, s_s)


def kernel(**inputs):
    inp = {k: np.asarray(v) for k, v in inputs.items()}
    n_phys = inp["cache_k"].shape[1]
    nc = build(n_phys)
    ck = np.ascontiguousarray(inp["cache_k"][0].reshape(-1, 512), dtype=np.float32)
    cv = np.ascontiguousarray(inp["cache_v"][0].reshape(-1, 512), dtype=np.float32)
    in_maps = [_core_inputs(c, inp, ck=ck, cv=cv) for c in range(NCORES)]
    res = run_bass_kernel_spmd(nc, in_maps, core_ids=list(range(NCORES)))
    return _assemble(res.results)
```

```python
import contextlib
import os
import numpy as np
import ml_dtypes
import concourse.bass as bass
import concourse.mybir as mybir
from concourse.bass_utils import run_bass_kernel_spmd

F32 = mybir.dt.float32
BF16 = mybir.dt.bfloat16
I32 = mybir.dt.int32
AF = mybir.ActivationFunctionType
ALU = mybir.AluOpType
AX = mybir.AxisListType

NCORES = 8
D = 1024
TP = 2048
NSEQ = 16
TS = 64
NTOK = TP + TS
NPG = 16
EPS = 1e-6
SCALE = 64 ** -0.5
NT = 17


def _consts():
    f = {}
    idx = np.arange(128)
    f["ident"] = np.eye(128, dtype=np.float32)
    s = idx[:, None]
    t = idx[None, :]
    same64 = (s // 64) == (t // 64)
    f["tri2"] = ((s <= t) & same64).astype(np.float32)
    f["tsu2"] = ((s > t) & same64).astype(np.float32)
    same4 = (s // 4) == (t // 4)
    f["tri4"] = ((s <= t) & same4).astype(np.float32)
    f["tsu4"] = ((s > t) & same4).astype(np.float32)
    ci = np.zeros((128, 2), np.float32)
    ci[:64, 0] = 1
    ci[64:, 1] = 1
    f["chunkind"] = ci
    si = np.zeros((128, 16), np.float32)
    for p in range(64):
        si[p, p // 4] = 1
    f["seqind"] = si
    mh = np.zeros((128, 64), np.float32)
    for p in range(128):
        mh[p, :] = (np.arange(64) >= (p % 64))
    f["maskH"] = mh
    ms = np.zeros((128, 64), np.float32)
    for p in range(64):
        for c in range(64):
            ms[p, c] = (p // 4 == c // 4) and (p % 4 <= c % 4)
    f["maskHS"] = ms
    ma = np.zeros((128, 64), np.float32)
    for p in range(64):
        for c in range(64):
            ma[p, c] = (p // 4 == c // 4) and (p % 4 < c % 4)
    f["maskS"] = ma
    smt = np.zeros((128, 16 * 64), np.float32)
    for b in range(16):
        smt[:, b * 64 + 4 * b: b * 64 + 4 * b + 4] = 1
    f["seqmaskT"] = smt
    md = np.zeros((128, 4 * 512), np.float32)
    for i in range(4):
        md[:, i * 512:(i + 1) * 512] = ((128 * i + idx[:, None]) < np.arange(512)[None, :])
    f["maskD"] = md
    f["uincl"] = (s >= t).astype(np.float32)
    f["lstrict"] = (s < t).astype(np.float32)
    io = np.zeros((128, 1), np.float32)
    off = {}
    cols = 0
    for k, v in f.items():
        off[k] = (cols, v.shape[1])
        cols += v.shape[1]
    bfk = ["maskD", "uincl", "lstrict", "seqmaskT"]
    off = {}
    cols = 0
    for k, v in f.items():
        if k in bfk:
            continue
        off[k] = (cols, v.shape[1])
        cols += v.shape[1]
    arr = np.concatenate([f[k] for k in f if k not in bfk], axis=1).astype(np.float32)
    boff = {}
    cols = 0
    for k in bfk:
        boff[k] = (cols, f[k].shape[1])
        cols += f[k].shape[1]
    barr = np.concatenate([f[k] for k in bfk], axis=1).astype(np.float32)
    return arr, off, barr, boff


_CARR, _COFF, _BARR, _BOFF = _consts()


class _Stop(Exception):
    pass


class Dep:
    __slots__ = ("name", "w", "re", "rd", "sem", "cnt")

    def __init__(self, name):
        self.name = name
        self.w = None
        self.re = {}
        self.rd = None
        self.sem = None
        self.cnt = 0


class Sched:
    def __init__(self, nc, es):
        self.nc = nc
        self.es = es
        self.eng = {}
        for name, h in [("pe", nc.tensor), ("act", nc.scalar), ("dve", nc.vector),
                        ("pool", nc.gpsimd), ("sp", nc.sync)]:
            sem = es.enter_context(nc.semaphore("sem_" + name))
            self.eng[name] = dict(h=h, sem=sem, cnt=0, waited={})
        self.dma_deps = []

    def _wait(self, ename, entry):
        E = self.eng[ename]
        if entry[0] == "e":
            _, src, idx = entry
            if src == ename and ename == "pe":
                return
            sem = self.eng[src]["sem"]
            val = idx
        else:
            _, dep, n = entry
            sem = dep.sem
            val = 16 * n
        key = id(sem)
        if E["waited"].get(key, 0) >= val:
            return
        E["waited"][key] = val
        E["h"].wait_ge(sem, val)

    def _deps(self, ename, reads, writes):
        for d in reads:
            if d.w is not None:
                self._wait(ename, d.w)
        for d in writes:
            if d.w is not None and not (d.w[0] == "e" and d.w[1] == ename):
                self._wait(ename, d.w)
            for src, idx in d.re.items():
                if src != ename:
                    self._wait(ename, ("e", src, idx))
            if d.rd is not None:
                self._wait(ename, d.rd)

    def op(self, ename, fn, reads=(), writes=(), inc=True):
        E = self.eng[ename]
        self._deps(ename, reads, writes)
        ins = fn(E["h"])
        if inc:
            E["cnt"] += 1
            idx = E["cnt"]
            ins.then_inc(E["sem"], 1)
        else:
            idx = E["cnt"] + 1
        for d in reads:
            d.re[ename] = idx
        for d in writes:
            d.w = ("e", ename, idx)
            d.re = {}
            d.rd = None
        return ins

    def dma(self, qname, fn, dep, direction, extra_reads=()):
        E = self.eng[qname]
        if dep.sem is None:
            dep.sem = self.es.enter_context(self.nc.semaphore("dsem_" + dep.name))
            self.dma_deps.append(dep)
        if direction == "in":
            if dep.w is not None and dep.w[0] == "d" and dep.w[1] is dep and not dep.re and dep.rd is None:
                for d in extra_reads:
                    if d.w is not None:
                        self._wait(qname, d.w)
            else:
                self._deps(qname, list(extra_reads), [dep])
        else:
            self._deps(qname, list(extra_reads) + [dep], [])
        ins = fn(E["h"])
        dep.cnt += 1
        ins.then_inc(dep.sem, 16)
        ent = ("d", dep, dep.cnt)
        if direction == "in":
            dep.w = ent
            dep.re = {}
            dep.rd = None
        else:
            dep.rd = ent
        return ins

    def barrier(self):
        names = ["pe", "act", "dve", "pool", "sp"]
        for a in names:
            for b in names:
                if a != b and self.eng[b]["cnt"] > 0:
                    self._wait(a, ("e", b, self.eng[b]["cnt"]))
            for d in self.dma_deps:
                if d.cnt > 0:
                    self._wait(a, ("d", d, d.cnt))

    def finish(self):
        for d in self.dma_deps:
            if d.cnt > 0:
                self._wait("sp", ("d", d, d.cnt))
        for b in ["pe", "act", "dve", "pool"]:
            if self.eng[b]["cnt"] > 0:
                self._wait("sp", ("e", b, self.eng[b]["cnt"]))


def build(n_phys):
    nc = bass.Bass("TRN2", target_bir_lowering=False)
    es = contextlib.ExitStack()

    def din(name, shape, dt=F32):
        return nc.dram_tensor(name, shape, dt, kind="ExternalInput").ap()

    def dout(name, shape, dt=F32):
        return nc.dram_tensor(name, shape, dt, kind="ExternalOutput").ap()

    xp = din("xp", [TP, D])
    xs_d = din("xs", [TS, D])
    ck = din("ck", [n_phys * 128, 512])
    cv = din("cv", [n_phys * 128, 512])
    st0 = din("st0", [NSEQ, 8, 64, 64])
    pt = din("pt", [1, NSEQ * NPG], I32)
    cc = din("cc", [17, D])
    w_ada = din("w_ada", [D, 6 * D])
    vecs = din("vecs", [64, 128])
    bg = din("bg", [1, 2048])
    w_in = din("w_in", [D, 3584])
    qkg = din("qkg", [1, 3 * 512])
    sbb = din("sbb", [1, 8])
    lbl = din("lbl", [1, 1024])
    w_out = din("w_out", [D, D])
    w_up = din("w_up", [D, 4 * D])
    w_down = din("w_down", [4 * D, D])
    cstf = din("cstf", list(_CARR.shape))
    cstb = din("cstb", list(_BARR.shape))

    yp = dout("yp", [TP, D])
    ys = dout("ys", [TS, D])
    kp = dout("kp", [TP, 512])
    vp = dout("vp", [TP, 512])
    ks = dout("ks", [TS, 512])
    vs = dout("vs", [TS, 512])
    sp_o = dout("sp", [8, 64, 64])
    ss_o = dout("ss", [NSEQ, 8, 64, 64])

    S = Sched(nc, es)
    ucnt = [0]

    def sb(shape, dt=F32, side="left", stack=None, name=None):
        ucnt[0] += 1
        nm = (name or "t") + str(ucnt[0])
        return (stack or es).enter_context(nc.sbuf_tensor(nm, shape, dt, side=side))

    def dep(name="d"):
        ucnt[0] += 1
        return Dep(name + str(ucnt[0]))

    PS = [es.enter_context(nc.psum_tensor(f"ps{i}", [128, 512], F32)) for i in range(8)]
    PD = [dep(f"ps{i}_") for i in range(8)]

    CF = sb([128, _CARR.shape[1]], F32)
    dCF = dep("cf")
    S.dma("sp", lambda e: e.dma_start(out=CF[:], in_=cstf[:, :]), dCF, "in")

    def cf(key, rows=128, c0=0, c1=None):
        o, n = _COFF[key]
        c1 = n if c1 is None else c1
        return CF[0:rows, o + c0:o + c1]

    def load_bf_const(key, dst_ap, ddst, scratch, dscr):
        o, n = _BOFF[key]
        for a in range(0, n, 512):
            w_ = min(512, n - a)
            S.dma("sp", lambda e, a=a, w_=w_: e.dma_start(out=scratch[:, 0:w_], in_=cstb[:, o + a:o + a + w_]), dscr, "in")
            S.op("dve", lambda e, a=a, w_=w_: e.tensor_copy(out=dst_ap[:, a:a + w_], in_=scratch[:, 0:w_]),
                 reads=[dscr], writes=[ddst])

    ident = cf("ident")

    vecT = sb([128, 64], F32)
    dVec = dep("vecT")
    biasT = sb([128, 8], F32)
    dPar = dep("par")
    cT = sb([128, 8, 17], F32)
    dcT = dep("cT")
    mod = sb([128, 4, 8, 17], F32)
    dMod = dep("mod")
    p1s = contextlib.ExitStack()
    qkgB = sb([128, 1536], F32, stack=p1s)
    lbB = sb([128, 512], F32, stack=p1s)
    omlB = sb([128, 512], F32, stack=p1s)
    mixT = sb([128, 8, NTOK], BF16, side="right")
    dMixA = [dep("mixA") for _ in range(NT)]
    dMixB = [dep("mixB") for _ in range(NT)]

    def tile_rows(i):
        return 128 if i < 16 else 64

    def tile_cols(i):
        return (i * 128, i * 128 + tile_rows(i))

    def rsqrt_small(src_ap, dst_ap, n_scale, deps_r, dep_w, tmp_ap):
        S.op("dve", lambda e: e.tensor_scalar(out=tmp_ap, in0=src_ap, scalar1=n_scale, scalar2=EPS,
                                              op0=ALU.mult, op1=ALU.add), reads=deps_r, writes=[dep_w])
        S.op("act", lambda e: e.activation(out=tmp_ap, in_=tmp_ap, func=AF.Ln), reads=[dep_w], writes=[dep_w])
        S.op("act", lambda e: e.activation(out=dst_ap, in_=tmp_ap, func=AF.Exp, scale=-0.5),
             reads=[dep_w], writes=[dep_w])

    stop = int(os.environ.get("KSTOP", "99"))

    def chk(n):
        if stop == n:
            raise _Stop()

    try:
        with contextlib.ExitStack() as p0:
            wst = [sb([128, 8, 512], F32, stack=p0) for _ in range(2)]
            dW = [dep("wst") for _ in range(2)]
            cct = sb([17, D], F32, stack=p0)
            dcc = dep("cc")
            adaT = sb([128, 48, 17], F32, stack=p0)
            dAda = dep("adaT")
            vrow = sb([64, 128], F32, stack=p0)
            lraw = sb([128, 1024], F32, stack=p0)
            dtmp = dep("p0tmp")

            S.dma("sp", lambda e: e.dma_start(out=cct[:], in_=cc[:, :]), dcc, "in")
            S.dma("sp", lambda e: e.dma_start(out=vrow[:], in_=vecs[:, :]), dtmp, "in")
            S.dma("sp", lambda e: e.dma_start(out=qkgB[:], in_=qkg[0:1, :].to_broadcast([128, 1536])), dPar, "in")
            S.dma("sp", lambda e: e.dma_start(out=biasT[:], in_=sbb[0:1, :].to_broadcast([128, 8])), dPar, "in")
            S.dma("sp", lambda e: e.dma_start(out=lraw[:], in_=lbl[0:1, :].to_broadcast([128, 1024])), dtmp, "in")
            S.op("dve", lambda e: e.tensor_tensor(out=lraw[:, 0:512], in0=lraw[:, 0:512], in1=lraw[:, 512:1024],
                                                  op=ALU.subtract), reads=[dtmp], writes=[dtmp])
            S.op("act", lambda e: e.activation(out=lbB[:], in_=lraw[:, 0:512], func=AF.Sigmoid),
                 reads=[dtmp], writes=[dPar])
            S.op("dve", lambda e: e.tensor_scalar(out=omlB[:], in0=lbB[:], scalar1=-1.0, scalar2=1.0,
                                                  op0=ALU.mult, op1=ALU.add), reads=[dPar], writes=[dPar])
            S.op("act", lambda e: e.activation(out=cct[:], in_=cct[:], func=AF.Silu), reads=[dcc], writes=[dcc])
            for c in range(8):
                S.op("pe", lambda e, c=c: e.transpose(out=PS[0][:, c * 17:(c + 1) * 17],
                                                      in_=cct[0:17, c * 128:(c + 1) * 128], identity=cf("ident", 17, 0, 17)),
                     reads=[dcc, dCF], writes=[PD[0]], inc=(c == 7))
            S.op("dve", lambda e: e.tensor_copy(out=cT[:].rearrange("p c s -> p (c s)"), in_=PS[0][:, 0:136]),
                 reads=[PD[0]], writes=[dcT])
            S.op("pe", lambda e: e.transpose(out=PS[1][:, 0:64], in_=vrow[0:64, :], identity=cf("ident", 64, 0, 64)),
                 reads=[dtmp, dCF], writes=[PD[1]])
            S.op("dve", lambda e: e.tensor_copy(out=vecT[:], in_=PS[1][:, 0:64]), reads=[PD[1]], writes=[dVec])

            w_ada_v = w_ada.rearrange("(k p) n -> p k n", p=128)
            pcs = [0, 1, 2, 3, 6, 7, 8, 9]
            for n_, pc in enumerate(pcs):
                buf = n_ % 2
                S.dma("sp", lambda e, pc=pc, buf=buf: e.dma_start(out=wst[buf][:], in_=w_ada_v[:, :, pc * 512:(pc + 1) * 512]),
                      dW[buf], "in")
                for q in range(4):
                    chunk = (pc * 512) // 128 + q
                    bank = 2 + (q % 2)
                    for k in range(8):
                        S.op("pe", lambda e, k=k, q=q, bank=bank, buf=buf: e.matmul(
                            out=PS[bank][:, 0:17], lhsT=wst[buf][:, k, q * 128:(q + 1) * 128], rhs=cT[:, k, :],
                            start=(k == 0), stop=(k == 7)),
                             reads=[dcT, dW[buf]], writes=[PD[bank]], inc=(k == 7))
                    S.op("dve", lambda e, chunk=chunk, bank=bank: e.tensor_scalar(
                        out=adaT[:, chunk, :], in0=PS[bank][:, 0:17], scalar1=vecT[:, chunk:chunk + 1], scalar2=None,
                        op0=ALU.add), reads=[PD[bank], dVec], writes=[dAda])
            for (mi, scc, shc, nof) in ((0, 8, 0, 48), (2, 32, 24, 56)):
                S.op("dve", lambda e, mi=mi, scc=scc, nof=nof: e.scalar_tensor_tensor(
                    out=mod[:, mi, :, :], in0=adaT[:, scc:scc + 8, :], scalar=1.0,
                    in1=vecT[:, nof:nof + 8].unsqueeze(2).to_broadcast([128, 8, 17]), op0=ALU.add, op1=ALU.mult),
                     reads=[dAda, dVec], writes=[dMod])
                S.op("dve", lambda e, mi=mi, shc=shc: e.tensor_copy(out=mod[:, mi + 1, :, :], in_=adaT[:, shc:shc + 8, :]),
                     reads=[dAda], writes=[dMod])
            S.barrier()
            chk(0)

        def load_x_tile(i, xt, dxt, q="sp"):
            T = tile_rows(i)
            src = xp[i * 128:(i + 1) * 128, :] if i < 16 else xs_d[:, :]
            S.dma(q, lambda e: e.dma_start(out=xt[0:T, :], in_=src), dxt, "in")

        def norm_transpose(i, src_ap, dsrc, mi, hT_ap, dhT, work, dwork, banks):
            T = tile_rows(i)
            xn, sqj, st4 = work
            S.op("act", lambda e: e.activation(out=sqj[0:T, :], in_=src_ap, func=AF.Square, accum_out=st4[0:T, 0:1]),
                 reads=[dsrc], writes=[dwork])
            rsqrt_small(st4[0:T, 0:1], st4[0:T, 2:3], 1.0 / D, [dwork], dwork, st4[0:T, 1:2])
            S.op("dve", lambda e: e.tensor_scalar(out=xn[0:T, :], in0=src_ap, scalar1=st4[0:T, 2:3], scalar2=None,
                                                  op0=ALU.mult), reads=[dsrc, dwork], writes=[dwork])
            nb, tpb = (1, 128) if i < 16 else (16, 4)
            c0 = 16 if i < 16 else 0
            for half in range(2):
                bank = banks[half]
                for c4 in range(4):
                    c = half * 4 + c4
                    S.op("pe", lambda e, c=c, c4=c4, bank=bank: e.transpose(
                        out=PS[bank][:, c4 * 128:c4 * 128 + T], in_=xn[0:T, c * 128:(c + 1) * 128],
                        identity=cf("ident", T, 0, T)), reads=[dwork, dCF], writes=[PD[bank]], inc=(c4 == 3))
                pv = PS[bank][:, :].rearrange("p (c t) -> p c t", c=4)[:, :, 0:T].rearrange("p c (b t) -> p c b t", t=tpb)
                sc = mod[:, mi, half * 4:half * 4 + 4, c0:c0 + nb].unsqueeze(3).to_broadcast([128, 4, nb, tpb])
                sh = mod[:, mi + 1, half * 4:half * 4 + 4, c0:c0 + nb].unsqueeze(3).to_broadcast([128, 4, nb, tpb])
                tmpm = xn[:, half * 512:(half + 1) * 512].rearrange("p (c t) -> p c t", c=4)[:, :, 0:T].rearrange(
                    "p c (b t) -> p c b t", t=tpb)
                tm = sqj[:, half * 512:(half + 1) * 512].rearrange("p (c t) -> p c t", c=4)[:, :, 0:T].rearrange(
                    "p c (b t) -> p c b t", t=tpb)
                S.op("dve", lambda e, pv=pv, sc=sc, tm=tm: e.tensor_tensor(out=tm, in0=pv, in1=sc, op=ALU.mult),
                     reads=[PD[bank], dMod], writes=[dwork])
                ho = hT_ap[:, half * 4:half * 4 + 4, :].rearrange("p c (b t) -> p c b t", t=tpb)
                S.op("dve", lambda e, ho=ho, sh=sh, tm=tm: e.tensor_tensor(out=ho, in0=tm, in1=sh, op=ALU.add),
                     reads=[dwork, dMod], writes=[dhT])

        def load_weight_piece(src_ap, dst_bf_ap, wst, dW, buf, ddst, q="sp", cast_eng="pool"):
            S.dma(q, lambda e: e.dma_start(out=wst[buf][:], in_=src_ap), dW[buf], "in")
            S.op(cast_eng, lambda e: e.tensor_copy(out=dst_bf_ap, in_=wst[buf][:]), reads=[dW[buf]], writes=[ddst])

        with contextlib.ExitStack() as p1:
            hT = sb([128, 8, NTOK], BF16, stack=p1)
            dhT = [dep("hT") for _ in range(NT)]
            with contextlib.ExitStack() as p1n:
                xt = [sb([128, D], F32, stack=p1n) for _ in range(2)]
                dxt = [dep("xt") for _ in range(2)]
                wk = [(sb([128, D], F32, stack=p1n), sb([128, D], F32, stack=p1n), sb([128, 4], F32, stack=p1n)) for _ in range(2)]
                dwk = [dep("wk") for _ in range(2)]
                for i in [int(x) for x in os.environ["KTILES"].split(",")] if "KTILES" in os.environ else range(NT):
                    b = i % 2
                    load_x_tile(i, xt[b], dxt[b])
                    c0, c1 = tile_cols(i)
                    norm_transpose(i, xt[b][0:tile_rows(i), :], dxt[b], 0, hT[:, :, c0:c1], dhT[i], wk[b], dwk[b],
                                   (0, 1) if b == 0 else (2, 3))
                S.barrier()
                chk(1)

            w_in_v = w_in.rearrange("(k p) n -> p k n", p=128)

            with contextlib.ExitStack() as pb:
                whg = sb([128, 8, 2048], BF16, stack=pb)
                dwhg = [dep("whg") for _ in range(4)]
                with contextlib.ExitStack() as pw:
                    wst = [sb([128, 8, 512], F32, stack=pw) for _ in range(2)]
                    dW = [dep("wst") for _ in range(2)]
                    for g in range(4):
                        load_weight_piece(w_in_v[:, :, 1536 + g * 512:1536 + (g + 1) * 512], whg[:, :, g * 512:(g + 1) * 512],
                                          wst, dW, g % 2, dwhg[g])
                    S.barrier()

                def wt(shape, dt=F32):
                    return sb(shape, dt, stack=pb)

                AB = (4, 2)
                XB = (6, 3)

                def hsel(ap, h2):
                    return ap.rearrange("p (j a t) -> p j a t", j=4, a=2)[:, :, h2, :]

                hq = wt([128, 512]); ff = wt([128, 512]); logf = wt([128, 512]); omf = wt([128, 512])
                hvL = [wt([128, 512], BF16) for _ in range(2)]; sgL = [wt([128, 512]) for _ in range(2)]
                eb = wt([128, 512]); enb = wt([128, 512])
                ec = wt([128, 512]); kkL = [wt([128, 512], BF16) for _ in range(2)]
                qdTL = [wt([128, 4, 128], BF16) for _ in range(2)]; kdTL = [wt([128, 4, 128], BF16) for _ in range(2)]
                attm = wt([128, 512], BF16); oo = wt([128, 512]); osq = wt([128, 512])
                st8 = wt([128, 24]); decL = [wt([128, 4, 16]) for _ in range(2)]
                Sst = wt([128, 4, 64]); Sbf = wt([128, 4, 64], BF16)
                dE = dep("hgE")
                dTL = [dep("hgT") for _ in range(2)]
                dEbL = [dep("hgEb") for _ in range(2)]
                dDecL = [dep("dec") for _ in range(2)]
                dA = dep("attm"); dO = dep("oo"); dS_ = dep("S"); dSb = dep("Sbf")
                S.op("dve", lambda e: e.memset(Sst[:], 0.0), writes=[dS_])
                S.op("dve", lambda e: e.memset(Sbf[:], 0.0), writes=[dSb])
                S0t = [wt([128, 16, 64]) for _ in range(2)]
                S0bt = wt([128, 16, 64], BF16)
                qdTm = wt([128, 4, 16, 64], BF16); hvm = wt([64, 16, 128], BF16)
                smT = wt([128, 1024], BF16)
                dS0 = [dep("S0") for _ in range(2)]; dS0b = dep("S0b"); dqm = dep("qdTm"); dhvm = dep("hvm"); dsm = dep("smT")
                st0_v = st0.rearrange("b (j h) k v -> (h k) j b v", h=2)
                ss_v = ss_o.rearrange("b (j h) k v -> (h k) j b v", h=2)
                load_bf_const("seqmaskT", smT, dsm, hq, dE)

                def tile_gen(i):
                    s_ = i % 2
                    hv, sg, kk, qdT, kdT, dec = hvL[s_], sgL[s_], kkL[s_], qdTL[s_], kdTL[s_], decL[s_]
                    dT_, dEb, dDec = dTL[s_], dEbL[s_], dDecL[s_]
                    T = tile_rows(i)
                    c0, c1 = tile_cols(i)
                    samp = (i == 16)
                    tri = cf("tri4" if samp else "tri2", T, 0, T)
                    tsu = cf("tsu4" if samp else "tsu2", T, 0, T)
                    ncn = 16 if samp else 2
                    ind = cf("seqind" if samp else "chunkind", T)
                    def proj(g, bank):
                        for k in range(8):
                            S.op("pe", lambda e, k=k: e.matmul(out=PS[bank][0:T, :], lhsT=hT[:, k, c0:c1],
                                                               rhs=whg[:, k, g * 512:(g + 1) * 512],
                                                               start=(k == 0), stop=(k == 7)),
                                 reads=[dhT[i], dwhg[g]], writes=[PD[bank]], inc=(k == 7))
                    proj(0, 2)
                    proj(1, 3)
                    proj(3, 0)
                    proj(2, 1)
                    S.op("act", lambda e: e.activation(out=hq[0:T, :], in_=PS[2][0:T, :], func=AF.Silu),
                         reads=[PD[2]], writes=[dE])
                    S.op("act", lambda e: e.activation(out=sg[0:T, :], in_=PS[0][0:T, :], func=AF.Silu),
                         reads=[PD[0]], writes=[dEb])
                    S.op("act", lambda e: e.activation(out=ff[0:T, :], in_=PS[3][0:T, :], func=AF.Sigmoid),
                         reads=[PD[3]], writes=[dE])
                    S.op("dve", lambda e: e.tensor_tensor(out=ff[0:T, :], in0=ff[0:T, :], in1=omlB[0:T, :], op=ALU.mult),
                         reads=[dE, dPar], writes=[dE])
                    S.op("dve", lambda e: e.tensor_tensor(out=ff[0:T, :], in0=ff[0:T, :], in1=lbB[0:T, :], op=ALU.add),
                         reads=[dE, dPar], writes=[dE])
                    S.op("act", lambda e: e.activation(out=hv[0:T, :], in_=PS[1][0:T, :], func=AF.Copy),
                         reads=[PD[1]], writes=[dEb])
                    S.op("act", lambda e: e.activation(out=logf[0:T, :], in_=ff[0:T, :], func=AF.Ln),
                         reads=[dE], writes=[dE])
                    S.op("dve", lambda e: e.tensor_scalar(out=omf[0:T, :], in0=ff[0:T, :], scalar1=-1.0, scalar2=1.0,
                                                          op0=ALU.mult, op1=ALU.add), reads=[dE], writes=[dE])
                    yield
                    S.op("pe", lambda e: e.matmul(out=PS[0][0:T, :], lhsT=tri, rhs=logf[0:T, :], start=True, stop=True),
                         reads=[dE, dCF], writes=[PD[0]])
                    S.op("pe", lambda e: e.matmul(out=PS[1][0:T, :], lhsT=tsu, rhs=logf[0:T, :], start=True, stop=True),
                         reads=[dE, dCF], writes=[PD[1]])
                    for j in range(4):
                        S.op("pe", lambda e, j=j: e.matmul(out=PS[7][:, 256 + j * ncn:256 + (j + 1) * ncn],
                                                           lhsT=logf[0:T, j * 128:(j + 1) * 128], rhs=ind,
                                                           start=True, stop=True),
                             reads=[dE, dCF], writes=[PD[7]], inc=(j == 3))
                    S.op("act", lambda e: e.activation(out=eb[0:T, :], in_=PS[0][0:T, :], func=AF.Exp),
                         reads=[PD[0]], writes=[dE])
                    S.op("act", lambda e: e.activation(out=enb[0:T, :], in_=PS[0][0:T, :], func=AF.Exp, scale=-1.0),
                         reads=[PD[0]], writes=[dE])
                    S.op("act", lambda e: e.activation(out=ec[0:T, :], in_=PS[1][0:T, :], func=AF.Exp),
                         reads=[PD[1]], writes=[dE])
                    S.op("act", lambda e: e.activation(out=dec[:, :, 0:ncn],
                                                       in_=PS[7][:, 256:256 + 4 * ncn].rearrange("p (j c) -> p j c", j=4),
                                                       func=AF.Exp), reads=[PD[7]], writes=[dDec])
                    S.op("dve", lambda e: e.tensor_tensor(out=eb[0:T, :], in0=hq[0:T, :], in1=eb[0:T, :], op=ALU.mult),
                         reads=[dE], writes=[dE])
                    S.op("dve", lambda e: e.tensor_tensor(out=enb[0:T, :], in0=omf[0:T, :], in1=enb[0:T, :], op=ALU.mult),
                         reads=[dE], writes=[dE])
                    S.op("dve", lambda e: e.tensor_tensor(out=kk[0:T, :], in0=omf[0:T, :], in1=ec[0:T, :], op=ALU.mult),
                         reads=[dE], writes=[dEb])
                    yield
                    for (src, dst, bank) in ((eb, qdT, 0), (enb, kdT, 1)):
                        for j in range(4):
                            S.op("pe", lambda e, j=j, src=src, bank=bank: e.transpose(
                                out=PS[bank][:, j * 128:j * 128 + T], in_=src[0:T, j * 128:(j + 1) * 128],
                                identity=cf("ident", T, 0, T)), reads=[dE, dCF], writes=[PD[bank]], inc=(j == 3))
                        S.op("act", lambda e, dst=dst, bank=bank: e.activation(
                            out=dst[:, :, 0:T], in_=PS[bank][:, :].rearrange("p (j t) -> p j t", j=4)[:, :, 0:T],
                            func=AF.Copy), reads=[PD[bank]], writes=[dT_])

                    yield
                    if not samp:
                        for c in range(2):
                            cp = 64 * c
                            for h in range(8):
                                j, h2 = h // 2, h % 2
                                hp = 64 * h2
                                ab = AB[h2]
                                S.op("pe", lambda e, h=h, j=j, hp=hp, cp=cp, ab=ab: e.matmul(
                                    out=PS[ab][cp:cp + 64, h * 64:(h + 1) * 64], lhsT=kdT[hp:hp + 64, j, cp:cp + 64],
                                    rhs=qdT[hp:hp + 64, j, cp:cp + 64], start=True, stop=True),
                                     reads=[dT_], writes=[PD[ab]], inc=(h >= 6))
                            for h2 in range(2):
                                ab = AB[h2]
                                S.op("dve", lambda e, cp=cp, ab=ab, h2=h2: e.tensor_tensor(
                                    out=hsel(attm[cp:cp + 64, :], h2), in0=hsel(PS[ab][cp:cp + 64, :], h2),
                                    in1=cf("maskH")[cp:cp + 64, :].unsqueeze(1).to_broadcast([64, 4, 64]), op=ALU.mult),
                                     reads=[PD[ab], dCF], writes=[dA])
                            for h in range(8):
                                j, h2 = h // 2, h % 2
                                hp = 64 * h2
                                hs = slice(h * 64, (h + 1) * 64)
                                S.op("pe", lambda e, hs=hs, cp=cp: e.matmul(
                                    out=PS[5][cp:cp + 64, hs], lhsT=attm[cp:cp + 64, hs], rhs=hv[cp:cp + 64, hs],
                                    start=True, stop=True), reads=[dA, dEb], writes=[PD[5]], inc=False)
                                xb = XB[h2]
                                S.op("pe", lambda e, hs=hs, cp=cp, hp=hp, j=j, xb=xb: e.matmul(
                                    out=PS[xb][cp:cp + 64, hs], lhsT=qdT[hp:hp + 64, j, cp:cp + 64], rhs=Sbf[hp:hp + 64, j, :],
                                    start=True, stop=True), reads=[dT_, dSb], writes=[PD[xb]], inc=False)
                                S.op("pe", lambda e, hs=hs, cp=cp, hp=hp, j=j: e.matmul(
                                    out=PS[7][hp:hp + 64, j * 64:(j + 1) * 64], lhsT=kk[cp:cp + 64, hs], rhs=hv[cp:cp + 64, hs],
                                    start=True, stop=True), reads=[dEb], writes=[PD[7]], inc=(h == 7))
                            S.op("dve", lambda e, c=c: e.tensor_tensor(
                                out=Sst[:], in0=Sst[:], in1=dec[:, :, c:c + 1].to_broadcast([128, 4, 64]), op=ALU.mult),
                                 reads=[dS_, dDec], writes=[dS_])
                            S.op("dve", lambda e: e.tensor_tensor(
                                out=Sst[:], in0=Sst[:], in1=PS[7][:, 0:256].rearrange("p (j v) -> p j v", j=4), op=ALU.add),
                                 reads=[dS_, PD[7]], writes=[dS_])
                            S.op("act", lambda e: e.activation(out=Sbf[:], in_=Sst[:], func=AF.Copy),
                                 reads=[dS_], writes=[dSb])
                            yield
                        if i == 15:
                            S.dma("sp", lambda e: e.dma_start(out=sp_o.rearrange("(j h) k v -> (h k) j v", h=2), in_=Sst[:]),
                                  dS_, "out")
                    else:
                        for h in range(8):
                            j, h2 = h // 2, h % 2
                            hp = 64 * h2
                            ab = AB[h2]
                            S.op("pe", lambda e, h=h, j=j, hp=hp, ab=ab: e.matmul(
                                out=PS[ab][0:64, h * 64:(h + 1) * 64], lhsT=kdT[hp:hp + 64, j, 0:64],
                                rhs=qdT[hp:hp + 64, j, 0:64], start=True, stop=True),
                                 reads=[dT_], writes=[PD[ab]], inc=(h >= 6))
                        for h2 in range(2):
                            ab = AB[h2]
                            S.op("dve", lambda e, ab=ab, h2=h2: e.tensor_tensor(
                                out=hsel(attm[0:64, :], h2), in0=hsel(PS[ab][0:64, :], h2),
                                in1=cf("maskHS")[0:64, :].unsqueeze(1).to_broadcast([64, 4, 64]), op=ALU.mult),
                                 reads=[PD[ab], dCF], writes=[dA])
                        S.op("dve", lambda e: e.tensor_tensor(
                            out=qdTm[:], in0=qdT[:, :, 0:64].unsqueeze(2).to_broadcast([128, 4, 16, 64]),
                            in1=smT[:].rearrange("p (b t) -> p b t", b=16).unsqueeze(1).to_broadcast([128, 4, 16, 64]),
                            op=ALU.mult), reads=[dT_, dsm], writes=[dqm])
                        for h in range(8):
                            hs = slice(h * 64, (h + 1) * 64)
                            S.op("pe", lambda e, hs=hs: e.matmul(out=PS[5][0:64, hs], lhsT=attm[0:64, hs], rhs=hv[0:64, hs],
                                                                 start=True, stop=True),
                                 reads=[dA, dEb], writes=[PD[5]], inc=(h == 7))
                        for j in range(4):
                            sb_ = j % 2
                            S0j = S0t[sb_]
                            S.dma("sp", lambda e, j=j, S0j=S0j: e.dma_start(out=S0j[:], in_=st0_v[:, j, :, :]), dS0[sb_], "in")
                            S.op("act", lambda e, S0j=S0j: e.activation(out=S0bt[:].rearrange("p b v -> p (b v)"),
                                                                        in_=S0j[:].rearrange("p b v -> p (b v)"), func=AF.Copy),
                                 reads=[dS0[sb_]], writes=[dS0b])
                            S.op("dve", lambda e, j=j: e.tensor_tensor(
                                out=hvm[:], in0=hv[0:64, j * 128:(j + 1) * 128].unsqueeze(1).to_broadcast([64, 16, 128]),
                                in1=cf("seqind", 64).unsqueeze(2).to_broadcast([64, 16, 128]), op=ALU.mult),
                                 reads=[dEb, dCF], writes=[dhvm])
                            for h2 in range(2):
                                h = 2 * j + h2
                                hp = 64 * h2
                                hs = slice(h * 64, (h + 1) * 64)
                                for b in range(16):
                                    S.op("pe", lambda e, hs=hs, hp=hp, j=j, b=b, h2=h2: e.matmul(
                                        out=PS[XB[h2]][0:64, hs], lhsT=qdTm[hp:hp + 64, j, b, :], rhs=S0bt[hp:hp + 64, b, :],
                                        start=(b == 0), stop=(b == 15)),
                                         reads=[dqm, dS0b], writes=[PD[XB[h2]]], inc=(b == 15))
                                for half in range(2):
                                    S.op("pe", lambda e, h=h, hp=hp, half=half, h2=h2: e.matmul(
                                        out=PS[half][hp:hp + 64, :], lhsT=kk[0:64, h * 64:(h + 1) * 64],
                                        rhs=hvm[0:64, half * 8:(half + 1) * 8, h2 * 64:(h2 + 1) * 64],
                                        start=True, stop=True), reads=[dEb, dhvm], writes=[PD[half]])
                            for half in range(2):
                                bs = slice(half * 8, (half + 1) * 8)
                                S.op("dve", lambda e, j=j, bs=bs, S0j=S0j: e.tensor_tensor(
                                    out=S0j[:, bs, :], in0=S0j[:, bs, :],
                                    in1=dec[:, j, bs].unsqueeze(2).to_broadcast([128, 8, 64]), op=ALU.mult),
                                     reads=[dS0[sb_], dDec], writes=[dS0[sb_]])
                                S.op("dve", lambda e, bs=bs, half=half, S0j=S0j: e.tensor_tensor(
                                    out=S0j[:, bs, :], in0=S0j[:, bs, :],
                                    in1=PS[half][:, :].rearrange("p (b v) -> p b v", b=8), op=ALU.add),
                                     reads=[dS0[sb_], PD[half]], writes=[dS0[sb_]])
                            S.dma("sp", lambda e, j=j, S0j=S0j: e.dma_start(out=ss_v[:, j, :, :], in_=S0j[:]), dS0[sb_], "out")

                    S.op("act", lambda e: e.activation(out=oo[0:T, :], in_=PS[5][0:T, :], func=AF.Copy),
                         reads=[PD[5]], writes=[dO])
                    for h2 in range(2):
                        S.op("dve", lambda e, h2=h2: e.tensor_tensor(out=hsel(oo[0:T, :], h2), in0=hsel(oo[0:T, :], h2),
                                                              in1=hsel(PS[XB[h2]][0:T, :], h2), op=ALU.add),
                             reads=[dO, PD[XB[h2]]], writes=[dO])
                    S.op("dve", lambda e: e.tensor_tensor(out=osq[0:T, :], in0=oo[0:T, :], in1=oo[0:T, :], op=ALU.mult),
                         reads=[dO], writes=[dO])
                    S.op("dve", lambda e: e.tensor_reduce(out=st8[0:T, 0:8], in_=osq[0:T, :].rearrange("p (h v) -> p h v", h=8),
                                                          axis=AX.X, op=ALU.add), reads=[dO], writes=[dO])
                    rsqrt_small(st8[0:T, 0:8], st8[0:T, 16:24], 1.0 / 64, [dO], dO, st8[0:T, 8:16])
                    S.op("dve", lambda e: e.tensor_tensor(
                        out=oo[0:T, :].rearrange("p (h v) -> p h v", h=8), in0=oo[0:T, :].rearrange("p (h v) -> p h v", h=8),
                        in1=st8[0:T, 16:24].unsqueeze(2).to_broadcast([T, 8, 64]), op=ALU.mult), reads=[dO], writes=[dO])
                    S.op("dve", lambda e: e.tensor_tensor(out=oo[0:T, :], in0=oo[0:T, :], in1=qkgB[0:T, 1024:1536], op=ALU.mult),
                         reads=[dO, dPar], writes=[dO])
                    S.op("dve", lambda e: e.tensor_tensor(out=oo[0:T, :], in0=oo[0:T, :], in1=sg[0:T, :], op=ALU.mult),
                         reads=[dO, dEb], writes=[dO])
                    for j in range(4):
                        S.op("pe", lambda e, j=j: e.transpose(out=PS[4][:, j * 128:j * 128 + T], in_=oo[0:T, j * 128:(j + 1) * 128],
                                                              identity=cf("ident", T, 0, T)),
                             reads=[dO, dCF], writes=[PD[4]], inc=(j == 3))
                    S.op("act", lambda e: e.activation(out=mixT[:, 4:8, c0:c1],
                                                       in_=PS[4][:, :].rearrange("p (j t) -> p j t", j=4)[:, :, 0:T],
                                                       func=AF.Copy), reads=[PD[4]], writes=[dMixB[i]])
                gens = [tile_gen(i) for i in range(NT)]
                for _ in range(3):
                    next(gens[0])
                ORD = os.environ.get("KORD", "aBBaBa")
                for i in range(NT):
                    nx = gens[i + 1] if i + 1 < NT else None
                    for ch in ORD:
                        if ch == "a":
                            if nx is not None:
                                next(nx)
                        else:
                            next(gens[i], None)
                    for _ in gens[i]:
                        pass
                S.barrier()
                chk(2)

            pa_r = contextlib.ExitStack()
            qT = sb([128, 4, NTOK], BF16, side="right", stack=pa_r)
            kT = sb([128, 4, NTOK], BF16, side="right", stack=pa_r)
            vres = sb([128, NT, 512], BF16, side="right", stack=pa_r)
            dQ = [dep("qT") for _ in range(NT)]
            dK = [dep("kT") for _ in range(NT)]
            dV = [dep("v") for _ in range(NT)]
            with contextlib.ExitStack() as pa:
                wat = sb([128, 8, 1536], BF16, stack=pa)
                dwat = [dep("wat") for _ in range(3)]
                with contextlib.ExitStack() as pw:
                    wst = [sb([128, 8, 512], F32, stack=pw) for _ in range(2)]
                    dW = [dep("wst") for _ in range(2)]
                    for g in range(3):
                        load_weight_piece(w_in_v[:, :, g * 512:(g + 1) * 512], wat[:, :, g * 512:(g + 1) * 512],
                                          wst, dW, g % 2, dwat[g])
                    S.barrier()
                sq = sb([128, 512], F32, stack=pa)
                qn = [sb([128, 512], F32, stack=pa) for _ in range(2)]
                dqn = [dep("qn") for _ in range(2)]
                kn = [sb([128, 512], F32, stack=pa) for _ in range(2)]
                dkn = [dep("kn") for _ in range(2)]
                vn = [sb([128, 512], F32, stack=pa) for _ in range(2)]
                dvn = [dep("vn") for _ in range(2)]
                s8 = sb([128, 24], F32, stack=pa)
                dsq = dep("sq")
                for i in range(NT):
                    T = tile_rows(i)
                    c0, c1 = tile_cols(i)
                    b = i % 2

                    def proj(g, bank):
                        for k in range(8):
                            S.op("pe", lambda e, k=k: e.matmul(out=PS[bank][0:T, :], lhsT=hT[:, k, c0:c1],
                                                               rhs=wat[:, k, g * 512:(g + 1) * 512],
                                                               start=(k == 0), stop=(k == 7)),
                                 reads=[dhT[i], dwat[g]], writes=[PD[bank]], inc=(k == 7))

                    def qknorm(bank, dst, ddst, goff):
                        S.op("act", lambda e: e.activation(out=sq[0:T, :], in_=PS[bank][0:T, :], func=AF.Square),
                             reads=[PD[bank]], writes=[dsq])
                        S.op("dve", lambda e: e.tensor_reduce(out=s8[0:T, 0:8], in_=sq[0:T, :].rearrange("p (h d) -> p h d", h=8),
                                                              axis=AX.X, op=ALU.add), reads=[dsq], writes=[dsq])
                        rsqrt_small(s8[0:T, 0:8], s8[0:T, 16:24], 1.0 / 64, [dsq], dsq, s8[0:T, 8:16])
                        S.op("dve", lambda e: e.tensor_tensor(
                            out=dst[0:T, :].rearrange("p (h d) -> p h d", h=8),
                            in0=PS[bank][0:T, :].rearrange("p (h d) -> p h d", h=8),
                            in1=s8[0:T, 16:24].unsqueeze(2).to_broadcast([T, 8, 64]), op=ALU.mult),
                             reads=[PD[bank], dsq], writes=[ddst])
                        S.op("dve", lambda e: e.tensor_tensor(out=dst[0:T, :], in0=dst[0:T, :], in1=qkgB[0:T, goff:goff + 512],
                                                              op=ALU.mult), reads=[ddst, dPar], writes=[ddst])

                    def to_featT(src, dsrc, bank, dst, ddst):
                        for j in range(4):
                            S.op("pe", lambda e, j=j: e.transpose(out=PS[bank][:, j * 128:j * 128 + T],
                                                                  in_=src[0:T, j * 128:(j + 1) * 128],
                                                                  identity=cf("ident", T, 0, T)),
                                 reads=[dsrc, dCF], writes=[PD[bank]], inc=(j == 3))
                        S.op("act", lambda e: e.activation(out=dst[:, :, c0:c1],
                                                           in_=PS[bank][:, :].rearrange("p (j t) -> p j t", j=4)[:, :, 0:T],
                                                           func=AF.Copy), reads=[PD[bank]], writes=[ddst])

                    proj(0, 0)
                    proj(1, 1)
                    proj(2, 2)
                    qknorm(0, qn[b], dqn[b], 0)
                    S.op("act", lambda e: e.activation(out=vn[b][0:T, :], in_=PS[2][0:T, :], func=AF.Copy),
                         reads=[PD[2]], writes=[dvn[b]])
                    qknorm(1, kn[b], dkn[b], 512)
                    to_featT(qn[b], dqn[b], 3, qT, dQ[i])
                    to_featT(kn[b], dkn[b], 4, kT, dK[i])
                    kdst = kp[i * 128:(i + 1) * 128, :] if i < 16 else ks[:, :]
                    S.dma("sp", lambda e: e.dma_start(out=kdst, in_=kn[b][0:T, :]), dkn[b], "out")
                    S.op("dve", lambda e: e.tensor_copy(out=vres[0:T, i, :], in_=vn[b][0:T, :]),
                         reads=[dvn[b]], writes=[dV[i]])
                    vdst = vp[i * 128:(i + 1) * 128, :] if i < 16 else vs[:, :]
                    S.dma("sp", lambda e: e.dma_start(out=vdst, in_=vn[b][0:T, :]), dvn[b], "out")
                S.barrier()
                chk(3)
        p1s.close()

        with contextlib.ExitStack() as p2:
            ulT = sb([128, 256], BF16, stack=p2)
            dCB = dep("cb")
            uincl = ulT[:, 0:128]
            lstrict = ulT[:, 128:256]
            qblk = sb([128, 4, 16, 2, 4], BF16, stack=p2)
            kTp = sb([128, 4, 128], BF16, stack=p2)
            vpad = sb([128, 512], BF16, stack=p2)
            ebF = sb([128, 512], F32, stack=p2)
            mS2 = sb([128, 128], F32, stack=p2)
            dpad = dep("pad")
            with contextlib.ExitStack() as p2a:
                NSET = 8
                eT = [sb([128, 512], F32, stack=p2a) for _ in range(NSET)]
                xT_ = [sb([128, 512], F32, stack=p2a) for _ in range(NSET)]
                LpT = [sb([128, 512], BF16, stack=p2a) for _ in range(NSET)]
                wT = [sb([128, 512], BF16, stack=p2a) for _ in range(NSET)]
                de = [dep("e") for _ in range(NSET)]
                dx = [dep("x") for _ in range(NSET)]
                dL = [dep("L") for _ in range(NSET)]
                dw = [dep("w") for _ in range(NSET)]
                mDT = sb([128, 2048], BF16, stack=p2a)
                load_bf_const("uincl", ulT[:, 0:128], dCB, eT[0], de[0])
                load_bf_const("lstrict", ulT[:, 128:256], dCB, eT[1], de[1])
                load_bf_const("maskD", mDT, dCB, eT[0], de[0])
                def stream_units(qbs):
                    out = []
                    for j in range(4):
                        for QB in qbs:
                            nkb = 4 * QB + 4
                            for st, kb in enumerate(range(nkb - 1, -1, -1)):
                                for h2 in range(2):
                                    out.append((j, QB, st, kb, h2))
                    return out
                ua, ub = stream_units([3, 0]), stream_units([2, 1])
                units = []
                for n_ in range(max(len(ua), len(ub))):
                    if n_ < len(ua):
                        units.append(ua[n_] + (0,))
                    if n_ < len(ub):
                        units.append(ub[n_] + (1,))

                def geom(u):
                    j, QB, st, kb, h2, sid = u
                    ii = kb - 4 * QB
                    clo = 128 * ii if ii > 0 else 0
                    return ii, clo, slice(clo, 512), slice(QB * 512 + clo, (QB + 1) * 512)

                def parms(n):
                    j, QB, st, kb, h2, sid = units[n]
                    ii, clo, cs, qs = geom(units[n])
                    return j, QB, st, kb, h2, sid, ii, clo, cs, qs, 2 * j + h2, 64 * h2, n % NSET

                def a1(n):
                    j, QB, st, kb, h2, sid, ii, clo, cs, qs, h, hp, si = parms(n)
                    qdeps = [dQ[t] for t in range(QB * 4, QB * 4 + 4)]
                    S.op("pe", lambda e: e.matmul(
                        out=PS[h2][:, cs], lhsT=kT[hp:hp + 64, j, kb * 128:(kb + 1) * 128], rhs=qT[hp:hp + 64, j, qs],
                        start=True, stop=True), reads=[dK[kb]] + qdeps, writes=[PD[h2]])

                def a2(n):
                    j, QB, st, kb, h2, sid, ii, clo, cs, qs, h, hp, si = parms(n)
                    S.op("act", lambda e: e.activation(
                        out=eT[si][:, cs], in_=PS[h2][:, cs], func=AF.Exp, scale=SCALE, bias=biasT[:, h:h + 1]),
                         reads=[PD[h2], dPar], writes=[de[si]])
                    if ii >= 0:
                        S.op("dve", lambda e: e.tensor_tensor(
                            out=eT[si][:, cs], in0=eT[si][:, cs], in1=mDT[:, ii * 512 + clo:(ii + 1) * 512],
                            op=ALU.mult), reads=[de[si], dCB], writes=[de[si]])
                    S.op("act", lambda e: e.activation(out=LpT[si][:, cs], in_=eT[si][:, cs], func=AF.Ln, bias=1.0),
                         reads=[de[si]], writes=[dL[si]])

                def cbank(sid, h2):
                    return (2 + h2) if sid == 0 else (5 + h2)

                def b1(n):
                    j, QB, st, kb, h2, sid, ii, clo, cs, qs, h, hp, si = parms(n)
                    Cb = cbank(sid, h2)
                    S.op("pe", lambda e: e.matmul(
                        out=PS[Cb][:, cs], lhsT=uincl, rhs=LpT[si][:, cs], start=(st == 0), stop=False,
                        skip_group_check=True), reads=[dL[si], dCB], writes=[PD[Cb]])
                    S.op("act", lambda e: e.activation(out=xT_[si][:, cs], in_=PS[Cb][:, cs], func=AF.Exp, scale=-1.0),
                         reads=[PD[Cb]], writes=[dx[si]])

                def b2(n):
                    j, QB, st, kb, h2, sid, ii, clo, cs, qs, h, hp, si = parms(n)
                    Cb = cbank(sid, h2)
                    if kb > 0:
                        S.op("pe", lambda e: e.matmul(
                            out=PS[Cb][:, cs], lhsT=lstrict, rhs=LpT[si][:, cs], start=False, stop=(kb == 1),
                            skip_group_check=True), reads=[dL[si], dCB], writes=[PD[Cb]])
                    S.op("dve", lambda e: e.tensor_tensor(out=wT[si][:, cs], in0=eT[si][:, cs], in1=xT_[si][:, cs], op=ALU.mult),
                         reads=[de[si], dx[si]], writes=[dw[si]])

                def b3(n):
                    j, QB, st, kb, h2, sid, ii, clo, cs, qs, h, hp, si = parms(n)
                    ob = 4 if sid == 0 else 7
                    S.op("pe", lambda e: e.matmul(
                        out=PS[ob][hp:hp + 64, cs], lhsT=vres[:, kb, h * 64:(h + 1) * 64], rhs=wT[si][:, cs],
                        start=(st == 0), stop=(kb == 0), skip_group_check=True),
                         reads=[dw[si], dV[kb]], writes=[PD[ob]])
                    if kb == 0 and h2 == 1:
                        S.op("act", lambda e: e.activation(out=mixT[:, j, QB * 512:(QB + 1) * 512], in_=PS[ob][:, :],
                                                           func=AF.Copy),
                             reads=[PD[ob]], writes=[dMixA[QB * 4 + tt] for tt in range(4)])

                a1(0)
                a2(0)
                for n in range(len(units)):
                    b1(n)
                    if n + 1 < len(units):
                        a1(n + 1)
                    b2(n)
                    if n + 1 < len(units):
                        a2(n + 1)
                    b3(n)
                S.op("dve", lambda e: e.memset(kTp[:], 0.0), writes=[dpad])
                S.op("dve", lambda e: e.memset(vpad[:], 0.0), writes=[dpad])
                S.op("dve", lambda e: e.memset(qblk[:].rearrange("p j b a t -> p (j b a t)"), 0.0), writes=[dpad])
                S.op("dve", lambda e: e.tensor_copy(out=kTp[:, :, 0:64], in_=kT[:, :, TP:NTOK]), reads=[dK[16], dpad], writes=[dpad])
                S.op("dve", lambda e: e.tensor_copy(out=vpad[0:64, :], in_=vres[0:64, 16, :]), reads=[dV[16], dpad], writes=[dpad])
                for h2 in range(2):
                    S.op("dve", lambda e, h2=h2: e.tensor_copy(
                        out=qblk[64 * h2:64 * h2 + 64, :, :, h2, :],
                        in_=qT[64 * h2:64 * h2 + 64, :, TP:NTOK].rearrange("p j (b t) -> p j b t", t=4)),
                         reads=[dQ[16], dpad], writes=[dpad])
                S.op("act", lambda e: e.activation(out=eT[0][:, 0:8], in_=biasT[:, :], func=AF.Exp), reads=[dPar, de[0]],
                     writes=[de[0]])
                for j in range(4):
                    S.op("dve", lambda e, j=j: e.tensor_copy(
                        out=ebF[:, j * 128:(j + 1) * 128].rearrange("p (b a t) -> p b a t", b=16, a=2),
                        in_=eT[0][:, 2 * j:2 * j + 2].unsqueeze(1).unsqueeze(3).to_broadcast([128, 16, 2, 4])),
                         reads=[de[0]], writes=[dpad])
                S.op("dve", lambda e: e.tensor_copy(
                    out=mS2[:].rearrange("p (b a t) -> p b a t", b=16, a=2),
                    in_=cf("maskS").rearrange("p (b t) -> p b t", t=4).unsqueeze(2).to_broadcast([128, 16, 2, 4])),
                     reads=[dCF], writes=[dpad])
                S.barrier()
            chk(4)
            pa_r.close()

            NQ = 8
            GW = NQ * 8
            Kst = [sb([128, NQ, 512], F32, stack=p2) for _ in range(2)]
            Vst = [sb([128, NQ, 512], F32, stack=p2) for _ in range(2)]
            KTs = [sb([128, NQ, 4, 128], BF16, stack=p2) for _ in range(2)]
            Vb = [sb([128, NQ, 512], BF16, stack=p2) for _ in range(2)]
            dVb = [dep("Vb") for _ in range(2)]
            dKst = [dep("Kst") for _ in range(2)]
            dVst = [dep("Vst") for _ in range(2)]
            dKTs = [dep("KTs") for _ in range(2)]
            ptb = sb([128, NSEQ * NPG], I32, stack=p2)
            idx = sb([128, NSEQ * NPG], I32, stack=p2)
            iop = sb([128, 1], I32, stack=p2)
            dIdx = dep("idx")
            es_ = [sb([128, 4, 128], F32, stack=p2) for _ in range(2)]
            xs2 = [sb([128, 4, 128], F32, stack=p2) for _ in range(2)]
            Ls = [sb([128, 4, 128], BF16, stack=p2) for _ in range(2)]
            ws = [sb([128, 4, 128], BF16, stack=p2) for _ in range(2)]
            wsb = sb([128, 4, 128], BF16, stack=p2)
            des = [dep("es") for _ in range(2)]
            dxs = [dep("xs") for _ in range(2)]
            dLs = [dep("Ls") for _ in range(2)]
            dws = [dep("ws") for _ in range(2)]

            S.dma("pool", lambda e: e.dma_start(out=ptb[:], in_=pt[0:1, :].to_broadcast([128, NSEQ * NPG])), dIdx, "in")
            S.op("pool", lambda e: e.iota(iop[:], pattern=[[0, 1]], base=0, channel_multiplier=1), writes=[dIdx])
            S.op("pool", lambda e: e.tensor_scalar(out=idx[:], in0=ptb[:], scalar1=128, scalar2=None, op0=ALU.mult),
                 reads=[dIdx], writes=[dIdx])
            S.op("pool", lambda e: e.tensor_tensor(out=idx[:], in0=idx[:], in1=iop[:].to_broadcast([128, NSEQ * NPG]),
                                                   op=ALU.add), reads=[dIdx], writes=[dIdx])

            dZ, dC, dOs = dep("Z"), dep("C"), dep("Os")

            def bview(bank, c0, w_):
                return PS[bank][:, :].rearrange("p (j c) -> p j c", j=4)[:, :, c0:c0 + w_]

            def sample_step(kind, gi=None, buf=None, last=False, first=False, b2=0):
                c0, w_ = (0, 128) if kind == "new" else (gi * GW, GW)
                if kind == "new":
                    for j in range(4):
                        S.op("pe", lambda e, j=j: e.matmul(out=PS[0][:, j * 128:(j + 1) * 128], lhsT=kTp[:, j, :],
                                                           rhs=qblk[:, j, :, :, :].rearrange("p b a t -> p (b a t)"),
                                                           start=True, stop=True), reads=[dpad], writes=[dZ], inc=(j == 3))
                else:
                    for bi in range(NQ):
                        b = gi * NQ + bi
                        for j in range(4):
                            S.op("pe", lambda e, bi=bi, b=b, j=j: e.matmul(
                                out=PS[0][:, j * 128 + b * 8:j * 128 + b * 8 + 8], lhsT=KTs[buf][:, bi, j, :],
                                rhs=qblk[:, j, b, :, :].rearrange("p a t -> p (a t)"), start=True, stop=True),
                                 reads=[dKTs[buf], dpad], writes=[dZ], inc=(bi == NQ - 1 and j == 3))
                ev = es_[b2][:, :, 0:w_]
                S.op("act", lambda e: e.activation(out=ev, in_=bview(0, c0, w_), func=AF.Exp, scale=SCALE),
                     reads=[dZ], writes=[des[b2]])
                S.op("dve", lambda e: e.tensor_tensor(out=ev, in0=ev,
                                                      in1=ebF[:, :].rearrange("p (j c) -> p j c", j=4)[:, :, c0:c0 + w_],
                                                      op=ALU.mult), reads=[des[b2], dpad], writes=[des[b2]])
                if kind == "new":
                    S.op("dve", lambda e: e.tensor_tensor(out=ev, in0=ev, in1=mS2[:, :].unsqueeze(1).to_broadcast([128, 4, 128]),
                                                          op=ALU.mult), reads=[des[b2], dpad], writes=[des[b2]])
                S.op("act", lambda e: e.activation(out=Ls[b2][:, :, 0:w_], in_=ev, func=AF.Ln, bias=1.0),
                     reads=[des[b2]], writes=[dLs[b2]])
                for j in range(4):
                    S.op("pe", lambda e, j=j: e.matmul(out=PS[1][:, j * 128 + c0:j * 128 + c0 + w_], lhsT=uincl,
                                                       rhs=Ls[b2][:, j, 0:w_], start=(first and j == 0), stop=False,
                                                       skip_group_check=True),
                         reads=[dLs[b2], dCB], writes=[dC], inc=(j == 3))
                S.op("act", lambda e: e.activation(out=xs2[b2][:, :, 0:w_], in_=bview(1, c0, w_), func=AF.Exp, scale=-1.0),
                     reads=[dC], writes=[dxs[b2]])
                if not last:
                    for j in range(4):
                        S.op("pe", lambda e, j=j: e.matmul(out=PS[1][:, j * 128 + c0:j * 128 + c0 + w_], lhsT=lstrict,
                                                           rhs=Ls[b2][:, j, 0:w_], start=False, stop=False,
                                                           skip_group_check=True),
                             reads=[dLs[b2], dCB], writes=[dC], inc=(j == 3))
                if kind == "new":
                    S.op("dve", lambda e: e.tensor_tensor(out=wsb[:, :, :], in0=ev, in1=xs2[b2][:, :, 0:w_], op=ALU.mult),
                         reads=[des[b2], dxs[b2]], writes=[dws[b2]])
                    for j in range(4):
                        S.op("pe", lambda e, j=j: e.matmul(out=PS[2][:, j * 128:(j + 1) * 128],
                                                           lhsT=vpad[:, j * 128:(j + 1) * 128], rhs=wsb[:, j, :],
                                                           start=(j == 0), stop=False, skip_group_check=True),
                             reads=[dws[b2], dpad], writes=[dOs], inc=(j == 3))
                else:
                    S.op("dve", lambda e: e.tensor_tensor(out=ws[b2][:, :, 0:w_], in0=ev, in1=xs2[b2][:, :, 0:w_], op=ALU.mult),
                         reads=[des[b2], dxs[b2]], writes=[dws[b2]])
                    for bi in range(NQ):
                        b = gi * NQ + bi
                        for j in range(4):
                            S.op("pe", lambda e, bi=bi, b=b, j=j: e.matmul(
                                out=PS[2][:, j * 128 + b * 8:j * 128 + b * 8 + 8], lhsT=Vb[buf][:, bi, j * 128:(j + 1) * 128],
                                rhs=ws[b2][:, j, bi * 8:bi * 8 + 8], start=False, stop=False, skip_group_check=True),
                                 reads=[dws[b2], dVb[buf]], writes=[dOs], inc=(bi == NQ - 1 and j == 3))

            sample_step("new", first=True)
            gcount = 0
            for p in range(NPG - 1, -1, -1):
                for gi in range(NSEQ // NQ):
                    buf = gcount % 2
                    gcount += 1
                    for bi in range(NQ):
                        b = gi * NQ + bi
                        col = b * NPG + p
                        S.dma("pool", lambda e, bi=bi, col=col, buf=buf: e.indirect_dma_start(
                            out=Kst[buf][:, bi, :], out_offset=None, in_=ck,
                            in_offset=bass.IndirectOffsetOnAxis(ap=idx[:, col:col + 1], axis=0)),
                              dKst[buf], "in", extra_reads=[dIdx])
                        S.dma("pool", lambda e, bi=bi, col=col, buf=buf: e.indirect_dma_start(
                            out=Vst[buf][:, bi, :], out_offset=None, in_=cv,
                            in_offset=bass.IndirectOffsetOnAxis(ap=idx[:, col:col + 1], axis=0)),
                              dVst[buf], "in", extra_reads=[dIdx])
                    for bi in range(NQ):
                        bank = 5 + (bi % 2)
                        for jj in range(4):
                            S.op("pe", lambda e, bi=bi, jj=jj, bank=bank, buf=buf: e.transpose(
                                out=PS[bank][:, jj * 128:(jj + 1) * 128], in_=Kst[buf][:, bi, jj * 128:(jj + 1) * 128],
                                identity=ident), reads=[dKst[buf], dCF], writes=[PD[bank]], inc=(jj == 3))
                        if bi % 2 == 0:
                            S.op("dve", lambda e, bi=bi, bank=bank, buf=buf: e.tensor_copy(
                                out=KTs[buf][:, bi, :, :].rearrange("p j t -> p (j t)"), in_=PS[bank][:, :]),
                                 reads=[PD[bank]], writes=[dKTs[buf]])
                        else:
                            S.op("act", lambda e, bi=bi, bank=bank, buf=buf: e.activation(
                                out=KTs[buf][:, bi, :, :].rearrange("p j t -> p (j t)"), in_=PS[bank][:, :], func=AF.Copy),
                                 reads=[PD[bank]], writes=[dKTs[buf]])
                    for hv_ in range(2):
                        S.op("act" if hv_ == 0 else "dve",
                             (lambda e, buf=buf: e.activation(
                                 out=Vb[buf][:, 0:NQ // 2, :].rearrange("p b n -> p (b n)"),
                                 in_=Vst[buf][:, 0:NQ // 2, :].rearrange("p b n -> p (b n)"), func=AF.Copy)) if hv_ == 0 else
                             (lambda e, buf=buf: e.tensor_copy(
                                 out=Vb[buf][:, NQ // 2:NQ, :].rearrange("p b n -> p (b n)"),
                                 in_=Vst[buf][:, NQ // 2:NQ, :].rearrange("p b n -> p (b n)"))),
                             reads=[dVst[buf]], writes=[dVb[buf]])
                    sample_step("page", gi=gi, buf=buf, last=(p == 0), b2=gcount % 2)
            for h2 in range(2):
                S.op("act", lambda e, h2=h2: e.activation(
                    out=mixT[64 * h2:64 * h2 + 64, 0:4, TP:NTOK].rearrange("p j (b t) -> p j b t", t=4),
                    in_=PS[2][64 * h2:64 * h2 + 64, :].rearrange("p (j b a t) -> p j b a t", j=4, b=16, a=2)[:, :, :, h2, :],
                    func=AF.Copy), reads=[dOs], writes=[dMixA[16]])
            S.barrier()

        w_out_v = w_out.rearrange("(k p) n -> p k n", p=128)
        w_up_v = w_up.rearrange("(k p) n -> p k n", p=128)
        w_down_v = w_down.rearrange("(c p) n -> p c n", p=128)
        w_ada_v = w_ada.rearrange("(k p) n -> p k n", p=128)
        gBp = sb([128, 2048], F32)
        gBs = sb([64, 2048], F32)
        dG = dep("gB")
        wst = [sb([128, 8, 256], F32) for _ in range(2)]
        dW = [dep("wst") for _ in range(2)]
        with contextlib.ExitStack() as pg:
            cTp = sb([128, 8, 128], F32, stack=pg)
            cTs = sb([128, 8, 64], F32, stack=pg)
            bgB = sb([128, 2048], F32, stack=pg)
            dbg = dep("bgB")
            dcTx = dep("cTx")
            S.dma("sp", lambda e: e.dma_start(out=bgB[:], in_=bg[0:1, :].to_broadcast([128, 2048])), dbg, "in")
            S.op("dve", lambda e: e.tensor_copy(out=cTp[:], in_=cT[:, :, 16:17].to_broadcast([128, 8, 128])),
                 reads=[dcT], writes=[dcTx])
            S.op("dve", lambda e: e.tensor_copy(out=cTs[:].rearrange("p c (b t) -> p c b t", t=4),
                                                in_=cT[:, :, 0:16].unsqueeze(3).to_broadcast([128, 8, 16, 4])),
                 reads=[dcT], writes=[dcTx])
            for n_ in range(8):
                buf = n_ % 2
                acol = (2048 if n_ < 4 else 5120) + (n_ % 4) * 256
                gcol = n_ * 256
                S.dma("sp", lambda e, acol=acol, buf=buf: e.dma_start(out=wst[buf][:], in_=w_ada_v[:, :, acol:acol + 256]),
                      dW[buf], "in")
                for (lhs, rows, dst, bank) in ((cTp, 128, gBp, 2), (cTs, 64, gBs, 3)):
                    for k in range(8):
                        S.op("pe", lambda e, k=k, lhs=lhs, rows=rows, bank=bank, buf=buf: e.matmul(
                            out=PS[bank][0:rows, 0:256], lhsT=lhs[:, k, :], rhs=wst[buf][:, k, :],
                            start=(k == 0), stop=(k == 7)),
                             reads=[dcTx, dW[buf]], writes=[PD[bank]], inc=(k == 7))
                    S.op("dve", lambda e, rows=rows, dst=dst, bank=bank, gcol=gcol: e.tensor_tensor(
                        out=dst[0:rows, gcol:gcol + 256], in0=PS[bank][0:rows, 0:256], in1=bgB[0:rows, gcol:gcol + 256],
                        op=ALU.add), reads=[PD[bank], dbg], writes=[dG])
            S.barrier()
            chk(6)

        def wst_flat(buf, c):
            return wst[buf][:].rearrange("p k n -> p (k n)").rearrange("p (c n) -> p c n", c=c)

        halves = [list(range(0, 8)), list(range(8, 17))]
        for hi, tiles in enumerate(halves):
            with contextlib.ExitStack() as p3:
                nt = len(tiles)
                ncols = sum(tile_rows(i) for i in tiles)
                col0 = tiles[0] * 128
                x1 = sb([128, nt, D], F32, stack=p3)
                dx1 = [dep("x1") for _ in range(nt)]
                h2T = sb([128, 8, ncols], BF16, stack=p3)
                dh2 = [dep("h2T") for _ in range(nt)]
                with contextlib.ExitStack() as p3a:
                    wo = sb([128, 8, D], BF16, stack=p3a)
                    dwo = [dep("wo") for _ in range(4)]
                    for g in range(4):
                        load_weight_piece(w_out_v[:, :, g * 256:(g + 1) * 256], wo[:, :, g * 256:(g + 1) * 256], wst, dW, g % 2,
                                          dwo[g])
                    xt = [sb([128, D], F32, stack=p3a) for _ in range(2)]
                    dxt = [dep("xt") for _ in range(2)]
                    wk = [(sb([128, D], F32, stack=p3a), sb([128, D], F32, stack=p3a), sb([128, 4], F32, stack=p3a))
                          for _ in range(2)]
                    dwk = [dep("wk") for _ in range(2)]
                    tmp = [sb([128, 512], F32, stack=p3a) for _ in range(2)]
                    dtm = [dep("tmp") for _ in range(2)]
                    def outproj(li):
                        i = tiles[li]
                        T = tile_rows(i)
                        c0, c1 = tile_cols(i)
                        b = li % 2
                        load_x_tile(i, xt[b], dxt[b])
                        gB = gBp if i < 16 else gBs
                        for nh in range(2):
                            bank = nh
                            ns = slice(nh * 512, (nh + 1) * 512)
                            for k in range(8):
                                md = dMixA[i] if k < 4 else dMixB[i]
                                S.op("pe", lambda e, k=k, bank=bank, ns=ns: e.matmul(
                                    out=PS[bank][0:T, :], lhsT=mixT[:, k, c0:c1], rhs=wo[:, k, ns], start=(k == 0), stop=(k == 7)),
                                     reads=[md, dwo[2 * nh], dwo[2 * nh + 1]], writes=[PD[bank]], inc=(k == 7))
                            S.op("dve", lambda e, bank=bank, ns=ns, nh=nh: e.tensor_tensor(
                                out=tmp[nh][0:T, :], in0=PS[bank][0:T, :], in1=gB[0:T, ns], op=ALU.mult),
                                 reads=[PD[bank], dG], writes=[dtm[nh]])
                            S.op("pool", lambda e, ns=ns, nh=nh: e.tensor_tensor(
                                out=x1[0:T, li, ns], in0=tmp[nh][0:T, :], in1=xt[b][0:T, ns], op=ALU.add),
                                 reads=[dtm[nh], dxt[b]], writes=[dx1[li]])

                    def norm2(li):
                        i = tiles[li]
                        T = tile_rows(i)
                        c0, c1 = tile_cols(i)
                        b = li % 2
                        lc0 = c0 - col0
                        norm_transpose(i, x1[0:T, li, :], dx1[li], 2, h2T[:, :, lc0:lc0 + T], dh2[li], wk[b], dwk[b],
                                       (2, 3) if b == 0 else (4, 5))

                    outproj(0)
                    for li in range(nt):
                        if li + 1 < nt:
                            outproj(li + 1)
                        norm2(li)
                    S.barrier()
                    chk(7)
                with contextlib.ExitStack() as p4:
                    wu = [sb([128, 8, 512], BF16, stack=p4) for _ in range(2)]
                    wd = [sb([128, 4, D], BF16, stack=p4) for _ in range(2)]
                    dwu = [dep("wu") for _ in range(2)]
                    dwd = [dep("wd") for _ in range(2)]
                    upT = [sb([128, 4, 512], BF16, stack=p4) for _ in range(2)]
                    dup = [dep("up") for _ in range(2)]
                    rl = [sb([128, 512], F32, stack=p4) for _ in range(2)]
                    drl = [dep("rl") for _ in range(2)]
                    tmp = [sb([128, 512], F32, stack=p4) for _ in range(2)]
                    dtm = [dep("tmp") for _ in range(2)]
                    groups = []
                    li = 0
                    while li < nt:
                        g = [l for l in range(li, min(li + 4, nt)) if tile_rows(tiles[l]) == 128]
                        if not g:
                            g = [li]
                        groups.append(g)
                        li = g[-1] + 1
                    gctr = 0
                    rctr = 0
                    dx1h = [[dep("x1h") for _ in range(2)] for _ in range(nt)]
                    for l_ in range(nt):
                        for nh_ in range(2):
                            dx1h[l_][nh_].w = dx1[l_].w
                    def load_w(E):
                        wb = E % 2
                        for hh in range(2):
                            S.dma("sp", lambda e, hh=hh: e.dma_start(
                                out=wst[0][:], in_=w_up_v[:, :, E * 512 + hh * 256:E * 512 + (hh + 1) * 256]), dW[0], "in")
                            S.op("act", lambda e, hh=hh: e.activation(out=wu[wb][:, :, hh * 256:(hh + 1) * 256], in_=wst[0][:],
                                                                      func=AF.Copy),
                                 reads=[dW[0]], writes=[dwu[wb]])
                            S.dma("sp", lambda e, hh=hh: e.dma_start(
                                out=wst_flat(1, 2), in_=w_down_v[:, E * 4 + hh * 2:E * 4 + (hh + 1) * 2, :]), dW[1], "in")
                            S.op("pool", lambda e, hh=hh: e.tensor_copy(out=wd[wb][:, hh * 2:(hh + 1) * 2, :], in_=wst_flat(1, 2)),
                                 reads=[dW[1]], writes=[dwd[wb]])

                    def up_part(E, g, ub):
                        wb = E % 2
                        gcol0 = tiles[g[0]] * 128 - col0
                        gn = sum(tile_rows(tiles[l]) for l in g)
                        for fc in range(4):
                            bank = fc % 2
                            rb = fc % 2
                            for k in range(8):
                                S.op("pe", lambda e, k=k, fc=fc, bank=bank: e.matmul(
                                    out=PS[bank][:, 0:gn], lhsT=wu[wb][:, k, fc * 128:(fc + 1) * 128],
                                    rhs=h2T[:, k, gcol0:gcol0 + gn], start=(k == 0), stop=(k == 7)),
                                     reads=[dwu[wb]] + [dh2[l] for l in g], writes=[PD[bank]], inc=(k == 7))
                            S.op("act", lambda e, bank=bank, rb=rb: e.activation(out=rl[rb][:, 0:gn], in_=PS[bank][:, 0:gn],
                                                                                 func=AF.Relu),
                                 reads=[PD[bank]], writes=[drl[rb]])
                            S.op("act", lambda e, rb=rb, fc=fc: e.activation(
                                out=upT[ub][:, fc, 0:gn], in_=rl[rb][:, 0:gn], func=AF.Square),
                                 reads=[drl[rb]], writes=[dup[ub]])

                    def down_part(E, g, ub):
                        wb = E % 2
                        gcol0 = tiles[g[0]] * 128 - col0
                        for l in g:
                            i = tiles[l]
                            T = tile_rows(i)
                            lc = tiles[l] * 128 - col0 - gcol0
                            gB = gBp if i < 16 else gBs
                            for nh in range(2):
                                bank = 2 + nh + 2 * (l % 2)
                                ns = slice(nh * 512, (nh + 1) * 512)
                                for fc in range(4):
                                    S.op("pe", lambda e, fc=fc, bank=bank, ns=ns: e.matmul(
                                        out=PS[bank][0:T, :], lhsT=upT[ub][:, fc, lc:lc + T], rhs=wd[wb][:, fc, ns],
                                        start=(fc == 0), stop=(fc == 3)),
                                         reads=[dup[ub], dwd[wb]], writes=[PD[bank]], inc=(fc == 3))
                                S.op("dve", lambda e, bank=bank, nh=nh: e.tensor_tensor(
                                    out=tmp[nh][0:T, :], in0=PS[bank][0:T, :], in1=gB[0:T, 1024 + nh * 512:1024 + (nh + 1) * 512],
                                    op=ALU.mult), reads=[PD[bank], dG], writes=[dtm[nh]])
                                S.op("dve" if nh == 0 else "pool", lambda e, ns=ns, nh=nh: e.tensor_tensor(
                                    out=x1[0:T, l, ns], in0=x1[0:T, l, ns], in1=tmp[nh][0:T, :], op=ALU.add),
                                     reads=[dtm[nh], dx1h[l][nh]], writes=[dx1h[l][nh]])
                            if E == 7:
                                ydst = yp[i * 128:(i + 1) * 128, :] if i < 16 else ys[:, :]
                                for nh in range(2):
                                    S.dma("sp", lambda e, ydst=ydst, nh=nh: e.dma_start(
                                        out=ydst[:, nh * 512:(nh + 1) * 512], in_=x1[0:T, l, nh * 512:(nh + 1) * 512]),
                                          dx1h[l][nh], "out")

                    seq = [(E, g) for E in range(8) for g in groups]
                    load_w(0)
                    up_part(seq[0][0], seq[0][1], 0)
                    for n_, (E, g) in enumerate(seq):
                        if g is groups[0] and E + 1 < 8:
                            load_w(E + 1)
                        if n_ + 1 < len(seq):
                            up_part(seq[n_ + 1][0], seq[n_ + 1][1], (n_ + 1) % 2)
                        down_part(E, g, n_ % 2)
                    S.barrier()
    except _Stop:
        S.finish()
        return nc
    S.finish()
    es.close()
    return nc


def _core_inputs(c, inp, n_phys=None, ck=None, cv=None, pt=None):
    f = np.float32
    d = {}
    d["xp"] = np.ascontiguousarray(inp["x_prompt"][c], dtype=f)
    d["xs"] = np.ascontiguousarray(inp["x_sample"][16 * c:16 * c + 16].reshape(TS, D), dtype=f)
    d["ck"] = ck if ck is not None else inp["cache_k"][0].reshape(-1, 512)
    d["cv"] = cv if cv is not None else inp["cache_v"][0].reshape(-1, 512)
    d["st0"] = np.ascontiguousarray(inp["state_hgrn"][0, 16 * c:16 * c + 16], dtype=f)
    ptc = pt if pt is not None else inp["page_table"][16 * c:16 * c + 16]
    d["pt"] = np.ascontiguousarray(ptc.reshape(1, -1), dtype=np.int32)
    d["cc"] = np.ascontiguousarray(np.concatenate([inp["c_sample"][16 * c:16 * c + 16], inp["c_prompt"][c:c + 1]], 0), dtype=f)
    d["w_ada"] = np.ascontiguousarray(inp["w_ada"][0], dtype=f)
    b_ada = inp["b_ada"][0]
    d["vecs"] = np.ascontiguousarray(np.concatenate([b_ada.reshape(48, 128), inp["norm1_g"][0].reshape(8, 128),
                                                     inp["norm2_g"][0].reshape(8, 128)], 0), dtype=f)
    d["bg"] = np.ascontiguousarray(np.concatenate([b_ada[2048:3072], b_ada[5120:6144]])[None], dtype=f)
    d["w_in"] = np.ascontiguousarray(inp["w_in"][0], dtype=f)
    d["qkg"] = np.ascontiguousarray(np.concatenate([np.tile(inp["q_norm_g"][0], 8), np.tile(inp["k_norm_g"][0], 8),
                                                    np.tile(inp["hg_out_g"][0], 8)])[None], dtype=f)
    d["sbb"] = np.ascontiguousarray(inp["sb_bias"][0][None], dtype=f)
    d["lbl"] = np.ascontiguousarray(inp["hg_lb_logits"].reshape(1, 1024), dtype=f)
    d["w_out"] = np.ascontiguousarray(inp["w_out"][0], dtype=f)
    d["w_up"] = np.ascontiguousarray(inp["w_up"][0], dtype=f)
    d["w_down"] = np.ascontiguousarray(inp["w_down"][0], dtype=f)
    d["cstf"] = _CARR
    d["cstb"] = _BARR
    return d


def _assemble(results):
    y_p = np.stack([r["yp"] for r in results]).astype(np.float32)
    y_s = np.concatenate([r["ys"].reshape(16, 4, D) for r in results]).astype(np.float32)
    k_p = np.stack([r["kp"].reshape(TP, 8, 64) for r in results])[None].astype(np.float32)
    v_p = np.stack([r["vp"].reshape(TP, 8, 64) for r in results])[None].astype(np.float32)
    k_s = np.concatenate([r["ks"].reshape(16, 4, 8, 64) for r in results])[None].astype(np.float32)
    v_s = np.concatenate([r["vs"].reshape(16, 4, 8, 64) for r in results])[None].astype(np.float32)
    s_p = np.stack([r["sp"] for r in results])[None].astype(np.float32)
    s_s = np.concatenate([r["ss"] for r in results])[None].astype(np.float32)
    return (y_p, y_s, k_p, v_p, k_s, v_s, s_p, s_s)


def kernel(**inputs):
    inp = {k: np.asarray(v) for k, v in inputs.items()}
    n_phys = inp["cache_k"].shape[1]
    nc = build(n_phys)
    ck = np.ascontiguousarray(inp["cache_k"][0].reshape(-1, 512), dtype=np.float32)
    cv = np.ascontiguousarray(inp["cache_v"][0].reshape(-1, 512), dtype=np.float32)
    in_maps = [_core_inputs(c, inp, ck=ck, cv=cv) for c in range(NCORES)]
    res = run_bass_kernel_spmd(nc, in_maps, core_ids=list(range(NCORES)))
    return _assemble(res.results)
```

```python
import contextlib
import os
import numpy as np
import ml_dtypes
import concourse.bass as bass
import concourse.mybir as mybir
from concourse.bass_utils import run_bass_kernel_spmd

F32 = mybir.dt.float32
BF16 = mybir.dt.bfloat16
I32 = mybir.dt.int32
AF = mybir.ActivationFunctionType
ALU = mybir.AluOpType
AX = mybir.AxisListType

NCORES = 8
D = 1024
TP = 2048
NSEQ = 16
TS = 64
NTOK = TP + TS
NPG = 16
EPS = 1e-6
SCALE = 64 ** -0.5
NT = 17


def _consts():
    f = {}
    idx = np.arange(128)
    f["ident"] = np.eye(128, dtype=np.float32)
    s = idx[:, None]
    t = idx[None, :]
    same64 = (s // 64) == (t // 64)
    f["tri2"] = ((s <= t) & same64).astype(np.float32)
    f["tsu2"] = ((s > t) & same64).astype(np.float32)
    same4 = (s // 4) == (t // 4)
    f["tri4"] = ((s <= t) & same4).astype(np.float32)
    f["tsu4"] = ((s > t) & same4).astype(np.float32)
    ci = np.zeros((128, 2), np.float32)
    ci[:64, 0] = 1
    ci[64:, 1] = 1
    f["chunkind"] = ci
    si = np.zeros((128, 16), np.float32)
    for p in range(64):
        si[p, p // 4] = 1
    f["seqind"] = si
    mh = np.zeros((128, 64), np.float32)
    for p in range(128):
        mh[p, :] = (np.arange(64) >= (p % 64))
    f["maskH"] = mh
    ms = np.zeros((128, 64), np.float32)
    for p in range(64):
        for c in range(64):
            ms[p, c] = (p // 4 == c // 4) and (p % 4 <= c % 4)
    f["maskHS"] = ms
    ma = np.zeros((128, 64), np.float32)
    for p in range(64):
        for c in range(64):
            ma[p, c] = (p // 4 == c // 4) and (p % 4 < c % 4)
    f["maskS"] = ma
    smt = np.zeros((128, 16 * 64), np.float32)
    for b in range(16):
        smt[:, b * 64 + 4 * b: b * 64 + 4 * b + 4] = 1
    f["seqmaskT"] = smt
    md = np.zeros((128, 4 * 512), np.float32)
    for i in range(4):
        md[:, i * 512:(i + 1) * 512] = ((128 * i + idx[:, None]) < np.arange(512)[None, :])
    f["maskD"] = md
    f["uincl"] = (s >= t).astype(np.float32)
    f["lstrict"] = (s < t).astype(np.float32)
    io = np.zeros((128, 1), np.float32)
    off = {}
    cols = 0
    for k, v in f.items():
        off[k] = (cols, v.shape[1])
        cols += v.shape[1]
    bfk = ["maskD", "uincl", "lstrict", "seqmaskT"]
    off = {}
    cols = 0
    for k, v in f.items():
        if k in bfk:
            continue
        off[k] = (cols, v.shape[1])
        cols += v.shape[1]
    arr = np.concatenate([f[k] for k in f if k not in bfk], axis=1).astype(np.float32)
    boff = {}
    cols = 0
    for k in bfk:
        boff[k] = (cols, f[k].shape[1])
        cols += f[k].shape[1]
    barr = np.concatenate([f[k] for k in bfk], axis=1).astype(np.float32)
    return arr, off, barr, boff


_CARR, _COFF, _BARR, _BOFF = _consts()


class _Stop(Exception):
    pass


class Dep:
    __slots__ = ("name", "w", "re", "rd", "sem", "cnt")

    def __init__(self, name):
        self.name = name
        self.w = None
        self.re = {}
        self.rd = None
        self.sem = None
        self.cnt = 0


class Sched:
    def __init__(self, nc, es):
        self.nc = nc
        self.es = es
        self.eng = {}
        for name, h in [("pe", nc.tensor), ("act", nc.scalar), ("dve", nc.vector),
                        ("pool", nc.gpsimd), ("sp", nc.sync)]:
            sem = es.enter_context(nc.semaphore("sem_" + name))
            self.eng[name] = dict(h=h, sem=sem, cnt=0, waited={})
        self.dma_deps = []

    def _wait(self, ename, entry):
        E = self.eng[ename]
        if entry[0] == "e":
            _, src, idx = entry
            if src == ename and ename == "pe":
                return
            sem = self.eng[src]["sem"]
            val = idx
        else:
            _, dep, n = entry
            sem = dep.sem
            val = 16 * n
        key = id(sem)
        if E["waited"].get(key, 0) >= val:
            return
        E["waited"][key] = val
        E["h"].wait_ge(sem, val)

    def _deps(self, ename, reads, writes):
        for d in reads:
            if d.w is not None:
                self._wait(ename, d.w)
        for d in writes:
            if d.w is not None and not (d.w[0] == "e" and d.w[1] == ename):
                self._wait(ename, d.w)
            for src, idx in d.re.items():
                if src != ename:
                    self._wait(ename, ("e", src, idx))
            if d.rd is not None:
                self._wait(ename, d.rd)

    def op(self, ename, fn, reads=(), writes=(), inc=True):
        E = self.eng[ename]
        self._deps(ename, reads, writes)
        ins = fn(E["h"])
        if inc:
            E["cnt"] += 1
            idx = E["cnt"]
            ins.then_inc(E["sem"], 1)
        else:
            idx = E["cnt"] + 1
        for d in reads:
            d.re[ename] = idx
        for d in writes:
            d.w = ("e", ename, idx)
            d.re = {}
            d.rd = None
        return ins

    def dma(self, qname, fn, dep, direction, extra_reads=()):
        E = self.eng[qname]
        if dep.sem is None:
            dep.sem = self.es.enter_context(self.nc.semaphore("dsem_" + dep.name))
            self.dma_deps.append(dep)
        if direction == "in":
            if dep.w is not None and dep.w[0] == "d" and dep.w[1] is dep and not dep.re and dep.rd is None:
                for d in extra_reads:
                    if d.w is not None:
                        self._wait(qname, d.w)
            else:
                self._deps(qname, list(extra_reads), [dep])
        else:
            self._deps(qname, list(extra_reads) + [dep], [])
        ins = fn(E["h"])
        dep.cnt += 1
        ins.then_inc(dep.sem, 16)
        ent = ("d", dep, dep.cnt)
        if direction == "in":
            dep.w = ent
            dep.re = {}
            dep.rd = None
        else:
            dep.rd = ent
        return ins

    def barrier(self):
        names = ["pe", "act", "dve", "pool", "sp"]
        for a in names:
            for b in names:
                if a != b and self.eng[b]["cnt"] > 0:
                    self._wait(a, ("e", b, self.eng[b]["cnt"]))
            for d in self.dma_deps:
                if d.cnt > 0:
                    self._wait(a, ("d", d, d.cnt))

    def finish(self):
        for d in self.dma_deps:
            if d.cnt > 0:
                self._wait("sp", ("d", d, d.cnt))
        for b in ["pe", "act", "dve", "pool"]:
            if self.eng[b]["cnt"] > 0:
                self._wait("sp", ("e", b, self.eng[b]["cnt"]))


def build(n_phys):
    nc = bass.Bass("TRN2", target_bir_lowering=False)
    es = contextlib.ExitStack()

    def din(name, shape, dt=F32):
        return nc.dram_tensor(name, shape, dt, kind="ExternalInput").ap()

    def dout(name, shape, dt=F32):
        return nc.dram_tensor(name, shape, dt, kind="ExternalOutput").ap()

    xp = din("xp", [TP, D])
    xs_d = din("xs", [TS, D])
    ck = din("ck", [n_phys * 128, 512])
    cv = din("cv", [n_phys * 128, 512])
    st0 = din("st0", [NSEQ, 8, 64, 64])
    pt = din("pt", [1, NSEQ * NPG], I32)
    cc = din("cc", [17, D])
    w_ada = din("w_ada", [D, 6 * D])
    vecs = din("vecs", [64, 128])
    bg = din("bg", [1, 2048])
    w_in = din("w_in", [D, 3584])
    qkg = din("qkg", [1, 3 * 512])
    sbb = din("sbb", [1, 8])
    lbl = din("lbl", [1, 1024])
    w_out = din("w_out", [D, D])
    w_up = din("w_up", [D, 4 * D])
    w_down = din("w_down", [4 * D, D])
    cstf = din("cstf", list(_CARR.shape))
    cstb = din("cstb", list(_BARR.shape))

    yp = dout("yp", [TP, D])
    ys = dout("ys", [TS, D])
    kp = dout("kp", [TP, 512])
    vp = dout("vp", [TP, 512])
    ks = dout("ks", [TS, 512])
    vs = dout("vs", [TS, 512])
    sp_o = dout("sp", [8, 64, 64])
    ss_o = dout("ss", [NSEQ, 8, 64, 64])

    S = Sched(nc, es)
    ucnt = [0]

    def sb(shape, dt=F32, side="left", stack=None, name=None):
        ucnt[0] += 1
        nm = (name or "t") + str(ucnt[0])
        return (stack or es).enter_context(nc.sbuf_tensor(nm, shape, dt, side=side))

    def dep(name="d"):
        ucnt[0] += 1
        return Dep(name + str(ucnt[0]))

    PS = [es.enter_context(nc.psum_tensor(f"ps{i}", [128, 512], F32)) for i in range(8)]
    PD = [dep(f"ps{i}_") for i in range(8)]

    CF = sb([128, _CARR.shape[1]], F32)
    dCF = dep("cf")
    S.dma("sp", lambda e: e.dma_start(out=CF[:], in_=cstf[:, :]), dCF, "in")

    def cf(key, rows=128, c0=0, c1=None):
        o, n = _COFF[key]
        c1 = n if c1 is None else c1
        return CF[0:rows, o + c0:o + c1]

    def load_bf_const(key, dst_ap, ddst, scratch, dscr):
        o, n = _BOFF[key]
        for a in range(0, n, 512):
            w_ = min(512, n - a)
            S.dma("sp", lambda e, a=a, w_=w_: e.dma_start(out=scratch[:, 0:w_], in_=cstb[:, o + a:o + a + w_]), dscr, "in")
            S.op("dve", lambda e, a=a, w_=w_: e.tensor_copy(out=dst_ap[:, a:a + w_], in_=scratch[:, 0:w_]),
                 reads=[dscr], writes=[ddst])

    ident = cf("ident")

    vecT = sb([128, 64], F32)
    dVec = dep("vecT")
    biasT = sb([128, 8], F32)
    dPar = dep("par")
    cT = sb([128, 8, 17], F32)
    dcT = dep("cT")
    mod = sb([128, 4, 8, 17], F32)
    dMod = dep("mod")
    p1s = contextlib.ExitStack()
    qkgB = sb([128, 1536], F32, stack=p1s)
    lbB = sb([128, 512], F32, stack=p1s)
    omlB = sb([128, 512], F32, stack=p1s)
    mixT = sb([128, 8, NTOK], BF16, side="right")
    dMixA = [dep("mixA") for _ in range(NT)]
    dMixB = [dep("mixB") for _ in range(NT)]

    def tile_rows(i):
        return 128 if i < 16 else 64

    def tile_cols(i):
        return (i * 128, i * 128 + tile_rows(i))

    def rsqrt_small(src_ap, dst_ap, n_scale, deps_r, dep_w, tmp_ap):
        S.op("dve", lambda e: e.tensor_scalar(out=tmp_ap, in0=src_ap, scalar1=n_scale, scalar2=EPS,
                                              op0=ALU.mult, op1=ALU.add), reads=deps_r, writes=[dep_w])
        S.op("act", lambda e: e.activation(out=tmp_ap, in_=tmp_ap, func=AF.Ln), reads=[dep_w], writes=[dep_w])
        S.op("act", lambda e: e.activation(out=dst_ap, in_=tmp_ap, func=AF.Exp, scale=-0.5),
             reads=[dep_w], writes=[dep_w])

    stop = int(os.environ.get("KSTOP", "99"))

    def chk(n):
        if stop == n:
            raise _Stop()

    try:
        with contextlib.ExitStack() as p0:
            wst = [sb([128, 8, 512], F32, stack=p0) for _ in range(2)]
            dW = [dep("wst") for _ in range(2)]
            cct = sb([17, D], F32, stack=p0)
            dcc = dep("cc")
            adaT = sb([128, 48, 17], F32, stack=p0)
            dAda = dep("adaT")
            vrow = sb([64, 128], F32, stack=p0)
            lraw = sb([128, 1024], F32, stack=p0)
            dtmp = dep("p0tmp")

            S.dma("sp", lambda e: e.dma_start(out=cct[:], in_=cc[:, :]), dcc, "in")
            S.dma("sp", lambda e: e.dma_start(out=vrow[:], in_=vecs[:, :]), dtmp, "in")
            S.dma("sp", lambda e: e.dma_start(out=qkgB[:], in_=qkg[0:1, :].to_broadcast([128, 1536])), dPar, "in")
            S.dma("sp", lambda e: e.dma_start(out=biasT[:], in_=sbb[0:1, :].to_broadcast([128, 8])), dPar, "in")
            S.dma("sp", lambda e: e.dma_start(out=lraw[:], in_=lbl[0:1, :].to_broadcast([128, 1024])), dtmp, "in")
            S.op("dve", lambda e: e.tensor_tensor(out=lraw[:, 0:512], in0=lraw[:, 0:512], in1=lraw[:, 512:1024],
                                                  op=ALU.subtract), reads=[dtmp], writes=[dtmp])
            S.op("act", lambda e: e.activation(out=lbB[:], in_=lraw[:, 0:512], func=AF.Sigmoid),
                 reads=[dtmp], writes=[dPar])
            S.op("dve", lambda e: e.tensor_scalar(out=omlB[:], in0=lbB[:], scalar1=-1.0, scalar2=1.0,
                                                  op0=ALU.mult, op1=ALU.add), reads=[dPar], writes=[dPar])
            S.op("act", lambda e: e.activation(out=cct[:], in_=cct[:], func=AF.Silu), reads=[dcc], writes=[dcc])
            for c in range(8):
                S.op("pe", lambda e, c=c: e.transpose(out=PS[0][:, c * 17:(c + 1) * 17],
                                                      in_=cct[0:17, c * 128:(c + 1) * 128], identity=cf("ident", 17, 0, 17)),
                     reads=[dcc, dCF], writes=[PD[0]], inc=(c == 7))
            S.op("dve", lambda e: e.tensor_copy(out=cT[:].rearrange("p c s -> p (c s)"), in_=PS[0][:, 0:136]),
                 reads=[PD[0]], writes=[dcT])
            S.op("pe", lambda e: e.transpose(out=PS[1][:, 0:64], in_=vrow[0:64, :], identity=cf("ident", 64, 0, 64)),
                 reads=[dtmp, dCF], writes=[PD[1]])
            S.op("dve", lambda e: e.tensor_copy(out=vecT[:], in_=PS[1][:, 0:64]), reads=[PD[1]], writes=[dVec])

            w_ada_v = w_ada.rearrange("(k p) n -> p k n", p=128)
            pcs = [0, 1, 2, 3, 6, 7, 8, 9]
            for n_, pc in enumerate(pcs):
                buf = n_ % 2
                S.dma("sp", lambda e, pc=pc, buf=buf: e.dma_start(out=wst[buf][:], in_=w_ada_v[:, :, pc * 512:(pc + 1) * 512]),
                      dW[buf], "in")
                for q in range(4):
                    chunk = (pc * 512) // 128 + q
                    bank = 2 + (q % 2)
                    for k in range(8):
                        S.op("pe", lambda e, k=k, q=q, bank=bank, buf=buf: e.matmul(
                            out=PS[bank][:, 0:17], lhsT=wst[buf][:, k, q * 128:(q + 1) * 128], rhs=cT[:, k, :],
                            start=(k == 0), stop=(k == 7)),
                             reads=[dcT, dW[buf]], writes=[PD[bank]], inc=(k == 7))
                    S.op("dve", lambda e, chunk=chunk, bank=bank: e.tensor_scalar(
                        out=adaT[:, chunk, :], in0=PS[bank][:, 0:17], scalar1=vecT[:, chunk:chunk + 1], scalar2=None,
                        op0=ALU.add), reads=[PD[bank], dVec], writes=[dAda])
            for (mi, scc, shc, nof) in ((0, 8, 0, 48), (2, 32, 24, 56)):
                S.op("dve", lambda e, mi=mi, scc=scc, nof=nof: e.scalar_tensor_tensor(
                    out=mod[:, mi, :, :], in0=adaT[:, scc:scc + 8, :], scalar=1.0,
                    in1=vecT[:, nof:nof + 8].unsqueeze(2).to_broadcast([128, 8, 17]), op0=ALU.add, op1=ALU.mult),
                     reads=[dAda, dVec], writes=[dMod])
                S.op("dve", lambda e, mi=mi, shc=shc: e.tensor_copy(out=mod[:, mi + 1, :, :], in_=adaT[:, shc:shc + 8, :]),
                     reads=[dAda], writes=[dMod])
            S.barrier()
            chk(0)

        def load_x_tile(i, xt, dxt, q="sp"):
            T = tile_rows(i)
            src = xp[i * 128:(i + 1) * 128, :] if i < 16 else xs_d[:, :]
            S.dma(q, lambda e: e.dma_start(out=xt[0:T, :], in_=src), dxt, "in")

        def norm_transpose(i, src_ap, dsrc, mi, hT_ap, dhT, work, dwork, banks):
            T = tile_rows(i)
            xn, sqj, st4 = work
            S.op("act", lambda e: e.activation(out=sqj[0:T, :], in_=src_ap, func=AF.Square, accum_out=st4[0:T, 0:1]),
                 reads=[dsrc], writes=[dwork])
            rsqrt_small(st4[0:T, 0:1], st4[0:T, 2:3], 1.0 / D, [dwork], dwork, st4[0:T, 1:2])
            S.op("dve", lambda e: e.tensor_scalar(out=xn[0:T, :], in0=src_ap, scalar1=st4[0:T, 2:3], scalar2=None,
                                                  op0=ALU.mult), reads=[dsrc, dwork], writes=[dwork])
            nb, tpb = (1, 128) if i < 16 else (16, 4)
            c0 = 16 if i < 16 else 0
            for half in range(2):
                bank = banks[half]
                for c4 in range(4):
                    c = half * 4 + c4
                    S.op("pe", lambda e, c=c, c4=c4, bank=bank: e.transpose(
                        out=PS[bank][:, c4 * 128:c4 * 128 + T], in_=xn[0:T, c * 128:(c + 1) * 128],
                        identity=cf("ident", T, 0, T)), reads=[dwork, dCF], writes=[PD[bank]], inc=(c4 == 3))
                pv = PS[bank][:, :].rearrange("p (c t) -> p c t", c=4)[:, :, 0:T].rearrange("p c (b t) -> p c b t", t=tpb)
                sc = mod[:, mi, half * 4:half * 4 + 4, c0:c0 + nb].unsqueeze(3).to_broadcast([128, 4, nb, tpb])
                sh = mod[:, mi + 1, half * 4:half * 4 + 4, c0:c0 + nb].unsqueeze(3).to_broadcast([128, 4, nb, tpb])
                tmpm = xn[:, half * 512:(half + 1) * 512].rearrange("p (c t) -> p c t", c=4)[:, :, 0:T].rearrange(
                    "p c (b t) -> p c b t", t=tpb)
                tm = sqj[:, half * 512:(half + 1) * 512].rearrange("p (c t) -> p c t", c=4)[:, :, 0:T].rearrange(
                    "p c (b t) -> p c b t", t=tpb)
                S.op("dve", lambda e, pv=pv, sc=sc, tm=tm: e.tensor_tensor(out=tm, in0=pv, in1=sc, op=ALU.mult),
                     reads=[PD[bank], dMod], writes=[dwork])
                ho = hT_ap[:, half * 4:half * 4 + 4, :].rearrange("p c (b t) -> p c b t", t=tpb)
                S.op("dve", lambda e, ho=ho, sh=sh, tm=tm: e.tensor_tensor(out=ho, in0=tm, in1=sh, op=ALU.add),
                     reads=[dwork, dMod], writes=[dhT])

        def load_weight_piece(src_ap, dst_bf_ap, wst, dW, buf, ddst, q="sp", cast_eng=None):
            cast_eng = cast_eng or ("act" if buf == 0 else "dve")
            S.dma(q, lambda e: e.dma_start(out=wst[buf][:], in_=src_ap), dW[buf], "in")
            if cast_eng == "act":
                S.op("act", lambda e: e.activation(out=dst_bf_ap, in_=wst[buf][:], func=AF.Copy),
                     reads=[dW[buf]], writes=[ddst])
            else:
                S.op(cast_eng, lambda e: e.tensor_copy(out=dst_bf_ap, in_=wst[buf][:]), reads=[dW[buf]], writes=[ddst])

        with contextlib.ExitStack() as p1:
            hT = sb([128, 8, NTOK], BF16, stack=p1)
            dhT = [dep("hT") for _ in range(NT)]
            with contextlib.ExitStack() as p1n:
                xt = [sb([128, D], F32, stack=p1n) for _ in range(2)]
                dxt = [dep("xt") for _ in range(2)]
                wk = [(sb([128, D], F32, stack=p1n), sb([128, D], F32, stack=p1n), sb([128, 4], F32, stack=p1n)) for _ in range(2)]
                dwk = [dep("wk") for _ in range(2)]
                for i in [int(x) for x in os.environ["KTILES"].split(",")] if "KTILES" in os.environ else range(NT):
                    b = i % 2
                    load_x_tile(i, xt[b], dxt[b])
                    c0, c1 = tile_cols(i)
                    norm_transpose(i, xt[b][0:tile_rows(i), :], dxt[b], 0, hT[:, :, c0:c1], dhT[i], wk[b], dwk[b],
                                   (0, 1) if b == 0 else (2, 3))
                S.barrier()
                chk(1)

            w_in_v = w_in.rearrange("(k p) n -> p k n", p=128)

            with contextlib.ExitStack() as pb:
                whg = sb([128, 8, 2048], BF16, stack=pb)
                dwhg = [dep("whg") for _ in range(4)]
                with contextlib.ExitStack() as pw:
                    wst = [sb([128, 8, 512], F32, stack=pw) for _ in range(2)]
                    dW = [dep("wst") for _ in range(2)]
                    for g in range(4):
                        load_weight_piece(w_in_v[:, :, 1536 + g * 512:1536 + (g + 1) * 512], whg[:, :, g * 512:(g + 1) * 512],
                                          wst, dW, g % 2, dwhg[g])
                    S.barrier()

                def wt(shape, dt=F32):
                    return sb(shape, dt, stack=pb)

                AB = (4, 2)
                XB = (6, 3)

                def hsel(ap, h2):
                    return ap.rearrange("p (j a t) -> p j a t", j=4, a=2)[:, :, h2, :]

                hq = wt([128, 512]); ff = wt([128, 512]); logf = wt([128, 512]); omf = wt([128, 512])
                hvL = [wt([128, 512], BF16) for _ in range(2)]; sgL = [wt([128, 512]) for _ in range(2)]
                eb = wt([128, 512]); enb = wt([128, 512])
                ec = wt([128, 512]); kkL = [wt([128, 512], BF16) for _ in range(2)]
                qdTL = [wt([128, 4, 128], BF16) for _ in range(2)]; kdTL = [wt([128, 4, 128], BF16) for _ in range(2)]
                attm = wt([128, 512], BF16); oo = wt([128, 512]); osq = wt([128, 512])
                st8 = wt([128, 24]); decL = [wt([128, 4, 16]) for _ in range(2)]
                Sst = wt([128, 4, 64]); Sbf = wt([128, 4, 64], BF16)
                dE = dep("hgE")
                dTL = [dep("hgT") for _ in range(2)]
                dEbL = [dep("hgEb") for _ in range(2)]
                dDecL = [dep("dec") for _ in range(2)]
                dA = dep("attm"); dO = dep("oo"); dS_ = dep("S"); dSb = dep("Sbf")
                S.op("dve", lambda e: e.memset(Sst[:], 0.0), writes=[dS_])
                S.op("dve", lambda e: e.memset(Sbf[:], 0.0), writes=[dSb])
                S0t = [wt([128, 16, 64]) for _ in range(2)]
                S0bt = wt([128, 16, 64], BF16)
                qdTm = wt([128, 4, 16, 64], BF16); hvm = wt([64, 16, 128], BF16)
                smT = wt([128, 1024], BF16)
                dS0 = [dep("S0") for _ in range(2)]; dS0b = dep("S0b"); dqm = dep("qdTm"); dhvm = dep("hvm"); dsm = dep("smT")
                st0_v = st0.rearrange("b (j h) k v -> (h k) j b v", h=2)
                ss_v = ss_o.rearrange("b (j h) k v -> (h k) j b v", h=2)
                load_bf_const("seqmaskT", smT, dsm, hq, dE)

                def tile_gen(i):
                    s_ = i % 2
                    hv, sg, kk, qdT, kdT, dec = hvL[s_], sgL[s_], kkL[s_], qdTL[s_], kdTL[s_], decL[s_]
                    dT_, dEb, dDec = dTL[s_], dEbL[s_], dDecL[s_]
                    T = tile_rows(i)
                    c0, c1 = tile_cols(i)
                    samp = (i == 16)
                    tri = cf("tri4" if samp else "tri2", T, 0, T)
                    tsu = cf("tsu4" if samp else "tsu2", T, 0, T)
                    ncn = 16 if samp else 2
                    ind = cf("seqind" if samp else "chunkind", T)
                    def proj(g, bank):
                        for k in range(8):
                            S.op("pe", lambda e, k=k: e.matmul(out=PS[bank][0:T, :], lhsT=hT[:, k, c0:c1],
                                                               rhs=whg[:, k, g * 512:(g + 1) * 512],
                                                               start=(k == 0), stop=(k == 7)),
                                 reads=[dhT[i], dwhg[g]], writes=[PD[bank]], inc=(k == 7))
                    proj(0, 2)
                    proj(1, 3)
                    proj(3, 0)
                    proj(2, 1)
                    S.op("act", lambda e: e.activation(out=hq[0:T, :], in_=PS[2][0:T, :], func=AF.Silu),
                         reads=[PD[2]], writes=[dE])
                    S.op("act", lambda e: e.activation(out=sg[0:T, :], in_=PS[0][0:T, :], func=AF.Silu),
                         reads=[PD[0]], writes=[dEb])
                    S.op("act", lambda e: e.activation(out=ff[0:T, :], in_=PS[3][0:T, :], func=AF.Sigmoid),
                         reads=[PD[3]], writes=[dE])
                    S.op("dve", lambda e: e.tensor_tensor(out=ff[0:T, :], in0=ff[0:T, :], in1=omlB[0:T, :], op=ALU.mult),
                         reads=[dE, dPar], writes=[dE])
                    S.op("dve", lambda e: e.tensor_tensor(out=ff[0:T, :], in0=ff[0:T, :], in1=lbB[0:T, :], op=ALU.add),
                         reads=[dE, dPar], writes=[dE])
                    S.op("act", lambda e: e.activation(out=hv[0:T, :], in_=PS[1][0:T, :], func=AF.Copy),
                         reads=[PD[1]], writes=[dEb])
                    S.op("act", lambda e: e.activation(out=logf[0:T, :], in_=ff[0:T, :], func=AF.Ln),
                         reads=[dE], writes=[dE])
                    S.op("dve", lambda e: e.tensor_scalar(out=omf[0:T, :], in0=ff[0:T, :], scalar1=-1.0, scalar2=1.0,
                                                          op0=ALU.mult, op1=ALU.add), reads=[dE], writes=[dE])
                    yield
                    S.op("pe", lambda e: e.matmul(out=PS[0][0:T, :], lhsT=tri, rhs=logf[0:T, :], start=True, stop=True),
                         reads=[dE, dCF], writes=[PD[0]])
                    S.op("pe", lambda e: e.matmul(out=PS[1][0:T, :], lhsT=tsu, rhs=logf[0:T, :], start=True, stop=True),
                         reads=[dE, dCF], writes=[PD[1]])
                    for j in range(4):
                        S.op("pe", lambda e, j=j: e.matmul(out=PS[7][:, 256 + j * ncn:256 + (j + 1) * ncn],
                                                           lhsT=logf[0:T, j * 128:(j + 1) * 128], rhs=ind,
                                                           start=True, stop=True),
                             reads=[dE, dCF], writes=[PD[7]], inc=(j == 3))
                    S.op("act", lambda e: e.activation(out=eb[0:T, :], in_=PS[0][0:T, :], func=AF.Exp),
                         reads=[PD[0]], writes=[dE])
                    S.op("act", lambda e: e.activation(out=enb[0:T, :], in_=PS[0][0:T, :], func=AF.Exp, scale=-1.0),
                         reads=[PD[0]], writes=[dE])
                    S.op("act", lambda e: e.activation(out=ec[0:T, :], in_=PS[1][0:T, :], func=AF.Exp),
                         reads=[PD[1]], writes=[dE])
                    S.op("act", lambda e: e.activation(out=dec[:, :, 0:ncn],
                                                       in_=PS[7][:, 256:256 + 4 * ncn].rearrange("p (j c) -> p j c", j=4),
                                                       func=AF.Exp), reads=[PD[7]], writes=[dDec])
                    S.op("dve", lambda e: e.tensor_tensor(out=eb[0:T, :], in0=hq[0:T, :], in1=eb[0:T, :], op=ALU.mult),
                         reads=[dE], writes=[dE])
                    S.op("dve", lambda e: e.tensor_tensor(out=enb[0:T, :], in0=omf[0:T, :], in1=enb[0:T, :], op=ALU.mult),
                         reads=[dE], writes=[dE])
                    S.op("dve", lambda e: e.tensor_tensor(out=kk[0:T, :], in0=omf[0:T, :], in1=ec[0:T, :], op=ALU.mult),
                         reads=[dE], writes=[dEb])
                    yield
                    for (src, dst, bank) in ((eb, qdT, 0), (enb, kdT, 1)):
                        for j in range(4):
                            S.op("pe", lambda e, j=j, src=src, bank=bank: e.transpose(
                                out=PS[bank][:, j * 128:j * 128 + T], in_=src[0:T, j * 128:(j + 1) * 128],
                                identity=cf("ident", T, 0, T)), reads=[dE, dCF], writes=[PD[bank]], inc=(j == 3))
                        S.op("act", lambda e, dst=dst, bank=bank: e.activation(
                            out=dst[:, :, 0:T], in_=PS[bank][:, :].rearrange("p (j t) -> p j t", j=4)[:, :, 0:T],
                            func=AF.Copy), reads=[PD[bank]], writes=[dT_])

                    yield
                    if not samp:
                        for c in range(2):
                            cp = 64 * c
                            for h in range(8):
                                j, h2 = h // 2, h % 2
                                hp = 64 * h2
                                ab = AB[h2]
                                S.op("pe", lambda e, h=h, j=j, hp=hp, cp=cp, ab=ab: e.matmul(
                                    out=PS[ab][cp:cp + 64, h * 64:(h + 1) * 64], lhsT=kdT[hp:hp + 64, j, cp:cp + 64],
                                    rhs=qdT[hp:hp + 64, j, cp:cp + 64], start=True, stop=True),
                                     reads=[dT_], writes=[PD[ab]], inc=(h >= 6))
                            for h2 in range(2):
                                ab = AB[h2]
                                S.op("dve", lambda e, cp=cp, ab=ab, h2=h2: e.tensor_tensor(
                                    out=hsel(attm[cp:cp + 64, :], h2), in0=hsel(PS[ab][cp:cp + 64, :], h2),
                                    in1=cf("maskH")[cp:cp + 64, :].unsqueeze(1).to_broadcast([64, 4, 64]), op=ALU.mult),
                                     reads=[PD[ab], dCF], writes=[dA])
                            for h in range(8):
                                j, h2 = h // 2, h % 2
                                hp = 64 * h2
                                hs = slice(h * 64, (h + 1) * 64)
                                S.op("pe", lambda e, hs=hs, cp=cp: e.matmul(
                                    out=PS[5][cp:cp + 64, hs], lhsT=attm[cp:cp + 64, hs], rhs=hv[cp:cp + 64, hs],
                                    start=True, stop=True), reads=[dA, dEb], writes=[PD[5]], inc=False)
                                xb = XB[h2]
                                S.op("pe", lambda e, hs=hs, cp=cp, hp=hp, j=j, xb=xb: e.matmul(
                                    out=PS[xb][cp:cp + 64, hs], lhsT=qdT[hp:hp + 64, j, cp:cp + 64], rhs=Sbf[hp:hp + 64, j, :],
                                    start=True, stop=True), reads=[dT_, dSb], writes=[PD[xb]], inc=False)
                                S.op("pe", lambda e, hs=hs, cp=cp, hp=hp, j=j: e.matmul(
                                    out=PS[7][hp:hp + 64, j * 64:(j + 1) * 64], lhsT=kk[cp:cp + 64, hs], rhs=hv[cp:cp + 64, hs],
                                    start=True, stop=True), reads=[dEb], writes=[PD[7]], inc=(h == 7))
                            S.op("dve", lambda e, c=c: e.tensor_tensor(
                                out=Sst[:], in0=Sst[:], in1=dec[:, :, c:c + 1].to_broadcast([128, 4, 64]), op=ALU.mult),
                                 reads=[dS_, dDec], writes=[dS_])
                            S.op("dve", lambda e: e.tensor_tensor(
                                out=Sst[:], in0=Sst[:], in1=PS[7][:, 0:256].rearrange("p (j v) -> p j v", j=4), op=ALU.add),
                                 reads=[dS_, PD[7]], writes=[dS_])
                            S.op("act", lambda e: e.activation(out=Sbf[:], in_=Sst[:], func=AF.Copy),
                                 reads=[dS_], writes=[dSb])
                            yield
                        if i == 15:
                            S.dma("sp", lambda e: e.dma_start(out=sp_o.rearrange("(j h) k v -> (h k) j v", h=2), in_=Sst[:]),
                                  dS_, "out")
                    else:
                        for h in range(8):
                            j, h2 = h // 2, h % 2
                            hp = 64 * h2
                            ab = AB[h2]
                            S.op("pe", lambda e, h=h, j=j, hp=hp, ab=ab: e.matmul(
                                out=PS[ab][0:64, h * 64:(h + 1) * 64], lhsT=kdT[hp:hp + 64, j, 0:64],
                                rhs=qdT[hp:hp + 64, j, 0:64], start=True, stop=True),
                                 reads=[dT_], writes=[PD[ab]], inc=(h >= 6))
                        for h2 in range(2):
                            ab = AB[h2]
                            S.op("dve", lambda e, ab=ab, h2=h2: e.tensor_tensor(
                                out=hsel(attm[0:64, :], h2), in0=hsel(PS[ab][0:64, :], h2),
                                in1=cf("maskHS")[0:64, :].unsqueeze(1).to_broadcast([64, 4, 64]), op=ALU.mult),
                                 reads=[PD[ab], dCF], writes=[dA])
                        S.op("dve", lambda e: e.tensor_tensor(
                            out=qdTm[:], in0=qdT[:, :, 0:64].unsqueeze(2).to_broadcast([128, 4, 16, 64]),
                            in1=smT[:].rearrange("p (b t) -> p b t", b=16).unsqueeze(1).to_broadcast([128, 4, 16, 64]),
                            op=ALU.mult), reads=[dT_, dsm], writes=[dqm])
                        for h in range(8):
                            hs = slice(h * 64, (h + 1) * 64)
                            S.op("pe", lambda e, hs=hs: e.matmul(out=PS[5][0:64, hs], lhsT=attm[0:64, hs], rhs=hv[0:64, hs],
                                                                 start=True, stop=True),
                                 reads=[dA, dEb], writes=[PD[5]], inc=(h == 7))
                        for j in range(4):
                            sb_ = j % 2
                            S0j = S0t[sb_]
                            S.dma("sp", lambda e, j=j, S0j=S0j: e.dma_start(out=S0j[:], in_=st0_v[:, j, :, :]), dS0[sb_], "in")
                            S.op("act", lambda e, S0j=S0j: e.activation(out=S0bt[:].rearrange("p b v -> p (b v)"),
                                                                        in_=S0j[:].rearrange("p b v -> p (b v)"), func=AF.Copy),
                                 reads=[dS0[sb_]], writes=[dS0b])
                            S.op("dve", lambda e, j=j: e.tensor_tensor(
                                out=hvm[:], in0=hv[0:64, j * 128:(j + 1) * 128].unsqueeze(1).to_broadcast([64, 16, 128]),
                                in1=cf("seqind", 64).unsqueeze(2).to_broadcast([64, 16, 128]), op=ALU.mult),
                                 reads=[dEb, dCF], writes=[dhvm])
                            for h2 in range(2):
                                h = 2 * j + h2
                                hp = 64 * h2
                                hs = slice(h * 64, (h + 1) * 64)
                                for b in range(16):
                                    S.op("pe", lambda e, hs=hs, hp=hp, j=j, b=b, h2=h2: e.matmul(
                                        out=PS[XB[h2]][0:64, hs], lhsT=qdTm[hp:hp + 64, j, b, :], rhs=S0bt[hp:hp + 64, b, :],
                                        start=(b == 0), stop=(b == 15)),
                                         reads=[dqm, dS0b], writes=[PD[XB[h2]]], inc=(b == 15))
                                for half in range(2):
                                    S.op("pe", lambda e, h=h, hp=hp, half=half, h2=h2: e.matmul(
                                        out=PS[half][hp:hp + 64, :], lhsT=kk[0:64, h * 64:(h + 1) * 64],
                                        rhs=hvm[0:64, half * 8:(half + 1) * 8, h2 * 64:(h2 + 1) * 64],
                                        start=True, stop=True), reads=[dEb, dhvm], writes=[PD[half]])
                            for half in range(2):
                                bs = slice(half * 8, (half + 1) * 8)
                                S.op("dve", lambda e, j=j, bs=bs, S0j=S0j: e.tensor_tensor(
                                    out=S0j[:, bs, :], in0=S0j[:, bs, :],
                                    in1=dec[:, j, bs].unsqueeze(2).to_broadcast([128, 8, 64]), op=ALU.mult),
                                     reads=[dS0[sb_], dDec], writes=[dS0[sb_]])
                                S.op("dve", lambda e, bs=bs, half=half, S0j=S0j: e.tensor_tensor(
                                    out=S0j[:, bs, :], in0=S0j[:, bs, :],
                                    in1=PS[half][:, :].rearrange("p (b v) -> p b v", b=8), op=ALU.add),
                                     reads=[dS0[sb_], PD[half]], writes=[dS0[sb_]])
                            S.dma("sp", lambda e, j=j, S0j=S0j: e.dma_start(out=ss_v[:, j, :, :], in_=S0j[:]), dS0[sb_], "out")

                    S.op("act", lambda e: e.activation(out=oo[0:T, :], in_=PS[5][0:T, :], func=AF.Copy),
                         reads=[PD[5]], writes=[dO])
                    for h2 in range(2):
                        S.op("dve", lambda e, h2=h2: e.tensor_tensor(out=hsel(oo[0:T, :], h2), in0=hsel(oo[0:T, :], h2),
                                                              in1=hsel(PS[XB[h2]][0:T, :], h2), op=ALU.add),
                             reads=[dO, PD[XB[h2]]], writes=[dO])
                    S.op("dve", lambda e: e.tensor_tensor(out=osq[0:T, :], in0=oo[0:T, :], in1=oo[0:T, :], op=ALU.mult),
                         reads=[dO], writes=[dO])
                    S.op("dve", lambda e: e.tensor_reduce(out=st8[0:T, 0:8], in_=osq[0:T, :].rearrange("p (h v) -> p h v", h=8),
                                                          axis=AX.X, op=ALU.add), reads=[dO], writes=[dO])
                    rsqrt_small(st8[0:T, 0:8], st8[0:T, 16:24], 1.0 / 64, [dO], dO, st8[0:T, 8:16])
                    S.op("dve", lambda e: e.tensor_tensor(
                        out=oo[0:T, :].rearrange("p (h v) -> p h v", h=8), in0=oo[0:T, :].rearrange("p (h v) -> p h v", h=8),
                        in1=st8[0:T, 16:24].unsqueeze(2).to_broadcast([T, 8, 64]), op=ALU.mult), reads=[dO], writes=[dO])
                    S.op("dve", lambda e: e.tensor_tensor(out=oo[0:T, :], in0=oo[0:T, :], in1=qkgB[0:T, 1024:1536], op=ALU.mult),
                         reads=[dO, dPar], writes=[dO])
                    S.op("dve", lambda e: e.tensor_tensor(out=oo[0:T, :], in0=oo[0:T, :], in1=sg[0:T, :], op=ALU.mult),
                         reads=[dO, dEb], writes=[dO])
                    for j in range(4):
                        S.op("pe", lambda e, j=j: e.transpose(out=PS[4][:, j * 128:j * 128 + T], in_=oo[0:T, j * 128:(j + 1) * 128],
                                                              identity=cf("ident", T, 0, T)),
                             reads=[dO, dCF], writes=[PD[4]], inc=(j == 3))
                    S.op("act", lambda e: e.activation(out=mixT[:, 4:8, c0:c1],
                                                       in_=PS[4][:, :].rearrange("p (j t) -> p j t", j=4)[:, :, 0:T],
                                                       func=AF.Copy), reads=[PD[4]], writes=[dMixB[i]])
                gens = [tile_gen(i) for i in range(NT)]
                for _ in range(3):
                    next(gens[0])
                ORD = os.environ.get("KORD", "aBBaBa")
                for i in range(NT):
                    nx = gens[i + 1] if i + 1 < NT else None
                    for ch in ORD:
                        if ch == "a":
                            if nx is not None:
                                next(nx)
                        else:
                            next(gens[i], None)
                    for _ in gens[i]:
                        pass
                S.barrier()
                chk(2)

            pa_r = contextlib.ExitStack()
            qT = sb([128, 4, NTOK], BF16, side="right", stack=pa_r)
            kT = sb([128, 4, NTOK], BF16, side="right", stack=pa_r)
            vres = sb([128, NT, 512], BF16, side="right", stack=pa_r)
            dQ = [dep("qT") for _ in range(NT)]
            dK = [dep("kT") for _ in range(NT)]
            dV = [dep("v") for _ in range(NT)]
            with contextlib.ExitStack() as pa:
                wat = sb([128, 8, 1536], BF16, stack=pa)
                dwat = [dep("wat") for _ in range(3)]
                with contextlib.ExitStack() as pw:
                    wst = [sb([128, 8, 512], F32, stack=pw) for _ in range(2)]
                    dW = [dep("wst") for _ in range(2)]
                    for g in range(3):
                        load_weight_piece(w_in_v[:, :, g * 512:(g + 1) * 512], wat[:, :, g * 512:(g + 1) * 512],
                                          wst, dW, g % 2, dwat[g])
                    S.barrier()
                sq = sb([128, 512], F32, stack=pa)
                qn = [sb([128, 512], F32, stack=pa) for _ in range(2)]
                dqn = [dep("qn") for _ in range(2)]
                kn = [sb([128, 512], F32, stack=pa) for _ in range(2)]
                dkn = [dep("kn") for _ in range(2)]
                vn = [sb([128, 512], F32, stack=pa) for _ in range(2)]
                dvn = [dep("vn") for _ in range(2)]
                s8 = sb([128, 24], F32, stack=pa)
                dsq = dep("sq")
                for i in range(NT):
                    T = tile_rows(i)
                    c0, c1 = tile_cols(i)
                    b = i % 2

                    def proj(g, bank):
                        for k in range(8):
                            S.op("pe", lambda e, k=k: e.matmul(out=PS[bank][0:T, :], lhsT=hT[:, k, c0:c1],
                                                               rhs=wat[:, k, g * 512:(g + 1) * 512],
                                                               start=(k == 0), stop=(k == 7)),
                                 reads=[dhT[i], dwat[g]], writes=[PD[bank]], inc=(k == 7))

                    def qknorm(bank, dst, ddst, goff):
                        S.op("act", lambda e: e.activation(out=sq[0:T, :], in_=PS[bank][0:T, :], func=AF.Square),
                             reads=[PD[bank]], writes=[dsq])
                        S.op("dve", lambda e: e.tensor_reduce(out=s8[0:T, 0:8], in_=sq[0:T, :].rearrange("p (h d) -> p h d", h=8),
                                                              axis=AX.X, op=ALU.add), reads=[dsq], writes=[dsq])
                        rsqrt_small(s8[0:T, 0:8], s8[0:T, 16:24], 1.0 / 64, [dsq], dsq, s8[0:T, 8:16])
                        S.op("dve", lambda e: e.tensor_tensor(
                            out=dst[0:T, :].rearrange("p (h d) -> p h d", h=8),
                            in0=PS[bank][0:T, :].rearrange("p (h d) -> p h d", h=8),
                            in1=s8[0:T, 16:24].unsqueeze(2).to_broadcast([T, 8, 64]), op=ALU.mult),
                             reads=[PD[bank], dsq], writes=[ddst])
                        S.op("dve", lambda e: e.tensor_tensor(out=dst[0:T, :], in0=dst[0:T, :], in1=qkgB[0:T, goff:goff + 512],
                                                              op=ALU.mult), reads=[ddst, dPar], writes=[ddst])

                    def to_featT(src, dsrc, bank, dst, ddst):
                        for j in range(4):
                            S.op("pe", lambda e, j=j: e.transpose(out=PS[bank][:, j * 128:j * 128 + T],
                                                                  in_=src[0:T, j * 128:(j + 1) * 128],
                                                                  identity=cf("ident", T, 0, T)),
                                 reads=[dsrc, dCF], writes=[PD[bank]], inc=(j == 3))
                        S.op("act", lambda e: e.activation(out=dst[:, :, c0:c1],
                                                           in_=PS[bank][:, :].rearrange("p (j t) -> p j t", j=4)[:, :, 0:T],
                                                           func=AF.Copy), reads=[PD[bank]], writes=[ddst])

                    proj(0, 0)
                    proj(1, 1)
                    proj(2, 2)
                    qknorm(0, qn[b], dqn[b], 0)
                    S.op("act", lambda e: e.activation(out=vn[b][0:T, :], in_=PS[2][0:T, :], func=AF.Copy),
                         reads=[PD[2]], writes=[dvn[b]])
                    qknorm(1, kn[b], dkn[b], 512)
                    to_featT(qn[b], dqn[b], 3, qT, dQ[i])
                    to_featT(kn[b], dkn[b], 4, kT, dK[i])
                    kdst = kp[i * 128:(i + 1) * 128, :] if i < 16 else ks[:, :]
                    S.dma("sp", lambda e: e.dma_start(out=kdst, in_=kn[b][0:T, :]), dkn[b], "out")
                    S.op("dve", lambda e: e.tensor_copy(out=vres[0:T, i, :], in_=vn[b][0:T, :]),
                         reads=[dvn[b]], writes=[dV[i]])
                    vdst = vp[i * 128:(i + 1) * 128, :] if i < 16 else vs[:, :]
                    S.dma("sp", lambda e: e.dma_start(out=vdst, in_=vn[b][0:T, :]), dvn[b], "out")
                S.barrier()
                chk(3)
        p1s.close()

        with contextlib.ExitStack() as p2:
            ulT = sb([128, 256], BF16, stack=p2)
            dCB = dep("cb")
            uincl = ulT[:, 0:128]
            lstrict = ulT[:, 128:256]
            qblk = sb([128, 4, 16, 2, 4], BF16, stack=p2)
            kTp = sb([128, 4, 128], BF16, stack=p2)
            vpad = sb([128, 512], BF16, stack=p2)
            ebF = sb([128, 512], F32, stack=p2)
            mS2 = sb([128, 128], F32, stack=p2)
            dpad = dep("pad")
            with contextlib.ExitStack() as p2a:
                NSET = 8
                eT = [sb([128, 512], F32, stack=p2a) for _ in range(NSET)]
                xT_ = [sb([128, 512], F32, stack=p2a) for _ in range(NSET)]
                LpT = [sb([128, 512], BF16, stack=p2a) for _ in range(NSET)]
                wT = [sb([128, 512], BF16, stack=p2a) for _ in range(NSET)]
                de = [dep("e") for _ in range(NSET)]
                dx = [dep("x") for _ in range(NSET)]
                dL = [dep("L") for _ in range(NSET)]
                dw = [dep("w") for _ in range(NSET)]
                mDT = sb([128, 2048], BF16, stack=p2a)
                load_bf_const("uincl", ulT[:, 0:128], dCB, eT[0], de[0])
                load_bf_const("lstrict", ulT[:, 128:256], dCB, eT[1], de[1])
                load_bf_const("maskD", mDT, dCB, eT[0], de[0])
                def stream_units(qbs):
                    out = []
                    for j in range(4):
                        for QB in qbs:
                            nkb = 4 * QB + 4
                            for st, kb in enumerate(range(nkb - 1, -1, -1)):
                                for h2 in range(2):
                                    out.append((j, QB, st, kb, h2))
                    return out
                ua, ub = stream_units([3, 0]), stream_units([2, 1])
                units = []
                for n_ in range(max(len(ua), len(ub))):
                    if n_ < len(ua):
                        units.append(ua[n_] + (0,))
                    if n_ < len(ub):
                        units.append(ub[n_] + (1,))

                def geom(u):
                    j, QB, st, kb, h2, sid = u
                    ii = kb - 4 * QB
                    clo = 128 * ii if ii > 0 else 0
                    return ii, clo, slice(clo, 512), slice(QB * 512 + clo, (QB + 1) * 512)

                def parms(n):
                    j, QB, st, kb, h2, sid = units[n]
                    ii, clo, cs, qs = geom(units[n])
                    return j, QB, st, kb, h2, sid, ii, clo, cs, qs, 2 * j + h2, 64 * h2, n % NSET

                def a1(n):
                    j, QB, st, kb, h2, sid, ii, clo, cs, qs, h, hp, si = parms(n)
                    qdeps = [dQ[t] for t in range(QB * 4, QB * 4 + 4)]
                    S.op("pe", lambda e: e.matmul(
                        out=PS[h2][:, cs], lhsT=kT[hp:hp + 64, j, kb * 128:(kb + 1) * 128], rhs=qT[hp:hp + 64, j, qs],
                        start=True, stop=True), reads=[dK[kb]] + qdeps, writes=[PD[h2]])

                def a2(n):
                    j, QB, st, kb, h2, sid, ii, clo, cs, qs, h, hp, si = parms(n)
                    S.op("act", lambda e: e.activation(
                        out=eT[si][:, cs], in_=PS[h2][:, cs], func=AF.Exp, scale=SCALE, bias=biasT[:, h:h + 1]),
                         reads=[PD[h2], dPar], writes=[de[si]])
                    if ii >= 0:
                        S.op("dve", lambda e: e.tensor_tensor(
                            out=eT[si][:, cs], in0=eT[si][:, cs], in1=mDT[:, ii * 512 + clo:(ii + 1) * 512],
                            op=ALU.mult), reads=[de[si], dCB], writes=[de[si]])
                    S.op("act", lambda e: e.activation(out=LpT[si][:, cs], in_=eT[si][:, cs], func=AF.Ln, bias=1.0),
                         reads=[de[si]], writes=[dL[si]])

                def cbank(sid, h2):
                    return (2 + h2) if sid == 0 else (5 + h2)

                def b1(n):
                    j, QB, st, kb, h2, sid, ii, clo, cs, qs, h, hp, si = parms(n)
                    Cb = cbank(sid, h2)
                    S.op("pe", lambda e: e.matmul(
                        out=PS[Cb][:, cs], lhsT=uincl, rhs=LpT[si][:, cs], start=(st == 0), stop=False,
                        skip_group_check=True), reads=[dL[si], dCB], writes=[PD[Cb]])
                    S.op("act", lambda e: e.activation(out=xT_[si][:, cs], in_=PS[Cb][:, cs], func=AF.Exp, scale=-1.0),
                         reads=[PD[Cb]], writes=[dx[si]])

                def b2(n):
                    j, QB, st, kb, h2, sid, ii, clo, cs, qs, h, hp, si = parms(n)
                    Cb = cbank(sid, h2)
                    if kb > 0:
                        S.op("pe", lambda e: e.matmul(
                            out=PS[Cb][:, cs], lhsT=lstrict, rhs=LpT[si][:, cs], start=False, stop=(kb == 1),
                            skip_group_check=True), reads=[dL[si], dCB], writes=[PD[Cb]])
                    S.op("dve", lambda e: e.tensor_tensor(out=wT[si][:, cs], in0=eT[si][:, cs], in1=xT_[si][:, cs], op=ALU.mult),
                         reads=[de[si], dx[si]], writes=[dw[si]])

                def b3(n):
                    j, QB, st, kb, h2, sid, ii, clo, cs, qs, h, hp, si = parms(n)
                    ob = 4 if sid == 0 else 7
                    S.op("pe", lambda e: e.matmul(
                        out=PS[ob][hp:hp + 64, cs], lhsT=vres[:, kb, h * 64:(h + 1) * 64], rhs=wT[si][:, cs],
                        start=(st == 0), stop=(kb == 0), skip_group_check=True),
                         reads=[dw[si], dV[kb]], writes=[PD[ob]])
                    if kb == 0 and h2 == 1:
                        S.op("act", lambda e: e.activation(out=mixT[:, j, QB * 512:(QB + 1) * 512], in_=PS[ob][:, :],
                                                           func=AF.Copy),
                             reads=[PD[ob]], writes=[dMixA[QB * 4 + tt] for tt in range(4)])

                a1(0)
                a2(0)
                for n in range(len(units)):
                    b1(n)
                    if n + 1 < len(units):
                        a1(n + 1)
                    b2(n)
                    if n + 1 < len(units):
                        a2(n + 1)
                    b3(n)
                S.op("dve", lambda e: e.memset(kTp[:], 0.0), writes=[dpad])
                S.op("dve", lambda e: e.memset(vpad[:], 0.0), writes=[dpad])
                S.op("dve", lambda e: e.memset(qblk[:].rearrange("p j b a t -> p (j b a t)"), 0.0), writes=[dpad])
                S.op("dve", lambda e: e.tensor_copy(out=kTp[:, :, 0:64], in_=kT[:, :, TP:NTOK]), reads=[dK[16], dpad], writes=[dpad])
                S.op("dve", lambda e: e.tensor_copy(out=vpad[0:64, :], in_=vres[0:64, 16, :]), reads=[dV[16], dpad], writes=[dpad])
                for h2 in range(2):
                    S.op("dve", lambda e, h2=h2: e.tensor_copy(
                        out=qblk[64 * h2:64 * h2 + 64, :, :, h2, :],
                        in_=qT[64 * h2:64 * h2 + 64, :, TP:NTOK].rearrange("p j (b t) -> p j b t", t=4)),
                         reads=[dQ[16], dpad], writes=[dpad])
                S.op("act", lambda e: e.activation(out=eT[0][:, 0:8], in_=biasT[:, :], func=AF.Exp), reads=[dPar, de[0]],
                     writes=[de[0]])
                for j in range(4):
                    S.op("dve", lambda e, j=j: e.tensor_copy(
                        out=ebF[:, j * 128:(j + 1) * 128].rearrange("p (b a t) -> p b a t", b=16, a=2),
                        in_=eT[0][:, 2 * j:2 * j + 2].unsqueeze(1).unsqueeze(3).to_broadcast([128, 16, 2, 4])),
                         reads=[de[0]], writes=[dpad])
                S.op("dve", lambda e: e.tensor_copy(
                    out=mS2[:].rearrange("p (b a t) -> p b a t", b=16, a=2),
                    in_=cf("maskS").rearrange("p (b t) -> p b t", t=4).unsqueeze(2).to_broadcast([128, 16, 2, 4])),
                     reads=[dCF], writes=[dpad])
                S.barrier()
            chk(4)
            pa_r.close()

            NQ = 8
            GW = NQ * 8
            Kst = [sb([128, NQ, 512], F32, stack=p2) for _ in range(2)]
            Vst = [sb([128, NQ, 512], F32, stack=p2) for _ in range(2)]
            KTs = [sb([128, NQ, 4, 128], BF16, stack=p2) for _ in range(2)]
            Vb = [sb([128, NQ, 512], BF16, stack=p2) for _ in range(2)]
            dVb = [dep("Vb") for _ in range(2)]
            dKst = [dep("Kst") for _ in range(2)]
            dVst = [dep("Vst") for _ in range(2)]
            dKTs = [dep("KTs") for _ in range(2)]
            ptb = sb([128, NSEQ * NPG], I32, stack=p2)
            idx = sb([128, NSEQ * NPG], I32, stack=p2)
            iop = sb([128, 1], I32, stack=p2)
            dIdx = dep("idx")
            es_ = [sb([128, 4, 128], F32, stack=p2) for _ in range(2)]
            xs2 = [sb([128, 4, 128], F32, stack=p2) for _ in range(2)]
            Ls = [sb([128, 4, 128], BF16, stack=p2) for _ in range(2)]
            ws = [sb([128, 4, 128], BF16, stack=p2) for _ in range(2)]
            wsb = sb([128, 4, 128], BF16, stack=p2)
            des = [dep("es") for _ in range(2)]
            dxs = [dep("xs") for _ in range(2)]
            dLs = [dep("Ls") for _ in range(2)]
            dws = [dep("ws") for _ in range(2)]

            S.dma("pool", lambda e: e.dma_start(out=ptb[:], in_=pt[0:1, :].to_broadcast([128, NSEQ * NPG])), dIdx, "in")
            S.op("pool", lambda e: e.iota(iop[:], pattern=[[0, 1]], base=0, channel_multiplier=1), writes=[dIdx])
            S.op("pool", lambda e: e.tensor_scalar(out=idx[:], in0=ptb[:], scalar1=128, scalar2=None, op0=ALU.mult),
                 reads=[dIdx], writes=[dIdx])
            S.op("pool", lambda e: e.tensor_tensor(out=idx[:], in0=idx[:], in1=iop[:].to_broadcast([128, NSEQ * NPG]),
                                                   op=ALU.add), reads=[dIdx], writes=[dIdx])

            dZ, dC, dOs = dep("Z"), dep("C"), dep("Os")

            def bview(bank, c0, w_):
                return PS[bank][:, :].rearrange("p (j c) -> p j c", j=4)[:, :, c0:c0 + w_]

            def sample_step(kind, gi=None, buf=None, last=False, first=False, b2=0):
                c0, w_ = (0, 128) if kind == "new" else (gi * GW, GW)
                if kind == "new":
                    for j in range(4):
                        S.op("pe", lambda e, j=j: e.matmul(out=PS[0][:, j * 128:(j + 1) * 128], lhsT=kTp[:, j, :],
                                                           rhs=qblk[:, j, :, :, :].rearrange("p b a t -> p (b a t)"),
                                                           start=True, stop=True), reads=[dpad], writes=[dZ], inc=(j == 3))
                else:
                    for bi in range(NQ):
                        b = gi * NQ + bi
                        for j in range(4):
                            S.op("pe", lambda e, bi=bi, b=b, j=j: e.matmul(
                                out=PS[0][:, j * 128 + b * 8:j * 128 + b * 8 + 8], lhsT=KTs[buf][:, bi, j, :],
                                rhs=qblk[:, j, b, :, :].rearrange("p a t -> p (a t)"), start=True, stop=True),
                                 reads=[dKTs[buf], dpad], writes=[dZ], inc=(bi == NQ - 1 and j == 3))
                ev = es_[b2][:, :, 0:w_]
                S.op("act", lambda e: e.activation(out=ev, in_=bview(0, c0, w_), func=AF.Exp, scale=SCALE),
                     reads=[dZ], writes=[des[b2]])
                S.op("dve", lambda e: e.tensor_tensor(out=ev, in0=ev,
                                                      in1=ebF[:, :].rearrange("p (j c) -> p j c", j=4)[:, :, c0:c0 + w_],
                                                      op=ALU.mult), reads=[des[b2], dpad], writes=[des[b2]])
                if kind == "new":
                    S.op("dve", lambda e: e.tensor_tensor(out=ev, in0=ev, in1=mS2[:, :].unsqueeze(1).to_broadcast([128, 4, 128]),
                                                          op=ALU.mult), reads=[des[b2], dpad], writes=[des[b2]])
                S.op("act", lambda e: e.activation(out=Ls[b2][:, :, 0:w_], in_=ev, func=AF.Ln, bias=1.0),
                     reads=[des[b2]], writes=[dLs[b2]])
                for j in range(4):
                    S.op("pe", lambda e, j=j: e.matmul(out=PS[1][:, j * 128 + c0:j * 128 + c0 + w_], lhsT=uincl,
                                                       rhs=Ls[b2][:, j, 0:w_], start=(first and j == 0), stop=False,
                                                       skip_group_check=True),
                         reads=[dLs[b2], dCB], writes=[dC], inc=(j == 3))
                S.op("act", lambda e: e.activation(out=xs2[b2][:, :, 0:w_], in_=bview(1, c0, w_), func=AF.Exp, scale=-1.0),
                     reads=[dC], writes=[dxs[b2]])
                if not last:
                    for j in range(4):
                        S.op("pe", lambda e, j=j: e.matmul(out=PS[1][:, j * 128 + c0:j * 128 + c0 + w_], lhsT=lstrict,
                                                           rhs=Ls[b2][:, j, 0:w_], start=False, stop=False,
                                                           skip_group_check=True),
                             reads=[dLs[b2], dCB], writes=[dC], inc=(j == 3))
                if kind == "new":
                    S.op("dve", lambda e: e.tensor_tensor(out=wsb[:, :, :], in0=ev, in1=xs2[b2][:, :, 0:w_], op=ALU.mult),
                         reads=[des[b2], dxs[b2]], writes=[dws[b2]])
                    for j in range(4):
                        S.op("pe", lambda e, j=j: e.matmul(out=PS[2][:, j * 128:(j + 1) * 128],
                                                           lhsT=vpad[:, j * 128:(j + 1) * 128], rhs=wsb[:, j, :],
                                                           start=(j == 0), stop=False, skip_group_check=True),
                             reads=[dws[b2], dpad], writes=[dOs], inc=(j == 3))
                else:
                    S.op("dve", lambda e: e.tensor_tensor(out=ws[b2][:, :, 0:w_], in0=ev, in1=xs2[b2][:, :, 0:w_], op=ALU.mult),
                         reads=[des[b2], dxs[b2]], writes=[dws[b2]])
                    for bi in range(NQ):
                        b = gi * NQ + bi
                        for j in range(4):
                            S.op("pe", lambda e, bi=bi, b=b, j=j: e.matmul(
                                out=PS[2][:, j * 128 + b * 8:j * 128 + b * 8 + 8], lhsT=Vb[buf][:, bi, j * 128:(j + 1) * 128],
                                rhs=ws[b2][:, j, bi * 8:bi * 8 + 8], start=False, stop=False, skip_group_check=True),
                                 reads=[dws[b2], dVb[buf]], writes=[dOs], inc=(bi == NQ - 1 and j == 3))

            sample_step("new", first=True)
            gcount = 0
            for p in range(NPG - 1, -1, -1):
                for gi in range(NSEQ // NQ):
                    buf = gcount % 2
                    gcount += 1
                    for bi in range(NQ):
                        b = gi * NQ + bi
                        col = b * NPG + p
                        S.dma("pool", lambda e, bi=bi, col=col, buf=buf: e.indirect_dma_start(
                            out=Kst[buf][:, bi, :], out_offset=None, in_=ck,
                            in_offset=bass.IndirectOffsetOnAxis(ap=idx[:, col:col + 1], axis=0)),
                              dKst[buf], "in", extra_reads=[dIdx])
                        S.dma("pool", lambda e, bi=bi, col=col, buf=buf: e.indirect_dma_start(
                            out=Vst[buf][:, bi, :], out_offset=None, in_=cv,
                            in_offset=bass.IndirectOffsetOnAxis(ap=idx[:, col:col + 1], axis=0)),
                              dVst[buf], "in", extra_reads=[dIdx])
                    for bi in range(NQ):
                        bank = 5 + (bi % 2)
                        for jj in range(4):
                            S.op("pe", lambda e, bi=bi, jj=jj, bank=bank, buf=buf: e.transpose(
                                out=PS[bank][:, jj * 128:(jj + 1) * 128], in_=Kst[buf][:, bi, jj * 128:(jj + 1) * 128],
                                identity=ident), reads=[dKst[buf], dCF], writes=[PD[bank]], inc=(jj == 3))
                        if bi % 2 == 0:
                            S.op("dve", lambda e, bi=bi, bank=bank, buf=buf: e.tensor_copy(
                                out=KTs[buf][:, bi, :, :].rearrange("p j t -> p (j t)"), in_=PS[bank][:, :]),
                                 reads=[PD[bank]], writes=[dKTs[buf]])
                        else:
                            S.op("act", lambda e, bi=bi, bank=bank, buf=buf: e.activation(
                                out=KTs[buf][:, bi, :, :].rearrange("p j t -> p (j t)"), in_=PS[bank][:, :], func=AF.Copy),
                                 reads=[PD[bank]], writes=[dKTs[buf]])
                    for hv_ in range(2):
                        S.op("act" if hv_ == 0 else "dve",
                             (lambda e, buf=buf: e.activation(
                                 out=Vb[buf][:, 0:NQ // 2, :].rearrange("p b n -> p (b n)"),
                                 in_=Vst[buf][:, 0:NQ // 2, :].rearrange("p b n -> p (b n)"), func=AF.Copy)) if hv_ == 0 else
                             (lambda e, buf=buf: e.tensor_copy(
                                 out=Vb[buf][:, NQ // 2:NQ, :].rearrange("p b n -> p (b n)"),
                                 in_=Vst[buf][:, NQ // 2:NQ, :].rearrange("p b n -> p (b n)"))),
                             reads=[dVst[buf]], writes=[dVb[buf]])
                    sample_step("page", gi=gi, buf=buf, last=(p == 0), b2=gcount % 2)
            for h2 in range(2):
                S.op("act", lambda e, h2=h2: e.activation(
                    out=mixT[64 * h2:64 * h2 + 64, 0:4, TP:NTOK].rearrange("p j (b t) -> p j b t", t=4),
                    in_=PS[2][64 * h2:64 * h2 + 64, :].rearrange("p (j b a t) -> p j b a t", j=4, b=16, a=2)[:, :, :, h2, :],
                    func=AF.Copy), reads=[dOs], writes=[dMixA[16]])
            S.barrier()

        w_out_v = w_out.rearrange("(k p) n -> p k n", p=128)
        w_up_v = w_up.rearrange("(k p) n -> p k n", p=128)
        w_down_v = w_down.rearrange("(c p) n -> p c n", p=128)
        w_ada_v = w_ada.rearrange("(k p) n -> p k n", p=128)
        gBp = sb([128, 2048], F32)
        gBs = sb([64, 2048], F32)
        dG = dep("gB")
        wst = [sb([128, 8, 256], F32) for _ in range(2)]
        dW = [dep("wst") for _ in range(2)]
        with contextlib.ExitStack() as pg:
            cTp = sb([128, 8, 128], F32, stack=pg)
            cTs = sb([128, 8, 64], F32, stack=pg)
            bgB = sb([128, 2048], F32, stack=pg)
            dbg = dep("bgB")
            dcTx = dep("cTx")
            S.dma("sp", lambda e: e.dma_start(out=bgB[:], in_=bg[0:1, :].to_broadcast([128, 2048])), dbg, "in")
            S.op("dve", lambda e: e.tensor_copy(out=cTp[:], in_=cT[:, :, 16:17].to_broadcast([128, 8, 128])),
                 reads=[dcT], writes=[dcTx])
            S.op("dve", lambda e: e.tensor_copy(out=cTs[:].rearrange("p c (b t) -> p c b t", t=4),
                                                in_=cT[:, :, 0:16].unsqueeze(3).to_broadcast([128, 8, 16, 4])),
                 reads=[dcT], writes=[dcTx])
            for n_ in range(8):
                buf = n_ % 2
                acol = (2048 if n_ < 4 else 5120) + (n_ % 4) * 256
                gcol = n_ * 256
                S.dma("sp", lambda e, acol=acol, buf=buf: e.dma_start(out=wst[buf][:], in_=w_ada_v[:, :, acol:acol + 256]),
                      dW[buf], "in")
                for (lhs, rows, dst, bank) in ((cTp, 128, gBp, 2), (cTs, 64, gBs, 3)):
                    for k in range(8):
                        S.op("pe", lambda e, k=k, lhs=lhs, rows=rows, bank=bank, buf=buf: e.matmul(
                            out=PS[bank][0:rows, 0:256], lhsT=lhs[:, k, :], rhs=wst[buf][:, k, :],
                            start=(k == 0), stop=(k == 7)),
                             reads=[dcTx, dW[buf]], writes=[PD[bank]], inc=(k == 7))
                    S.op("dve", lambda e, rows=rows, dst=dst, bank=bank, gcol=gcol: e.tensor_tensor(
                        out=dst[0:rows, gcol:gcol + 256], in0=PS[bank][0:rows, 0:256], in1=bgB[0:rows, gcol:gcol + 256],
                        op=ALU.add), reads=[PD[bank], dbg], writes=[dG])
            S.barrier()
            chk(6)

        def wst_flat(buf, c):
            return wst[buf][:].rearrange("p k n -> p (k n)").rearrange("p (c n) -> p c n", c=c)

        halves = [list(range(0, 8)), list(range(8, 17))]
        for hi, tiles in enumerate(halves):
            with contextlib.ExitStack() as p3:
                nt = len(tiles)
                ncols = sum(tile_rows(i) for i in tiles)
                col0 = tiles[0] * 128
                x1 = sb([128, nt, D], F32, stack=p3)
                dx1 = [dep("x1") for _ in range(nt)]
                h2T = sb([128, 8, ncols], BF16, stack=p3)
                dh2 = [dep("h2T") for _ in range(nt)]
                with contextlib.ExitStack() as p3a:
                    wo = sb([128, 8, D], BF16, stack=p3a)
                    dwo = [dep("wo") for _ in range(4)]
                    for g in range(4):
                        load_weight_piece(w_out_v[:, :, g * 256:(g + 1) * 256], wo[:, :, g * 256:(g + 1) * 256], wst, dW, g % 2,
                                          dwo[g])
                    xt = [sb([128, D], F32, stack=p3a) for _ in range(2)]
                    dxt = [dep("xt") for _ in range(2)]
                    wk = [(sb([128, D], F32, stack=p3a), sb([128, D], F32, stack=p3a), sb([128, 4], F32, stack=p3a))
                          for _ in range(2)]
                    dwk = [dep("wk") for _ in range(2)]
                    tmp = [sb([128, 512], F32, stack=p3a) for _ in range(2)]
                    dtm = [dep("tmp") for _ in range(2)]
                    def outproj(li):
                        i = tiles[li]
                        T = tile_rows(i)
                        c0, c1 = tile_cols(i)
                        b = li % 2
                        load_x_tile(i, xt[b], dxt[b])
                        gB = gBp if i < 16 else gBs
                        for nh in range(2):
                            bank = nh
                            ns = slice(nh * 512, (nh + 1) * 512)
                            for k in range(8):
                                md = dMixA[i] if k < 4 else dMixB[i]
                                S.op("pe", lambda e, k=k, bank=bank, ns=ns: e.matmul(
                                    out=PS[bank][0:T, :], lhsT=mixT[:, k, c0:c1], rhs=wo[:, k, ns], start=(k == 0), stop=(k == 7)),
                                     reads=[md, dwo[2 * nh], dwo[2 * nh + 1]], writes=[PD[bank]], inc=(k == 7))
                            S.op("dve", lambda e, bank=bank, ns=ns, nh=nh: e.tensor_tensor(
                                out=tmp[nh][0:T, :], in0=PS[bank][0:T, :], in1=gB[0:T, ns], op=ALU.mult),
                                 reads=[PD[bank], dG], writes=[dtm[nh]])
                            S.op("pool", lambda e, ns=ns, nh=nh: e.tensor_tensor(
                                out=x1[0:T, li, ns], in0=tmp[nh][0:T, :], in1=xt[b][0:T, ns], op=ALU.add),
                                 reads=[dtm[nh], dxt[b]], writes=[dx1[li]])

                    def norm2(li):
                        i = tiles[li]
                        T = tile_rows(i)
                        c0, c1 = tile_cols(i)
                        b = li % 2
                        lc0 = c0 - col0
                        norm_transpose(i, x1[0:T, li, :], dx1[li], 2, h2T[:, :, lc0:lc0 + T], dh2[li], wk[b], dwk[b],
                                       (2, 3) if b == 0 else (4, 5))

                    outproj(0)
                    for li in range(nt):
                        if li + 1 < nt:
                            outproj(li + 1)
                        norm2(li)
                    S.barrier()
                    chk(7)
                with contextlib.ExitStack() as p4:
                    wu = [sb([128, 8, 512], BF16, stack=p4) for _ in range(2)]
                    wd = [sb([128, 4, D], BF16, stack=p4) for _ in range(2)]
                    dwu = [dep("wu") for _ in range(2)]
                    dwd = [dep("wd") for _ in range(2)]
                    upT = [sb([128, 4, 512], BF16, stack=p4) for _ in range(2)]
                    dup = [dep("up") for _ in range(2)]
                    rl = [sb([128, 512], F32, stack=p4) for _ in range(2)]
                    drl = [dep("rl") for _ in range(2)]
                    tmp = [sb([128, 512], F32, stack=p4) for _ in range(2)]
                    dtm = [dep("tmp") for _ in range(2)]
                    groups = []
                    li = 0
                    while li < nt:
                        g = [l for l in range(li, min(li + 4, nt)) if tile_rows(tiles[l]) == 128]
                        if not g:
                            g = [li]
                        groups.append(g)
                        li = g[-1] + 1
                    gctr = 0
                    rctr = 0
                    dx1h = [[dep("x1h") for _ in range(2)] for _ in range(nt)]
                    for l_ in range(nt):
                        for nh_ in range(2):
                            dx1h[l_][nh_].w = dx1[l_].w
                    def load_w(E):
                        wb = E % 2
                        for hh in range(2):
                            S.dma("sp", lambda e, hh=hh: e.dma_start(
                                out=wst[0][:], in_=w_up_v[:, :, E * 512 + hh * 256:E * 512 + (hh + 1) * 256]), dW[0], "in")
                            S.op("act", lambda e, hh=hh: e.activation(out=wu[wb][:, :, hh * 256:(hh + 1) * 256], in_=wst[0][:],
                                                                      func=AF.Copy),
                                 reads=[dW[0]], writes=[dwu[wb]])
                            S.dma("sp", lambda e, hh=hh: e.dma_start(
                                out=wst_flat(1, 2), in_=w_down_v[:, E * 4 + hh * 2:E * 4 + (hh + 1) * 2, :]), dW[1], "in")
                            S.op("pool", lambda e, hh=hh: e.tensor_copy(out=wd[wb][:, hh * 2:(hh + 1) * 2, :], in_=wst_flat(1, 2)),
                                 reads=[dW[1]], writes=[dwd[wb]])

                    def up_part(E, g, ub):
                        wb = E % 2
                        gcol0 = tiles[g[0]] * 128 - col0
                        gn = sum(tile_rows(tiles[l]) for l in g)
                        for fc in range(4):
                            bank = fc % 2
                            rb = fc % 2
                            for k in range(8):
                                S.op("pe", lambda e, k=k, fc=fc, bank=bank: e.matmul(
                                    out=PS[bank][:, 0:gn], lhsT=wu[wb][:, k, fc * 128:(fc + 1) * 128],
                                    rhs=h2T[:, k, gcol0:gcol0 + gn], start=(k == 0), stop=(k == 7)),
                                     reads=[dwu[wb]] + [dh2[l] for l in g], writes=[PD[bank]], inc=(k == 7))
                            S.op("act", lambda e, bank=bank, rb=rb: e.activation(out=rl[rb][:, 0:gn], in_=PS[bank][:, 0:gn],
                                                                                 func=AF.Relu),
                                 reads=[PD[bank]], writes=[drl[rb]])
                            S.op("act", lambda e, rb=rb, fc=fc: e.activation(
                                out=upT[ub][:, fc, 0:gn], in_=rl[rb][:, 0:gn], func=AF.Square),
                                 reads=[drl[rb]], writes=[dup[ub]])

                    def down_part(E, g, ub):
                        wb = E % 2
                        gcol0 = tiles[g[0]] * 128 - col0
                        for l in g:
                            i = tiles[l]
                            T = tile_rows(i)
                            lc = tiles[l] * 128 - col0 - gcol0
                            gB = gBp if i < 16 else gBs
                            for nh in range(2):
                                bank = 2 + nh + 2 * (l % 2)
                                ns = slice(nh * 512, (nh + 1) * 512)
                                for fc in range(4):
                                    S.op("pe", lambda e, fc=fc, bank=bank, ns=ns: e.matmul(
                                        out=PS[bank][0:T, :], lhsT=upT[ub][:, fc, lc:lc + T], rhs=wd[wb][:, fc, ns],
                                        start=(fc == 0), stop=(fc == 3)),
                                         reads=[dup[ub], dwd[wb]], writes=[PD[bank]], inc=(fc == 3))
                                S.op("dve", lambda e, bank=bank, nh=nh: e.tensor_tensor(
                                    out=tmp[nh][0:T, :], in0=PS[bank][0:T, :], in1=gB[0:T, 1024 + nh * 512:1024 + (nh + 1) * 512],
                                    op=ALU.mult), reads=[PD[bank], dG], writes=[dtm[nh]])
                                S.op("dve" if nh == 0 else "pool", lambda e, ns=ns, nh=nh: e.tensor_tensor(
                                    out=x1[0:T, l, ns], in0=x1[0:T, l, ns], in1=tmp[nh][0:T, :], op=ALU.add),
                                     reads=[dtm[nh], dx1h[l][nh]], writes=[dx1h[l][nh]])
                            if E == 7:
                                ydst = yp[i * 128:(i + 1) * 128, :] if i < 16 else ys[:, :]
                                for nh in range(2):
                                    S.dma("sp", lambda e, ydst=ydst, nh=nh: e.dma_start(
                                        out=ydst[:, nh * 512:(nh + 1) * 512], in_=x1[0:T, l, nh * 512:(nh + 1) * 512]),
                                          dx1h[l][nh], "out")

                    seq = [(E, g) for E in range(8) for g in groups]
                    load_w(0)
                    up_part(seq[0][0], seq[0][1], 0)
                    for n_, (E, g) in enumerate(seq):
                        if g is groups[0] and E + 1 < 8:
                            load_w(E + 1)
                        if n_ + 1 < len(seq):
                            up_part(seq[n_ + 1][0], seq[n_ + 1][1], (n_ + 1) % 2)
                        down_part(E, g, n_ % 2)
                    S.barrier()
    except _Stop:
        S.finish()
        return nc
    S.finish()
    es.close()
    return nc


def _core_inputs(c, inp, n_phys=None, ck=None, cv=None, pt=None):
    f = np.float32
    d = {}
    d["xp"] = np.ascontiguousarray(inp["x_prompt"][c], dtype=f)
    d["xs"] = np.ascontiguousarray(inp["x_sample"][16 * c:16 * c + 16].reshape(TS, D), dtype=f)
    d["ck"] = ck if ck is not None else inp["cache_k"][0].reshape(-1, 512)
    d["cv"] = cv if cv is not None else inp["cache_v"][0].reshape(-1, 512)
    d["st0"] = np.ascontiguousarray(inp["state_hgrn"][0, 16 * c:16 * c + 16], dtype=f)
    ptc = pt if pt is not None else inp["page_table"][16 * c:16 * c + 16]
    d["pt"] = np.ascontiguousarray(ptc.reshape(1, -1), dtype=np.int32)
    d["cc"] = np.ascontiguousarray(np.concatenate([inp["c_sample"][16 * c:16 * c + 16], inp["c_prompt"][c:c + 1]], 0), dtype=f)
    d["w_ada"] = np.ascontiguousarray(inp["w_ada"][0], dtype=f)
    b_ada = inp["b_ada"][0]
    d["vecs"] = np.ascontiguousarray(np.concatenate([b_ada.reshape(48, 128), inp["norm1_g"][0].reshape(8, 128),
                                                     inp["norm2_g"][0].reshape(8, 128)], 0), dtype=f)
    d["bg"] = np.ascontiguousarray(np.concatenate([b_ada[2048:3072], b_ada[5120:6144]])[None], dtype=f)
    d["w_in"] = np.ascontiguousarray(inp["w_in"][0], dtype=f)
    d["qkg"] = np.ascontiguousarray(np.concatenate([np.tile(inp["q_norm_g"][0], 8), np.tile(inp["k_norm_g"][0], 8),
                                                    np.tile(inp["hg_out_g"][0], 8)])[None], dtype=f)
    d["sbb"] = np.ascontiguousarray(inp["sb_bias"][0][None], dtype=f)
    d["lbl"] = np.ascontiguousarray(inp["hg_lb_logits"].reshape(1, 1024), dtype=f)
    d["w_out"] = np.ascontiguousarray(inp["w_out"][0], dtype=f)
    d["w_up"] = np.ascontiguousarray(inp["w_up"][0], dtype=f)
    d["w_down"] = np.ascontiguousarray(inp["w_down"][0], dtype=f)
    d["cstf"] = _CARR
    d["cstb"] = _BARR
    return d


def _assemble(results):
    y_p = np.stack([r["yp"] for r in results]).astype(np.float32)
    y_s = np.concatenate([r["ys"].reshape(16, 4, D) for r in results]).astype(np.float32)
    k_p = np.stack([r["kp"].reshape(TP, 8, 64) for r in results])[None].astype(np.float32)
    v_p = np.stack([r["vp"].reshape(TP, 8, 64) for r in results])[None].astype(np.float32)
    k_s = np.concatenate([r["ks"].reshape(16, 4, 8, 64) for r in results])[None].astype(np.float32)
    v_s = np.concatenate([r["vs"].reshape(16, 4, 8, 64) for r in results])[None].astype(np.float32)
    s_p = np.stack([r["sp"] for r in results])[None].astype(np.float32)
    s_s = np.concatenate([r["ss"] for r in results])[None].astype(np.float32)
    return (y_p, y_s, k_p, v_p, k_s, v_s, s_p, s_s)


def kernel(**inputs):
    inp = {k: np.asarray(v) for k, v in inputs.items()}
    n_phys = inp["cache_k"].shape[1]
    nc = build(n_phys)
    ck = np.ascontiguousarray(inp["cache_k"][0].reshape(-1, 512), dtype=np.float32)
    cv = np.ascontiguousarray(inp["cache_v"][0].reshape(-1, 512), dtype=np.float32)
    in_maps = [_core_inputs(c, inp, ck=ck, cv=cv) for c in range(NCORES)]
    res = run_bass_kernel_spmd(nc, in_maps, core_ids=list(range(NCORES)))
    return _assemble(res.results)
```
